# Optimizing a Trainium2 kernel written in Bass

```python
import math
import jax, jax.numpy as jnp
from jax import lax
import numpy as np

D_MODEL = 1024
BATCH = 4
SEQ = 8192
DEPTH = 1

HEAD_DIM = 64
DIL_PAIRS = ((128, 1), (512, 4), (2048, 16))
N_GROUPS = len(DIL_PAIRS)
H_A = 8
H_B = 8
DV_B = 2 * HEAD_DIM
D_FF = 4 * D_MODEL
NUM_BUCKETS = 32
T5_MAX_DISTANCE = 128
BLK = 128
LN_EPS = 1e-5
NEG_INF = -1e30
H_TOTAL = N_GROUPS * H_A + H_B
W_A_OUT = H_A * HEAD_DIM
W_B_OUT = H_B * DV_B
COLS_A = 3 * N_GROUPS * H_A * HEAD_DIM
COLS_B_QK = 4 * H_B * HEAD_DIM
COLS_B = COLS_B_QK + H_B * DV_B
COLS_GATE = 2 * D_MODEL
COLS_IN = COLS_A + COLS_B + COLS_GATE
DEEPNORM_ALPHA = (2.0 * DEPTH) ** 0.25
DEEPNORM_BETA = (8.0 * DEPTH) ** -0.25

kernel_name = "hybrid_dilated_diff_attn_gated_deepnorm"


def t5_bucket(dist):
    n = jnp.maximum(dist, 0)
    max_exact = NUM_BUCKETS // 2
    nf = jnp.maximum(n, 1).astype(jnp.float32)
    large = max_exact + (jnp.log(nf / max_exact) / math.log(T5_MAX_DISTANCE / max_exact)
                         * (NUM_BUCKETS - max_exact)).astype(jnp.int32)
    large = jnp.minimum(large, NUM_BUCKETS - 1)
    return jnp.where(n < max_exact, n, large)


def layer_norm(x, g, b):
    xf = x.astype(jnp.float32)
    mu = jnp.mean(xf, axis=-1, keepdims=True)
    var = jnp.mean(jnp.square(xf - mu), axis=-1, keepdims=True)
    return ((xf - mu) * lax.rsqrt(var + LN_EPS) * g + b).astype(x.dtype)


def rms_norm(x, g):
    xf = x.astype(jnp.float32)
    return xf * lax.rsqrt(jnp.mean(jnp.square(xf), axis=-1, keepdims=True) + LN_EPS) * g


def dilated_group_attention(q, k, v, bias_table, dil, n_steps):
    B, S, H, dh = q.shape
    L = S // dil
    nb = -(-L // BLK)
    Lp = nb * BLK

    def to_blocks(t):
        t = t.reshape(B, L, dil, H, dh).transpose(0, 2, 1, 3, 4)
        t = jnp.pad(t, ((0, 0), (0, 0), (0, Lp - L), (0, 0), (0, 0)))
        return t.reshape(B, dil, nb, BLK, H, dh)

    def with_prev(t):
        prev = jnp.pad(t, ((0, 0), (0, 0), (1, 0), (0, 0), (0, 0), (0, 0)))[:, :, :-1]
        return jnp.concatenate([prev, t], axis=3)

    qb = to_blocks(q * (HEAD_DIM ** -0.5))
    kw = with_prev(to_blocks(k))
    vw = with_prev(to_blocks(v))
    s = jnp.einsum('brnqhd,brnkhd->brnhqk', qb, kw).astype(jnp.float32)

    qi = jnp.arange(BLK)[:, None]
    ki = jnp.arange(2 * BLK)[None, :]
    steps = BLK + qi - ki
    bias = bias_table[t5_bucket(steps * dil)].astype(jnp.float32).transpose(2, 0, 1)
    in_band = (steps >= 0) & (steps <= n_steps)
    first_ok = (jnp.arange(nb)[:, None, None] > 0) | (ki >= BLK)[None]
    valid = in_band[None] & first_ok
    s = jnp.where(valid[None, None, :, None], s + bias, NEG_INF)

    m = jnp.max(s, axis=-1, keepdims=True)
    p = jnp.exp(s - m)
    den = jnp.sum(p, axis=-1, keepdims=True)
    o = jnp.einsum('brnhqk,brnkhd->brnqhd', p, vw) / jnp.moveaxis(den, 3, 4)
    lse = jnp.moveaxis((m + jnp.log(den))[..., 0], 3, 4)

    o = o.reshape(B, dil, Lp, H, dh)[:, :, :L].transpose(0, 2, 1, 3, 4).reshape(B, S, H, dh)
    lse = lse.reshape(B, dil, Lp, H)[:, :, :L].transpose(0, 2, 1, 3).reshape(B, S, H)
    return o, lse


def diff_attention(q1, q2, k1, k2, v, bias_table, lam):
    B, S, H, dh = q1.shape
    nb = S // BLK
    kpos = jnp.arange(S)
    scale = HEAD_DIM ** -0.5

    def block(n):
        qpos = n * BLK + jnp.arange(BLK)
        dist = qpos[:, None] - kpos[None, :]
        bias = bias_table[t5_bucket(dist)].astype(jnp.float32).transpose(2, 0, 1)
        causal = dist >= 0

        def attn_map(qf, kf):
            qb = lax.dynamic_slice_in_dim(qf, n * BLK, BLK, axis=1)
            s = jnp.einsum('bqhd,bkhd->bhqk', qb, kf).astype(jnp.float32) * scale + bias
            return jax.nn.softmax(jnp.where(causal, s, NEG_INF), axis=-1)

        a = attn_map(q1, k1) - lam * attn_map(q2, k2)
        return jnp.einsum('bhqk,bkhe->bqhe', a, v)

    out = lax.map(block, jnp.arange(nb))
    return out.transpose(1, 0, 2, 3, 4).reshape(B, S, H, DV_B)


def mixing_sublayer(h, w_in, b_gate, lq1, lk1, lq2, lk2, subln_g, rel_bias,
                    w_proj_a, w_proj_b, w_out, layer_idx):
    B, S, D = h.shape
    proj = h @ w_in

    pa = proj[..., :COLS_A].reshape(B, S, 3, N_GROUPS, H_A, HEAD_DIM)
    outs, lses = [], []
    for g, (win, dil) in enumerate(DIL_PAIRS):
        o, l = dilated_group_attention(pa[:, :, 0, g], pa[:, :, 1, g], pa[:, :, 2, g],
                                       rel_bias[:, g * H_A:(g + 1) * H_A], dil, win // dil)
        outs.append(o)
        lses.append(l)
    wts = jax.nn.softmax(jnp.stack(lses), axis=0)
    o_a = jnp.sum(wts[..., None] * jnp.stack(outs), axis=0)
    y_a = o_a.reshape(B, S, W_A_OUT).astype(h.dtype) @ w_proj_a

    pb = proj[..., COLS_A:COLS_A + COLS_B]
    qk = pb[..., :COLS_B_QK].reshape(B, S, 4, H_B, HEAD_DIM)
    v_b = pb[..., COLS_B_QK:].reshape(B, S, H_B, DV_B)
    lam_init = 0.8 - 0.6 * math.exp(-0.3 * layer_idx)
    lam = (jnp.exp(jnp.sum(lq1.astype(jnp.float32) * lk1.astype(jnp.float32)))
           - jnp.exp(jnp.sum(lq2.astype(jnp.float32) * lk2.astype(jnp.float32))) + lam_init)
    o_b = diff_attention(qk[:, :, 0], qk[:, :, 1], qk[:, :, 2], qk[:, :, 3], v_b,
                         rel_bias[:, N_GROUPS * H_A:], lam)
    o_b = rms_norm(o_b, subln_g) * (1.0 - lam_init)
    y_b = o_b.reshape(B, S, W_B_OUT).astype(h.dtype) @ w_proj_b

    gates = jax.nn.sigmoid(proj[..., COLS_A + COLS_B:] + b_gate).reshape(B, S, 2, D)
    return (gates[:, :, 0] * y_a + gates[:, :, 1] * y_b) @ w_out


def setup_inputs(seed: int = 0) -> dict:
    key = jax.random.key(seed)
    ks = jax.random.split(key, 20)
    nrm = jax.random.normal
    beta = DEEPNORM_BETA
    col_scale = np.ones((COLS_IN,), dtype=np.float32)
    col_scale[2 * N_GROUPS * H_A * HEAD_DIM:COLS_A] = beta
    col_scale[COLS_A + COLS_B_QK:COLS_A + COLS_B] = beta
    return {
        "x": nrm(ks[0], (BATCH, SEQ, D_MODEL), jnp.float32),
        "w_in": nrm(ks[1], (DEPTH, D_MODEL, COLS_IN), jnp.float32) * (D_MODEL ** -0.5) * jnp.asarray(col_scale),
        "b_gate": 0.1 * nrm(ks[2], (DEPTH, COLS_GATE), jnp.float32),
        "lambda_q1": 0.1 * nrm(ks[3], (DEPTH, HEAD_DIM), jnp.float32),
        "lambda_k1": 0.1 * nrm(ks[4], (DEPTH, HEAD_DIM), jnp.float32),
        "lambda_q2": 0.1 * nrm(ks[5], (DEPTH, HEAD_DIM), jnp.float32),
        "lambda_k2": 0.1 * nrm(ks[6], (DEPTH, HEAD_DIM), jnp.float32),
        "subln_g": 1.0 + 0.05 * nrm(ks[7], (DEPTH, DV_B), jnp.float32),
        "rel_bias": 0.1 * nrm(ks[8], (NUM_BUCKETS, H_TOTAL), jnp.float32),
        "w_proj_a": nrm(ks[9], (DEPTH, W_A_OUT, D_MODEL), jnp.float32) * beta * (W_A_OUT ** -0.5),
        "w_proj_b": nrm(ks[10], (DEPTH, W_B_OUT, D_MODEL), jnp.float32) * beta * (W_B_OUT ** -0.5),
        "w_out": nrm(ks[11], (DEPTH, D_MODEL, D_MODEL), jnp.float32) * beta * (D_MODEL ** -0.5),
        "ln1_g": 1.0 + 0.05 * nrm(ks[12], (DEPTH, D_MODEL), jnp.float32),
        "ln1_b": 0.02 * nrm(ks[13], (DEPTH, D_MODEL), jnp.float32),
        "ln2_g": 1.0 + 0.05 * nrm(ks[14], (DEPTH, D_MODEL), jnp.float32),
        "ln2_b": 0.02 * nrm(ks[15], (DEPTH, D_MODEL), jnp.float32),
        "w_mlp1": nrm(ks[16], (DEPTH, D_MODEL, D_FF), jnp.float32) * beta * (D_MODEL ** -0.5),
        "w_mlp2": nrm(ks[17], (DEPTH, D_FF, D_MODEL), jnp.float32) * beta * (D_FF ** -0.5),
    }


def reference(x, w_in, b_gate, lambda_q1, lambda_k1, lambda_q2, lambda_k2, subln_g, rel_bias,
              w_proj_a, w_proj_b, w_out, ln1_g, ln1_b, ln2_g, ln2_b, w_mlp1, w_mlp2):
    h = x
    for l in range(DEPTH):
        mix = mixing_sublayer(h, w_in[l], b_gate[l], lambda_q1[l], lambda_k1[l], lambda_q2[l],
                              lambda_k2[l], subln_g[l], rel_bias, w_proj_a[l], w_proj_b[l],
                              w_out[l], l)
        h = layer_norm(DEEPNORM_ALPHA * h + mix, ln1_g[l], ln1_b[l])
        ff = jnp.square(jax.nn.relu(h @ w_mlp1[l])) @ w_mlp2[l]
        h = layer_norm(DEEPNORM_ALPHA * h + ff, ln2_g[l], ln2_b[l])
    return h
```

```python
import math
from contextlib import ExitStack

import numpy as np

import concourse.bass as bass
import concourse.mybir as mybir
from concourse.bass_utils import run_bass_kernel_spmd

F32 = mybir.dt.float32
BF16 = mybir.dt.bfloat16
AF = mybir.ActivationFunctionType
ALU = mybir.AluOpType
AX = mybir.AxisListType

D_MODEL = 1024
SEQ = 8192
BATCH = 4
NQ = 4096
NK = 8192
NKA = 6144
COLS_IN = 9728
DILS = (1, 4, 16)
ALPHA = 2.0 ** 0.25
LAM_INIT = 0.8 - 0.6 * math.exp(0.0)
LN_EPS = 1e-5
MASKV = -240000.0
ENGS = ("sync", "scalar", "vector", "gpsimd", "tensor")

C_FQ, C_FK, C_AQ, C_AK, C_GT, C_BV, C_AV = 0, 1024, 2048, 3584, 5120, 7168, 8192


class Sem:
    __slots__ = ("h", "v", "name")

    def __init__(self, h, name):
        self.h, self.v, self.name = h, 0, name


class Prog:
    def __init__(self, nc, es):
        self.nc, self.es = nc, es
        self.q = {k: [] for k in ENGS}
        self.waited = {k: {} for k in ENGS}
        self.sems = []
        self.nphase = 0
        self.bar = self.sem("bar")

    def sem(self, name):
        s = Sem(self.es.enter_context(self.nc.semaphore(name)), name)
        self.sems.append(s)
        return s

    def add(self, eng, fn, waits=(), inc=None, dma=False):
        ws = []
        for w in waits:
            if w is None:
                continue
            s, v = w
            if v <= 0:
                continue
            assert v <= s.v, f"deadlock: {eng} waits {s.name}>={v} but only {s.v} scheduled"
            if self.waited[eng].get(s.name, 0) >= v:
                continue
            self.waited[eng][s.name] = v
            ws.append((s, v))
        amt = 16 if dma else 1
        newv = None
        if inc is not None:
            inc.v += amt
            newv = inc.v
        self.q[eng].append((ws, fn, inc, amt))
        return newv

    def end_phase(self, junk):
        waits = [(s, s.v) for s in self.sems if s is not self.bar]
        self.nphase += 1
        self.add("gpsimd", lambda e: e.memset(junk, 0.0), waits=waits, inc=self.bar)
        with self.nc.Block() as block:
            for name in ENGS:
                items = self.q[name]

                def body(e, items=items):
                    for ws, fn, inc, amt in items:
                        for s, v in ws:
                            e.wait_ge(s.h, v)
                        ins = fn(e)
                        if inc is not None:
                            ins.then_inc(inc.h, amt)

                getattr(block, name)(body)
        self.q = {k: [] for k in ENGS}
        for name in ENGS:
            if name == "gpsimd":
                continue
            self.waited[name].pop("bar", None)
        self.phase_wait = (self.bar, self.nphase)

    def pw(self):
        return getattr(self, "phase_wait", None)


def _sb(es, nc, name, shape, dt):
    return es.enter_context(nc.sbuf_tensor(name, shape, dt))


def _ps(es, nc, name, shape, dt):
    return es.enter_context(nc.psum_tensor(name, shape, dt))


def phase1(nc, P, D):
    with ExitStack() as es:
        W = _sb(es, nc, "p1_W", [128, 8, COLS_IN], BF16)
        WF = [_sb(es, nc, f"p1_WF{i}", [128, 8, 128], F32) for i in range(2)]
        XF = _sb(es, nc, "p1_XF", [128, 8, 512], F32)
        XB = [_sb(es, nc, f"p1_XB{i}", [128, 8, 512], BF16) for i in range(2)]
        OST = [_sb(es, nc, f"p1_OST{i}", [128, 512], BF16) for i in range(4)]
        BG = _sb(es, nc, "p1_BG", [128, 16], F32)
        JK = _sb(es, nc, "p1_JK", [128, 1], F32)
        PS = _ps(es, nc, "p1_PS", [128, 4, 512], F32)
        s_xld, s_xcv, s_wcv, s_misc = P.sem("p1xld"), P.sem("p1xcv"), P.sem("p1wcv"), P.sem("p1misc")
        s_wld = [P.sem(f"p1wld{i}") for i in range(2)]
        s_mm = [P.sem(f"p1mm{i}") for i in range(4)]
        s_ev = [P.sem(f"p1ev{i}") for i in range(4)]
        s_st = [P.sem(f"p1st{i}") for i in range(4)]
        xT = D["xT"].rearrange("(kc p) t -> p kc t", p=128)
        wp = D["wp"].rearrange("(kc p) c -> p kc c", p=128)
        pw = P.pw()

        v_bg = P.add("sync", lambda e: e.dma_start(out=BG[:], in_=D["bg"]), waits=[pw], inc=s_misc, dma=True)

        wcv_of = {}
        wl = [0]

        def load_w(cid):
            n = wl[0]
            slot = n % 2
            v = P.add("sync", lambda e: e.dma_start(out=WF[slot][:], in_=wp[:, :, cid * 128:(cid + 1) * 128]),
                      waits=[pw, (s_wcv, n - 1)], inc=s_wld[slot], dma=True)
            P.add("gpsimd", lambda e: e.tensor_copy(out=W[:, :, cid * 128:(cid + 1) * 128], in_=WF[slot][:]),
                  waits=[pw, (s_wld[slot], v)], inc=s_wcv)
            wcv_of[cid] = s_wcv.v
            wl[0] += 1

        worder = (list(range(8, 16)) + list(range(56, 64)) + list(range(28, 40)) + list(range(64, 76))
                  + list(range(0, 8)) + list(range(16, 28)) + list(range(40, 56)))
        wplan = {-1: worder[0:16]}
        for t in range(4):
            wplan[t] = worder[16 + 6 * t:16 + 6 * (t + 1)]
        for t in range(4, 8):
            wplan[t] = worder[40 + 9 * (t - 4):40 + 9 * (t - 3)]

        tile_last = {}

        def load_x(t):
            v = P.add("sync", lambda e: e.dma_start(out=XF[:], in_=xT[:, :, t * 512:(t + 1) * 512]),
                      waits=[pw, (s_xcv, t)], inc=s_xld, dma=True)
            P.add("vector", lambda e: e.tensor_copy(out=XB[t % 2][:], in_=XF[:]),
                  waits=[pw, (s_xld, v)] + (tile_last[t - 2] if t >= 2 else []), inc=s_xcv)

        def jobs_for(t):
            jobs = []
            own = t >= 8
            to = t - 8
            for h in range(8):
                jobs.append(("F", C_FK + h * 128, D["KTB"][h][:, t * 512:(t + 1) * 512], None))
            for cb in range(2):
                for s in range(4):
                    r0 = t * 512 + s * 128
                    jobs.append(("T", C_BV + cb * 512, D["VB"][r0:r0 + 128, cb * 512:(cb + 1) * 512], s))
            if t >= 4:
                ta = t - 4
                for m in range(12):
                    jobs.append(("F", C_AK + m * 128, D["KTA"][m][:, ta * 512:(ta + 1) * 512], None))
                for g in range(3):
                    for s in range(4):
                        r0 = ta * 512 + s * 128
                        jobs.append(("T", C_AV + g * 512, D["VA"][r0:r0 + 128, g * 512:(g + 1) * 512], s))
            if own:
                for h in range(8):
                    jobs.append(("F", C_FQ + h * 128, D["QTB"][h][:, to * 512:(to + 1) * 512], None))
                for m in range(12):
                    jobs.append(("F", C_AQ + m * 128, D["QTA"][m][:, to * 512:(to + 1) * 512], None))
                for i in range(16):
                    jobs.append(("G", C_GT + i * 128, D["SG"][i][:, to * 512:(to + 1) * 512], i))
            return jobs

        for cid in wplan[-1]:
            load_w(cid)
        load_x(0)
        jc = 0
        for t in range(16):
            jobs = jobs_for(t)
            half = len(jobs) // 2
            for ji, (kind, col, dst, aux) in enumerate(jobs):
                if ji == half:
                    if t + 1 < 16:
                        load_x(t + 1)
                    for cid in wplan.get(t, []):
                        load_w(cid)
                slot = jc % 4
                k = jc // 4
                xb = XB[t % 2]
                if kind == "T":
                    wneed = max(wcv_of[col // 128 + i] for i in range(4))
                else:
                    wneed = wcv_of[col // 128]

                def mm(e, kind=kind, col=col, aux=aux, xb=xb, slot=slot):
                    ins = None
                    for kc in range(8):
                        if kind == "T":
                            ins = e.matmul(PS[:, slot, :], lhsT=xb[:, kc, aux * 128:(aux + 1) * 128],
                                           rhs=W[:, kc, col:col + 512], start=(kc == 0), stop=(kc == 7))
                        else:
                            ins = e.matmul(PS[:, slot, :], lhsT=W[:, kc, col:col + 128],
                                           rhs=xb[:, kc, :], start=(kc == 0), stop=(kc == 7))
                    return ins

                P.add("tensor", mm, waits=[pw, (s_xcv, t + 1), (s_wcv, wneed), (s_ev[slot], k)], inc=s_mm[slot])
                evw = [pw, (s_mm[slot], k + 1), (s_st[slot], 16 * k)]
                if kind == "G":
                    P.add("scalar", lambda e, slot=slot, aux=aux: e.activation(
                        out=OST[slot][:], in_=PS[:, slot, :], func=AF.Sigmoid, bias=BG[:, aux:aux + 1], scale=1.0),
                        waits=evw + [(s_misc, v_bg)], inc=s_ev[slot])
                elif jc % 2 == 0:
                    P.add("vector", lambda e, slot=slot: e.tensor_copy(out=OST[slot][:], in_=PS[:, slot, :]),
                          waits=evw, inc=s_ev[slot])
                else:
                    P.add("scalar", lambda e, slot=slot: e.copy(out=OST[slot][:], in_=PS[:, slot, :]),
                          waits=evw, inc=s_ev[slot])
                P.add("gpsimd", lambda e, slot=slot, dst=dst: e.dma_start(out=dst, in_=OST[slot][:]),
                      waits=[pw, (s_ev[slot], k + 1)], inc=s_st[slot], dma=True)
                jc += 1
            tile_last[t] = [(s_mm[s], s_mm[s].v) for s in range(4)]
        P.end_phase(JK[:])


def phase2(nc, P, D):
    with ExitStack() as es:
        QA = [_sb(es, nc, f"p2_QA{i}", [128, NQ], BF16) for i in range(2)]
        KA = [_sb(es, nc, f"p2_KA{i}", [128, NKA], BF16) for i in range(2)]
        VS = [_sb(es, nc, f"p2_VS{i}", [128, 48, 128], BF16) for i in range(2)]
        GF = [_sb(es, nc, f"p2_GF{i}", [128, 2, 256], F32) for i in range(2)]
        BTF = _sb(es, nc, "p2_BTF", [128, 2, 256], F32)
        BT = [_sb(es, nc, f"p2_BT{i}", [128, 2, 2, 256], BF16) for i in range(2)]
        MA = _sb(es, nc, "p2_MA", [128, 256], F32)
        IDF = _sb(es, nc, "p2_IDF", [128, 128], F32)
        IDB = _sb(es, nc, "p2_IDB", [128, 128], BF16)
        PVF = _sb(es, nc, "p2_PVF", [128, 64], F32)
        PV64 = _sb(es, nc, "p2_PV64", [128, 64], BF16)
        ONE64 = _sb(es, nc, "p2_ONE64", [128, 64], BF16)
        PT = [_sb(es, nc, f"p2_PT{i}", [128, 2, 256], BF16) for i in range(2)]
        NACC = _sb(es, nc, "p2_NACC", [128, NQ], F32)
        DACC = _sb(es, nc, "p2_DACC", [128, NQ], F32)
        OAS = _sb(es, nc, "p2_OAS", [128, NQ], BF16)
        JK = _sb(es, nc, "p2_JK", [128, 1], F32)
        S = _ps(es, nc, "p2_S", [128, 4, 512], F32)
        ND = _ps(es, nc, "p2_ND", [128, 4, 512], F32)
        s_c = P.sem("p2c")
        s_ld = [P.sem(f"p2ld{i}") for i in range(2)]
        s_bt = P.sem("p2bt")
        s_qk = [P.sem(f"p2qk{i}") for i in range(2)]
        s_ex = [P.sem(f"p2ex{i}") for i in range(2)]
        s_pv = [P.sem(f"p2pv{i}") for i in range(2)]
        s_nf = [P.sem(f"p2nf{i}") for i in range(2)]
        s_dv = P.sem("p2dv")
        s_fin = P.sem("p2fin")
        s_st = P.sem("p2st")
        pw = P.pw()

        vc = P.add("sync", lambda e: e.dma_start(out=MA[:], in_=D["ma"]), waits=[pw], inc=s_c, dma=True)
        vc = P.add("sync", lambda e: e.dma_start(out=IDF[:], in_=D["ident"]), waits=[pw], inc=s_c, dma=True)
        vc = P.add("sync", lambda e: e.dma_start(out=PVF[:], in_=D["pvalid"]), waits=[pw], inc=s_c, dma=True)
        P.add("vector", lambda e: e.tensor_copy(out=IDB[:], in_=IDF[:]), waits=[pw, (s_c, vc)], inc=s_c)
        P.add("vector", lambda e: e.tensor_copy(out=PV64[:], in_=PVF[:]), waits=[pw, (s_c, vc)], inc=s_c)
        vconst = P.add("vector", lambda e: e.memset(ONE64[:], 1.0), waits=[pw], inc=s_c)

        passes = [(hp, g) for hp in range(4) for g in range(3)]
        pass_last_pe = {}
        ld_val = {}
        bt_val = {}

        def sched_loads(pi):
            hp, g = passes[pi]
            d = DILS[g]
            slot = pi % 2
            nb = 48 // d
            w = [pw] + (pass_last_pe[pi - 2] if pi >= 2 else [])
            P.add("sync", lambda e: e.dma_start(out=QA[slot][:], in_=D["QTA"][g * 4 + hp]), waits=w, inc=s_ld[slot], dma=True)
            P.add("sync", lambda e: e.dma_start(out=KA[slot][:], in_=D["KTA"][g * 4 + hp]), waits=w, inc=s_ld[slot], dma=True)
            vav = D["VA"].rearrange("(n i r) c -> r i n c", i=128, r=d)
            c0 = g * 512 + hp * 128
            for r in range(d):
                P.add("sync", lambda e, r=r: e.dma_start(out=VS[slot][:, r * nb:(r + 1) * nb, :],
                                                       in_=vav[r][:, :, c0:c0 + 128]),
                      waits=w, inc=s_ld[slot], dma=True)
            gh0 = g * 8 + 2 * hp
            v = P.add("sync", lambda e: e.dma_start(out=GF[slot][:], in_=D["ga"][gh0:gh0 + 2].rearrange("h i c -> i h c")),
                      waits=w, inc=s_ld[slot], dma=True)
            ld_val[pi] = v
            wb = [pw, (s_ld[slot], v), (s_c, vconst)] + (pass_last_pe[pi - 2] if pi >= 2 else [])
            for hh in range(2):
                v1 = P.add("vector", lambda e, hh=hh: e.scalar_tensor_tensor(
                    out=BTF[:, hh, :], in0=GF[slot][:, hh, :], scalar=8.0, in1=MA[:], op0=ALU.mult, op1=ALU.add),
                    waits=wb + [(s_bt, s_bt.v)], inc=s_bt)
                v2 = P.add("vector", lambda e, hh=hh: e.tensor_copy(out=BT[slot][:, hh, 0, :], in_=BTF[:, hh, :]),
                           waits=[(s_bt, v1)], inc=s_bt)
                v3 = P.add("vector", lambda e, hh=hh: e.tensor_tensor(
                    out=BT[slot][:, hh, 1, :], in0=BTF[:, hh, :], in1=BT[slot][:, hh, 0, :], op=ALU.subtract),
                    waits=[(s_bt, v2)], inc=s_bt)
            bt_val[pi] = s_bt.v

        sched_loads(0)
        cn = {"qi": 0, "gi": 0, "fin": 0}

        def run_pass(pi, hp, g):
            d = DILS[g]
            slot = pi % 2
            nb = 48 // d
            pblk = 16 // d
            nqb = 32 // d
            gsz = min(4, nqb)
            if pi + 1 < len(passes):
                sched_loads(pi + 1)
            qblocks = [(r, m) for r in range(d) for m in range(nqb)]
            pend = None
            pass_dv_start = s_dv.v

            def emit_pv(info):
                (qslot, kq, r, m, gslot, kg, first, last, grp_m0) = info

                def pv(e):
                    ins = None
                    for hh in range(2):
                        for ch in range(2):
                            n = m + pblk - 1 + ch
                            blk = r * nb + n
                            col = (m - grp_m0) * 128
                            ins = e.matmul(ND[hh * 64:(hh + 1) * 64, 2 * gslot, col:col + 128],
                                           lhsT=VS[slot][:, blk, hh * 64:(hh + 1) * 64],
                                           rhs=PT[qslot][:, hh, ch * 128:(ch + 1) * 128],
                                           start=(ch == 0), stop=(ch == 1))
                            vt = PV64 if n < pblk else ONE64
                            ins = e.matmul(ND[hh * 64:(hh + 1) * 64, 2 * gslot + 1, col:col + 128],
                                           lhsT=vt[:, :], rhs=PT[qslot][:, hh, ch * 128:(ch + 1) * 128],
                                           start=(ch == 0), stop=(ch == 1))
                    return ins

                w = [(s_ex[qslot], kq + 1)]
                if first:
                    w.append((s_nf[gslot], kg))
                vpv = P.add("tensor", pv, waits=w, inc=s_pv[qslot])
                if last:
                    tok0 = d * 128 * grp_m0 + r
                    ncol = gsz * 128
                    sl = slice(tok0, tok0 + (ncol - 1) * d + 1, d)
                    wd = [(s_pv[qslot], vpv)]
                    if g > 0:
                        wd += [(s_dv, pass_dv_start), (s_nf[0], nf_start[0]), (s_nf[1], nf_start[1])]
                    else:
                        wd += [(s_fin, cn["fin"])]
                    if g == 0:
                        P.add("vector", lambda e: e.tensor_copy(out=NACC[:, sl], in_=ND[:, 2 * gslot, 0:ncol]),
                              waits=wd, inc=s_dv)
                        P.add("vector", lambda e: e.tensor_copy(out=DACC[:, sl], in_=ND[:, 2 * gslot + 1, 0:ncol]),
                              waits=wd, inc=s_nf[gslot])
                    else:
                        P.add("vector", lambda e: e.tensor_tensor(out=NACC[:, sl], in0=ND[:, 2 * gslot, 0:ncol],
                                                                  in1=NACC[:, sl], op=ALU.add), waits=wd, inc=s_dv)
                        P.add("vector", lambda e: e.tensor_tensor(out=DACC[:, sl], in0=ND[:, 2 * gslot + 1, 0:ncol],
                                                                  in1=DACC[:, sl], op=ALU.add), waits=wd, inc=s_nf[gslot])

            nf_start = [s_nf[0].v, s_nf[1].v]
            for bi, (r, m) in enumerate(qblocks):
                qslot = cn["qi"] % 2
                kq = cn["qi"] // 2
                grp_m0 = (m // gsz) * gsz
                first = (m % gsz == 0)
                last = (m % gsz == gsz - 1)
                if first:
                    gslot = cn["gi"] % 2
                    kg = cn["gi"] // 2
                    cn["gi"] += 1
                qs = slice(d * 128 * m + r, d * 128 * m + r + 127 * d + 1, d)

                def qk(e, r=r, m=m, qslot=qslot, qs=qs):
                    ins = None
                    for hh in range(2):
                        bank = 2 * qslot + hh
                        e.matmul(S[:, bank, 0:256], lhsT=IDB[:], rhs=BT[slot][:, hh, 0, :], start=True, stop=False)
                        e.matmul(S[:, bank, 0:256], lhsT=IDB[:], rhs=BT[slot][:, hh, 1, :], start=False, stop=False)
                        for ch in range(2):
                            n = m + pblk - 1 + ch
                            ks = slice(d * 128 * n + r, d * 128 * n + r + 127 * d + 1, d)
                            ins = e.matmul(S[:, bank, ch * 128:(ch + 1) * 128],
                                           lhsT=KA[slot][hh * 64:(hh + 1) * 64, ks],
                                           rhs=QA[slot][hh * 64:(hh + 1) * 64, qs],
                                           start=False, stop=(ch == 1), skip_group_check=True)
                    return ins

                P.add("tensor", qk, waits=[pw, (s_ld[slot], ld_val[pi]), (s_bt, bt_val[pi]), (s_c, vconst), (s_ex[qslot], kq)],
                      inc=s_qk[qslot])
                P.add("scalar", lambda e, qslot=qslot: e.activation(
                    out=PT[qslot][:], in_=S[:, 2 * qslot:2 * qslot + 2, 0:256], func=AF.Exp, scale=0.125),
                    waits=[pw, (s_qk[qslot], kq + 1), (s_pv[qslot], kq)], inc=s_ex[qslot])
                if pend is not None:
                    emit_pv(pend)
                pend = (qslot, kq, r, m, gslot, kg, first, last, grp_m0)
                cn["qi"] += 1
            emit_pv(pend)
            pass_last_pe[pi] = [(s_pv[0], s_pv[0].v), (s_pv[1], s_pv[1].v)]
            if g == 2:
                wd = [(s_dv, s_dv.v), (s_nf[0], s_nf[0].v), (s_nf[1], s_nf[1].v)]
                v1 = P.add("vector", lambda e: e.reciprocal(out=DACC[:], in_=DACC[:]), waits=wd, inc=s_fin)
                v2 = P.add("vector", lambda e: e.tensor_tensor(out=OAS[:], in0=NACC[:], in1=DACC[:], op=ALU.mult),
                           waits=[(s_fin, v1), (s_st, 16 * hp)], inc=s_fin)
                cn["fin"] = v2
                P.add("gpsimd", lambda e, hp=hp: e.dma_start(out=D["OAT"][hp], in_=OAS[:]),
                      waits=[pw, (s_fin, v2)], inc=s_st, dma=True)
        for pi, (hp, g) in enumerate(passes):
            run_pass(pi, hp, g)
        P.end_phase(JK[:])


def phase3(nc, P, D):
    with ExitStack() as es:
        KT = [_sb(es, nc, f"p3_KT{i}", [128, NK], BF16) for i in range(2)]
        QT = [_sb(es, nc, f"p3_QT{i}", [128, NQ], BF16) for i in range(2)]
        VH = [_sb(es, nc, f"p3_VH{i}", [128, 64, 129], BF16) for i in range(2)]
        GB = _sb(es, nc, "p3_GB", [128, 8, 256], F32)
        CB = _sb(es, nc, "p3_CB", [128, 8], F32)
        MB = _sb(es, nc, "p3_MB", [128, 256], F32)
        BTF = _sb(es, nc, "p3_BTF", [128, 256], F32)
        BT = _sb(es, nc, "p3_BT", [128, 8, 2, 256], BF16)
        IDF = _sb(es, nc, "p3_IDF", [128, 128], F32)
        IDB = _sb(es, nc, "p3_IDB", [128, 128], BF16)
        ZB = _sb(es, nc, "p3_ZB", [128, 512], BF16)
        PVF = _sb(es, nc, "p3_PVF", [128, 64], F32)
        LAMI = _sb(es, nc, "p3_LAMI", [128, 4, 64], F32)
        LPR = _sb(es, nc, "p3_LPR", [128, 2, 64], F32)
        LS = _sb(es, nc, "p3_LS", [128, 2], F32)
        LE = _sb(es, nc, "p3_LE", [128, 2], F32)
        NLAM = _sb(es, nc, "p3_NLAM", [128, 1], F32)
        GS = _sb(es, nc, "p3_GS", [128, 128], F32)
        EPSB = _sb(es, nc, "p3_EPSB", [128, 1], F32)
        PT = [_sb(es, nc, f"p3_PT{i}", [128, 2, 512], BF16) for i in range(2)]
        EP = _sb(es, nc, "p3_EP", [128, 3, 512], F32)
        RD = _sb(es, nc, "p3_RD", [128, 8], F32)
        TT = _sb(es, nc, "p3_TT", [128, 128], F32)
        OO = _sb(es, nc, "p3_OO", [128, 4, 128], F32)
        SQ = _sb(es, nc, "p3_SQ", [128, 128], F32)
        SSQ = _sb(es, nc, "p3_SSQ", [128, 4], F32)
        LNV = _sb(es, nc, "p3_LNV", [128, 4], F32)
        RSTD = _sb(es, nc, "p3_RSTD", [128, 4], F32)
        ON = _sb(es, nc, "p3_ON", [128, 4, 128], BF16)
        OBS = [_sb(es, nc, f"p3_OBS{i}", [128, 512], BF16) for i in range(2)]
        JK = _sb(es, nc, "p3_JK", [128, 1], F32)
        S = _ps(es, nc, "p3_S", [128, 4, 512], F32)
        ACC = _ps(es, nc, "p3_ACC", [128, 3, 512], F32)
        TP = _ps(es, nc, "p3_TP", [128, 4, 128], BF16)
        s_c = P.sem("p3c")
        s_ld = [P.sem(f"p3ld{i}") for i in range(2)]
        s_qk = [P.sem(f"p3qk{i}") for i in range(2)]
        s_ex = [P.sem(f"p3ex{i}") for i in range(2)]
        s_pv = [P.sem(f"p3pv{i}") for i in range(2)]
        s_af = P.sem("p3af")
        s_ep = P.sem("p3ep")
        s_ea = P.sem("p3ea")
        s_tp = P.sem("p3tp")
        s_tc = P.sem("p3tc")
        s_st = [P.sem(f"p3st{i}") for i in range(2)]
        pw = P.pw()

        def accap(m, j):
            idx = m * 4 + j
            return idx // 3, (idx % 3) * 170

        for (dst, src) in ((GB, D["gb"].rearrange("h i c -> i h c")), (CB, D["cb"]), (MB, D["mb"]), (IDF, D["ident"]),
                           (PVF, D["pvalid"]), (LAMI, D["lam"]), (GS, D["gs"])):
            vc = P.add("sync", lambda e, dst=dst, src=src: e.dma_start(out=dst[:], in_=src), waits=[pw], inc=s_c, dma=True)
        w0 = [pw, (s_c, vc)]
        P.add("vector", lambda e: e.tensor_copy(out=IDB[:], in_=IDF[:]), waits=w0, inc=s_c)
        P.add("vector", lambda e: e.memset(ZB[:], 0.0), waits=w0, inc=s_c)
        P.add("vector", lambda e: e.memset(EPSB[:], LN_EPS), waits=w0, inc=s_c)
        v = P.add("vector", lambda e: e.tensor_scalar(out=GS[:], in0=GS[:], scalar1=1.0 - LAM_INIT, scalar2=None, op0=ALU.mult),
                  waits=w0, inc=s_c)
        for i in range(2):
            P.add("vector", lambda e, i=i: e.memset(VH[i][:, 32:64, 128:129], 1.0), waits=w0, inc=s_c)
            P.add("vector", lambda e, i=i: e.tensor_copy(out=VH[i][:, 0:32, 128:129],
                                                         in_=PVF[:, 0:32].rearrange("p (a b) -> p a b", b=1)),
                  waits=w0, inc=s_c)
        v = P.add("vector", lambda e: e.tensor_tensor(out=LPR[:], in0=LAMI[:, 0:2, :], in1=LAMI[:, 2:4, :], op=ALU.mult),
                  waits=w0, inc=s_c)
        v = P.add("vector", lambda e: e.reduce_sum(out=LS[:], in_=LPR[:], axis=AX.X), waits=[(s_c, v)], inc=s_c)
        v = P.add("scalar", lambda e: e.activation(out=LE[:], in_=LS[:], func=AF.Exp), waits=[pw, (s_c, v)], inc=s_c)
        v = P.add("vector", lambda e: e.tensor_tensor(out=NLAM[:], in0=LE[:, 1:2], in1=LE[:, 0:1], op=ALU.subtract),
                  waits=[(s_c, v)], inc=s_c)
        v = P.add("vector", lambda e: e.tensor_scalar(out=NLAM[:], in0=NLAM[:], scalar1=-LAM_INIT, scalar2=None, op0=ALU.add),
                  waits=[(s_c, v)], inc=s_c)
        for h in range(8):
            v = P.add("vector", lambda e, h=h: e.tensor_scalar(out=BTF[:], in0=GB[:, h, :], scalar1=CB[:, h:h + 1], scalar2=8.0,
                                                               op0=ALU.subtract, op1=ALU.mult), waits=[(s_c, v)], inc=s_c)
            v = P.add("vector", lambda e: e.tensor_tensor(out=BTF[:], in0=BTF[:], in1=MB[:], op=ALU.add), waits=[(s_c, v)], inc=s_c)
            v = P.add("vector", lambda e, h=h: e.tensor_copy(out=BT[:, h, 0, :], in_=BTF[:]), waits=[(s_c, v)], inc=s_c)
            v = P.add("vector", lambda e, h=h: e.tensor_tensor(out=BT[:, h, 1, :], in0=BTF[:], in1=BT[:, h, 0, :], op=ALU.subtract),
                      waits=[(s_c, v)], inc=s_c)
        vconst = v

        head_last_pe = {}
        ld_val = {}
        vbv = D["VB"].rearrange("(kb p) c -> p kb c", p=128)

        def sched_loads(h):
            slot = h % 2
            w = [pw, (s_c, vconst)] + (head_last_pe[h - 2] if h >= 2 else [])
            P.add("sync", lambda e: e.dma_start(out=KT[slot][:], in_=D["KTB"][h]), waits=w, inc=s_ld[slot], dma=True)
            P.add("sync", lambda e: e.dma_start(out=QT[slot][:], in_=D["QTB"][h]), waits=w, inc=s_ld[slot], dma=True)
            for q in range(4):
                v = P.add("sync", lambda e, q=q: e.dma_start(out=VH[slot][:, q * 16:(q + 1) * 16, 0:128],
                                                           in_=vbv[:, q * 16:(q + 1) * 16, h * 128:(h + 1) * 128]),
                          waits=w, inc=s_ld[slot], dma=True)
            ld_val[h] = v

        deferred = []
        ui = [0]

        def flush(force=False):
            while deferred and (force or deferred[0][0] <= ui[0]):
                _, fn = deferred.pop(0)
                fn()

        sched_loads(0)
        cn = {"nqt": 0}

        def run_qt(h, qt):
                hs = h % 2
                base = 32 + 4 * qt
                nkb = base + 4
                pend = None
                kq_tile = cn["nqt"]

                def emit_pv(info):
                    (slot, ku, kb, jmin, firstu, lastu) = info

                    def pv(e):
                        ins = None
                        if firstu:
                            for b in range(3):
                                e.matmul(ACC[:, b, :], lhsT=ZB[:, 0:128], rhs=ZB[:, :], start=True, stop=True,
                                         skip_group_check=True)
                        for m in range(2):
                            for j in range(jmin, 4):
                                b, off = accap(m, j)
                                ins = e.matmul(ACC[:, b, off:off + 129], lhsT=PT[slot][:, m, j * 128:(j + 1) * 128],
                                               rhs=VH[hs][:, kb, :], start=False, stop=lastu, skip_group_check=True)
                        return ins

                    w = [(s_ex[slot], ku + 1)]
                    if firstu:
                        w.append((s_af, kq_tile))
                    return P.add("tensor", pv, waits=w, inc=s_pv[slot])

                vlast = None
                for kb in range(nkb):
                    u = ui[0]
                    slot = u % 2
                    ku = u // 2
                    jd = kb - base
                    c0 = max(jd, 0) * 128
                    jmin = max(jd, 0)

                    def qk(e, kb=kb, jd=jd, c0=c0, slot=slot, qt=qt):
                        ins = None
                        for m in range(2):
                            bank = 2 * slot + m
                            near = jd >= -1
                            if near:
                                if jd == -1:
                                    oc, bc, n = 0, 128, 128
                                elif jd == 3:
                                    oc, bc, n = 384, 0, 128
                                else:
                                    oc, bc, n = jd * 128, 0, 256
                                e.matmul(S[:, bank, oc:oc + n], lhsT=IDB[:], rhs=BT[:, h, 0, bc:bc + n], start=True, stop=False)
                                e.matmul(S[:, bank, oc:oc + n], lhsT=IDB[:], rhs=BT[:, h, 1, bc:bc + n], start=False, stop=False)
                            ins = e.matmul(S[:, bank, c0:512], lhsT=KT[hs][m * 64:(m + 1) * 64, kb * 128:(kb + 1) * 128],
                                           rhs=QT[hs][m * 64:(m + 1) * 64, qt * 512 + c0:(qt + 1) * 512],
                                           start=(not near), stop=True, skip_group_check=True)
                        return ins

                    P.add("tensor", qk, waits=[pw, (s_ld[hs], ld_val[h]), (s_c, vconst), (s_ex[slot], ku)], inc=s_qk[slot])
                    P.add("scalar", lambda e, slot=slot, c0=c0: e.activation(
                        out=PT[slot][:, :, c0:512], in_=S[:, 2 * slot:2 * slot + 2, c0:512], func=AF.Exp, scale=0.125),
                        waits=[pw, (s_qk[slot], ku + 1), (s_pv[slot], ku)], inc=s_ex[slot])
                    if pend is not None:
                        emit_pv(pend)
                    pend = (slot, ku, kb, jmin, kb == 0, kb == nkb - 1)
                    ui[0] += 1
                    flush()
                vlast = emit_pv(pend)
                last_slot = pend[0]
                cn["nqt"] += 1
                nqt = cn["nqt"]

                def stage_a(last_slot=last_slot, vlast=vlast, h=h, qt=qt, k=nqt):
                    v = P.add("vector", lambda e: e.tensor_copy(out=EP[:], in_=ACC[:]),
                              waits=[(s_pv[last_slot], vlast), (s_ep, s_ep.v)], inc=s_af)
                    we = [(s_af, v)]
                    for m in range(2):
                        for j in range(4):
                            b, off = accap(m, j)
                            idx = m * 4 + j
                            P.add("vector", lambda e, b=b, off=off, idx=idx: e.reciprocal(
                                out=RD[:, idx:idx + 1], in_=EP[:, b, off + 128:off + 129]), waits=we, inc=s_ep)
                    v = P.add("vector", lambda e: e.tensor_scalar(out=RD[:, 4:8], in0=RD[:, 4:8], scalar1=NLAM[:, 0:1], scalar2=None,
                                                                  op0=ALU.mult), waits=[(s_ep, s_ep.v)], inc=s_ep)
                    for j in range(4):
                        b1, o1 = accap(0, j)
                        b2, o2 = accap(1, j)
                        v = P.add("vector", lambda e, b2=b2, o2=o2, j=j: e.tensor_scalar(
                            out=TT[:], in0=EP[:, b2, o2:o2 + 128], scalar1=RD[:, 4 + j:5 + j], scalar2=None, op0=ALU.mult),
                            waits=[(s_ep, v)], inc=s_ep)
                        v = P.add("vector", lambda e, b1=b1, o1=o1, j=j: e.scalar_tensor_tensor(
                            out=OO[:, j, :], in0=EP[:, b1, o1:o1 + 128], scalar=RD[:, j:j + 1], in1=TT[:],
                            op0=ALU.mult, op1=ALU.add), waits=[(s_ep, v)], inc=s_ep)
                        v = P.add("vector", lambda e, j=j: e.scalar_tensor_tensor(
                            out=SQ[:], in0=OO[:, j, :], scalar=1.0, in1=OO[:, j, :], op0=ALU.mult, op1=ALU.mult,
                            accum_out=SSQ[:, j:j + 1]), waits=[(s_ep, v)], inc=s_ep)
                    return v

                va = stage_a()

                def stage_bc(va=va):
                    v = P.add("scalar", lambda e: e.activation(out=LNV[:], in_=SSQ[:], func=AF.Ln, bias=EPSB[:, 0:1], scale=1.0 / 128.0),
                              waits=[(s_ep, va), (s_ea, s_ea.v)], inc=s_ea)
                    v = P.add("scalar", lambda e: e.activation(out=RSTD[:], in_=LNV[:], func=AF.Exp, scale=-0.5),
                              waits=[(s_ea, v)], inc=s_ea)
                    vv = None
                    for j in range(4):
                        vv = P.add("vector", lambda e, j=j: e.scalar_tensor_tensor(
                            out=ON[:, j, :], in0=OO[:, j, :], scalar=RSTD[:, j:j + 1], in1=GS[:], op0=ALU.mult, op1=ALU.mult),
                            waits=[(s_ea, v), (s_tp, s_tp.v)], inc=s_ep)
                    return vv

                def stage_de(vc, h=h, qt=qt, k=nqt):
                    def tp(e):
                        ins = None
                        for j in range(4):
                            ins = e.transpose(out=TP[:, j, :], in_=ON[:, j, :], identity=IDB[:])
                        return ins
                    v = P.add("tensor", tp, waits=[(s_ep, vc), (s_tc, s_tc.v)], inc=s_tp)
                    os_ = (k - 1) % 2
                    v2 = P.add("vector", lambda e: e.tensor_copy(out=OBS[os_][:].rearrange("p (a b) -> p a b", a=4), in_=TP[:]),
                               waits=[(s_tp, v), (s_st[os_], 16 * ((k - 1) // 2))], inc=s_tc)
                    P.add("gpsimd", lambda e: e.dma_start(out=D["OBT"][h][:, qt * 512:(qt + 1) * 512], in_=OBS[os_][:]),
                          waits=[pw, (s_tc, v2)], inc=s_st[os_], dma=True)

                def chain(stage_bc=stage_bc, stage_de=stage_de):
                    vc = stage_bc()
                    deferred.append((ui[0] + 3, lambda: stage_de(vc)))

                deferred.append((ui[0] + 3, chain))
        for h in range(8):
            if h + 1 < 8:
                sched_loads(h + 1)
            for qt in range(8):
                run_qt(h, qt)
            head_last_pe[h] = [(s_pv[0], s_pv[0].v), (s_pv[1], s_pv[1].v)]
        flush(force=True)
        flush(force=True)
        P.end_phase(JK[:])


def layer_norm_ops(P, V, ST6, MV, STD, RSTD, TMP, OUT, LG, LB, EPSB, sem, wait_in):
    v = None
    for hf in range(2):
        v = P.add("vector", lambda e, hf=hf: e.bn_stats(out=ST6[:, hf * 6:(hf + 1) * 6], in_=V[:, hf * 512:(hf + 1) * 512]),
                  waits=wait_in + [(sem, sem.v)], inc=sem)
    v = P.add("vector", lambda e: e.bn_aggr(out=MV[:], in_=ST6[:]), waits=[(sem, v)], inc=sem)
    v = P.add("scalar", lambda e: e.activation(out=STD[:], in_=MV[:, 1:2], func=AF.Sqrt, bias=EPSB[:, 0:1], scale=1.0),
              waits=[(sem, v)], inc=sem)
    v = P.add("vector", lambda e: e.reciprocal(out=RSTD[:], in_=STD[:]), waits=[(sem, v)], inc=sem)
    v = P.add("vector", lambda e: e.tensor_scalar(out=TMP[:], in0=V[:], scalar1=MV[:, 0:1], scalar2=RSTD[:, 0:1],
                                                  op0=ALU.subtract, op1=ALU.mult), waits=[(sem, v)], inc=sem)
    v = P.add("gpsimd", lambda e: e.tensor_tensor(out=TMP[:], in0=TMP[:], in1=LG[:], op=ALU.mult), waits=[(sem, v)], inc=sem)
    v = P.add("gpsimd", lambda e: e.tensor_tensor(out=OUT, in0=TMP[:], in1=LB[:], op=ALU.add), waits=[(sem, v)], inc=sem)
    return v


def load_cast_weight(P, pw, dst_chunks, src_chunks, WF, s_wld, s_wcv, cnt):
    for dst, src in zip(dst_chunks, src_chunks):
        n = cnt[0]
        slot = n % 2
        v = P.add("sync", lambda e, slot=slot, src=src: e.dma_start(out=WF[slot][:], in_=src),
                  waits=[pw, (s_wcv, n - 1)], inc=s_wld[slot], dma=True)
        P.add("gpsimd", lambda e, slot=slot, dst=dst: e.tensor_copy(out=dst, in_=WF[slot][:]),
              waits=[pw, (s_wld[slot], v)], inc=s_wcv)
        cnt[0] += 1
    return s_wcv.v


def phase4a(nc, P, D):
    with ExitStack() as es:
        WPA = _sb(es, nc, "p4_WPA", [128, 4, 1024], BF16)
        WPB = _sb(es, nc, "p4_WPB", [128, 8, 1024], BF16)
        WO = _sb(es, nc, "p4_WO", [128, 8, 1024], BF16)
        WF = [_sb(es, nc, f"p4_WF{i}", [128, 2, 1024], F32) for i in range(2)]
        OAt = [_sb(es, nc, f"p4_OA{i}", [128, 4, 512], BF16) for i in range(2)]
        OBt = [_sb(es, nc, f"p4_OB{i}", [128, 8, 512], BF16) for i in range(2)]
        SGt = _sb(es, nc, "p4_SG", [128, 16, 512], BF16)
        XS = _sb(es, nc, "p4_XS", [128, 4, 1024], F32)
        MT = _sb(es, nc, "p4_MT", [128, 8, 512], BF16)
        T1 = [_sb(es, nc, f"p4_T1{i}", [128, 512], F32) for i in range(2)]
        T2 = [_sb(es, nc, f"p4_T2{i}", [128, 512], F32) for i in range(2)]
        V = _sb(es, nc, "p4_V", [128, 1024], F32)
        TMP = _sb(es, nc, "p4_TMP", [128, 1024], F32)
        X1O = [_sb(es, nc, f"p4_X1O{i}", [128, 1024], F32) for i in range(2)]
        X1B = _sb(es, nc, "p4_X1B", [128, 1024], BF16)
        X1TS = _sb(es, nc, "p4_X1TS", [128, 8, 512], BF16)
        LG = _sb(es, nc, "p4_LG", [128, 1024], F32)
        LB = _sb(es, nc, "p4_LB", [128, 1024], F32)
        ST6 = _sb(es, nc, "p4_ST6", [128, 12], F32)
        MV = _sb(es, nc, "p4_MV", [128, 2], F32)
        STD = _sb(es, nc, "p4_STD", [128, 1], F32)
        RSTD = _sb(es, nc, "p4_RSTD", [128, 1], F32)
        EPSB = _sb(es, nc, "p4_EPSB", [128, 1], F32)
        IDF = _sb(es, nc, "p4_IDF", [128, 128], F32)
        IDB = _sb(es, nc, "p4_IDB", [128, 128], BF16)
        JK = _sb(es, nc, "p4_JK", [128, 1], F32)
        PSY = _ps(es, nc, "p4_PSY", [128, 4, 512], F32)
        PSM = _ps(es, nc, "p4_PSM", [128, 2, 512], F32)
        PST = _ps(es, nc, "p4_PST", [128, 8, 128], BF16)
        s_c = P.sem("p4c")
        s_wld = [P.sem(f"p4wld{i}") for i in range(2)]
        s_wcv = P.sem("p4wcv")
        s_ldab = [P.sem(f"p4ldab{i}") for i in range(2)]
        s_ldsg = P.sem("p4ldsg")
        s_ldx = P.sem("p4ldx")
        s_y = [P.sem(f"p4y{i}") for i in range(2)]
        s_t = [P.sem(f"p4t{i}") for i in range(2)]
        s_mt = P.sem("p4mt")
        s_mm = [P.sem(f"p4mm{i}") for i in range(2)]
        s_v = [P.sem(f"p4v{i}") for i in range(2)]
        s_ln = P.sem("p4ln")
        s_xb = P.sem("p4xb")
        s_tp = P.sem("p4tp")
        s_tc = P.sem("p4tc")
        s_so = [P.sem(f"p4so{i}") for i in range(2)]
        s_sx = P.sem("p4sx")
        pw = P.pw()

        for (dst, src) in ((LG, D["ln1g"]), (LB, D["ln1b"]), (IDF, D["ident"])):
            vc = P.add("sync", lambda e, dst=dst, src=src: e.dma_start(out=dst[:], in_=src), waits=[pw], inc=s_c, dma=True)
        P.add("vector", lambda e: e.tensor_copy(out=IDB[:], in_=IDF[:]), waits=[pw, (s_c, vc)], inc=s_c)
        vconst = P.add("vector", lambda e: e.memset(EPSB[:], LN_EPS), waits=[pw], inc=s_c)

        cnt = [0]
        wpa = D["wpa"].rearrange("(kc p) c -> p kc c", p=128)
        wpb = D["wpb"].rearrange("(kc p) c -> p kc c", p=128)
        wo = D["wo"].rearrange("(kc p) c -> p kc c", p=128)
        dsts, srcs = [], []
        for (Wt, wsrc, nk) in ((WPA, wpa, 4), (WPB, wpb, 8), (WO, wo, 8)):
            for k2 in range(nk // 2):
                dsts.append(Wt[:, 2 * k2:2 * k2 + 2, :])
                srcs.append(wsrc[:, 2 * k2:2 * k2 + 2, :])
        vw = load_cast_weight(P, pw, dsts, srcs, WF, s_wld, s_wcv, cnt)

        oat = D["OAT"].rearrange("k p t -> p k t")
        obt = D["OBT"].rearrange("k p t -> p k t")
        sgt = D["SG"].rearrange("k p t -> p k t")
        xo = D["xo"].rearrange("(s p) d -> p s d", p=128)
        x1 = D["X1"].rearrange("(s p) d -> p s d", p=128)
        x1t = D["X1T"].rearrange("k p t -> p k t")

        last_y_pe = {}
        ld_ab = {}

        def load_ab(t):
            sl = t % 2
            w = [pw] + (last_y_pe[t - 2] if t >= 2 else [])
            P.add("sync", lambda e: e.dma_start(out=OAt[sl][:], in_=oat[:, :, t * 512:(t + 1) * 512]), waits=w, inc=s_ldab[sl], dma=True)
            ld_ab[t] = P.add("sync", lambda e: e.dma_start(out=OBt[sl][:], in_=obt[:, :, t * 512:(t + 1) * 512]), waits=w,
                             inc=s_ldab[sl], dma=True)

        load_ab(0)
        yj = 0
        mj = 0
        sbk = 0
        for t in range(8):
            sl = t % 2
            if t + 1 < 8:
                load_ab(t + 1)
            v_sg = P.add("sync", lambda e, t=t: e.dma_start(out=SGt[:], in_=sgt[:, :, t * 512:(t + 1) * 512]),
                         waits=[pw, (s_t[0], s_t[0].v), (s_t[1], s_t[1].v)], inc=s_ldsg, dma=True)
            v_x = P.add("sync", lambda e, t=t: e.dma_start(out=XS[:], in_=xo[:, 4 * t:4 * t + 4, :]),
                        waits=[pw, (s_v[0], s_v[0].v), (s_v[1], s_v[1].v)], inc=s_ldx, dma=True)
            mt_w0 = (s_mm[0], s_mm[0].v), (s_mm[1], s_mm[1].v)
            for dc in range(8):
                ys = yj % 2
                ky = yj // 2

                def ymm(e, ys=ys, dc=dc, sl=sl):
                    ins = None
                    for k in range(4):
                        ins = e.matmul(PSY[:, 2 * ys, :], lhsT=WPA[:, k, dc * 128:(dc + 1) * 128], rhs=OAt[sl][:, k, :],
                                       start=(k == 0), stop=(k == 3))
                    for k in range(8):
                        ins = e.matmul(PSY[:, 2 * ys + 1, :], lhsT=WPB[:, k, dc * 128:(dc + 1) * 128], rhs=OBt[sl][:, k, :],
                                       start=(k == 0), stop=(k == 7))
                    return ins

                P.add("tensor", ymm, waits=[pw, (s_wcv, vw), (s_ldab[sl], ld_ab[t]), (s_t[ys], 2 * ky)], inc=s_y[ys])
                wt = [pw, (s_y[ys], ky + 1), (s_ldsg, v_sg), (s_mt, s_mt.v - 1)]
                P.add("vector", lambda e, ys=ys, dc=dc: e.tensor_tensor(out=T1[ys][:], in0=PSY[:, 2 * ys, :], in1=SGt[:, dc, :], op=ALU.mult),
                      waits=wt, inc=s_t[ys])
                vt = P.add("vector", lambda e, ys=ys, dc=dc: e.tensor_tensor(out=T2[ys][:], in0=PSY[:, 2 * ys + 1, :], in1=SGt[:, 8 + dc, :],
                                                                         op=ALU.mult), waits=wt, inc=s_t[ys])
                P.add("gpsimd", lambda e, ys=ys, dc=dc: e.tensor_tensor(out=MT[:, dc, :], in0=T1[ys][:], in1=T2[ys][:], op=ALU.add),
                      waits=[pw, (s_t[ys], vt)] + list(mt_w0), inc=s_mt)
                yj += 1
            last_y_pe[t] = [(s_y[0], s_y[0].v), (s_y[1], s_y[1].v)]
            v_mt = s_mt.v
            for s in range(4):
                xs_o = sbk % 2
                for hf in range(2):
                    ms = mj % 2
                    km = mj // 2

                    def mmm(e, ms=ms, s=s, hf=hf):
                        ins = None
                        for k in range(8):
                            ins = e.matmul(PSM[:, ms, :], lhsT=MT[:, k, s * 128:(s + 1) * 128], rhs=WO[:, k, hf * 512:(hf + 1) * 512],
                                           start=(k == 0), stop=(k == 7))
                        return ins

                    P.add("tensor", mmm, waits=[pw, (s_mt, v_mt), (s_v[ms], km)], inc=s_mm[ms])
                    vv = P.add("vector", lambda e, ms=ms, s=s, hf=hf: e.scalar_tensor_tensor(
                        out=V[:, hf * 512:(hf + 1) * 512], in0=XS[:, s, hf * 512:(hf + 1) * 512], scalar=ALPHA, in1=PSM[:, ms, :],
                        op0=ALU.mult, op1=ALU.add), waits=[pw, (s_mm[ms], km + 1), (s_ldx, v_x), (s_ln, s_ln.v)], inc=s_v[ms])
                    mj += 1
                win = [(s_v[0], s_v[0].v), (s_v[1], s_v[1].v), (s_c, vconst), (s_so[xs_o], 16 * (sbk // 2)), (s_xb, s_xb.v)]
                vln = layer_norm_ops(P, V, ST6, MV, STD, RSTD, TMP, X1O[xs_o][:], LG, LB, EPSB, s_ln, win)
                row = 4 * t + s
                P.add("sync", lambda e, xs_o=xs_o, row=row: e.dma_start(out=x1[:, row, :], in_=X1O[xs_o][:]),
                      waits=[pw, (s_ln, vln)], inc=s_so[xs_o], dma=True)
                vb = P.add("scalar", lambda e, xs_o=xs_o: e.copy(out=X1B[:], in_=X1O[xs_o][:]),
                           waits=[pw, (s_ln, vln), (s_tp, s_tp.v)], inc=s_xb)

                def tp(e):
                    ins = None
                    for k in range(8):
                        ins = e.transpose(out=PST[:, k, :], in_=X1B[:, k * 128:(k + 1) * 128], identity=IDB[:])
                    return ins

                vtp = P.add("tensor", tp, waits=[pw, (s_xb, vb), (s_tc, s_tc.v)], inc=s_tp)
                P.add("scalar", lambda e, s=s: e.copy(out=X1TS[:, :, s * 128:(s + 1) * 128], in_=PST[:]),
                      waits=[pw, (s_tp, vtp), (s_sx, 16 * t)], inc=s_tc)
                sbk += 1
            P.add("gpsimd", lambda e, t=t: e.dma_start(out=x1t[:, :, t * 512:(t + 1) * 512], in_=X1TS[:]),
                  waits=[pw, (s_tc, s_tc.v)], inc=s_sx, dma=True)
        P.end_phase(JK[:])


def phase4b(nc, P, D):
    TT = 256
    NT = NQ // TT
    with ExitStack() as es:
        W1 = _sb(es, nc, "p5_W1", [128, 8, 4096], BF16)
        W2 = _sb(es, nc, "p5_W2", [128, 32, 1024], BF16)
        WF = [_sb(es, nc, f"p5_WF{i}", [128, 1024], F32) for i in range(2)]
        XT = [_sb(es, nc, f"p5_XT{i}", [128, 8, TT], BF16) for i in range(2)]
        X1 = _sb(es, nc, "p5_X1", [128, 2, 1024], F32)
        HT = _sb(es, nc, "p5_HT", [128, 32, TT], BF16)
        RT = [_sb(es, nc, f"p5_RT{i}", [128, TT], F32) for i in range(2)]
        V = _sb(es, nc, "p5_V", [128, 1024], F32)
        TMP = _sb(es, nc, "p5_TMP", [128, 1024], F32)
        YO = [_sb(es, nc, f"p5_YO{i}", [128, 1024], F32) for i in range(2)]
        LG = _sb(es, nc, "p5_LG", [128, 1024], F32)
        LB = _sb(es, nc, "p5_LB", [128, 1024], F32)
        ST6 = _sb(es, nc, "p5_ST6", [128, 12], F32)
        MV = _sb(es, nc, "p5_MV", [128, 2], F32)
        STD = _sb(es, nc, "p5_STD", [128, 1], F32)
        RSTD = _sb(es, nc, "p5_RSTD", [128, 1], F32)
        EPSB = _sb(es, nc, "p5_EPSB", [128, 1], F32)
        JK = _sb(es, nc, "p5_JK", [128, 1], F32)
        PSH = _ps(es, nc, "p5_PSH", [128, 4, 512], F32)
        PSF = _ps(es, nc, "p5_PSF", [128, 2, 512], F32)
        s_c = P.sem("p5c")
        s_wld = [P.sem(f"p5wld{i}") for i in range(2)]
        s_wcv = P.sem("p5wcv")
        s_ldt = [P.sem(f"p5ldt{i}") for i in range(2)]
        s_ldx = P.sem("p5ldx")
        s_h = [P.sem(f"p5h{i}") for i in range(4)]
        s_r = [P.sem(f"p5r{i}") for i in range(4)]
        s_sq = [P.sem(f"p5sq{i}") for i in range(2)]
        s_f = [P.sem(f"p5f{i}") for i in range(2)]
        s_v = [P.sem(f"p5v{i}") for i in range(2)]
        s_ln = P.sem("p5ln")
        s_so = [P.sem(f"p5so{i}") for i in range(2)]
        pw = P.pw()

        for (dst, src) in ((LG, D["ln2g"]), (LB, D["ln2b"])):
            vc = P.add("sync", lambda e, dst=dst, src=src: e.dma_start(out=dst[:], in_=src), waits=[pw], inc=s_c, dma=True)
        vconst = P.add("vector", lambda e: e.memset(EPSB[:], LN_EPS), waits=[pw, (s_c, vc)], inc=s_c)

        cnt = [0]
        w1 = D["w1"].rearrange("(kc p) f -> p kc f", p=128)
        w2 = D["w2"].rearrange("(fc p) d -> p fc d", p=128)
        dsts, srcs = [], []
        for kc in range(8):
            for q in range(4):
                dsts.append(W1[:, kc, q * 1024:(q + 1) * 1024])
                srcs.append(w1[:, kc, q * 1024:(q + 1) * 1024])
        vw1 = load_cast_weight(P, pw, dsts, srcs, WF, s_wld, s_wcv, cnt)
        dsts, srcs = [], []
        for fc in range(32):
            dsts.append(W2[:, fc, :])
            srcs.append(w2[:, fc, :])
        vw2 = load_cast_weight(P, pw, dsts, srcs, WF, s_wld, s_wcv, cnt)

        x1t = D["X1T"].rearrange("k p t -> p k t")
        x1 = D["X1"].rearrange("(s p) d -> p s d", p=128)
        yv = D["y"].rearrange("(s p) d -> p s d", p=128)
        last_h_pe = {}
        ld_t = {}

        def load_t(t):
            sl = t % 2
            w = [pw] + (last_h_pe[t - 2] if t >= 2 else [])
            ld_t[t] = P.add("sync", lambda e: e.dma_start(out=XT[sl][:], in_=x1t[:, :, t * TT:(t + 1) * TT]), waits=w,
                            inc=s_ldt[sl], dma=True)

        load_t(0)
        hj = 0
        fj = 0
        sbk = 0
        for t in range(NT):
            sl = t % 2
            if t + 1 < NT:
                load_t(t + 1)
            v_x = P.add("sync", lambda e, t=t: e.dma_start(out=X1[:], in_=x1[:, 2 * t:2 * t + 2, :]),
                        waits=[pw, (s_v[0], s_v[0].v), (s_v[1], s_v[1].v)], inc=s_ldx, dma=True)
            ht_w0 = [(s_f[0], s_f[0].v), (s_f[1], s_f[1].v)]
            for fc in range(32):
                hs = hj % 4
                kh = hj // 4
                rs = hj % 2
                kr = hj // 2

                def hmm(e, hs=hs, fc=fc, sl=sl):
                    ins = None
                    for k in range(8):
                        ins = e.matmul(PSH[:, hs, 0:TT], lhsT=W1[:, k, fc * 128:(fc + 1) * 128], rhs=XT[sl][:, k, :],
                                       start=(k == 0), stop=(k == 7))
                    return ins

                P.add("tensor", hmm, waits=[pw, (s_wcv, vw1), (s_ldt[sl], ld_t[t]), (s_r[hs], kh)], inc=s_h[hs])
                P.add("scalar", lambda e, hs=hs, rs=rs: e.activation(out=RT[rs][:], in_=PSH[:, hs, 0:TT], func=AF.Relu),
                      waits=[pw, (s_h[hs], kh + 1), (s_sq[rs], kr)], inc=s_r[hs])
                P.add("gpsimd" if rs == 0 else "vector",
                      lambda e, rs=rs, fc=fc: e.tensor_tensor(out=HT[:, fc, :], in0=RT[rs][:], in1=RT[rs][:], op=ALU.mult),
                      waits=[pw, (s_r[hs], kh + 1)] + ht_w0, inc=s_sq[rs])
                hj += 1
            last_h_pe[t] = [(s_h[i], s_h[i].v) for i in range(4)]
            v_sq = [(s_sq[0], s_sq[0].v), (s_sq[1], s_sq[1].v)]
            for s in range(2):
                yo = sbk % 2
                for hf in range(2):
                    fs = fj % 2
                    kf = fj // 2

                    def fmm(e, fs=fs, s=s, hf=hf):
                        ins = None
                        for fc in range(32):
                            ins = e.matmul(PSF[:, fs, :], lhsT=HT[:, fc, s * 128:(s + 1) * 128], rhs=W2[:, fc, hf * 512:(hf + 1) * 512],
                                           start=(fc == 0), stop=(fc == 31))
                        return ins

                    P.add("tensor", fmm, waits=[pw, (s_wcv, vw2), (s_v[fs], kf)] + v_sq, inc=s_f[fs])
                    P.add("vector", lambda e, fs=fs, s=s, hf=hf: e.scalar_tensor_tensor(
                        out=V[:, hf * 512:(hf + 1) * 512], in0=X1[:, s, hf * 512:(hf + 1) * 512], scalar=ALPHA, in1=PSF[:, fs, :],
                        op0=ALU.mult, op1=ALU.add), waits=[pw, (s_f[fs], kf + 1), (s_ldx, v_x), (s_ln, s_ln.v)], inc=s_v[fs])
                    fj += 1
                win = [(s_v[0], s_v[0].v), (s_v[1], s_v[1].v), (s_c, vconst), (s_so[yo], 16 * (sbk // 2))]
                vln = layer_norm_ops(P, V, ST6, MV, STD, RSTD, TMP, YO[yo][:], LG, LB, EPSB, s_ln, win)
                row = 2 * t + s
                P.add("sync", lambda e, yo=yo, row=row: e.dma_start(out=yv[:, row, :], in_=YO[yo][:]),
                      waits=[pw, (s_ln, vln)], inc=s_so[yo], dma=True)
                sbk += 1
        P.end_phase(JK[:])


def build_nc(debug=False, phases=(1, 2, 3, 4, 5)):
    nc = bass.Bass("TRN2", target_bir_lowering=False)
    D = {}

    def din(name, shape):
        D[name] = nc.dram_tensor(name, shape, F32, kind="ExternalInput").ap()

    din("xT", [1024, NK]); din("xo", [NQ, 1024]); din("wp", [1024, COLS_IN]); din("bg", [128, 16])
    din("ga", [24, 128, 256]); din("ma", [128, 256]); din("gb", [8, 128, 256]); din("cb", [128, 8]); din("mb", [128, 256])
    din("ident", [128, 128]); din("pvalid", [128, 64]); din("lam", [128, 4, 64]); din("gs", [128, 128])
    din("ln1g", [128, 1024]); din("ln1b", [128, 1024]); din("ln2g", [128, 1024]); din("ln2b", [128, 1024])
    din("wpa", [512, 1024]); din("wpb", [1024, 1024]); din("wo", [1024, 1024]); din("w1", [1024, 4096]); din("w2", [4096, 1024])
    D["y"] = nc.dram_tensor("y", [NQ, 1024], F32, kind="ExternalOutput").ap()
    kind = "ExternalOutput" if debug else "Internal"

    def scr(name, shape, dt):
        D[name] = nc.dram_tensor(name, shape, dt, kind=kind).ap()

    scr("QTB", [8, 128, NQ], BF16); scr("KTB", [8, 128, NK], BF16); scr("VB", [NK, 1024], BF16)
    scr("QTA", [12, 128, NQ], BF16); scr("KTA", [12, 128, NKA], BF16); scr("VA", [NKA, 1536], BF16)
    scr("SG", [16, 128, NQ], BF16); scr("OAT", [4, 128, NQ], BF16); scr("OBT", [8, 128, NQ], BF16)
    scr("X1", [NQ, 1024], F32); scr("X1T", [8, 128, NQ], BF16)
    with ExitStack() as es:
        P = Prog(nc, es)
        if 1 in phases:
            phase1(nc, P, D)
        if 2 in phases:
            phase2(nc, P, D)
        if 3 in phases:
            phase3(nc, P, D)
        if 4 in phases:
            phase4a(nc, P, D)
        if 5 in phases:
            phase4b(nc, P, D)
    return nc


def t5_bucket_np(dist):
    n = np.maximum(dist, 0)
    nf = np.maximum(n, 1).astype(np.float32)
    large = 16 + (np.log(nf / np.float32(16.0)) / np.float32(math.log(8.0)) * np.float32(16.0)).astype(np.int32)
    large = np.minimum(large, 31)
    return np.where(n < 16, n, large)


def prep_shared(inp):
    f = lambda a: np.ascontiguousarray(np.asarray(a, dtype=np.float32))
    w = np.asarray(inp["w_in"], dtype=np.float32)[0]
    B0 = 4608
    cols = []
    for h in range(8):
        cols += list(range(B0 + h * 64, B0 + h * 64 + 64)) + list(range(B0 + 512 + h * 64, B0 + 512 + h * 64 + 64))
    for h in range(8):
        cols += list(range(B0 + 1024 + h * 64, B0 + 1024 + h * 64 + 64)) + list(range(B0 + 1536 + h * 64, B0 + 1536 + h * 64 + 64))
    cols += list(range(0, 1536)) + list(range(1536, 3072)) + list(range(7680, 9728))
    cols += list(range(B0 + 2048, B0 + 3072)) + list(range(3072, 4608))
    assert len(cols) == COLS_IN
    sh = {"wp": f(w[:, cols])}
    sh["bg"] = f(np.asarray(inp["b_gate"], np.float32)[0].reshape(16, 128).T)
    rb = np.asarray(inp["rel_bias"], np.float32)
    i = np.arange(128)[:, None]
    c = np.arange(256)[None, :]
    ch = c // 128
    j = c % 128
    steps = (1 - ch) * 128 + j - i
    ga = np.zeros((24, 128, 256), np.float32)
    for g, d in enumerate(DILS):
        bk = t5_bucket_np(np.maximum(steps, 0) * d)
        for h in range(8):
            ga[g * 8 + h] = rb[bk, g * 8 + h]
    sh["ga"] = ga
    sh["ma"] = f(np.where((steps >= 0) & (steps <= 128), 0.0, MASKV))
    dist = c - i
    bk = t5_bucket_np(np.maximum(dist, 0))
    gb = np.zeros((8, 128, 256), np.float32)
    for h in range(8):
        gb[h] = rb[bk, 24 + h]
    sh["gb"] = gb
    sh["cb"] = f(np.broadcast_to(rb[31, 24:32][None, :], (128, 8)))
    sh["mb"] = f(np.where(dist >= 0, 0.0, MASKV))
    sh["ident"] = np.eye(128, dtype=np.float32)
    lam = np.stack([np.asarray(inp[k], np.float32)[0] for k in ("lambda_q1", "lambda_q2", "lambda_k1", "lambda_k2")])
    sh["lam"] = f(np.broadcast_to(lam[None], (128, 4, 64)))
    sh["gs"] = f(np.broadcast_to(np.asarray(inp["subln_g"], np.float32)[0][None, :], (128, 128)))
    for k, n in (("ln1g", "ln1_g"), ("ln1b", "ln1_b"), ("ln2g", "ln2_g"), ("ln2b", "ln2_b")):
        sh[k] = f(np.broadcast_to(np.asarray(inp[n], np.float32)[0][None, :], (128, 1024)))
    sh["wpa"] = f(np.asarray(inp["w_proj_a"])[0]); sh["wpb"] = f(np.asarray(inp["w_proj_b"])[0])
    sh["wo"] = f(np.asarray(inp["w_out"])[0]); sh["w1"] = f(np.asarray(inp["w_mlp1"])[0]); sh["w2"] = f(np.asarray(inp["w_mlp2"])[0])
    return sh


def prep_core(x, c):
    b, hf = c // 2, c % 2
    xb = np.asarray(x[b], dtype=np.float32)
    own = xb[hf * NQ:(hf + 1) * NQ]
    xT = np.zeros((1024, NK), np.float32)
    if hf == 1:
        xT[:, 0:NQ] = xb[0:NQ].T
    xT[:, NQ:] = own.T
    pv = np.full((128, 64), float(hf), np.float32)
    return {"xT": xT, "xo": np.ascontiguousarray(own), "pvalid": pv}


_NC_CACHE = {}


def kernel(**inputs):
    x = np.asarray(inputs["x"], dtype=np.float32)
    sh = prep_shared(inputs)
    in_maps = []
    for c in range(8):
        m = dict(sh)
        m.update(prep_core(x, c))
        in_maps.append(m)
    if "nc" not in _NC_CACHE:
        _NC_CACHE["nc"] = build_nc()
    res = run_bass_kernel_spmd(_NC_CACHE["nc"], in_maps, core_ids=list(range(8)))
    out = np.zeros((BATCH, SEQ, D_MODEL), np.float32)
    for c in range(8):
        b, hf = c // 2, c % 2
        out[b, hf * NQ:(hf + 1) * NQ] = np.asarray(res.results[c]["y"], dtype=np.float32)
    return out
```

```python
import math
from contextlib import ExitStack

import numpy as np

import concourse.bass as bass
import concourse.mybir as mybir
from concourse.bass_utils import run_bass_kernel_spmd

F32 = mybir.dt.float32
BF16 = mybir.dt.bfloat16
AF = mybir.ActivationFunctionType
ALU = mybir.AluOpType
AX = mybir.AxisListType

D_MODEL = 1024
SEQ = 8192
BATCH = 4
NQ = 4096
NK = 8192
NKA = 6144
COLS_IN = 9728
DILS = (1, 4, 16)
ALPHA = 2.0 ** 0.25
LAM_INIT = 0.8 - 0.6 * math.exp(0.0)
LN_EPS = 1e-5
MASKV = -240000.0
ENGS = ("sync", "scalar", "vector", "gpsimd", "tensor")

C_FQ, C_FK, C_AQ, C_AK, C_GT, C_BV, C_AV = 0, 1024, 2048, 3584, 5120, 7168, 8192


class Sem:
    __slots__ = ("h", "abs", "base", "name")

    def __init__(self, h, name):
        self.h, self.abs, self.base, self.name = h, 0, 0, name

    @property
    def v(self):
        return self.abs - self.base


class Prog:
    def __init__(self, nc, es):
        self.nc, self.es = nc, es
        self.q = {k: [] for k in ENGS}
        self.waited = {k: {} for k in ENGS}
        self.sems = []
        self.pool = []
        self.used = 0
        self.nphase = 0
        self.bar = self.sem("bar")

    def sem(self, name):
        if name != "bar" and self.used < len(self.pool):
            s = self.pool[self.used]
            self.used += 1
            s.base = s.abs
            return s
        s = Sem(self.es.enter_context(self.nc.semaphore(f"s{len(self.sems)}")), f"s{len(self.sems)}")
        self.sems.append(s)
        if name != "bar":
            self.pool.append(s)
            self.used += 1
        return s

    def add(self, eng, fn, waits=(), inc=None, dma=False):
        ws = []
        for w in waits:
            if w is None:
                continue
            s, v = w
            if v <= 0:
                continue
            assert v <= s.v, f"deadlock: {eng} waits {s.name}>={v} but only {s.v} scheduled"
            va = s.base + v
            if self.waited[eng].get(s.name, 0) >= va:
                continue
            self.waited[eng][s.name] = va
            ws.append((s, va))
        amt = 16 if dma else 1
        newv = None
        if inc is not None:
            inc.abs += amt
            newv = inc.v
        self.q[eng].append((ws, fn, inc, amt))
        return newv

    def end_phase(self, junk):
        waits = [(s, s.v) for s in self.sems if s is not self.bar]
        self.nphase += 1
        self.add("gpsimd", lambda e: e.memset(junk, 0.0), waits=waits, inc=self.bar)
        with self.nc.Block() as block:
            for name in ENGS:
                items = self.q[name]

                def body(e, items=items):
                    for ws, fn, inc, amt in items:
                        for s, v in ws:
                            e.wait_ge(s.h, v)
                        ins = fn(e)
                        if inc is not None:
                            ins.then_inc(inc.h, amt)

                getattr(block, name)(body)
        self.q = {k: [] for k in ENGS}
        self.used = 0
        for name in ENGS:
            if name == "gpsimd":
                continue
            self.waited[name].pop(self.bar.name, None)
        self.phase_wait = (self.bar, self.nphase)

    def pw(self):
        return getattr(self, "phase_wait", None)


def _sb(es, nc, name, shape, dt):
    return es.enter_context(nc.sbuf_tensor(name, shape, dt))


def _ps(es, nc, name, shape, dt):
    return es.enter_context(nc.psum_tensor(name, shape, dt))


def phase1(nc, P, D):
    with ExitStack() as es:
        W = _sb(es, nc, "p1_W", [128, 8, COLS_IN], BF16)
        WF = [_sb(es, nc, f"p1_WF{i}", [128, 8, 128], F32) for i in range(2)]
        XF = _sb(es, nc, "p1_XF", [128, 8, 512], F32)
        XB = [_sb(es, nc, f"p1_XB{i}", [128, 8, 512], BF16) for i in range(2)]
        OST = [_sb(es, nc, f"p1_OST{i}", [128, 512], BF16) for i in range(4)]
        BG = _sb(es, nc, "p1_BG", [128, 16], F32)
        JK = _sb(es, nc, "p1_JK", [128, 1], F32)
        PS = _ps(es, nc, "p1_PS", [128, 4, 512], F32)
        s_xld, s_xcv, s_wcv, s_misc = P.sem("p1xld"), P.sem("p1xcv"), P.sem("p1wcv"), P.sem("p1misc")
        s_wld = [P.sem(f"p1wld{i}") for i in range(2)]
        s_mm = [P.sem(f"p1mm{i}") for i in range(4)]
        s_ev = [P.sem(f"p1ev{i}") for i in range(4)]
        s_st = [P.sem(f"p1st{i}") for i in range(4)]
        xT = D["xT"].rearrange("(kc p) t -> p kc t", p=128)
        wp = D["wp"].rearrange("(kc p) c -> p kc c", p=128)
        pw = P.pw()

        v_bg = P.add("sync", lambda e: e.dma_start(out=BG[:], in_=D["bg"]), waits=[pw], inc=s_misc, dma=True)

        wcv_of = {}
        wl = [0]

        def load_w(cid):
            n = wl[0]
            slot = n % 2
            v = P.add("sync", lambda e: e.dma_start(out=WF[slot][:], in_=wp[:, :, cid * 128:(cid + 1) * 128]),
                      waits=[pw, (s_wcv, n - 1)], inc=s_wld[slot], dma=True)
            P.add("vector", lambda e: e.tensor_copy(out=W[:, :, cid * 128:(cid + 1) * 128], in_=WF[slot][:]),
                  waits=[pw, (s_wld[slot], v)], inc=s_wcv)
            wcv_of[cid] = s_wcv.v
            wl[0] += 1

        worder = (list(range(8, 16)) + list(range(56, 64)) + list(range(28, 40)) + list(range(64, 76))
                  + list(range(0, 8)) + list(range(16, 28)) + list(range(40, 56)))
        wplan = {-1: worder[0:16]}
        for t in range(4):
            wplan[t] = worder[16 + 6 * t:16 + 6 * (t + 1)]
        for t in range(4, 8):
            wplan[t] = worder[40 + 9 * (t - 4):40 + 9 * (t - 3)]

        tile_last = {}

        def load_x(t):
            v = P.add("sync", lambda e: e.dma_start(out=XF[:], in_=xT[:, :, t * 512:(t + 1) * 512]),
                      waits=[pw, (s_xcv, t)], inc=s_xld, dma=True)
            P.add("vector", lambda e: e.tensor_copy(out=XB[t % 2][:], in_=XF[:]),
                  waits=[pw, (s_xld, v)] + (tile_last[t - 2] if t >= 2 else []), inc=s_xcv)

        def jobs_for(t):
            jobs = []
            own = t >= 8
            to = t - 8
            for h in range(8):
                jobs.append(("F", C_FK + h * 128, D["KTB"][h][:, t * 512:(t + 1) * 512], None))
            for cb in range(2):
                for s in range(4):
                    r0 = t * 512 + s * 128
                    jobs.append(("T", C_BV + cb * 512, D["VB"][r0:r0 + 128, cb * 512:(cb + 1) * 512], s))
            if t >= 4:
                ta = t - 4
                for m in range(12):
                    jobs.append(("F", C_AK + m * 128, D["KTA"][m][:, ta * 512:(ta + 1) * 512], None))
                for g in range(3):
                    for s in range(4):
                        r0 = ta * 512 + s * 128
                        jobs.append(("T", C_AV + g * 512, D["VA"][r0:r0 + 128, g * 512:(g + 1) * 512], s))
            if own:
                for h in range(8):
                    jobs.append(("F", C_FQ + h * 128, D["QTB"][h][:, to * 512:(to + 1) * 512], None))
                for m in range(12):
                    jobs.append(("F", C_AQ + m * 128, D["QTA"][m][:, to * 512:(to + 1) * 512], None))
                for i in range(16):
                    jobs.append(("G", C_GT + i * 128, D["SG"][i][:, to * 512:(to + 1) * 512], i))
            return jobs

        for cid in wplan[-1]:
            load_w(cid)
        load_x(0)
        jc = 0
        for t in range(16):
            jobs = jobs_for(t)
            half = len(jobs) // 2
            for ji, (kind, col, dst, aux) in enumerate(jobs):
                if ji == half:
                    if t + 1 < 16:
                        load_x(t + 1)
                    for cid in wplan.get(t, []):
                        load_w(cid)
                slot = jc % 4
                k = jc // 4
                xb = XB[t % 2]
                if kind == "T":
                    wneed = max(wcv_of[col // 128 + i] for i in range(4))
                else:
                    wneed = wcv_of[col // 128]

                def mm(e, kind=kind, col=col, aux=aux, xb=xb, slot=slot):
                    ins = None
                    for kc in range(8):
                        if kind == "T":
                            ins = e.matmul(PS[:, slot, :], lhsT=xb[:, kc, aux * 128:(aux + 1) * 128],
                                           rhs=W[:, kc, col:col + 512], start=(kc == 0), stop=(kc == 7))
                        else:
                            ins = e.matmul(PS[:, slot, :], lhsT=W[:, kc, col:col + 128],
                                           rhs=xb[:, kc, :], start=(kc == 0), stop=(kc == 7))
                    return ins

                P.add("tensor", mm, waits=[pw, (s_xcv, t + 1), (s_wcv, wneed), (s_ev[slot], k)], inc=s_mm[slot])
                evw = [pw, (s_mm[slot], k + 1), (s_st[slot], 16 * k)]
                if kind == "G":
                    P.add("scalar", lambda e, slot=slot, aux=aux: e.activation(
                        out=OST[slot][:], in_=PS[:, slot, :], func=AF.Sigmoid, bias=BG[:, aux:aux + 1], scale=1.0),
                        waits=evw + [(s_misc, v_bg)], inc=s_ev[slot])
                elif jc % 2 == 0:
                    P.add("vector", lambda e, slot=slot: e.tensor_copy(out=OST[slot][:], in_=PS[:, slot, :]),
                          waits=evw, inc=s_ev[slot])
                else:
                    P.add("scalar", lambda e, slot=slot: e.copy(out=OST[slot][:], in_=PS[:, slot, :]),
                          waits=evw, inc=s_ev[slot])
                P.add("gpsimd", lambda e, slot=slot, dst=dst: e.dma_start(out=dst, in_=OST[slot][:]),
                      waits=[pw, (s_ev[slot], k + 1)], inc=s_st[slot], dma=True)
                jc += 1
            tile_last[t] = [(s_mm[s], s_mm[s].v) for s in range(4)]
        P.end_phase(JK[:])


def phase2(nc, P, D):
    with ExitStack() as es:
        QA = [_sb(es, nc, f"p2_QA{i}", [128, NQ], BF16) for i in range(2)]
        KA = [_sb(es, nc, f"p2_KA{i}", [128, NKA], BF16) for i in range(2)]
        VS = [_sb(es, nc, f"p2_VS{i}", [128, 48, 128], BF16) for i in range(2)]
        GF = [_sb(es, nc, f"p2_GF{i}", [128, 2, 256], F32) for i in range(2)]
        BTF = _sb(es, nc, "p2_BTF", [128, 2, 256], F32)
        EB = [_sb(es, nc, f"p2_EB{i}", [128, 2, 256], F32) for i in range(2)]
        PTF = [_sb(es, nc, f"p2_PTF{i}", [128, 2, 256], F32) for i in range(2)]
        MA = _sb(es, nc, "p2_MA", [128, 256], F32)
        IDF = _sb(es, nc, "p2_IDF", [128, 128], F32)
        IDB = _sb(es, nc, "p2_IDB", [128, 128], BF16)
        PVF = _sb(es, nc, "p2_PVF", [128, 64], F32)
        PV64 = _sb(es, nc, "p2_PV64", [128, 64], BF16)
        ONE64 = _sb(es, nc, "p2_ONE64", [128, 64], BF16)
        PT = [_sb(es, nc, f"p2_PT{i}", [128, 2, 256], BF16) for i in range(2)]
        NACC = _sb(es, nc, "p2_NACC", [128, NQ], F32)
        DACC = _sb(es, nc, "p2_DACC", [128, NQ], F32)
        OAS = _sb(es, nc, "p2_OAS", [128, NQ], BF16)
        JK = _sb(es, nc, "p2_JK", [128, 1], F32)
        S = _ps(es, nc, "p2_S", [128, 4, 512], F32)
        ND = _ps(es, nc, "p2_ND", [128, 4, 512], F32)
        s_c = P.sem("p2c")
        s_ld = [P.sem(f"p2ld{i}") for i in range(2)]
        s_bt = P.sem("p2bt")
        s_qk = [P.sem(f"p2qk{i}") for i in range(2)]
        s_ex = [P.sem(f"p2ex{i}") for i in range(2)]
        s_pv = [P.sem(f"p2pv{i}") for i in range(2)]
        s_nf = [P.sem(f"p2nf{i}") for i in range(2)]
        s_dv = P.sem("p2dv")
        s_fin = P.sem("p2fin")
        s_pm = [P.sem(f"p2pm{i}") for i in range(2)]
        s_st = P.sem("p2st")
        pw = P.pw()

        vc = P.add("sync", lambda e: e.dma_start(out=MA[:], in_=D["ma"]), waits=[pw], inc=s_c, dma=True)
        vc = P.add("sync", lambda e: e.dma_start(out=PVF[:], in_=D["pvalid"]), waits=[pw], inc=s_c, dma=True)
        P.add("vector", lambda e: e.tensor_copy(out=PV64[:], in_=PVF[:]), waits=[pw, (s_c, vc)], inc=s_c)
        vconst = P.add("vector", lambda e: e.memset(ONE64[:], 1.0), waits=[pw], inc=s_c)

        passes = [(hp, g) for hp in range(4) for g in range(3)]
        pass_last_pe = {}
        pass_last_dve = {}
        ld_val = {}
        bt_val = {}

        def sched_loads(pi):
            hp, g = passes[pi]
            d = DILS[g]
            slot = pi % 2
            nb = 48 // d
            w = [pw] + (pass_last_pe[pi - 2] if pi >= 2 else [])
            P.add("sync", lambda e: e.dma_start(out=QA[slot][:], in_=D["QTA"][g * 4 + hp]), waits=w, inc=s_ld[slot], dma=True)
            P.add("sync", lambda e: e.dma_start(out=KA[slot][:], in_=D["KTA"][g * 4 + hp]), waits=w, inc=s_ld[slot], dma=True)
            vav = D["VA"].rearrange("(n i r) c -> r i n c", i=128, r=d)
            c0 = g * 512 + hp * 128
            for r in range(d):
                P.add("sync", lambda e, r=r: e.dma_start(out=VS[slot][:, r * nb:(r + 1) * nb, :],
                                                       in_=vav[r][:, :, c0:c0 + 128]),
                      waits=w, inc=s_ld[slot], dma=True)
            gh0 = g * 8 + 2 * hp
            v = P.add("sync", lambda e: e.dma_start(out=GF[slot][:], in_=D["ga"][gh0:gh0 + 2].rearrange("h i c -> i h c")),
                      waits=w, inc=s_ld[slot], dma=True)
            ld_val[pi] = v

        def sched_tables(pi):
            slot = pi % 2
            wb = [pw, (s_ld[slot], ld_val[pi]), (s_c, vconst)]
            v1 = None
            for hh in range(2):
                v1 = P.add("vector", lambda e, hh=hh: e.tensor_tensor(
                    out=BTF[:, hh, :], in0=GF[slot][:, hh, :], in1=MA[:], op=ALU.add),
                    waits=wb + [(s_bt, s_bt.v)], inc=s_bt)
            P.add("scalar", lambda e: e.activation(out=EB[slot][:], in_=BTF[:], func=AF.Exp),
                  waits=[pw, (s_bt, v1)] + (pass_last_dve[pi - 2] if pi >= 2 else []), inc=s_bt)
            bt_val[pi] = s_bt.v

        sched_loads(0)
        sched_tables(0)
        cn = {"qi": 0, "gi": 0, "fin": 0}

        def run_pass(pi, hp, g):
            d = DILS[g]
            slot = pi % 2
            nb = 48 // d
            pblk = 16 // d
            nqb = 32 // d
            gsz = min(4, nqb)
            if pi + 1 < len(passes):
                sched_loads(pi + 1)
            qblocks = [(r, m) for r in range(d) for m in range(nqb)]
            pend = None
            pass_dv_start = s_dv.v

            def emit_pv(info):
                (qslot, kq, r, m, gslot, kg, first, last, grp_m0) = info

                def pv(e):
                    ins = None
                    for hh in range(2):
                        for ch in range(2):
                            n = m + pblk - 1 + ch
                            blk = r * nb + n
                            col = (m - grp_m0) * 128
                            ins = e.matmul(ND[hh * 64:(hh + 1) * 64, 2 * gslot, col:col + 128],
                                           lhsT=VS[slot][:, blk, hh * 64:(hh + 1) * 64],
                                           rhs=PT[qslot][:, hh, ch * 128:(ch + 1) * 128],
                                           start=(ch == 0), stop=(ch == 1))
                            vt = PV64 if n < pblk else ONE64
                            ins = e.matmul(ND[hh * 64:(hh + 1) * 64, 2 * gslot + 1, col:col + 128],
                                           lhsT=vt[:, :], rhs=PT[qslot][:, hh, ch * 128:(ch + 1) * 128],
                                           start=(ch == 0), stop=(ch == 1))
                    return ins

                w = [(s_pm[qslot], kq + 1)]
                if first:
                    w.append((s_nf[gslot], kg))
                vpv = P.add("tensor", pv, waits=w, inc=s_pv[qslot])
                if last:
                    tok0 = d * 128 * grp_m0 + r
                    ncol = gsz * 128
                    sl = slice(tok0, tok0 + (ncol - 1) * d + 1, d)
                    wd = [(s_pv[qslot], vpv)]
                    if g > 0:
                        wd += [(s_dv, pass_dv_start), (s_nf[0], nf_start[0]), (s_nf[1], nf_start[1])]
                    else:
                        wd += [(s_fin, cn["fin"])]
                    if g == 0:
                        P.add("vector", lambda e: e.tensor_copy(out=NACC[:, sl], in_=ND[:, 2 * gslot, 0:ncol]),
                              waits=wd, inc=s_dv)
                        P.add("vector", lambda e: e.tensor_copy(out=DACC[:, sl], in_=ND[:, 2 * gslot + 1, 0:ncol]),
                              waits=wd, inc=s_nf[gslot])
                    else:
                        P.add("vector", lambda e: e.tensor_tensor(out=NACC[:, sl], in0=ND[:, 2 * gslot, 0:ncol],
                                                                  in1=NACC[:, sl], op=ALU.add), waits=wd, inc=s_dv)
                        P.add("vector", lambda e: e.tensor_tensor(out=DACC[:, sl], in0=ND[:, 2 * gslot + 1, 0:ncol],
                                                                  in1=DACC[:, sl], op=ALU.add), waits=wd, inc=s_nf[gslot])

            nf_start = [s_nf[0].v, s_nf[1].v]
            for bi, (r, m) in enumerate(qblocks):
                if bi == len(qblocks) // 2 and pi + 1 < len(passes):
                    sched_tables(pi + 1)
                qslot = cn["qi"] % 2
                kq = cn["qi"] // 2
                grp_m0 = (m // gsz) * gsz
                first = (m % gsz == 0)
                last = (m % gsz == gsz - 1)
                if first:
                    gslot = cn["gi"] % 2
                    kg = cn["gi"] // 2
                    cn["gi"] += 1
                qs = slice(d * 128 * m + r, d * 128 * m + r + 127 * d + 1, d)

                def qk(e, r=r, m=m, qslot=qslot, qs=qs):
                    ins = None
                    for hh in range(2):
                        bank = 2 * qslot + hh
                        for ch in range(2):
                            n = m + pblk - 1 + ch
                            ks = slice(d * 128 * n + r, d * 128 * n + r + 127 * d + 1, d)
                            ins = e.matmul(S[:, bank, ch * 128:(ch + 1) * 128],
                                           lhsT=KA[slot][hh * 64:(hh + 1) * 64, ks],
                                           rhs=QA[slot][hh * 64:(hh + 1) * 64, qs],
                                           start=True, stop=True)
                    return ins

                P.add("tensor", qk, waits=[pw, (s_ld[slot], ld_val[pi]), (s_c, vconst), (s_ex[qslot], kq)],
                      inc=s_qk[qslot])
                P.add("scalar", lambda e, qslot=qslot: e.activation(
                    out=PTF[qslot][:], in_=S[:, 2 * qslot:2 * qslot + 2, 0:256], func=AF.Exp, scale=0.125),
                    waits=[pw, (s_qk[qslot], kq + 1), (s_pm[qslot], kq)], inc=s_ex[qslot])
                P.add("vector", lambda e, qslot=qslot: e.tensor_tensor(
                    out=PT[qslot][:], in0=PTF[qslot][:], in1=EB[slot][:], op=ALU.mult),
                    waits=[pw, (s_ex[qslot], kq + 1), (s_pv[qslot], kq), (s_bt, bt_val[pi])], inc=s_pm[qslot])
                if pend is not None:
                    emit_pv(pend)
                pend = (qslot, kq, r, m, gslot, kg, first, last, grp_m0)
                cn["qi"] += 1
            emit_pv(pend)
            pass_last_pe[pi] = [(s_pv[0], s_pv[0].v), (s_pv[1], s_pv[1].v)]
            pass_last_dve[pi] = [(s_pm[0], s_pm[0].v), (s_pm[1], s_pm[1].v)]
            if g == 2:
                wd = [(s_dv, s_dv.v), (s_nf[0], s_nf[0].v), (s_nf[1], s_nf[1].v)]
                v1 = P.add("vector", lambda e: e.reciprocal(out=DACC[:], in_=DACC[:]), waits=wd, inc=s_fin)
                v2 = P.add("vector", lambda e: e.tensor_tensor(out=OAS[:], in0=NACC[:], in1=DACC[:], op=ALU.mult),
                           waits=[(s_fin, v1), (s_st, 16 * hp)], inc=s_fin)
                cn["fin"] = v2
                P.add("gpsimd", lambda e, hp=hp: e.dma_start(out=D["OAT"][hp], in_=OAS[:]),
                      waits=[pw, (s_fin, v2)], inc=s_st, dma=True)
        for pi, (hp, g) in enumerate(passes):
            run_pass(pi, hp, g)
        P.end_phase(JK[:])


def phase3(nc, P, D):
    with ExitStack() as es:
        KT = [_sb(es, nc, f"p3_KT{i}", [128, NK], BF16) for i in range(2)]
        QT = [_sb(es, nc, f"p3_QT{i}", [128, NQ], BF16) for i in range(2)]
        VH = [_sb(es, nc, f"p3_VH{i}", [128, 64, 129], BF16) for i in range(2)]
        GB = _sb(es, nc, "p3_GB", [128, 8, 256], F32)
        CB = _sb(es, nc, "p3_CB", [128, 8], F32)
        MB = _sb(es, nc, "p3_MB", [128, 256], F32)
        BTF = _sb(es, nc, "p3_BTF", [128, 256], F32)
        BT = _sb(es, nc, "p3_BT", [128, 8, 2, 256], BF16)
        IDF = _sb(es, nc, "p3_IDF", [128, 128], F32)
        IDB = _sb(es, nc, "p3_IDB", [128, 128], BF16)
        ZB = _sb(es, nc, "p3_ZB", [128, 512], BF16)
        PVF = _sb(es, nc, "p3_PVF", [128, 64], F32)
        LAMI = _sb(es, nc, "p3_LAMI", [128, 4, 64], F32)
        LPR = _sb(es, nc, "p3_LPR", [128, 2, 64], F32)
        LS = _sb(es, nc, "p3_LS", [128, 2], F32)
        LE = _sb(es, nc, "p3_LE", [128, 2], F32)
        NLAM = _sb(es, nc, "p3_NLAM", [128, 1], F32)
        GS = _sb(es, nc, "p3_GS", [128, 128], F32)
        EPSB = _sb(es, nc, "p3_EPSB", [128, 1], F32)
        PT = [_sb(es, nc, f"p3_PT{i}", [128, 2, 512], BF16) for i in range(3)]
        EP = _sb(es, nc, "p3_EP", [128, 3, 512], F32)
        RD = _sb(es, nc, "p3_RD", [128, 8], F32)
        TT = _sb(es, nc, "p3_TT", [128, 128], F32)
        OO = _sb(es, nc, "p3_OO", [128, 4, 128], F32)
        SQ = _sb(es, nc, "p3_SQ", [128, 128], F32)
        SSQ = _sb(es, nc, "p3_SSQ", [128, 4], F32)
        LNV = _sb(es, nc, "p3_LNV", [128, 4], F32)
        RSTD = _sb(es, nc, "p3_RSTD", [128, 4], F32)
        ON = _sb(es, nc, "p3_ON", [128, 4, 128], BF16)
        OBS = [_sb(es, nc, f"p3_OBS{i}", [128, 512], BF16) for i in range(2)]
        JK = _sb(es, nc, "p3_JK", [128, 1], F32)
        S = _ps(es, nc, "p3_S", [128, 4, 512], F32)
        ACC = _ps(es, nc, "p3_ACC", [128, 3, 512], F32)
        TP = _ps(es, nc, "p3_TP", [128, 4, 128], BF16)
        s_c = P.sem("p3c")
        s_ld = [P.sem(f"p3ld{i}") for i in range(2)]
        s_qk = [P.sem(f"p3qk{i}") for i in range(2)]
        s_ex = [P.sem(f"p3ex{i}") for i in range(2)]
        s_pv = [P.sem(f"p3pv{i}") for i in range(3)]
        s_af = P.sem("p3af")
        s_ep = P.sem("p3ep")
        s_ea = P.sem("p3ea")
        s_tp = P.sem("p3tp")
        s_tc = P.sem("p3tc")
        s_st = [P.sem(f"p3st{i}") for i in range(2)]
        pw = P.pw()

        def accap(m, j):
            idx = m * 4 + j
            return idx // 3, (idx % 3) * 170

        for (dst, src) in ((GB, D["gb"].rearrange("h i c -> i h c")), (CB, D["cb"]), (MB, D["mb"]), (IDF, D["ident"]),
                           (PVF, D["pvalid"]), (LAMI, D["lam"]), (GS, D["gs"])):
            vc = P.add("sync", lambda e, dst=dst, src=src: e.dma_start(out=dst[:], in_=src), waits=[pw], inc=s_c, dma=True)
        w0 = [pw, (s_c, vc)]
        P.add("vector", lambda e: e.tensor_copy(out=IDB[:], in_=IDF[:]), waits=w0, inc=s_c)
        P.add("vector", lambda e: e.memset(ZB[:], 0.0), waits=w0, inc=s_c)
        P.add("vector", lambda e: e.memset(EPSB[:], LN_EPS), waits=w0, inc=s_c)
        v = P.add("vector", lambda e: e.tensor_scalar(out=GS[:], in0=GS[:], scalar1=1.0 - LAM_INIT, scalar2=None, op0=ALU.mult),
                  waits=w0, inc=s_c)
        for i in range(2):
            P.add("vector", lambda e, i=i: e.memset(VH[i][:, 32:64, 128:129], 1.0), waits=w0, inc=s_c)
            P.add("vector", lambda e, i=i: e.tensor_copy(out=VH[i][:, 0:32, 128:129],
                                                         in_=PVF[:, 0:32].rearrange("p (a b) -> p a b", b=1)),
                  waits=w0, inc=s_c)
        v = P.add("vector", lambda e: e.tensor_tensor(out=LPR[:], in0=LAMI[:, 0:2, :], in1=LAMI[:, 2:4, :], op=ALU.mult),
                  waits=w0, inc=s_c)
        v = P.add("vector", lambda e: e.reduce_sum(out=LS[:], in_=LPR[:], axis=AX.X), waits=[(s_c, v)], inc=s_c)
        v = P.add("scalar", lambda e: e.activation(out=LE[:], in_=LS[:], func=AF.Exp), waits=[pw, (s_c, v)], inc=s_c)
        v = P.add("vector", lambda e: e.tensor_tensor(out=NLAM[:], in0=LE[:, 1:2], in1=LE[:, 0:1], op=ALU.subtract),
                  waits=[(s_c, v)], inc=s_c)
        v = P.add("vector", lambda e: e.tensor_scalar(out=NLAM[:], in0=NLAM[:], scalar1=-LAM_INIT, scalar2=None, op0=ALU.add),
                  waits=[(s_c, v)], inc=s_c)
        for h in range(8):
            v = P.add("vector", lambda e, h=h: e.tensor_scalar(out=BTF[:], in0=GB[:, h, :], scalar1=CB[:, h:h + 1], scalar2=8.0,
                                                               op0=ALU.subtract, op1=ALU.mult), waits=[(s_c, v)], inc=s_c)
            v = P.add("vector", lambda e: e.tensor_tensor(out=BTF[:], in0=BTF[:], in1=MB[:], op=ALU.add), waits=[(s_c, v)], inc=s_c)
            v = P.add("vector", lambda e, h=h: e.tensor_copy(out=BT[:, h, 0, :], in_=BTF[:]), waits=[(s_c, v)], inc=s_c)
            v = P.add("vector", lambda e, h=h: e.tensor_tensor(out=BT[:, h, 1, :], in0=BTF[:], in1=BT[:, h, 0, :], op=ALU.subtract),
                      waits=[(s_c, v)], inc=s_c)
        vconst = v

        head_last_pe = {}
        ld_val = {}
        vbv = D["VB"].rearrange("(kb p) c -> p kb c", p=128)

        def sched_loads(h):
            slot = h % 2
            w = [pw, (s_c, vconst)] + (head_last_pe[h - 2] if h >= 2 else [])
            P.add("sync", lambda e: e.dma_start(out=KT[slot][:], in_=D["KTB"][h]), waits=w, inc=s_ld[slot], dma=True)
            P.add("sync", lambda e: e.dma_start(out=QT[slot][:], in_=D["QTB"][h]), waits=w, inc=s_ld[slot], dma=True)
            for q in range(4):
                v = P.add("sync", lambda e, q=q: e.dma_start(out=VH[slot][:, q * 16:(q + 1) * 16, 0:128],
                                                           in_=vbv[:, q * 16:(q + 1) * 16, h * 128:(h + 1) * 128]),
                          waits=w, inc=s_ld[slot], dma=True)
            ld_val[h] = v

        deferred = []
        ui = [0]

        def flush(force=False):
            while deferred and (force or deferred[0][0] <= ui[0]):
                _, fn = deferred.pop(0)
                fn()

        sched_loads(0)
        cn = {"nqt": 0}
        pending = []

        def emit_pv(info):
            (sslot, ku, pslot, kb, jmin, firstu, lastu, hs, kq_tile, on_last) = info

            def pv(e):
                ins = None
                if firstu:
                    for b in range(3):
                        e.matmul(ACC[:, b, :], lhsT=ZB[:, 0:128], rhs=ZB[:, :], start=True, stop=True,
                                 skip_group_check=True)
                for m in range(2):
                    for j in range(jmin, 4):
                        b, off = accap(m, j)
                        ins = e.matmul(ACC[:, b, off:off + 129], lhsT=PT[pslot][:, m, j * 128:(j + 1) * 128],
                                       rhs=VH[hs][:, kb, :], start=False, stop=lastu, skip_group_check=True)
                return ins

            w = [(s_ex[sslot], ku + 1)]
            if firstu:
                w.append((s_af, kq_tile))
            v = P.add("tensor", pv, waits=w, inc=s_pv[pslot])
            if lastu:
                on_last(pslot, v)

        def drain(keep):
            while len(pending) > keep:
                emit_pv(pending.pop(0))

        def run_qt(h, qt):
                hs = h % 2
                base = 32 + 4 * qt
                nkb = base + 4
                kq_tile = cn["nqt"]
                cn["nqt"] += 1
                nqt = cn["nqt"]

                def on_last(last_slot, vlast):
                    epilogue(last_slot, vlast)

                for kb in range(nkb):
                    u = ui[0]
                    slot = u % 2
                    ku = u // 2
                    pslot = u % 3
                    kp = u // 3
                    jd = kb - base
                    c0 = max(jd, 0) * 128
                    jmin = max(jd, 0)

                    def qk(e, kb=kb, jd=jd, c0=c0, slot=slot, qt=qt):
                        ins = None
                        for m in range(2):
                            bank = 2 * slot + m
                            near = jd >= -1
                            if near:
                                if jd == -1:
                                    oc, bc, n = 0, 128, 128
                                elif jd == 3:
                                    oc, bc, n = 384, 0, 128
                                else:
                                    oc, bc, n = jd * 128, 0, 256
                                e.matmul(S[:, bank, oc:oc + n], lhsT=IDB[:], rhs=BT[:, h, 0, bc:bc + n], start=True, stop=False)
                                e.matmul(S[:, bank, oc:oc + n], lhsT=IDB[:], rhs=BT[:, h, 1, bc:bc + n], start=False, stop=False)
                            ins = e.matmul(S[:, bank, c0:512], lhsT=KT[hs][m * 64:(m + 1) * 64, kb * 128:(kb + 1) * 128],
                                           rhs=QT[hs][m * 64:(m + 1) * 64, qt * 512 + c0:(qt + 1) * 512],
                                           start=(not near), stop=True, skip_group_check=True)
                        return ins

                    P.add("tensor", qk, waits=[pw, (s_ld[hs], ld_val[h]), (s_c, vconst), (s_ex[slot], ku)], inc=s_qk[slot])
                    P.add("scalar", lambda e, slot=slot, c0=c0, pslot=pslot: e.activation(
                        out=PT[pslot][:, :, c0:512], in_=S[:, 2 * slot:2 * slot + 2, c0:512], func=AF.Exp, scale=0.125),
                        waits=[pw, (s_qk[slot], ku + 1), (s_pv[pslot], kp)], inc=s_ex[slot])
                    pending.append((slot, ku, pslot, kb, jmin, kb == 0, kb == nkb - 1, hs, kq_tile, on_last))
                    drain(2)
                    ui[0] += 1
                    flush()

                def epilogue(last_slot, vlast):

                    def stage_a(last_slot=last_slot, vlast=vlast, h=h, qt=qt, k=nqt):
                        v = P.add("vector", lambda e: e.tensor_copy(out=EP[:], in_=ACC[:]),
                                  waits=[(s_pv[last_slot], vlast), (s_ep, s_ep.v)], inc=s_af)
                        we = [(s_af, v)]
                        for m in range(2):
                            for j in range(4):
                                b, off = accap(m, j)
                                idx = m * 4 + j
                                P.add("vector", lambda e, b=b, off=off, idx=idx: e.reciprocal(
                                    out=RD[:, idx:idx + 1], in_=EP[:, b, off + 128:off + 129]), waits=we, inc=s_ep)
                        v = P.add("vector", lambda e: e.tensor_scalar(out=RD[:, 4:8], in0=RD[:, 4:8], scalar1=NLAM[:, 0:1], scalar2=None,
                                                                      op0=ALU.mult), waits=[(s_ep, s_ep.v)], inc=s_ep)
                        for j in range(4):
                            b1, o1 = accap(0, j)
                            b2, o2 = accap(1, j)
                            v = P.add("vector", lambda e, b2=b2, o2=o2, j=j: e.tensor_scalar(
                                out=TT[:], in0=EP[:, b2, o2:o2 + 128], scalar1=RD[:, 4 + j:5 + j], scalar2=None, op0=ALU.mult),
                                waits=[(s_ep, v)], inc=s_ep)
                            v = P.add("vector", lambda e, b1=b1, o1=o1, j=j: e.scalar_tensor_tensor(
                                out=OO[:, j, :], in0=EP[:, b1, o1:o1 + 128], scalar=RD[:, j:j + 1], in1=TT[:],
                                op0=ALU.mult, op1=ALU.add), waits=[(s_ep, v)], inc=s_ep)
                            v = P.add("vector", lambda e, j=j: e.scalar_tensor_tensor(
                                out=SQ[:], in0=OO[:, j, :], scalar=1.0, in1=OO[:, j, :], op0=ALU.mult, op1=ALU.mult,
                                accum_out=SSQ[:, j:j + 1]), waits=[(s_ep, v)], inc=s_ep)
                        return v

                    va = stage_a()

                    def stage_bc(va=va):
                        v = P.add("scalar", lambda e: e.activation(out=LNV[:], in_=SSQ[:], func=AF.Ln, bias=EPSB[:, 0:1], scale=1.0 / 128.0),
                                  waits=[(s_ep, va), (s_ea, s_ea.v)], inc=s_ea)
                        v = P.add("scalar", lambda e: e.activation(out=RSTD[:], in_=LNV[:], func=AF.Exp, scale=-0.5),
                                  waits=[(s_ea, v)], inc=s_ea)
                        vv = None
                        for j in range(4):
                            vv = P.add("vector", lambda e, j=j: e.scalar_tensor_tensor(
                                out=ON[:, j, :], in0=OO[:, j, :], scalar=RSTD[:, j:j + 1], in1=GS[:], op0=ALU.mult, op1=ALU.mult),
                                waits=[(s_ea, v), (s_tp, s_tp.v)], inc=s_ep)
                        return vv

                    def stage_de(vc, h=h, qt=qt, k=nqt):
                        def tp(e):
                            ins = None
                            for j in range(4):
                                ins = e.transpose(out=TP[:, j, :], in_=ON[:, j, :], identity=IDB[:])
                            return ins
                        v = P.add("tensor", tp, waits=[(s_ep, vc), (s_tc, s_tc.v)], inc=s_tp)
                        os_ = (k - 1) % 2
                        v2 = P.add("vector", lambda e: e.tensor_copy(out=OBS[os_][:].rearrange("p (a b) -> p a b", a=4), in_=TP[:]),
                                   waits=[(s_tp, v), (s_st[os_], 16 * ((k - 1) // 2))], inc=s_tc)
                        P.add("gpsimd", lambda e: e.dma_start(out=D["OBT"][h][:, qt * 512:(qt + 1) * 512], in_=OBS[os_][:]),
                              waits=[pw, (s_tc, v2)], inc=s_st[os_], dma=True)

                    def chain(stage_bc=stage_bc, stage_de=stage_de):
                        vc = stage_bc()
                        deferred.append((ui[0] + 3, lambda: stage_de(vc)))

                    deferred.append((ui[0] + 3, chain))
        for h in range(8):
            if h + 1 < 8:
                sched_loads(h + 1)
            for qt in range(8):
                run_qt(h, qt)
            drain(0)
            head_last_pe[h] = [(s_pv[i], s_pv[i].v) for i in range(3)]
        flush(force=True)
        flush(force=True)
        P.end_phase(JK[:])


def layer_norm_ops(P, V, ST6, MV, STD, RSTD, TMP, OUT, LG, LB, EPSB, sem, wait_in):
    v = None
    for hf in range(2):
        v = P.add("vector", lambda e, hf=hf: e.bn_stats(out=ST6[:, hf * 6:(hf + 1) * 6], in_=V[:, hf * 512:(hf + 1) * 512]),
                  waits=wait_in + [(sem, sem.v)], inc=sem)
    v = P.add("vector", lambda e: e.bn_aggr(out=MV[:], in_=ST6[:]), waits=[(sem, v)], inc=sem)
    v = P.add("scalar", lambda e: e.activation(out=STD[:], in_=MV[:, 1:2], func=AF.Sqrt, bias=EPSB[:, 0:1], scale=1.0),
              waits=[(sem, v)], inc=sem)
    v = P.add("vector", lambda e: e.reciprocal(out=RSTD[:], in_=STD[:]), waits=[(sem, v)], inc=sem)
    v = P.add("vector", lambda e: e.scalar_tensor_tensor(out=TMP[:], in0=V[:], scalar=MV[:, 0:1], in1=LG[:],
                                                         op0=ALU.subtract, op1=ALU.mult), waits=[(sem, v)], inc=sem)
    v = P.add("vector", lambda e: e.scalar_tensor_tensor(out=OUT, in0=TMP[:], scalar=RSTD[:, 0:1], in1=LB[:],
                                                         op0=ALU.mult, op1=ALU.add), waits=[(sem, v)], inc=sem)
    return v


def load_cast_weight(P, pw, dst_chunks, src_chunks, WF, s_wld, s_wcv, cnt):
    for dst, src in zip(dst_chunks, src_chunks):
        n = cnt[0]
        slot = n % 2
        v = P.add("sync", lambda e, slot=slot, src=src: e.dma_start(out=WF[slot][:], in_=src),
                  waits=[pw, (s_wcv, n - 1)], inc=s_wld[slot], dma=True)
        P.add("vector", lambda e, slot=slot, dst=dst: e.tensor_copy(out=dst, in_=WF[slot][:]),
              waits=[pw, (s_wld[slot], v)], inc=s_wcv)
        cnt[0] += 1
    return s_wcv.v


def phase4a(nc, P, D):
    with ExitStack() as es:
        WPA = _sb(es, nc, "p4_WPA", [128, 4, 1024], BF16)
        WPB = _sb(es, nc, "p4_WPB", [128, 8, 1024], BF16)
        WO = _sb(es, nc, "p4_WO", [128, 8, 1024], BF16)
        WF = [_sb(es, nc, f"p4_WF{i}", [128, 2, 1024], F32) for i in range(2)]
        OAt = [_sb(es, nc, f"p4_OA{i}", [128, 4, 512], BF16) for i in range(2)]
        OBt = [_sb(es, nc, f"p4_OB{i}", [128, 8, 512], BF16) for i in range(2)]
        SGt = _sb(es, nc, "p4_SG", [128, 16, 512], BF16)
        XS = _sb(es, nc, "p4_XS", [128, 4, 1024], F32)
        MT = _sb(es, nc, "p4_MT", [128, 8, 512], BF16)
        T1 = [_sb(es, nc, f"p4_T1{i}", [128, 512], F32) for i in range(2)]
        T2 = [_sb(es, nc, f"p4_T2{i}", [128, 512], F32) for i in range(2)]
        V = _sb(es, nc, "p4_V", [128, 1024], F32)
        TMP = _sb(es, nc, "p4_TMP", [128, 1024], F32)
        X1O = [_sb(es, nc, f"p4_X1O{i}", [128, 1024], F32) for i in range(2)]
        X1B = _sb(es, nc, "p4_X1B", [128, 1024], BF16)
        X1TS = _sb(es, nc, "p4_X1TS", [128, 8, 512], BF16)
        LG = _sb(es, nc, "p4_LG", [128, 1024], F32)
        LB = _sb(es, nc, "p4_LB", [128, 1024], F32)
        ST6 = _sb(es, nc, "p4_ST6", [128, 12], F32)
        MV = _sb(es, nc, "p4_MV", [128, 2], F32)
        STD = _sb(es, nc, "p4_STD", [128, 1], F32)
        RSTD = _sb(es, nc, "p4_RSTD", [128, 1], F32)
        EPSB = _sb(es, nc, "p4_EPSB", [128, 1], F32)
        IDF = _sb(es, nc, "p4_IDF", [128, 128], F32)
        IDB = _sb(es, nc, "p4_IDB", [128, 128], BF16)
        JK = _sb(es, nc, "p4_JK", [128, 1], F32)
        PSY = _ps(es, nc, "p4_PSY", [128, 4, 512], F32)
        PSM = _ps(es, nc, "p4_PSM", [128, 2, 512], F32)
        PST = _ps(es, nc, "p4_PST", [128, 8, 128], BF16)
        s_c = P.sem("p4c")
        s_wld = [P.sem(f"p4wld{i}") for i in range(2)]
        s_wcv = P.sem("p4wcv")
        s_ldab = [P.sem(f"p4ldab{i}") for i in range(2)]
        s_ldsg = P.sem("p4ldsg")
        s_ldx = P.sem("p4ldx")
        s_y = [P.sem(f"p4y{i}") for i in range(2)]
        s_t = [P.sem(f"p4t{i}") for i in range(2)]
        s_mt = P.sem("p4mt")
        s_mm = [P.sem(f"p4mm{i}") for i in range(2)]
        s_v = [P.sem(f"p4v{i}") for i in range(2)]
        s_ln = P.sem("p4ln")
        s_xb = P.sem("p4xb")
        s_tp = P.sem("p4tp")
        s_tc = P.sem("p4tc")
        s_so = [P.sem(f"p4so{i}") for i in range(2)]
        s_sx = P.sem("p4sx")
        pw = P.pw()

        for (dst, src) in ((LG, D["ln1g"]), (LB, D["ln1b"]), (IDF, D["ident"])):
            vc = P.add("sync", lambda e, dst=dst, src=src: e.dma_start(out=dst[:], in_=src), waits=[pw], inc=s_c, dma=True)
        P.add("vector", lambda e: e.tensor_copy(out=IDB[:], in_=IDF[:]), waits=[pw, (s_c, vc)], inc=s_c)
        vconst = P.add("vector", lambda e: e.memset(EPSB[:], LN_EPS), waits=[pw], inc=s_c)

        cnt = [0]
        wpa = D["wpa"].rearrange("(kc p) c -> p kc c", p=128)
        wpb = D["wpb"].rearrange("(kc p) c -> p kc c", p=128)
        wo = D["wo"].rearrange("(kc p) c -> p kc c", p=128)
        dsts, srcs = [], []
        for (Wt, wsrc, nk) in ((WPA, wpa, 4), (WPB, wpb, 8), (WO, wo, 8)):
            for k2 in range(nk // 2):
                dsts.append(Wt[:, 2 * k2:2 * k2 + 2, :])
                srcs.append(wsrc[:, 2 * k2:2 * k2 + 2, :])
        vw = load_cast_weight(P, pw, dsts, srcs, WF, s_wld, s_wcv, cnt)

        oat = D["OAT"].rearrange("k p t -> p k t")
        obt = D["OBT"].rearrange("k p t -> p k t")
        sgt = D["SG"].rearrange("k p t -> p k t")
        xo = D["xo"].rearrange("(s p) d -> p s d", p=128)
        x1 = D["X1"].rearrange("(s p) d -> p s d", p=128)
        x1t = D["X1T"].rearrange("k p t -> p k t")

        last_y_pe = {}
        ld_ab = {}

        def load_ab(t):
            sl = t % 2
            w = [pw] + (last_y_pe[t - 2] if t >= 2 else [])
            P.add("sync", lambda e: e.dma_start(out=OAt[sl][:], in_=oat[:, :, t * 512:(t + 1) * 512]), waits=w, inc=s_ldab[sl], dma=True)
            ld_ab[t] = P.add("sync", lambda e: e.dma_start(out=OBt[sl][:], in_=obt[:, :, t * 512:(t + 1) * 512]), waits=w,
                             inc=s_ldab[sl], dma=True)

        load_ab(0)
        yj = 0
        mj = 0
        sbk = 0
        for t in range(8):
            sl = t % 2
            if t + 1 < 8:
                load_ab(t + 1)
            v_sg = P.add("sync", lambda e, t=t: e.dma_start(out=SGt[:], in_=sgt[:, :, t * 512:(t + 1) * 512]),
                         waits=[pw, (s_t[0], s_t[0].v), (s_t[1], s_t[1].v)], inc=s_ldsg, dma=True)
            v_x = P.add("sync", lambda e, t=t: e.dma_start(out=XS[:], in_=xo[:, 4 * t:4 * t + 4, :]),
                        waits=[pw, (s_v[0], s_v[0].v), (s_v[1], s_v[1].v)], inc=s_ldx, dma=True)
            mt_w0 = (s_mm[0], s_mm[0].v), (s_mm[1], s_mm[1].v)
            for dc in range(8):
                ys = yj % 2
                ky = yj // 2

                def ymm(e, ys=ys, dc=dc, sl=sl):
                    ins = None
                    for k in range(4):
                        ins = e.matmul(PSY[:, 2 * ys, :], lhsT=WPA[:, k, dc * 128:(dc + 1) * 128], rhs=OAt[sl][:, k, :],
                                       start=(k == 0), stop=(k == 3))
                    for k in range(8):
                        ins = e.matmul(PSY[:, 2 * ys + 1, :], lhsT=WPB[:, k, dc * 128:(dc + 1) * 128], rhs=OBt[sl][:, k, :],
                                       start=(k == 0), stop=(k == 7))
                    return ins

                P.add("tensor", ymm, waits=[pw, (s_wcv, vw), (s_ldab[sl], ld_ab[t]), (s_t[ys], 2 * ky)], inc=s_y[ys])
                wt = [pw, (s_y[ys], ky + 1), (s_ldsg, v_sg), (s_mt, s_mt.v - 1)]
                P.add("vector", lambda e, ys=ys, dc=dc: e.tensor_tensor(out=T1[ys][:], in0=PSY[:, 2 * ys, :], in1=SGt[:, dc, :], op=ALU.mult),
                      waits=wt, inc=s_t[ys])
                vt = P.add("vector", lambda e, ys=ys, dc=dc: e.tensor_tensor(out=T2[ys][:], in0=PSY[:, 2 * ys + 1, :], in1=SGt[:, 8 + dc, :],
                                                                         op=ALU.mult), waits=wt, inc=s_t[ys])
                P.add("vector", lambda e, ys=ys, dc=dc: e.tensor_tensor(out=MT[:, dc, :], in0=T1[ys][:], in1=T2[ys][:], op=ALU.add),
                      waits=[pw, (s_t[ys], vt)] + list(mt_w0), inc=s_mt)
                yj += 1
            last_y_pe[t] = [(s_y[0], s_y[0].v), (s_y[1], s_y[1].v)]
            v_mt = s_mt.v
            for s in range(4):
                xs_o = sbk % 2
                for hf in range(2):
                    ms = mj % 2
                    km = mj // 2

                    def mmm(e, ms=ms, s=s, hf=hf):
                        ins = None
                        for k in range(8):
                            ins = e.matmul(PSM[:, ms, :], lhsT=MT[:, k, s * 128:(s + 1) * 128], rhs=WO[:, k, hf * 512:(hf + 1) * 512],
                                           start=(k == 0), stop=(k == 7))
                        return ins

                    P.add("tensor", mmm, waits=[pw, (s_mt, v_mt), (s_v[ms], km)], inc=s_mm[ms])
                    vv = P.add("vector", lambda e, ms=ms, s=s, hf=hf: e.scalar_tensor_tensor(
                        out=V[:, hf * 512:(hf + 1) * 512], in0=XS[:, s, hf * 512:(hf + 1) * 512], scalar=ALPHA, in1=PSM[:, ms, :],
                        op0=ALU.mult, op1=ALU.add), waits=[pw, (s_mm[ms], km + 1), (s_ldx, v_x), (s_ln, s_ln.v)], inc=s_v[ms])
                    mj += 1
                win = [(s_v[0], s_v[0].v), (s_v[1], s_v[1].v), (s_c, vconst), (s_so[xs_o], 16 * (sbk // 2)), (s_xb, s_xb.v)]
                vln = layer_norm_ops(P, V, ST6, MV, STD, RSTD, TMP, X1O[xs_o][:], LG, LB, EPSB, s_ln, win)
                row = 4 * t + s
                P.add("sync", lambda e, xs_o=xs_o, row=row: e.dma_start(out=x1[:, row, :], in_=X1O[xs_o][:]),
                      waits=[pw, (s_ln, vln)], inc=s_so[xs_o], dma=True)
                vb = P.add("scalar", lambda e, xs_o=xs_o: e.copy(out=X1B[:], in_=X1O[xs_o][:]),
                           waits=[pw, (s_ln, vln), (s_tp, s_tp.v)], inc=s_xb)

                def tp(e):
                    ins = None
                    for k in range(8):
                        ins = e.transpose(out=PST[:, k, :], in_=X1B[:, k * 128:(k + 1) * 128], identity=IDB[:])
                    return ins

                vtp = P.add("tensor", tp, waits=[pw, (s_xb, vb), (s_tc, s_tc.v)], inc=s_tp)
                P.add("scalar", lambda e, s=s: e.copy(out=X1TS[:, :, s * 128:(s + 1) * 128], in_=PST[:]),
                      waits=[pw, (s_tp, vtp), (s_sx, 16 * t)], inc=s_tc)
                sbk += 1
            P.add("gpsimd", lambda e, t=t: e.dma_start(out=x1t[:, :, t * 512:(t + 1) * 512], in_=X1TS[:]),
                  waits=[pw, (s_tc, s_tc.v)], inc=s_sx, dma=True)
        P.end_phase(JK[:])


def phase4b(nc, P, D):
    TT = 256
    NT = NQ // TT
    with ExitStack() as es:
        W1 = _sb(es, nc, "p5_W1", [128, 8, 4096], BF16)
        W2 = _sb(es, nc, "p5_W2", [128, 32, 1024], BF16)
        WF = [_sb(es, nc, f"p5_WF{i}", [128, 1024], F32) for i in range(2)]
        XT = [_sb(es, nc, f"p5_XT{i}", [128, 8, TT], BF16) for i in range(2)]
        X1 = _sb(es, nc, "p5_X1", [128, 2, 1024], F32)
        HT = _sb(es, nc, "p5_HT", [128, 32, TT], BF16)
        RT = [_sb(es, nc, f"p5_RT{i}", [128, TT], F32) for i in range(2)]
        V = _sb(es, nc, "p5_V", [128, 1024], F32)
        TMP = _sb(es, nc, "p5_TMP", [128, 1024], F32)
        YO = [_sb(es, nc, f"p5_YO{i}", [128, 1024], F32) for i in range(2)]
        LG = _sb(es, nc, "p5_LG", [128, 1024], F32)
        LB = _sb(es, nc, "p5_LB", [128, 1024], F32)
        ST6 = _sb(es, nc, "p5_ST6", [128, 12], F32)
        MV = _sb(es, nc, "p5_MV", [128, 2], F32)
        STD = _sb(es, nc, "p5_STD", [128, 1], F32)
        RSTD = _sb(es, nc, "p5_RSTD", [128, 1], F32)
        EPSB = _sb(es, nc, "p5_EPSB", [128, 1], F32)
        JK = _sb(es, nc, "p5_JK", [128, 1], F32)
        PSH = _ps(es, nc, "p5_PSH", [128, 4, 512], F32)
        PSF = _ps(es, nc, "p5_PSF", [128, 2, 512], F32)
        s_c = P.sem("p5c")
        s_wld = [P.sem(f"p5wld{i}") for i in range(2)]
        s_wcv = P.sem("p5wcv")
        s_ldt = [P.sem(f"p5ldt{i}") for i in range(2)]
        s_ldx = P.sem("p5ldx")
        s_h = [P.sem(f"p5h{i}") for i in range(4)]
        s_r = [P.sem(f"p5r{i}") for i in range(4)]
        s_sq = [P.sem(f"p5sq{i}") for i in range(2)]
        s_f = [P.sem(f"p5f{i}") for i in range(2)]
        s_v = [P.sem(f"p5v{i}") for i in range(2)]
        s_ln = P.sem("p5ln")
        s_so = [P.sem(f"p5so{i}") for i in range(2)]
        pw = P.pw()

        for (dst, src) in ((LG, D["ln2g"]), (LB, D["ln2b"])):
            vc = P.add("sync", lambda e, dst=dst, src=src: e.dma_start(out=dst[:], in_=src), waits=[pw], inc=s_c, dma=True)
        vconst = P.add("vector", lambda e: e.memset(EPSB[:], LN_EPS), waits=[pw, (s_c, vc)], inc=s_c)

        cnt = [0]
        w1 = D["w1"].rearrange("(kc p) f -> p kc f", p=128)
        w2 = D["w2"].rearrange("(fc p) d -> p fc d", p=128)
        dsts, srcs = [], []
        for kc in range(8):
            for q in range(4):
                dsts.append(W1[:, kc, q * 1024:(q + 1) * 1024])
                srcs.append(w1[:, kc, q * 1024:(q + 1) * 1024])
        vw1 = load_cast_weight(P, pw, dsts, srcs, WF, s_wld, s_wcv, cnt)
        dsts, srcs = [], []
        for fc in range(32):
            dsts.append(W2[:, fc, :])
            srcs.append(w2[:, fc, :])
        vw2 = load_cast_weight(P, pw, dsts, srcs, WF, s_wld, s_wcv, cnt)

        x1t = D["X1T"].rearrange("k p t -> p k t")
        x1 = D["X1"].rearrange("(s p) d -> p s d", p=128)
        yv = D["y"].rearrange("(s p) d -> p s d", p=128)
        last_h_pe = {}
        ld_t = {}

        def load_t(t):
            sl = t % 2
            w = [pw] + (last_h_pe[t - 2] if t >= 2 else [])
            ld_t[t] = P.add("sync", lambda e: e.dma_start(out=XT[sl][:], in_=x1t[:, :, t * TT:(t + 1) * TT]), waits=w,
                            inc=s_ldt[sl], dma=True)

        load_t(0)
        hj = 0
        fj = 0
        sbk = 0
        for t in range(NT):
            sl = t % 2
            if t + 1 < NT:
                load_t(t + 1)
            v_x = P.add("sync", lambda e, t=t: e.dma_start(out=X1[:], in_=x1[:, 2 * t:2 * t + 2, :]),
                        waits=[pw, (s_v[0], s_v[0].v), (s_v[1], s_v[1].v)], inc=s_ldx, dma=True)
            ht_w0 = [(s_f[0], s_f[0].v), (s_f[1], s_f[1].v)]
            for fc in range(32):
                hs = hj % 4
                kh = hj // 4
                rs = hj % 2
                kr = hj // 2

                def hmm(e, hs=hs, fc=fc, sl=sl):
                    ins = None
                    for k in range(8):
                        ins = e.matmul(PSH[:, hs, 0:TT], lhsT=W1[:, k, fc * 128:(fc + 1) * 128], rhs=XT[sl][:, k, :],
                                       start=(k == 0), stop=(k == 7))
                    return ins

                P.add("tensor", hmm, waits=[pw, (s_wcv, vw1), (s_ldt[sl], ld_t[t]), (s_r[hs], kh)], inc=s_h[hs])
                P.add("scalar", lambda e, hs=hs, rs=rs: e.activation(out=RT[rs][:], in_=PSH[:, hs, 0:TT], func=AF.Relu),
                      waits=[pw, (s_h[hs], kh + 1), (s_sq[rs], kr)], inc=s_r[hs])
                P.add("vector",
                      lambda e, rs=rs, fc=fc: e.tensor_tensor(out=HT[:, fc, :], in0=RT[rs][:], in1=RT[rs][:], op=ALU.mult),
                      waits=[pw, (s_r[hs], kh + 1)] + ht_w0, inc=s_sq[rs])
                hj += 1
            last_h_pe[t] = [(s_h[i], s_h[i].v) for i in range(4)]
            v_sq = [(s_sq[0], s_sq[0].v), (s_sq[1], s_sq[1].v)]
            for s in range(2):
                yo = sbk % 2
                for hf in range(2):
                    fs = fj % 2
                    kf = fj // 2

                    def fmm(e, fs=fs, s=s, hf=hf):
                        ins = None
                        for fc in range(32):
                            ins = e.matmul(PSF[:, fs, :], lhsT=HT[:, fc, s * 128:(s + 1) * 128], rhs=W2[:, fc, hf * 512:(hf + 1) * 512],
                                           start=(fc == 0), stop=(fc == 31))
                        return ins

                    P.add("tensor", fmm, waits=[pw, (s_wcv, vw2), (s_v[fs], kf)] + v_sq, inc=s_f[fs])
                    P.add("vector", lambda e, fs=fs, s=s, hf=hf: e.scalar_tensor_tensor(
                        out=V[:, hf * 512:(hf + 1) * 512], in0=X1[:, s, hf * 512:(hf + 1) * 512], scalar=ALPHA, in1=PSF[:, fs, :],
                        op0=ALU.mult, op1=ALU.add), waits=[pw, (s_f[fs], kf + 1), (s_ldx, v_x), (s_ln, s_ln.v)], inc=s_v[fs])
                    fj += 1
                win = [(s_v[0], s_v[0].v), (s_v[1], s_v[1].v), (s_c, vconst), (s_so[yo], 16 * (sbk // 2))]
                vln = layer_norm_ops(P, V, ST6, MV, STD, RSTD, TMP, YO[yo][:], LG, LB, EPSB, s_ln, win)
                row = 2 * t + s
                P.add("sync", lambda e, yo=yo, row=row: e.dma_start(out=yv[:, row, :], in_=YO[yo][:]),
                      waits=[pw, (s_ln, vln)], inc=s_so[yo], dma=True)
                sbk += 1
        P.end_phase(JK[:])


def build_nc(debug=False, phases=(1, 2, 3, 4, 5)):
    nc = bass.Bass("TRN2", target_bir_lowering=False)
    D = {}

    def din(name, shape):
        D[name] = nc.dram_tensor(name, shape, F32, kind="ExternalInput").ap()

    din("xT", [1024, NK]); din("xo", [NQ, 1024]); din("wp", [1024, COLS_IN]); din("bg", [128, 16])
    din("ga", [24, 128, 256]); din("ma", [128, 256]); din("gb", [8, 128, 256]); din("cb", [128, 8]); din("mb", [128, 256])
    din("ident", [128, 128]); din("pvalid", [128, 64]); din("lam", [128, 4, 64]); din("gs", [128, 128])
    din("ln1g", [128, 1024]); din("ln1b", [128, 1024]); din("ln2g", [128, 1024]); din("ln2b", [128, 1024])
    din("wpa", [512, 1024]); din("wpb", [1024, 1024]); din("wo", [1024, 1024]); din("w1", [1024, 4096]); din("w2", [4096, 1024])
    D["y"] = nc.dram_tensor("y", [NQ, 1024], F32, kind="ExternalOutput").ap()
    kind = "ExternalOutput" if debug else "Internal"

    def scr(name, shape, dt):
        D[name] = nc.dram_tensor(name, shape, dt, kind=kind).ap()

    scr("QTB", [8, 128, NQ], BF16); scr("KTB", [8, 128, NK], BF16); scr("VB", [NK, 1024], BF16)
    scr("QTA", [12, 128, NQ], BF16); scr("KTA", [12, 128, NKA], BF16); scr("VA", [NKA, 1536], BF16)
    scr("SG", [16, 128, NQ], BF16); scr("OAT", [4, 128, NQ], BF16); scr("OBT", [8, 128, NQ], BF16)
    scr("X1", [NQ, 1024], F32); scr("X1T", [8, 128, NQ], BF16)
    with ExitStack() as es:
        P = Prog(nc, es)
        if 1 in phases:
            phase1(nc, P, D)
        if 2 in phases:
            phase2(nc, P, D)
        if 3 in phases:
            phase3(nc, P, D)
        if 4 in phases:
            phase4a(nc, P, D)
        if 5 in phases:
            phase4b(nc, P, D)
    return nc


def t5_bucket_np(dist):
    n = np.maximum(dist, 0)
    nf = np.maximum(n, 1).astype(np.float32)
    large = 16 + (np.log(nf / np.float32(16.0)) / np.float32(math.log(8.0)) * np.float32(16.0)).astype(np.int32)
    large = np.minimum(large, 31)
    return np.where(n < 16, n, large)


def prep_shared(inp):
    f = lambda a: np.ascontiguousarray(np.asarray(a, dtype=np.float32))
    w = np.asarray(inp["w_in"], dtype=np.float32)[0]
    B0 = 4608
    cols = []
    for h in range(8):
        cols += list(range(B0 + h * 64, B0 + h * 64 + 64)) + list(range(B0 + 512 + h * 64, B0 + 512 + h * 64 + 64))
    for h in range(8):
        cols += list(range(B0 + 1024 + h * 64, B0 + 1024 + h * 64 + 64)) + list(range(B0 + 1536 + h * 64, B0 + 1536 + h * 64 + 64))
    cols += list(range(0, 1536)) + list(range(1536, 3072)) + list(range(7680, 9728))
    cols += list(range(B0 + 2048, B0 + 3072)) + list(range(3072, 4608))
    assert len(cols) == COLS_IN
    sh = {"wp": f(w[:, cols])}
    sh["bg"] = f(np.asarray(inp["b_gate"], np.float32)[0].reshape(16, 128).T)
    rb = np.asarray(inp["rel_bias"], np.float32)
    i = np.arange(128)[:, None]
    c = np.arange(256)[None, :]
    ch = c // 128
    j = c % 128
    steps = (1 - ch) * 128 + j - i
    ga = np.zeros((24, 128, 256), np.float32)
    for g, d in enumerate(DILS):
        bk = t5_bucket_np(np.maximum(steps, 0) * d)
        for h in range(8):
            ga[g * 8 + h] = rb[bk, g * 8 + h]
    sh["ga"] = ga
    sh["ma"] = f(np.where((steps >= 0) & (steps <= 128), 0.0, MASKV / 8.0))
    dist = c - i
    bk = t5_bucket_np(np.maximum(dist, 0))
    gb = np.zeros((8, 128, 256), np.float32)
    for h in range(8):
        gb[h] = rb[bk, 24 + h]
    sh["gb"] = gb
    sh["cb"] = f(np.broadcast_to(rb[31, 24:32][None, :], (128, 8)))
    sh["mb"] = f(np.where(dist >= 0, 0.0, MASKV))
    sh["ident"] = np.eye(128, dtype=np.float32)
    lam = np.stack([np.asarray(inp[k], np.float32)[0] for k in ("lambda_q1", "lambda_q2", "lambda_k1", "lambda_k2")])
    sh["lam"] = f(np.broadcast_to(lam[None], (128, 4, 64)))
    sh["gs"] = f(np.broadcast_to(np.asarray(inp["subln_g"], np.float32)[0][None, :], (128, 128)))
    for k, n in (("ln1g", "ln1_g"), ("ln1b", "ln1_b"), ("ln2g", "ln2_g"), ("ln2b", "ln2_b")):
        sh[k] = f(np.broadcast_to(np.asarray(inp[n], np.float32)[0][None, :], (128, 1024)))
    sh["wpa"] = f(np.asarray(inp["w_proj_a"])[0]); sh["wpb"] = f(np.asarray(inp["w_proj_b"])[0])
    sh["wo"] = f(np.asarray(inp["w_out"])[0]); sh["w1"] = f(np.asarray(inp["w_mlp1"])[0]); sh["w2"] = f(np.asarray(inp["w_mlp2"])[0])
    return sh


def prep_core(x, c):
    b, hf = c // 2, c % 2
    xb = np.asarray(x[b], dtype=np.float32)
    own = xb[hf * NQ:(hf + 1) * NQ]
    xT = np.zeros((1024, NK), np.float32)
    if hf == 1:
        xT[:, 0:NQ] = xb[0:NQ].T
    xT[:, NQ:] = own.T
    pv = np.full((128, 64), float(hf), np.float32)
    return {"xT": xT, "xo": np.ascontiguousarray(own), "pvalid": pv}


_NC_CACHE = {}


def kernel(**inputs):
    x = np.asarray(inputs["x"], dtype=np.float32)
    sh = prep_shared(inputs)
    in_maps = []
    for c in range(8):
        m = dict(sh)
        m.update(prep_core(x, c))
        in_maps.append(m)
    if "nc" not in _NC_CACHE:
        _NC_CACHE["nc"] = build_nc()
    res = run_bass_kernel_spmd(_NC_CACHE["nc"], in_maps, core_ids=list(range(8)))
    out = np.zeros((BATCH, SEQ, D_MODEL), np.float32)
    for c in range(8):
        b, hf = c // 2, c % 2
        out[b, hf * NQ:(hf + 1) * NQ] = np.asarray(res.results[c]["y"], dtype=np.float32)
    return out
```

```python
import math
from contextlib import ExitStack

import numpy as np

import concourse.bass as bass
import concourse.mybir as mybir
from concourse.bass_utils import run_bass_kernel_spmd

F32 = mybir.dt.float32
BF16 = mybir.dt.bfloat16
AF = mybir.ActivationFunctionType
ALU = mybir.AluOpType
AX = mybir.AxisListType

D_MODEL = 1024
SEQ = 8192
BATCH = 4
NQ = 4096
NK = 8192
NKA = 8192
COLS_IN = 9728
DILS = (1, 4, 16)
ALPHA = 2.0 ** 0.25
LAM_INIT = 0.8 - 0.6 * math.exp(0.0)
LN_EPS = 1e-5
MASKV = -240000.0
ENGS = ("sync", "scalar", "vector", "gpsimd", "tensor")

C_FQ, C_FK, C_AQ, C_AK, C_GT, C_BV, C_AV = 0, 1024, 2048, 3584, 5120, 7168, 8192


class Sem:
    __slots__ = ("h", "abs", "base", "name")

    def __init__(self, h, name):
        self.h, self.abs, self.base, self.name = h, 0, 0, name

    @property
    def v(self):
        return self.abs - self.base


class Prog:
    def __init__(self, nc, es):
        self.nc, self.es = nc, es
        self.q = {k: [] for k in ENGS}
        self.waited = {k: {} for k in ENGS}
        self.sems = []
        self.pool = []
        self.used = 0
        self.nphase = 0
        self.bar = self.sem("bar")

    def sem(self, name):
        if name != "bar" and self.used < len(self.pool):
            s = self.pool[self.used]
            self.used += 1
            s.base = s.abs
            return s
        s = Sem(self.es.enter_context(self.nc.semaphore(f"s{len(self.sems)}")), f"s{len(self.sems)}")
        self.sems.append(s)
        if name != "bar":
            self.pool.append(s)
            self.used += 1
        return s

    def add(self, eng, fn, waits=(), inc=None, dma=False):
        ws = []
        for w in waits:
            if w is None:
                continue
            s, v = w
            if v <= 0:
                continue
            assert v <= s.v, f"deadlock: {eng} waits {s.name}>={v} but only {s.v} scheduled"
            va = s.base + v
            if self.waited[eng].get(s.name, 0) >= va:
                continue
            self.waited[eng][s.name] = va
            ws.append((s, va))
        amt = 16 if dma else 1
        newv = None
        if inc is not None:
            inc.abs += amt
            newv = inc.v
        self.q[eng].append((ws, fn, inc, amt))
        return newv

    def end_phase(self, junk):
        waits = [(s, s.v) for s in self.sems if s is not self.bar]
        self.nphase += 1
        self.add("gpsimd", lambda e: e.memset(junk, 0.0), waits=waits, inc=self.bar)
        with self.nc.Block() as block:
            for name in ENGS:
                items = self.q[name]

                def body(e, items=items):
                    for ws, fn, inc, amt in items:
                        for s, v in ws:
                            e.wait_ge(s.h, v)
                        ins = fn(e)
                        if inc is not None:
                            ins.then_inc(inc.h, amt)

                getattr(block, name)(body)
        self.q = {k: [] for k in ENGS}
        self.used = 0
        for name in ENGS:
            if name == "gpsimd":
                continue
            self.waited[name].pop(self.bar.name, None)
        self.phase_wait = (self.bar, self.nphase)

    def pw(self):
        return getattr(self, "phase_wait", None)


def _sb(es, nc, name, shape, dt):
    return es.enter_context(nc.sbuf_tensor(name, shape, dt))


def _ps(es, nc, name, shape, dt):
    return es.enter_context(nc.psum_tensor(name, shape, dt))


def phase1(nc, P, D):
    with ExitStack() as es:
        W = _sb(es, nc, "p1_W", [128, 8, COLS_IN], BF16)
        WF = [_sb(es, nc, f"p1_WF{i}", [128, 8, 128], F32) for i in range(2)]
        XF = _sb(es, nc, "p1_XF", [128, 8, 512], F32)
        XB = [_sb(es, nc, f"p1_XB{i}", [128, 8, 512], BF16) for i in range(2)]
        OST = [_sb(es, nc, f"p1_OST{i}", [128, 512], BF16) for i in range(4)]
        BG = _sb(es, nc, "p1_BG", [128, 16], F32)
        JK = _sb(es, nc, "p1_JK", [128, 1], F32)
        PS = _ps(es, nc, "p1_PS", [128, 4, 512], F32)
        s_xld, s_xcv, s_wcv, s_misc = P.sem("p1xld"), P.sem("p1xcv"), P.sem("p1wcv"), P.sem("p1misc")
        s_wld = [P.sem(f"p1wld{i}") for i in range(2)]
        s_mm = [P.sem(f"p1mm{i}") for i in range(4)]
        s_ev = [P.sem(f"p1ev{i}") for i in range(4)]
        s_st = [P.sem(f"p1st{i}") for i in range(4)]
        xT = D["xT"].rearrange("(kc p) t -> p kc t", p=128)
        wp = D["wp"].rearrange("(kc p) c -> p kc c", p=128)
        pw = P.pw()

        v_bg = P.add("sync", lambda e: e.dma_start(out=BG[:], in_=D["bg"]), waits=[pw], inc=s_misc, dma=True)

        wcv_of = {}
        wl = [0]

        def load_w(cid):
            n = wl[0]
            slot = n % 2
            v = P.add("sync", lambda e: e.dma_start(out=WF[slot][:], in_=wp[:, :, cid * 128:(cid + 1) * 128]),
                      waits=[pw, (s_wcv, n - 1)], inc=s_wld[slot], dma=True)
            P.add("vector", lambda e: e.tensor_copy(out=W[:, :, cid * 128:(cid + 1) * 128], in_=WF[slot][:]),
                  waits=[pw, (s_wld[slot], v)], inc=s_wcv)
            wcv_of[cid] = s_wcv.v
            wl[0] += 1

        worder = (list(range(8, 16)) + list(range(56, 64)) + list(range(28, 40)) + list(range(64, 76))
                  + list(range(0, 8)) + list(range(16, 28)) + list(range(40, 56)))
        wplan = {-1: worder[0:40]}
        for t in range(4):
            wplan[t] = worder[40 + 9 * t:40 + 9 * (t + 1)]

        tile_last = {}

        def load_x(t):
            v = P.add("sync", lambda e: e.dma_start(out=XF[:], in_=xT[:, :, t * 512:(t + 1) * 512]),
                      waits=[pw, (s_xcv, t)], inc=s_xld, dma=True)
            P.add("vector", lambda e: e.tensor_copy(out=XB[t % 2][:], in_=XF[:]),
                  waits=[pw, (s_xld, v)] + (tile_last[t - 2] if t >= 2 else []), inc=s_xcv)

        def jobs_for(t):
            jobs = []
            own = (t // 4) % 2 == 1
            to = t - 4 if t < 8 else t - 8
            for h in range(8):
                jobs.append(("F", C_FK + h * 128, D["KTB"][h][:, t * 512:(t + 1) * 512], None))
            for cb in range(2):
                for s in range(4):
                    r0 = t * 512 + s * 128
                    jobs.append(("T", C_BV + cb * 512, D["VB"][r0:r0 + 128, cb * 512:(cb + 1) * 512], s))
            for m in range(12):
                jobs.append(("F", C_AK + m * 128, D["KTA"][m][:, t * 512:(t + 1) * 512], None))
            for g in range(3):
                for s in range(4):
                    r0 = t * 512 + s * 128
                    jobs.append(("T", C_AV + g * 512, D["VA"][r0:r0 + 128, g * 512:(g + 1) * 512], s))
            if own:
                for h in range(8):
                    jobs.append(("F", C_FQ + h * 128, D["QTB"][h][:, to * 512:(to + 1) * 512], None))
                for m in range(12):
                    jobs.append(("F", C_AQ + m * 128, D["QTA"][m][:, to * 512:(to + 1) * 512], None))
                for i in range(16):
                    jobs.append(("G", C_GT + i * 128, D["SG"][i][:, to * 512:(to + 1) * 512], i))
            return jobs

        for cid in wplan[-1]:
            load_w(cid)
        load_x(0)
        jc = 0
        for t in range(16):
            jobs = jobs_for(t)
            half = len(jobs) // 2
            for ji, (kind, col, dst, aux) in enumerate(jobs):
                if ji == half:
                    if t + 1 < 16:
                        load_x(t + 1)
                    for cid in wplan.get(t, []):
                        load_w(cid)
                slot = jc % 4
                k = jc // 4
                xb = XB[t % 2]
                if kind == "T":
                    wneed = max(wcv_of[col // 128 + i] for i in range(4))
                else:
                    wneed = wcv_of[col // 128]

                def mm(e, kind=kind, col=col, aux=aux, xb=xb, slot=slot):
                    ins = None
                    for kc in range(8):
                        if kind == "T":
                            ins = e.matmul(PS[:, slot, :], lhsT=xb[:, kc, aux * 128:(aux + 1) * 128],
                                           rhs=W[:, kc, col:col + 512], start=(kc == 0), stop=(kc == 7))
                        else:
                            ins = e.matmul(PS[:, slot, :], lhsT=W[:, kc, col:col + 128],
                                           rhs=xb[:, kc, :], start=(kc == 0), stop=(kc == 7))
                    return ins

                P.add("tensor", mm, waits=[pw, (s_xcv, t + 1), (s_wcv, wneed), (s_ev[slot], k)], inc=s_mm[slot])
                evw = [pw, (s_mm[slot], k + 1), (s_st[slot], 16 * k)]
                if kind == "G":
                    P.add("scalar", lambda e, slot=slot, aux=aux: e.activation(
                        out=OST[slot][:], in_=PS[:, slot, :], func=AF.Sigmoid, bias=BG[:, aux:aux + 1], scale=1.0),
                        waits=evw + [(s_misc, v_bg)], inc=s_ev[slot])
                elif jc % 2 == 0:
                    P.add("vector", lambda e, slot=slot: e.tensor_copy(out=OST[slot][:], in_=PS[:, slot, :]),
                          waits=evw, inc=s_ev[slot])
                else:
                    P.add("scalar", lambda e, slot=slot: e.copy(out=OST[slot][:], in_=PS[:, slot, :]),
                          waits=evw, inc=s_ev[slot])
                P.add("gpsimd", lambda e, slot=slot, dst=dst: e.dma_start(out=dst, in_=OST[slot][:]),
                      waits=[pw, (s_ev[slot], k + 1)], inc=s_st[slot], dma=True)
                jc += 1
            tile_last[t] = [(s_mm[s], s_mm[s].v) for s in range(4)]
        P.end_phase(JK[:])


def phase2(nc, P, D):
    with ExitStack() as es:
        QA = [_sb(es, nc, f"p2_QA{i}", [128, NQ], BF16) for i in range(2)]
        KA = [_sb(es, nc, f"p2_KA{i}", [128, NKA], BF16) for i in range(2)]
        VS = [_sb(es, nc, f"p2_VS{i}", [128, 64, 128], BF16) for i in range(2)]
        GF = [_sb(es, nc, f"p2_GF{i}", [128, 2, 256], F32) for i in range(2)]
        BTF = _sb(es, nc, "p2_BTF", [128, 2, 256], F32)
        EB = [_sb(es, nc, f"p2_EB{i}", [128, 2, 256], F32) for i in range(2)]
        PTF = [_sb(es, nc, f"p2_PTF{i}", [128, 2, 256], F32) for i in range(2)]
        MA = _sb(es, nc, "p2_MA", [128, 256], F32)
        IDF = _sb(es, nc, "p2_IDF", [128, 128], F32)
        IDB = _sb(es, nc, "p2_IDB", [128, 128], BF16)
        PVF = _sb(es, nc, "p2_PVF", [128, 64], F32)
        PV64 = _sb(es, nc, "p2_PV64", [128, 64], BF16)
        ONE64 = _sb(es, nc, "p2_ONE64", [128, 64], BF16)
        PT = [_sb(es, nc, f"p2_PT{i}", [128, 2, 256], BF16) for i in range(2)]
        NACC = _sb(es, nc, "p2_NACC", [128, NQ], F32)
        DACC = _sb(es, nc, "p2_DACC", [128, NQ], F32)
        OAS = _sb(es, nc, "p2_OAS", [128, NQ], BF16)
        JK = _sb(es, nc, "p2_JK", [128, 1], F32)
        S = _ps(es, nc, "p2_S", [128, 4, 512], F32)
        ND = _ps(es, nc, "p2_ND", [128, 4, 512], F32)
        s_c = P.sem("p2c")
        s_ld = [P.sem(f"p2ld{i}") for i in range(2)]
        s_bt = P.sem("p2bt")
        s_qk = [P.sem(f"p2qk{i}") for i in range(2)]
        s_ex = [P.sem(f"p2ex{i}") for i in range(2)]
        s_pv = [P.sem(f"p2pv{i}") for i in range(2)]
        s_nf = [P.sem(f"p2nf{i}") for i in range(2)]
        s_dv = P.sem("p2dv")
        s_fin = P.sem("p2fin")
        s_pm = [P.sem(f"p2pm{i}") for i in range(2)]
        s_st = P.sem("p2st")
        pw = P.pw()

        vc = P.add("sync", lambda e: e.dma_start(out=MA[:], in_=D["ma"]), waits=[pw], inc=s_c, dma=True)
        vc = P.add("sync", lambda e: e.dma_start(out=PVF[:], in_=D["pvalid"]), waits=[pw], inc=s_c, dma=True)
        P.add("vector", lambda e: e.tensor_copy(out=PV64[:], in_=PVF[:]), waits=[pw, (s_c, vc)], inc=s_c)
        vconst = P.add("vector", lambda e: e.memset(ONE64[:], 1.0), waits=[pw], inc=s_c)

        passes = [(hp, g) for hp in range(4) for g in range(3)]
        pass_last_pe = {}
        pass_last_dve = {}
        ld_val = {}
        bt_val = {}

        def sched_loads(pi):
            hp, g = passes[pi]
            d = DILS[g]
            slot = pi % 2
            nb = 64 // d
            w = [pw] + (pass_last_pe[pi - 2] if pi >= 2 else [])
            P.add("sync", lambda e: e.dma_start(out=QA[slot][:], in_=D["QTA"][g * 4 + hp]), waits=w, inc=s_ld[slot], dma=True)
            P.add("sync", lambda e: e.dma_start(out=KA[slot][:], in_=D["KTA"][g * 4 + hp]), waits=w, inc=s_ld[slot], dma=True)
            vav = D["VA"].rearrange("(n i r) c -> r i n c", i=128, r=d)
            c0 = g * 512 + hp * 128
            for r in range(d):
                for n0 in range(0, nb, 16):
                    n1 = min(nb, n0 + 16)
                    P.add("sync", lambda e, r=r, n0=n0, n1=n1: e.dma_start(
                        out=VS[slot][:, r * nb + n0:r * nb + n1, :], in_=vav[r][:, n0:n1, c0:c0 + 128]),
                        waits=w, inc=s_ld[slot], dma=True)
            gh0 = g * 8 + 2 * hp
            v = P.add("sync", lambda e: e.dma_start(out=GF[slot][:], in_=D["ga"][gh0:gh0 + 2].rearrange("h i c -> i h c")),
                      waits=w, inc=s_ld[slot], dma=True)
            ld_val[pi] = v

        def sched_tables(pi):
            slot = pi % 2
            wb = [pw, (s_ld[slot], ld_val[pi]), (s_c, vconst)]
            v1 = None
            for hh in range(2):
                v1 = P.add("vector", lambda e, hh=hh: e.tensor_tensor(
                    out=BTF[:, hh, :], in0=GF[slot][:, hh, :], in1=MA[:], op=ALU.add),
                    waits=wb + [(s_bt, s_bt.v)], inc=s_bt)
            P.add("scalar", lambda e: e.activation(out=EB[slot][:], in_=BTF[:], func=AF.Exp),
                  waits=[pw, (s_bt, v1)] + (pass_last_dve[pi - 2] if pi >= 2 else []), inc=s_bt)
            bt_val[pi] = s_bt.v

        sched_loads(0)
        sched_tables(0)
        cn = {"qi": 0, "gi": 0, "fin": 0}

        def run_pass(pi, hp, g):
            d = DILS[g]
            slot = pi % 2
            nb = 64 // d
            nbB = 16 // d
            if pi + 1 < len(passes):
                sched_loads(pi + 1)

            def own_off(n):
                return 2048 if n < 2 * nbB else 4096

            groups = []
            if d == 16:
                for B in (1, 3):
                    base = 0 if B == 1 else 2048
                    for r0 in range(0, 16, 4):
                        qbs = [(r0 + i, B) for i in range(4)]
                        groups.append((qbs,
                                       lambda T, base=base, r0=r0: T[:, base:base + 2048].rearrange("p (j r) -> p r j", r=16)[:, r0:r0 + 4, :],
                                       lambda bank: ND[:, bank, 0:512].rearrange("p (r j) -> p r j", r=4)))
            else:
                for r in range(d):
                    for B in (1, 3):
                        for n0 in range(B * nbB, (B + 1) * nbB, 4):
                            qbs = [(r, n0 + i) for i in range(4)]
                            t0 = d * 128 * n0 + r - own_off(n0)
                            sl = slice(t0, t0 + 511 * d + 1, d)
                            groups.append((qbs, lambda T, sl=sl: T[:, sl], lambda bank: ND[:, bank, 0:512]))
            qblocks = [(r, n, gidx, i) for gidx, (qbs, _, _) in enumerate(groups) for i, (r, n) in enumerate(qbs)]
            pend = None
            pass_dv_start = s_dv.v

            def emit_pv(info):
                (qslot, kq, r, n, gslot, kg, first, last, gidx, col) = info

                def pv(e):
                    ins = None
                    for hh in range(2):
                        for ch in range(2):
                            nk = n - 1 + ch
                            blk = r * nb + nk
                            ins = e.matmul(ND[hh * 64:(hh + 1) * 64, 2 * gslot, col:col + 128],
                                           lhsT=VS[slot][:, blk, hh * 64:(hh + 1) * 64],
                                           rhs=PT[qslot][:, hh, ch * 128:(ch + 1) * 128],
                                           start=(ch == 0), stop=(ch == 1))
                            vt = PV64 if nk < nbB else ONE64
                            ins = e.matmul(ND[hh * 64:(hh + 1) * 64, 2 * gslot + 1, col:col + 128],
                                           lhsT=vt[:, :], rhs=PT[qslot][:, hh, ch * 128:(ch + 1) * 128],
                                           start=(ch == 0), stop=(ch == 1))
                    return ins

                w = [(s_pm[qslot], kq + 1)]
                if first:
                    w.append((s_nf[gslot], kg))
                vpv = P.add("tensor", pv, waits=w, inc=s_pv[qslot])
                if last:
                    _, dst, srcf = groups[gidx]
                    wd = [(s_pv[qslot], vpv)]
                    if g > 0:
                        wd += [(s_dv, pass_dv_start), (s_nf[0], nf_start[0]), (s_nf[1], nf_start[1])]
                    else:
                        wd += [(s_fin, cn["fin"])]
                    if g == 0:
                        P.add("vector", lambda e: e.tensor_copy(out=dst(NACC), in_=srcf(2 * gslot)), waits=wd, inc=s_dv)
                        P.add("vector", lambda e: e.tensor_copy(out=dst(DACC), in_=srcf(2 * gslot + 1)), waits=wd, inc=s_nf[gslot])
                    else:
                        P.add("vector", lambda e: e.tensor_tensor(out=dst(NACC), in0=srcf(2 * gslot), in1=dst(NACC), op=ALU.add),
                              waits=wd, inc=s_dv)
                        P.add("vector", lambda e: e.tensor_tensor(out=dst(DACC), in0=srcf(2 * gslot + 1), in1=dst(DACC), op=ALU.add),
                              waits=wd, inc=s_nf[gslot])

            nf_start = [s_nf[0].v, s_nf[1].v]
            for bi, (r, n, gidx, gi_) in enumerate(qblocks):
                if bi == len(qblocks) // 2 and pi + 1 < len(passes):
                    sched_tables(pi + 1)
                qslot = cn["qi"] % 2
                kq = cn["qi"] // 2
                first = (gi_ == 0)
                last = (gi_ == 3)
                if first:
                    gslot = cn["gi"] % 2
                    kg = cn["gi"] // 2
                    cn["gi"] += 1
                t0 = d * 128 * n + r - own_off(n)
                qs = slice(t0, t0 + 127 * d + 1, d)

                def qk(e, r=r, n=n, qslot=qslot, qs=qs):
                    ins = None
                    for hh in range(2):
                        bank = 2 * qslot + hh
                        for ch in range(2):
                            nk = n - 1 + ch
                            ks = slice(d * 128 * nk + r, d * 128 * nk + r + 127 * d + 1, d)
                            ins = e.matmul(S[:, bank, ch * 128:(ch + 1) * 128],
                                           lhsT=KA[slot][hh * 64:(hh + 1) * 64, ks],
                                           rhs=QA[slot][hh * 64:(hh + 1) * 64, qs],
                                           start=True, stop=True)
                    return ins

                P.add("tensor", qk, waits=[pw, (s_ld[slot], ld_val[pi]), (s_c, vconst), (s_ex[qslot], kq)],
                      inc=s_qk[qslot])
                P.add("scalar", lambda e, qslot=qslot: e.activation(
                    out=PTF[qslot][:], in_=S[:, 2 * qslot:2 * qslot + 2, 0:256], func=AF.Exp, scale=0.125),
                    waits=[pw, (s_qk[qslot], kq + 1), (s_pm[qslot], kq)], inc=s_ex[qslot])
                P.add("vector", lambda e, qslot=qslot: e.tensor_tensor(
                    out=PT[qslot][:], in0=PTF[qslot][:], in1=EB[slot][:], op=ALU.mult),
                    waits=[pw, (s_ex[qslot], kq + 1), (s_pv[qslot], kq), (s_bt, bt_val[pi])], inc=s_pm[qslot])
                if pend is not None:
                    emit_pv(pend)
                pend = (qslot, kq, r, n, gslot, kg, first, last, gidx, gi_ * 128)
                cn["qi"] += 1
            emit_pv(pend)
            pass_last_pe[pi] = [(s_pv[0], s_pv[0].v), (s_pv[1], s_pv[1].v)]
            pass_last_dve[pi] = [(s_pm[0], s_pm[0].v), (s_pm[1], s_pm[1].v)]
            if g == 2:
                wd = [(s_dv, s_dv.v), (s_nf[0], s_nf[0].v), (s_nf[1], s_nf[1].v)]
                v1 = P.add("vector", lambda e: e.reciprocal(out=DACC[:], in_=DACC[:]), waits=wd, inc=s_fin)
                v2 = P.add("vector", lambda e: e.tensor_tensor(out=OAS[:], in0=NACC[:], in1=DACC[:], op=ALU.mult),
                           waits=[(s_fin, v1), (s_st, 16 * hp)], inc=s_fin)
                cn["fin"] = v2
                P.add("gpsimd", lambda e, hp=hp: e.dma_start(out=D["OAT"][hp], in_=OAS[:]),
                      waits=[pw, (s_fin, v2)], inc=s_st, dma=True)
        for pi, (hp, g) in enumerate(passes):
            run_pass(pi, hp, g)
        P.end_phase(JK[:])


def phase3(nc, P, D):
    with ExitStack() as es:
        KT = [_sb(es, nc, f"p3_KT{i}", [128, NK], BF16) for i in range(2)]
        QT = [_sb(es, nc, f"p3_QT{i}", [128, NQ], BF16) for i in range(2)]
        VH = [_sb(es, nc, f"p3_VH{i}", [128, 64, 129], BF16) for i in range(2)]
        GB = _sb(es, nc, "p3_GB", [128, 8, 256], F32)
        CB = _sb(es, nc, "p3_CB", [128, 8], F32)
        MB = _sb(es, nc, "p3_MB", [128, 256], F32)
        BTF = _sb(es, nc, "p3_BTF", [128, 256], F32)
        BT = _sb(es, nc, "p3_BT", [128, 8, 2, 256], BF16)
        IDF = _sb(es, nc, "p3_IDF", [128, 128], F32)
        IDB = _sb(es, nc, "p3_IDB", [128, 128], BF16)
        ZB = _sb(es, nc, "p3_ZB", [128, 512], BF16)
        PVF = _sb(es, nc, "p3_PVF", [128, 64], F32)
        LAMI = _sb(es, nc, "p3_LAMI", [128, 4, 64], F32)
        LPR = _sb(es, nc, "p3_LPR", [128, 2, 64], F32)
        LS = _sb(es, nc, "p3_LS", [128, 2], F32)
        LE = _sb(es, nc, "p3_LE", [128, 2], F32)
        NLAM = _sb(es, nc, "p3_NLAM", [128, 1], F32)
        GS = _sb(es, nc, "p3_GS", [128, 128], F32)
        EPSB = _sb(es, nc, "p3_EPSB", [128, 1], F32)
        PT = [_sb(es, nc, f"p3_PT{i}", [128, 2, 512], BF16) for i in range(3)]
        EP = _sb(es, nc, "p3_EP", [128, 3, 512], F32)
        RD = _sb(es, nc, "p3_RD", [128, 8], F32)
        TT = _sb(es, nc, "p3_TT", [128, 128], F32)
        OO = _sb(es, nc, "p3_OO", [128, 4, 128], F32)
        SQ = _sb(es, nc, "p3_SQ", [128, 128], F32)
        SSQ = _sb(es, nc, "p3_SSQ", [128, 4], F32)
        LNV = _sb(es, nc, "p3_LNV", [128, 4], F32)
        RSTD = _sb(es, nc, "p3_RSTD", [128, 4], F32)
        ON = _sb(es, nc, "p3_ON", [128, 4, 128], BF16)
        OBS = [_sb(es, nc, f"p3_OBS{i}", [128, 512], BF16) for i in range(2)]
        JK = _sb(es, nc, "p3_JK", [128, 1], F32)
        S = _ps(es, nc, "p3_S", [128, 4, 512], F32)
        ACC = _ps(es, nc, "p3_ACC", [128, 3, 512], F32)
        TP = _ps(es, nc, "p3_TP", [128, 4, 128], BF16)
        s_c = P.sem("p3c")
        s_ld = [P.sem(f"p3ld{i}") for i in range(2)]
        s_qk = [P.sem(f"p3qk{i}") for i in range(2)]
        s_ex = [P.sem(f"p3ex{i}") for i in range(2)]
        s_pv = [P.sem(f"p3pv{i}") for i in range(3)]
        s_af = P.sem("p3af")
        s_ep = P.sem("p3ep")
        s_ea = P.sem("p3ea")
        s_tp = P.sem("p3tp")
        s_tc = P.sem("p3tc")
        s_st = [P.sem(f"p3st{i}") for i in range(2)]
        pw = P.pw()

        def accap(m, j):
            idx = m * 4 + j
            return idx // 3, (idx % 3) * 170

        for (dst, src) in ((GB, D["gb"].rearrange("h i c -> i h c")), (CB, D["cb"]), (MB, D["mb"]), (IDF, D["ident"]),
                           (PVF, D["pvalid"]), (LAMI, D["lam"]), (GS, D["gs"])):
            vc = P.add("sync", lambda e, dst=dst, src=src: e.dma_start(out=dst[:], in_=src), waits=[pw], inc=s_c, dma=True)
        w0 = [pw, (s_c, vc)]
        P.add("vector", lambda e: e.tensor_copy(out=IDB[:], in_=IDF[:]), waits=w0, inc=s_c)
        P.add("vector", lambda e: e.memset(ZB[:], 0.0), waits=w0, inc=s_c)
        P.add("vector", lambda e: e.memset(EPSB[:], LN_EPS), waits=w0, inc=s_c)
        v = P.add("vector", lambda e: e.tensor_scalar(out=GS[:], in0=GS[:], scalar1=1.0 - LAM_INIT, scalar2=None, op0=ALU.mult),
                  waits=w0, inc=s_c)
        for i in range(2):
            P.add("vector", lambda e, i=i: e.memset(VH[i][:, 16:64, 128:129], 1.0), waits=w0, inc=s_c)
            P.add("vector", lambda e, i=i: e.tensor_copy(out=VH[i][:, 0:16, 128:129],
                                                         in_=PVF[:, 0:16].rearrange("p (a b) -> p a b", b=1)),
                  waits=w0, inc=s_c)
        v = P.add("vector", lambda e: e.tensor_tensor(out=LPR[:], in0=LAMI[:, 0:2, :], in1=LAMI[:, 2:4, :], op=ALU.mult),
                  waits=w0, inc=s_c)
        v = P.add("vector", lambda e: e.reduce_sum(out=LS[:], in_=LPR[:], axis=AX.X), waits=[(s_c, v)], inc=s_c)
        v = P.add("scalar", lambda e: e.activation(out=LE[:], in_=LS[:], func=AF.Exp), waits=[pw, (s_c, v)], inc=s_c)
        v = P.add("vector", lambda e: e.tensor_tensor(out=NLAM[:], in0=LE[:, 1:2], in1=LE[:, 0:1], op=ALU.subtract),
                  waits=[(s_c, v)], inc=s_c)
        v = P.add("vector", lambda e: e.tensor_scalar(out=NLAM[:], in0=NLAM[:], scalar1=-LAM_INIT, scalar2=None, op0=ALU.add),
                  waits=[(s_c, v)], inc=s_c)
        for h in range(8):
            v = P.add("vector", lambda e, h=h: e.tensor_scalar(out=BTF[:], in0=GB[:, h, :], scalar1=CB[:, h:h + 1], scalar2=8.0,
                                                               op0=ALU.subtract, op1=ALU.mult), waits=[(s_c, v)], inc=s_c)
            v = P.add("vector", lambda e: e.tensor_tensor(out=BTF[:], in0=BTF[:], in1=MB[:], op=ALU.add), waits=[(s_c, v)], inc=s_c)
            v = P.add("vector", lambda e, h=h: e.tensor_copy(out=BT[:, h, 0, :], in_=BTF[:]), waits=[(s_c, v)], inc=s_c)
            v = P.add("vector", lambda e, h=h: e.tensor_tensor(out=BT[:, h, 1, :], in0=BTF[:], in1=BT[:, h, 0, :], op=ALU.subtract),
                      waits=[(s_c, v)], inc=s_c)
        vconst = v

        head_last_pe = {}
        ld_val = {}
        vbv = D["VB"].rearrange("(kb p) c -> p kb c", p=128)

        def sched_loads(h):
            slot = h % 2
            w = [pw, (s_c, vconst)] + (head_last_pe[h - 2] if h >= 2 else [])
            P.add("sync", lambda e: e.dma_start(out=KT[slot][:], in_=D["KTB"][h]), waits=w, inc=s_ld[slot], dma=True)
            P.add("sync", lambda e: e.dma_start(out=QT[slot][:], in_=D["QTB"][h]), waits=w, inc=s_ld[slot], dma=True)
            for q in range(4):
                v = P.add("sync", lambda e, q=q: e.dma_start(out=VH[slot][:, q * 16:(q + 1) * 16, 0:128],
                                                           in_=vbv[:, q * 16:(q + 1) * 16, h * 128:(h + 1) * 128]),
                          waits=w, inc=s_ld[slot], dma=True)
            ld_val[h] = v

        deferred = []
        ui = [0]

        def flush(force=False):
            while deferred and (force or deferred[0][0] <= ui[0]):
                _, fn = deferred.pop(0)
                fn()

        sched_loads(0)
        cn = {"nqt": 0}
        pending = []

        def emit_pv(info):
            (sslot, ku, pslot, kb, jmin, firstu, lastu, hs, kq_tile, on_last) = info

            def pv(e):
                ins = None
                if firstu:
                    for b in range(3):
                        e.matmul(ACC[:, b, :], lhsT=ZB[:, 0:128], rhs=ZB[:, :], start=True, stop=True,
                                 skip_group_check=True)
                for m in range(2):
                    for j in range(jmin, 4):
                        b, off = accap(m, j)
                        ins = e.matmul(ACC[:, b, off:off + 129], lhsT=PT[pslot][:, m, j * 128:(j + 1) * 128],
                                       rhs=VH[hs][:, kb, :], start=False, stop=lastu, skip_group_check=True)
                return ins

            w = [(s_ex[sslot], ku + 1)]
            if firstu:
                w.append((s_af, kq_tile))
            v = P.add("tensor", pv, waits=w, inc=s_pv[pslot])
            if lastu:
                on_last(pslot, v)

        def drain(keep):
            while len(pending) > keep:
                emit_pv(pending.pop(0))

        def run_qt(h, qt):
                hs = h % 2
                base = 4 * (qt + 4 if qt < 4 else qt + 8)
                nkb = base + 4
                kq_tile = cn["nqt"]
                cn["nqt"] += 1
                nqt = cn["nqt"]

                def on_last(last_slot, vlast):
                    epilogue(last_slot, vlast)

                for kb in range(nkb):
                    u = ui[0]
                    slot = u % 2
                    ku = u // 2
                    pslot = u % 3
                    kp = u // 3
                    jd = kb - base
                    c0 = max(jd, 0) * 128
                    jmin = max(jd, 0)

                    def qk(e, kb=kb, jd=jd, c0=c0, slot=slot, qt=qt):
                        ins = None
                        for m in range(2):
                            bank = 2 * slot + m
                            near = jd >= -1
                            if near:
                                if jd == -1:
                                    oc, bc, n = 0, 128, 128
                                elif jd == 3:
                                    oc, bc, n = 384, 0, 128
                                else:
                                    oc, bc, n = jd * 128, 0, 256
                                e.matmul(S[:, bank, oc:oc + n], lhsT=IDB[:], rhs=BT[:, h, 0, bc:bc + n], start=True, stop=False)
                                e.matmul(S[:, bank, oc:oc + n], lhsT=IDB[:], rhs=BT[:, h, 1, bc:bc + n], start=False, stop=False)
                            ins = e.matmul(S[:, bank, c0:512], lhsT=KT[hs][m * 64:(m + 1) * 64, kb * 128:(kb + 1) * 128],
                                           rhs=QT[hs][m * 64:(m + 1) * 64, qt * 512 + c0:(qt + 1) * 512],
                                           start=(not near), stop=True, skip_group_check=True)
                        return ins

                    P.add("tensor", qk, waits=[pw, (s_ld[hs], ld_val[h]), (s_c, vconst), (s_ex[slot], ku)], inc=s_qk[slot])
                    P.add("scalar", lambda e, slot=slot, c0=c0, pslot=pslot: e.activation(
                        out=PT[pslot][:, :, c0:512], in_=S[:, 2 * slot:2 * slot + 2, c0:512], func=AF.Exp, scale=0.125),
                        waits=[pw, (s_qk[slot], ku + 1), (s_pv[pslot], kp)], inc=s_ex[slot])
                    pending.append((slot, ku, pslot, kb, jmin, kb == 0, kb == nkb - 1, hs, kq_tile, on_last))
                    drain(2)
                    ui[0] += 1
                    flush()

                def epilogue(last_slot, vlast):

                    def stage_a(last_slot=last_slot, vlast=vlast, h=h, qt=qt, k=nqt):
                        v = P.add("vector", lambda e: e.tensor_copy(out=EP[:], in_=ACC[:]),
                                  waits=[(s_pv[last_slot], vlast), (s_ep, s_ep.v)], inc=s_af)
                        we = [(s_af, v)]
                        for m in range(2):
                            for j in range(4):
                                b, off = accap(m, j)
                                idx = m * 4 + j
                                P.add("vector", lambda e, b=b, off=off, idx=idx: e.reciprocal(
                                    out=RD[:, idx:idx + 1], in_=EP[:, b, off + 128:off + 129]), waits=we, inc=s_ep)
                        v = P.add("vector", lambda e: e.tensor_scalar(out=RD[:, 4:8], in0=RD[:, 4:8], scalar1=NLAM[:, 0:1], scalar2=None,
                                                                      op0=ALU.mult), waits=[(s_ep, s_ep.v)], inc=s_ep)
                        for j in range(4):
                            b1, o1 = accap(0, j)
                            b2, o2 = accap(1, j)
                            v = P.add("vector", lambda e, b2=b2, o2=o2, j=j: e.tensor_scalar(
                                out=TT[:], in0=EP[:, b2, o2:o2 + 128], scalar1=RD[:, 4 + j:5 + j], scalar2=None, op0=ALU.mult),
                                waits=[(s_ep, v)], inc=s_ep)
                            v = P.add("vector", lambda e, b1=b1, o1=o1, j=j: e.scalar_tensor_tensor(
                                out=OO[:, j, :], in0=EP[:, b1, o1:o1 + 128], scalar=RD[:, j:j + 1], in1=TT[:],
                                op0=ALU.mult, op1=ALU.add), waits=[(s_ep, v)], inc=s_ep)
                            v = P.add("vector", lambda e, j=j: e.scalar_tensor_tensor(
                                out=SQ[:], in0=OO[:, j, :], scalar=1.0, in1=OO[:, j, :], op0=ALU.mult, op1=ALU.mult,
                                accum_out=SSQ[:, j:j + 1]), waits=[(s_ep, v)], inc=s_ep)
                        return v

                    va = stage_a()

                    def stage_bc(va=va):
                        v = P.add("scalar", lambda e: e.activation(out=LNV[:], in_=SSQ[:], func=AF.Ln, bias=EPSB[:, 0:1], scale=1.0 / 128.0),
                                  waits=[(s_ep, va), (s_ea, s_ea.v)], inc=s_ea)
                        v = P.add("scalar", lambda e: e.activation(out=RSTD[:], in_=LNV[:], func=AF.Exp, scale=-0.5),
                                  waits=[(s_ea, v)], inc=s_ea)
                        vv = None
                        for j in range(4):
                            vv = P.add("vector", lambda e, j=j: e.scalar_tensor_tensor(
                                out=ON[:, j, :], in0=OO[:, j, :], scalar=RSTD[:, j:j + 1], in1=GS[:], op0=ALU.mult, op1=ALU.mult),
                                waits=[(s_ea, v), (s_tp, s_tp.v)], inc=s_ep)
                        return vv

                    def stage_de(vc, h=h, qt=qt, k=nqt):
                        def tp(e):
                            ins = None
                            for j in range(4):
                                ins = e.transpose(out=TP[:, j, :], in_=ON[:, j, :], identity=IDB[:])
                            return ins
                        v = P.add("tensor", tp, waits=[(s_ep, vc), (s_tc, s_tc.v)], inc=s_tp)
                        os_ = (k - 1) % 2
                        v2 = P.add("vector", lambda e: e.tensor_copy(out=OBS[os_][:].rearrange("p (a b) -> p a b", a=4), in_=TP[:]),
                                   waits=[(s_tp, v), (s_st[os_], 16 * ((k - 1) // 2))], inc=s_tc)
                        P.add("gpsimd", lambda e: e.dma_start(out=D["OBT"][h][:, qt * 512:(qt + 1) * 512], in_=OBS[os_][:]),
                              waits=[pw, (s_tc, v2)], inc=s_st[os_], dma=True)

                    def chain(stage_bc=stage_bc, stage_de=stage_de):
                        vc = stage_bc()
                        deferred.append((ui[0] + 3, lambda: stage_de(vc)))

                    deferred.append((ui[0] + 3, chain))
        for h in range(8):
            if h + 1 < 8:
                sched_loads(h + 1)
            for qt in range(8):
                run_qt(h, qt)
            drain(0)
            head_last_pe[h] = [(s_pv[i], s_pv[i].v) for i in range(3)]
        flush(force=True)
        flush(force=True)
        P.end_phase(JK[:])


def layer_norm_ops(P, V, ST6, MV, STD, RSTD, TMP, OUT, LG, LB, EPSB, sem, wait_in):
    v = None
    for hf in range(2):
        v = P.add("vector", lambda e, hf=hf: e.bn_stats(out=ST6[:, hf * 6:(hf + 1) * 6], in_=V[:, hf * 512:(hf + 1) * 512]),
                  waits=wait_in + [(sem, sem.v)], inc=sem)
    v = P.add("vector", lambda e: e.bn_aggr(out=MV[:], in_=ST6[:]), waits=[(sem, v)], inc=sem)
    v = P.add("scalar", lambda e: e.activation(out=STD[:], in_=MV[:, 1:2], func=AF.Sqrt, bias=EPSB[:, 0:1], scale=1.0),
              waits=[(sem, v)], inc=sem)
    v = P.add("vector", lambda e: e.reciprocal(out=RSTD[:], in_=STD[:]), waits=[(sem, v)], inc=sem)
    v = P.add("vector", lambda e: e.scalar_tensor_tensor(out=TMP[:], in0=V[:], scalar=MV[:, 0:1], in1=LG[:],
                                                         op0=ALU.subtract, op1=ALU.mult), waits=[(sem, v)], inc=sem)
    v = P.add("vector", lambda e: e.scalar_tensor_tensor(out=OUT, in0=TMP[:], scalar=RSTD[:, 0:1], in1=LB[:],
                                                         op0=ALU.mult, op1=ALU.add), waits=[(sem, v)], inc=sem)
    return v


def load_cast_weight(P, pw, dst_chunks, src_chunks, WF, s_wld, s_wcv, cnt):
    for dst, src in zip(dst_chunks, src_chunks):
        n = cnt[0]
        slot = n % 2
        v = P.add("sync", lambda e, slot=slot, src=src: e.dma_start(out=WF[slot][:], in_=src),
                  waits=[pw, (s_wcv, n - 1)], inc=s_wld[slot], dma=True)
        P.add("vector", lambda e, slot=slot, dst=dst: e.tensor_copy(out=dst, in_=WF[slot][:]),
              waits=[pw, (s_wld[slot], v)], inc=s_wcv)
        cnt[0] += 1
    return s_wcv.v


def phase4a(nc, P, D):
    with ExitStack() as es:
        WPA = _sb(es, nc, "p4_WPA", [128, 4, 1024], BF16)
        WPB = _sb(es, nc, "p4_WPB", [128, 8, 1024], BF16)
        WO = _sb(es, nc, "p4_WO", [128, 8, 1024], BF16)
        WF = [_sb(es, nc, f"p4_WF{i}", [128, 2, 1024], F32) for i in range(2)]
        OAt = [_sb(es, nc, f"p4_OA{i}", [128, 4, 512], BF16) for i in range(2)]
        OBt = [_sb(es, nc, f"p4_OB{i}", [128, 8, 512], BF16) for i in range(2)]
        SGt = _sb(es, nc, "p4_SG", [128, 16, 512], BF16)
        XS = _sb(es, nc, "p4_XS", [128, 4, 1024], F32)
        MT = _sb(es, nc, "p4_MT", [128, 8, 512], BF16)
        T1 = [_sb(es, nc, f"p4_T1{i}", [128, 512], F32) for i in range(2)]
        T2 = [_sb(es, nc, f"p4_T2{i}", [128, 512], F32) for i in range(2)]
        V = _sb(es, nc, "p4_V", [128, 1024], F32)
        TMP = _sb(es, nc, "p4_TMP", [128, 1024], F32)
        X1O = [_sb(es, nc, f"p4_X1O{i}", [128, 1024], F32) for i in range(2)]
        X1B = _sb(es, nc, "p4_X1B", [128, 1024], BF16)
        X1TS = _sb(es, nc, "p4_X1TS", [128, 8, 512], BF16)
        LG = _sb(es, nc, "p4_LG", [128, 1024], F32)
        LB = _sb(es, nc, "p4_LB", [128, 1024], F32)
        ST6 = _sb(es, nc, "p4_ST6", [128, 12], F32)
        MV = _sb(es, nc, "p4_MV", [128, 2], F32)
        STD = _sb(es, nc, "p4_STD", [128, 1], F32)
        RSTD = _sb(es, nc, "p4_RSTD", [128, 1], F32)
        EPSB = _sb(es, nc, "p4_EPSB", [128, 1], F32)
        IDF = _sb(es, nc, "p4_IDF", [128, 128], F32)
        IDB = _sb(es, nc, "p4_IDB", [128, 128], BF16)
        JK = _sb(es, nc, "p4_JK", [128, 1], F32)
        PSY = _ps(es, nc, "p4_PSY", [128, 4, 512], F32)
        PSM = _ps(es, nc, "p4_PSM", [128, 2, 512], F32)
        PST = _ps(es, nc, "p4_PST", [128, 8, 128], BF16)
        s_c = P.sem("p4c")
        s_wld = [P.sem(f"p4wld{i}") for i in range(2)]
        s_wcv = P.sem("p4wcv")
        s_ldab = [P.sem(f"p4ldab{i}") for i in range(2)]
        s_ldsg = P.sem("p4ldsg")
        s_ldx = P.sem("p4ldx")
        s_y = [P.sem(f"p4y{i}") for i in range(2)]
        s_t = [P.sem(f"p4t{i}") for i in range(2)]
        s_mt = P.sem("p4mt")
        s_mm = [P.sem(f"p4mm{i}") for i in range(2)]
        s_v = [P.sem(f"p4v{i}") for i in range(2)]
        s_ln = P.sem("p4ln")
        s_xb = P.sem("p4xb")
        s_tp = P.sem("p4tp")
        s_tc = P.sem("p4tc")
        s_so = [P.sem(f"p4so{i}") for i in range(2)]
        s_sx = P.sem("p4sx")
        pw = P.pw()

        for (dst, src) in ((LG, D["ln1g"]), (LB, D["ln1b"]), (IDF, D["ident"])):
            vc = P.add("sync", lambda e, dst=dst, src=src: e.dma_start(out=dst[:], in_=src), waits=[pw], inc=s_c, dma=True)
        P.add("vector", lambda e: e.tensor_copy(out=IDB[:], in_=IDF[:]), waits=[pw, (s_c, vc)], inc=s_c)
        vconst = P.add("vector", lambda e: e.memset(EPSB[:], LN_EPS), waits=[pw], inc=s_c)

        cnt = [0]
        wpa = D["wpa"].rearrange("(kc p) c -> p kc c", p=128)
        wpb = D["wpb"].rearrange("(kc p) c -> p kc c", p=128)
        wo = D["wo"].rearrange("(kc p) c -> p kc c", p=128)
        dsts, srcs = [], []
        for (Wt, wsrc, nk) in ((WPA, wpa, 4), (WPB, wpb, 8), (WO, wo, 8)):
            for k2 in range(nk // 2):
                dsts.append(Wt[:, 2 * k2:2 * k2 + 2, :])
                srcs.append(wsrc[:, 2 * k2:2 * k2 + 2, :])
        vw = load_cast_weight(P, pw, dsts, srcs, WF, s_wld, s_wcv, cnt)

        oat = D["OAT"].rearrange("k p t -> p k t")
        obt = D["OBT"].rearrange("k p t -> p k t")
        sgt = D["SG"].rearrange("k p t -> p k t")
        xo = D["xo"].rearrange("(s p) d -> p s d", p=128)
        x1 = D["X1"].rearrange("(s p) d -> p s d", p=128)
        x1t = D["X1T"].rearrange("k p t -> p k t")

        last_y_pe = {}
        ld_ab = {}

        def load_ab(t):
            sl = t % 2
            w = [pw] + (last_y_pe[t - 2] if t >= 2 else [])
            P.add("sync", lambda e: e.dma_start(out=OAt[sl][:], in_=oat[:, :, t * 512:(t + 1) * 512]), waits=w, inc=s_ldab[sl], dma=True)
            ld_ab[t] = P.add("sync", lambda e: e.dma_start(out=OBt[sl][:], in_=obt[:, :, t * 512:(t + 1) * 512]), waits=w,
                             inc=s_ldab[sl], dma=True)

        load_ab(0)
        yj = 0
        mj = 0
        sbk = 0
        for t in range(8):
            sl = t % 2
            if t + 1 < 8:
                load_ab(t + 1)
            v_sg = P.add("sync", lambda e, t=t: e.dma_start(out=SGt[:], in_=sgt[:, :, t * 512:(t + 1) * 512]),
                         waits=[pw, (s_t[0], s_t[0].v), (s_t[1], s_t[1].v)], inc=s_ldsg, dma=True)
            v_x = P.add("sync", lambda e, t=t: e.dma_start(out=XS[:], in_=xo[:, 4 * t:4 * t + 4, :]),
                        waits=[pw, (s_v[0], s_v[0].v), (s_v[1], s_v[1].v)], inc=s_ldx, dma=True)
            mt_w0 = (s_mm[0], s_mm[0].v), (s_mm[1], s_mm[1].v)
            for dc in range(8):
                ys = yj % 2
                ky = yj // 2

                def ymm(e, ys=ys, dc=dc, sl=sl):
                    ins = None
                    for k in range(4):
                        ins = e.matmul(PSY[:, 2 * ys, :], lhsT=WPA[:, k, dc * 128:(dc + 1) * 128], rhs=OAt[sl][:, k, :],
                                       start=(k == 0), stop=(k == 3))
                    for k in range(8):
                        ins = e.matmul(PSY[:, 2 * ys + 1, :], lhsT=WPB[:, k, dc * 128:(dc + 1) * 128], rhs=OBt[sl][:, k, :],
                                       start=(k == 0), stop=(k == 7))
                    return ins

                P.add("tensor", ymm, waits=[pw, (s_wcv, vw), (s_ldab[sl], ld_ab[t]), (s_t[ys], 2 * ky)], inc=s_y[ys])
                wt = [pw, (s_y[ys], ky + 1), (s_ldsg, v_sg), (s_mt, s_mt.v - 1)]
                P.add("vector", lambda e, ys=ys, dc=dc: e.tensor_tensor(out=T1[ys][:], in0=PSY[:, 2 * ys, :], in1=SGt[:, dc, :], op=ALU.mult),
                      waits=wt, inc=s_t[ys])
                vt = P.add("vector", lambda e, ys=ys, dc=dc: e.tensor_tensor(out=T2[ys][:], in0=PSY[:, 2 * ys + 1, :], in1=SGt[:, 8 + dc, :],
                                                                         op=ALU.mult), waits=wt, inc=s_t[ys])
                P.add("vector", lambda e, ys=ys, dc=dc: e.tensor_tensor(out=MT[:, dc, :], in0=T1[ys][:], in1=T2[ys][:], op=ALU.add),
                      waits=[pw, (s_t[ys], vt)] + list(mt_w0), inc=s_mt)
                yj += 1
            last_y_pe[t] = [(s_y[0], s_y[0].v), (s_y[1], s_y[1].v)]
            v_mt = s_mt.v
            for s in range(4):
                xs_o = sbk % 2
                for hf in range(2):
                    ms = mj % 2
                    km = mj // 2

                    def mmm(e, ms=ms, s=s, hf=hf):
                        ins = None
                        for k in range(8):
                            ins = e.matmul(PSM[:, ms, :], lhsT=MT[:, k, s * 128:(s + 1) * 128], rhs=WO[:, k, hf * 512:(hf + 1) * 512],
                                           start=(k == 0), stop=(k == 7))
                        return ins

                    P.add("tensor", mmm, waits=[pw, (s_mt, v_mt), (s_v[ms], km)], inc=s_mm[ms])
                    vv = P.add("vector", lambda e, ms=ms, s=s, hf=hf: e.scalar_tensor_tensor(
                        out=V[:, hf * 512:(hf + 1) * 512], in0=XS[:, s, hf * 512:(hf + 1) * 512], scalar=ALPHA, in1=PSM[:, ms, :],
                        op0=ALU.mult, op1=ALU.add), waits=[pw, (s_mm[ms], km + 1), (s_ldx, v_x), (s_ln, s_ln.v)], inc=s_v[ms])
                    mj += 1
                win = [(s_v[0], s_v[0].v), (s_v[1], s_v[1].v), (s_c, vconst), (s_so[xs_o], 16 * (sbk // 2)), (s_xb, s_xb.v)]
                vln = layer_norm_ops(P, V, ST6, MV, STD, RSTD, TMP, X1O[xs_o][:], LG, LB, EPSB, s_ln, win)
                row = 4 * t + s
                P.add("sync", lambda e, xs_o=xs_o, row=row: e.dma_start(out=x1[:, row, :], in_=X1O[xs_o][:]),
                      waits=[pw, (s_ln, vln)], inc=s_so[xs_o], dma=True)
                vb = P.add("scalar", lambda e, xs_o=xs_o: e.copy(out=X1B[:], in_=X1O[xs_o][:]),
                           waits=[pw, (s_ln, vln), (s_tp, s_tp.v)], inc=s_xb)

                def tp(e):
                    ins = None
                    for k in range(8):
                        ins = e.transpose(out=PST[:, k, :], in_=X1B[:, k * 128:(k + 1) * 128], identity=IDB[:])
                    return ins

                vtp = P.add("tensor", tp, waits=[pw, (s_xb, vb), (s_tc, s_tc.v)], inc=s_tp)
                P.add("scalar", lambda e, s=s: e.copy(out=X1TS[:, :, s * 128:(s + 1) * 128], in_=PST[:]),
                      waits=[pw, (s_tp, vtp), (s_sx, 16 * t)], inc=s_tc)
                sbk += 1
            P.add("gpsimd", lambda e, t=t: e.dma_start(out=x1t[:, :, t * 512:(t + 1) * 512], in_=X1TS[:]),
                  waits=[pw, (s_tc, s_tc.v)], inc=s_sx, dma=True)
        P.end_phase(JK[:])


def phase4b(nc, P, D):
    TT = 256
    NT = NQ // TT
    with ExitStack() as es:
        W1 = _sb(es, nc, "p5_W1", [128, 8, 4096], BF16)
        W2 = _sb(es, nc, "p5_W2", [128, 32, 1024], BF16)
        WF = [_sb(es, nc, f"p5_WF{i}", [128, 1024], F32) for i in range(2)]
        XT = [_sb(es, nc, f"p5_XT{i}", [128, 8, TT], BF16) for i in range(2)]
        X1 = _sb(es, nc, "p5_X1", [128, 2, 1024], F32)
        HT = _sb(es, nc, "p5_HT", [128, 32, TT], BF16)
        RT = [_sb(es, nc, f"p5_RT{i}", [128, TT], F32) for i in range(2)]
        V = _sb(es, nc, "p5_V", [128, 1024], F32)
        TMP = _sb(es, nc, "p5_TMP", [128, 1024], F32)
        YO = [_sb(es, nc, f"p5_YO{i}", [128, 1024], F32) for i in range(2)]
        LG = _sb(es, nc, "p5_LG", [128, 1024], F32)
        LB = _sb(es, nc, "p5_LB", [128, 1024], F32)
        ST6 = _sb(es, nc, "p5_ST6", [128, 12], F32)
        MV = _sb(es, nc, "p5_MV", [128, 2], F32)
        STD = _sb(es, nc, "p5_STD", [128, 1], F32)
        RSTD = _sb(es, nc, "p5_RSTD", [128, 1], F32)
        EPSB = _sb(es, nc, "p5_EPSB", [128, 1], F32)
        JK = _sb(es, nc, "p5_JK", [128, 1], F32)
        PSH = _ps(es, nc, "p5_PSH", [128, 4, 512], F32)
        PSF = _ps(es, nc, "p5_PSF", [128, 2, 512], F32)
        s_c = P.sem("p5c")
        s_wld = [P.sem(f"p5wld{i}") for i in range(2)]
        s_wcv = P.sem("p5wcv")
        s_ldt = [P.sem(f"p5ldt{i}") for i in range(2)]
        s_ldx = P.sem("p5ldx")
        s_h = [P.sem(f"p5h{i}") for i in range(4)]
        s_r = [P.sem(f"p5r{i}") for i in range(4)]
        s_sq = [P.sem(f"p5sq{i}") for i in range(2)]
        s_f = [P.sem(f"p5f{i}") for i in range(2)]
        s_v = [P.sem(f"p5v{i}") for i in range(2)]
        s_ln = P.sem("p5ln")
        s_so = [P.sem(f"p5so{i}") for i in range(2)]
        pw = P.pw()

        for (dst, src) in ((LG, D["ln2g"]), (LB, D["ln2b"])):
            vc = P.add("sync", lambda e, dst=dst, src=src: e.dma_start(out=dst[:], in_=src), waits=[pw], inc=s_c, dma=True)
        vconst = P.add("vector", lambda e: e.memset(EPSB[:], LN_EPS), waits=[pw, (s_c, vc)], inc=s_c)

        cnt = [0]
        w1 = D["w1"].rearrange("(kc p) f -> p kc f", p=128)
        w2 = D["w2"].rearrange("(fc p) d -> p fc d", p=128)
        dsts, srcs = [], []
        for kc in range(8):
            for q in range(4):
                dsts.append(W1[:, kc, q * 1024:(q + 1) * 1024])
                srcs.append(w1[:, kc, q * 1024:(q + 1) * 1024])
        vw1 = load_cast_weight(P, pw, dsts, srcs, WF, s_wld, s_wcv, cnt)
        dsts, srcs = [], []
        for fc in range(32):
            dsts.append(W2[:, fc, :])
            srcs.append(w2[:, fc, :])
        vw2 = load_cast_weight(P, pw, dsts, srcs, WF, s_wld, s_wcv, cnt)

        x1t = D["X1T"].rearrange("k p t -> p k t")
        x1 = D["X1"].rearrange("(s p) d -> p s d", p=128)
        yv = D["y"].rearrange("(s p) d -> p s d", p=128)
        last_h_pe = {}
        ld_t = {}

        def load_t(t):
            sl = t % 2
            w = [pw] + (last_h_pe[t - 2] if t >= 2 else [])
            ld_t[t] = P.add("sync", lambda e: e.dma_start(out=XT[sl][:], in_=x1t[:, :, t * TT:(t + 1) * TT]), waits=w,
                            inc=s_ldt[sl], dma=True)

        load_t(0)
        hj = 0
        fj = 0
        sbk = 0
        for t in range(NT):
            sl = t % 2
            if t + 1 < NT:
                load_t(t + 1)
            v_x = P.add("sync", lambda e, t=t: e.dma_start(out=X1[:], in_=x1[:, 2 * t:2 * t + 2, :]),
                        waits=[pw, (s_v[0], s_v[0].v), (s_v[1], s_v[1].v)], inc=s_ldx, dma=True)
            ht_w0 = [(s_f[0], s_f[0].v), (s_f[1], s_f[1].v)]
            for fc in range(32):
                hs = hj % 4
                kh = hj // 4
                rs = hj % 2
                kr = hj // 2

                def hmm(e, hs=hs, fc=fc, sl=sl):
                    ins = None
                    for k in range(8):
                        ins = e.matmul(PSH[:, hs, 0:TT], lhsT=W1[:, k, fc * 128:(fc + 1) * 128], rhs=XT[sl][:, k, :],
                                       start=(k == 0), stop=(k == 7))
                    return ins

                P.add("tensor", hmm, waits=[pw, (s_wcv, vw1), (s_ldt[sl], ld_t[t]), (s_r[hs], kh)], inc=s_h[hs])
                P.add("scalar", lambda e, hs=hs, rs=rs: e.activation(out=RT[rs][:], in_=PSH[:, hs, 0:TT], func=AF.Relu),
                      waits=[pw, (s_h[hs], kh + 1), (s_sq[rs], kr)], inc=s_r[hs])
                P.add("vector",
                      lambda e, rs=rs, fc=fc: e.tensor_tensor(out=HT[:, fc, :], in0=RT[rs][:], in1=RT[rs][:], op=ALU.mult),
                      waits=[pw, (s_r[hs], kh + 1)] + ht_w0, inc=s_sq[rs])
                hj += 1
            last_h_pe[t] = [(s_h[i], s_h[i].v) for i in range(4)]
            v_sq = [(s_sq[0], s_sq[0].v), (s_sq[1], s_sq[1].v)]
            for s in range(2):
                yo = sbk % 2
                for hf in range(2):
                    fs = fj % 2
                    kf = fj // 2

                    def fmm(e, fs=fs, s=s, hf=hf):
                        ins = None
                        for fc in range(32):
                            ins = e.matmul(PSF[:, fs, :], lhsT=HT[:, fc, s * 128:(s + 1) * 128], rhs=W2[:, fc, hf * 512:(hf + 1) * 512],
                                           start=(fc == 0), stop=(fc == 31))
                        return ins

                    P.add("tensor", fmm, waits=[pw, (s_wcv, vw2), (s_v[fs], kf)] + v_sq, inc=s_f[fs])
                    P.add("vector", lambda e, fs=fs, s=s, hf=hf: e.scalar_tensor_tensor(
                        out=V[:, hf * 512:(hf + 1) * 512], in0=X1[:, s, hf * 512:(hf + 1) * 512], scalar=ALPHA, in1=PSF[:, fs, :],
                        op0=ALU.mult, op1=ALU.add), waits=[pw, (s_f[fs], kf + 1), (s_ldx, v_x), (s_ln, s_ln.v)], inc=s_v[fs])
                    fj += 1
                win = [(s_v[0], s_v[0].v), (s_v[1], s_v[1].v), (s_c, vconst), (s_so[yo], 16 * (sbk // 2))]
                vln = layer_norm_ops(P, V, ST6, MV, STD, RSTD, TMP, YO[yo][:], LG, LB, EPSB, s_ln, win)
                row = 2 * t + s
                P.add("sync", lambda e, yo=yo, row=row: e.dma_start(out=yv[:, row, :], in_=YO[yo][:]),
                      waits=[pw, (s_ln, vln)], inc=s_so[yo], dma=True)
                sbk += 1
        P.end_phase(JK[:])


def build_nc(debug=False, phases=(1, 2, 3, 4, 5)):
    nc = bass.Bass("TRN2", target_bir_lowering=False)
    D = {}

    def din(name, shape):
        D[name] = nc.dram_tensor(name, shape, F32, kind="ExternalInput").ap()

    din("xT", [1024, NK]); din("xo", [NQ, 1024]); din("wp", [1024, COLS_IN]); din("bg", [128, 16])
    din("ga", [24, 128, 256]); din("ma", [128, 256]); din("gb", [8, 128, 256]); din("cb", [128, 8]); din("mb", [128, 256])
    din("ident", [128, 128]); din("pvalid", [128, 64]); din("lam", [128, 4, 64]); din("gs", [128, 128])
    din("ln1g", [128, 1024]); din("ln1b", [128, 1024]); din("ln2g", [128, 1024]); din("ln2b", [128, 1024])
    din("wpa", [512, 1024]); din("wpb", [1024, 1024]); din("wo", [1024, 1024]); din("w1", [1024, 4096]); din("w2", [4096, 1024])
    D["y"] = nc.dram_tensor("y", [NQ, 1024], F32, kind="ExternalOutput").ap()
    kind = "ExternalOutput" if debug else "Internal"

    def scr(name, shape, dt):
        D[name] = nc.dram_tensor(name, shape, dt, kind=kind).ap()

    scr("QTB", [8, 128, NQ], BF16); scr("KTB", [8, 128, NK], BF16); scr("VB", [NK, 1024], BF16)
    scr("QTA", [12, 128, NQ], BF16); scr("KTA", [12, 128, NKA], BF16); scr("VA", [NKA, 1536], BF16)
    scr("SG", [16, 128, NQ], BF16); scr("OAT", [4, 128, NQ], BF16); scr("OBT", [8, 128, NQ], BF16)
    scr("X1", [NQ, 1024], F32); scr("X1T", [8, 128, NQ], BF16)
    with ExitStack() as es:
        P = Prog(nc, es)
        if 1 in phases:
            phase1(nc, P, D)
        if 2 in phases:
            phase2(nc, P, D)
        if 3 in phases:
            phase3(nc, P, D)
        if 4 in phases:
            phase4a(nc, P, D)
        if 5 in phases:
            phase4b(nc, P, D)
    return nc


def t5_bucket_np(dist):
    n = np.maximum(dist, 0)
    nf = np.maximum(n, 1).astype(np.float32)
    large = 16 + (np.log(nf / np.float32(16.0)) / np.float32(math.log(8.0)) * np.float32(16.0)).astype(np.int32)
    large = np.minimum(large, 31)
    return np.where(n < 16, n, large)


def prep_shared(inp):
    f = lambda a: np.ascontiguousarray(np.asarray(a, dtype=np.float32))
    w = np.asarray(inp["w_in"], dtype=np.float32)[0]
    B0 = 4608
    cols = []
    for h in range(8):
        cols += list(range(B0 + h * 64, B0 + h * 64 + 64)) + list(range(B0 + 512 + h * 64, B0 + 512 + h * 64 + 64))
    for h in range(8):
        cols += list(range(B0 + 1024 + h * 64, B0 + 1024 + h * 64 + 64)) + list(range(B0 + 1536 + h * 64, B0 + 1536 + h * 64 + 64))
    cols += list(range(0, 1536)) + list(range(1536, 3072)) + list(range(7680, 9728))
    cols += list(range(B0 + 2048, B0 + 3072)) + list(range(3072, 4608))
    assert len(cols) == COLS_IN
    sh = {"wp": f(w[:, cols])}
    sh["bg"] = f(np.asarray(inp["b_gate"], np.float32)[0].reshape(16, 128).T)
    rb = np.asarray(inp["rel_bias"], np.float32)
    i = np.arange(128)[:, None]
    c = np.arange(256)[None, :]
    ch = c // 128
    j = c % 128
    steps = (1 - ch) * 128 + j - i
    ga = np.zeros((24, 128, 256), np.float32)
    for g, d in enumerate(DILS):
        bk = t5_bucket_np(np.maximum(steps, 0) * d)
        for h in range(8):
            ga[g * 8 + h] = rb[bk, g * 8 + h]
    sh["ga"] = ga
    sh["ma"] = f(np.where((steps >= 0) & (steps <= 128), 0.0, MASKV / 8.0))
    dist = c - i
    bk = t5_bucket_np(np.maximum(dist, 0))
    gb = np.zeros((8, 128, 256), np.float32)
    for h in range(8):
        gb[h] = rb[bk, 24 + h]
    sh["gb"] = gb
    sh["cb"] = f(np.broadcast_to(rb[31, 24:32][None, :], (128, 8)))
    sh["mb"] = f(np.where(dist >= 0, 0.0, MASKV))
    sh["ident"] = np.eye(128, dtype=np.float32)
    lam = np.stack([np.asarray(inp[k], np.float32)[0] for k in ("lambda_q1", "lambda_q2", "lambda_k1", "lambda_k2")])
    sh["lam"] = f(np.broadcast_to(lam[None], (128, 4, 64)))
    sh["gs"] = f(np.broadcast_to(np.asarray(inp["subln_g"], np.float32)[0][None, :], (128, 128)))
    for k, n in (("ln1g", "ln1_g"), ("ln1b", "ln1_b"), ("ln2g", "ln2_g"), ("ln2b", "ln2_b")):
        sh[k] = f(np.broadcast_to(np.asarray(inp[n], np.float32)[0][None, :], (128, 1024)))
    sh["wpa"] = f(np.asarray(inp["w_proj_a"])[0]); sh["wpb"] = f(np.asarray(inp["w_proj_b"])[0])
    sh["wo"] = f(np.asarray(inp["w_out"])[0]); sh["w1"] = f(np.asarray(inp["w_mlp1"])[0]); sh["w2"] = f(np.asarray(inp["w_mlp2"])[0])
    return sh


def prep_core(x, c):
    b, par = c // 2, c % 2
    xb = np.asarray(x[b], dtype=np.float32)
    xl = np.zeros((NK, 1024), np.float32)
    if par == 1:
        xl[:] = xb
    else:
        xl[2048:] = xb[0:NK - 2048]
    own = np.concatenate([xl[2048:4096], xl[6144:8192]], axis=0)
    pv = np.full((128, 64), float(par), np.float32)
    return {"xT": np.ascontiguousarray(xl.T), "xo": np.ascontiguousarray(own), "pvalid": pv}


_NC_CACHE = {}


def kernel(**inputs):
    x = np.asarray(inputs["x"], dtype=np.float32)
    sh = prep_shared(inputs)
    in_maps = []
    for c in range(8):
        m = dict(sh)
        m.update(prep_core(x, c))
        in_maps.append(m)
    if "nc" not in _NC_CACHE:
        _NC_CACHE["nc"] = build_nc()
    res = run_bass_kernel_spmd(_NC_CACHE["nc"], in_maps, core_ids=list(range(8)))
    out = np.zeros((BATCH, SEQ, D_MODEL), np.float32)
    for c in range(8):
        b, par = c // 2, c % 2
        y = np.asarray(res.results[c]["y"], dtype=np.float32)
        r0 = 2048 if par == 1 else 0
        out[b, r0:r0 + 2048] = y[0:2048]
        out[b, r0 + 4096:r0 + 6144] = y[2048:4096]
    return out
```

```python
import math
from contextlib import ExitStack

import numpy as np

import concourse.bass as bass
import concourse.mybir as mybir
from concourse.bass_utils import run_bass_kernel_spmd

F32 = mybir.dt.float32
BF16 = mybir.dt.bfloat16
AF = mybir.ActivationFunctionType
ALU = mybir.AluOpType
AX = mybir.AxisListType

D_MODEL = 1024
SEQ = 8192
BATCH = 4
NQ = 4096
NK = 8192
NKA = 8192
COLS_IN = 9728
DILS = (1, 4, 16)
ALPHA = 2.0 ** 0.25
LAM_INIT = 0.8 - 0.6 * math.exp(0.0)
LN_EPS = 1e-5
MASKV = -240000.0
ENGS = ("sync", "scalar", "vector", "gpsimd", "tensor")

C_FQ, C_FK, C_AQ, C_AK, C_GT, C_BV, C_AV = 0, 1024, 2048, 3584, 5120, 7168, 8192


class Sem:
    __slots__ = ("h", "abs", "base", "name")

    def __init__(self, h, name):
        self.h, self.abs, self.base, self.name = h, 0, 0, name

    @property
    def v(self):
        return self.abs - self.base


class Prog:
    def __init__(self, nc, es):
        self.nc, self.es = nc, es
        self.q = {k: [] for k in ENGS}
        self.waited = {k: {} for k in ENGS}
        self.sems = []
        self.pool = []
        self.used = 0
        self.nphase = 0
        self.bar = self.sem("bar")

    def sem(self, name):
        if name != "bar" and self.used < len(self.pool):
            s = self.pool[self.used]
            self.used += 1
            s.base = s.abs
            return s
        s = Sem(self.es.enter_context(self.nc.semaphore(f"s{len(self.sems)}")), f"s{len(self.sems)}")
        self.sems.append(s)
        if name != "bar":
            self.pool.append(s)
            self.used += 1
        return s

    def add(self, eng, fn, waits=(), inc=None, dma=False):
        ws = []
        for w in waits:
            if w is None:
                continue
            s, v = w
            if v <= 0:
                continue
            assert v <= s.v, f"deadlock: {eng} waits {s.name}>={v} but only {s.v} scheduled"
            va = s.base + v
            if self.waited[eng].get(s.name, 0) >= va:
                continue
            self.waited[eng][s.name] = va
            ws.append((s, va))
        amt = 16 if dma else 1
        newv = None
        if inc is not None:
            inc.abs += amt
            newv = inc.v
        self.q[eng].append((ws, fn, inc, amt))
        return newv

    def end_phase(self, junk):
        waits = [(s, s.v) for s in self.sems if s is not self.bar]
        self.nphase += 1
        self.add("gpsimd", lambda e: e.memset(junk, 0.0), waits=waits, inc=self.bar)
        with self.nc.Block() as block:
            for name in ENGS:
                items = self.q[name]

                def body(e, items=items):
                    for ws, fn, inc, amt in items:
                        for s, v in ws:
                            e.wait_ge(s.h, v)
                        ins = fn(e)
                        if inc is not None:
                            ins.then_inc(inc.h, amt)

                getattr(block, name)(body)
        self.q = {k: [] for k in ENGS}
        self.used = 0
        for name in ENGS:
            if name == "gpsimd":
                continue
            self.waited[name].pop(self.bar.name, None)
        self.phase_wait = (self.bar, self.nphase)

    def pw(self):
        return getattr(self, "phase_wait", None)


def _sb(es, nc, name, shape, dt):
    return es.enter_context(nc.sbuf_tensor(name, shape, dt))


def _ps(es, nc, name, shape, dt):
    return es.enter_context(nc.psum_tensor(name, shape, dt))


def phase1(nc, P, D):
    with ExitStack() as es:
        W = _sb(es, nc, "p1_W", [128, 8, COLS_IN], BF16)
        WF = [_sb(es, nc, f"p1_WF{i}", [128, 8, 128], F32) for i in range(2)]
        XF = _sb(es, nc, "p1_XF", [128, 8, 512], F32)
        XB = [_sb(es, nc, f"p1_XB{i}", [128, 8, 512], BF16) for i in range(2)]
        OST = [_sb(es, nc, f"p1_OST{i}", [128, 512], BF16) for i in range(4)]
        BG = _sb(es, nc, "p1_BG", [128, 16], F32)
        JK = _sb(es, nc, "p1_JK", [128, 1], F32)
        PS = _ps(es, nc, "p1_PS", [128, 4, 512], F32)
        s_xld, s_xcv, s_wcv, s_misc = P.sem("p1xld"), P.sem("p1xcv"), P.sem("p1wcv"), P.sem("p1misc")
        s_wld = [P.sem(f"p1wld{i}") for i in range(2)]
        s_mm = [P.sem(f"p1mm{i}") for i in range(4)]
        s_ev = [P.sem(f"p1ev{i}") for i in range(4)]
        s_st = [P.sem(f"p1st{i}") for i in range(4)]
        xT = D["xT"].rearrange("(kc p) t -> p kc t", p=128)
        wp = D["wp"].rearrange("(kc p) c -> p kc c", p=128)
        pw = P.pw()

        v_bg = P.add("sync", lambda e: e.dma_start(out=BG[:], in_=D["bg"]), waits=[pw], inc=s_misc, dma=True)

        wcv_of = {}
        wl = [0]

        def load_w(cid):
            n = wl[0]
            slot = n % 2
            v = P.add("sync", lambda e: e.dma_start(out=WF[slot][:], in_=wp[:, :, cid * 128:(cid + 1) * 128]),
                      waits=[pw, (s_wcv, n - 1)], inc=s_wld[slot], dma=True)
            P.add("vector", lambda e: e.tensor_copy(out=W[:, :, cid * 128:(cid + 1) * 128], in_=WF[slot][:]),
                  waits=[pw, (s_wld[slot], v)], inc=s_wcv)
            wcv_of[cid] = s_wcv.v
            wl[0] += 1

        worder = (list(range(8, 16)) + list(range(56, 64)) + list(range(28, 40)) + list(range(64, 76))
                  + list(range(0, 8)) + list(range(16, 28)) + list(range(40, 56)))
        wplan = {-1: worder[0:40]}
        for t in range(4):
            wplan[t] = worder[40 + 9 * t:40 + 9 * (t + 1)]

        tile_last = {}

        def load_x(t):
            v = P.add("sync", lambda e: e.dma_start(out=XF[:], in_=xT[:, :, t * 512:(t + 1) * 512]),
                      waits=[pw, (s_xcv, t)], inc=s_xld, dma=True)
            P.add("vector", lambda e: e.tensor_copy(out=XB[t % 2][:], in_=XF[:]),
                  waits=[pw, (s_xld, v)] + (tile_last[t - 2] if t >= 2 else []), inc=s_xcv)

        def jobs_for(t):
            jobs = []
            own = (t // 4) % 2 == 1
            to = t - 4 if t < 8 else t - 8
            for h in range(8):
                jobs.append(("F", C_FK + h * 128, D["KTB"][h][:, t * 512:(t + 1) * 512], None))
            for cb in range(2):
                for s in range(4):
                    r0 = t * 512 + s * 128
                    jobs.append(("T", C_BV + cb * 512, D["VB"][r0:r0 + 128, cb * 512:(cb + 1) * 512], s))
            for m in range(12):
                jobs.append(("F", C_AK + m * 128, D["KTA"][m][:, t * 512:(t + 1) * 512], None))
            for g in range(3):
                for s in range(4):
                    r0 = t * 512 + s * 128
                    jobs.append(("T", C_AV + g * 512, D["VA"][r0:r0 + 128, g * 512:(g + 1) * 512], s))
            if own:
                for h in range(8):
                    jobs.append(("F", C_FQ + h * 128, D["QTB"][h][:, to * 512:(to + 1) * 512], None))
                for m in range(12):
                    jobs.append(("F", C_AQ + m * 128, D["QTA"][m][:, to * 512:(to + 1) * 512], None))
                for i in range(16):
                    jobs.append(("G", C_GT + i * 128, D["SG"][i][:, to * 512:(to + 1) * 512], i))
            return jobs

        for cid in wplan[-1]:
            load_w(cid)
        load_x(0)
        jc = 0
        for t in range(16):
            jobs = jobs_for(t)
            half = len(jobs) // 2
            for ji, (kind, col, dst, aux) in enumerate(jobs):
                if ji == half:
                    if t + 1 < 16:
                        load_x(t + 1)
                    for cid in wplan.get(t, []):
                        load_w(cid)
                slot = jc % 4
                k = jc // 4
                xb = XB[t % 2]
                if kind == "T":
                    wneed = max(wcv_of[col // 128 + i] for i in range(4))
                else:
                    wneed = wcv_of[col // 128]

                def mm(e, kind=kind, col=col, aux=aux, xb=xb, slot=slot):
                    ins = None
                    for kc in range(8):
                        if kind == "T":
                            ins = e.matmul(PS[:, slot, :], lhsT=xb[:, kc, aux * 128:(aux + 1) * 128],
                                           rhs=W[:, kc, col:col + 512], start=(kc == 0), stop=(kc == 7))
                        else:
                            ins = e.matmul(PS[:, slot, :], lhsT=W[:, kc, col:col + 128],
                                           rhs=xb[:, kc, :], start=(kc == 0), stop=(kc == 7))
                    return ins

                P.add("tensor", mm, waits=[pw, (s_xcv, t + 1), (s_wcv, wneed), (s_ev[slot], k)], inc=s_mm[slot])
                evw = [pw, (s_mm[slot], k + 1), (s_st[slot], 16 * k)]
                if kind == "G":
                    P.add("scalar", lambda e, slot=slot, aux=aux: e.activation(
                        out=OST[slot][:], in_=PS[:, slot, :], func=AF.Sigmoid, bias=BG[:, aux:aux + 1], scale=1.0),
                        waits=evw + [(s_misc, v_bg)], inc=s_ev[slot])
                elif jc % 2 == 0:
                    P.add("vector", lambda e, slot=slot: e.tensor_copy(out=OST[slot][:], in_=PS[:, slot, :]),
                          waits=evw, inc=s_ev[slot])
                else:
                    P.add("scalar", lambda e, slot=slot: e.copy(out=OST[slot][:], in_=PS[:, slot, :]),
                          waits=evw, inc=s_ev[slot])
                P.add("gpsimd", lambda e, slot=slot, dst=dst: e.dma_start(out=dst, in_=OST[slot][:]),
                      waits=[pw, (s_ev[slot], k + 1)], inc=s_st[slot], dma=True)
                jc += 1
            tile_last[t] = [(s_mm[s], s_mm[s].v) for s in range(4)]
        P.end_phase(JK[:])


def phase2(nc, P, D):
    with ExitStack() as es:
        QA = [_sb(es, nc, f"p2_QA{i}", [128, NQ], BF16) for i in range(2)]
        KA = [_sb(es, nc, f"p2_KA{i}", [128, NKA], BF16) for i in range(2)]
        VS = [_sb(es, nc, f"p2_VS{i}", [128, 64, 128], BF16) for i in range(2)]
        GF = [_sb(es, nc, f"p2_GF{i}", [128, 2, 256], F32) for i in range(2)]
        BTF = _sb(es, nc, "p2_BTF", [128, 2, 256], F32)
        EB = [_sb(es, nc, f"p2_EB{i}", [128, 2, 256], F32) for i in range(2)]
        PTF = [_sb(es, nc, f"p2_PTF{i}", [128, 2, 256], F32) for i in range(2)]
        MA = _sb(es, nc, "p2_MA", [128, 256], F32)
        IDF = _sb(es, nc, "p2_IDF", [128, 128], F32)
        IDB = _sb(es, nc, "p2_IDB", [128, 128], BF16)
        PVF = _sb(es, nc, "p2_PVF", [128, 64], F32)
        PV64 = _sb(es, nc, "p2_PV64", [128, 64], BF16)
        ONE64 = _sb(es, nc, "p2_ONE64", [128, 64], BF16)
        PT = [_sb(es, nc, f"p2_PT{i}", [128, 2, 256], BF16) for i in range(2)]
        NACC = _sb(es, nc, "p2_NACC", [128, NQ], F32)
        DACC = _sb(es, nc, "p2_DACC", [128, NQ], F32)
        OAS = _sb(es, nc, "p2_OAS", [128, NQ], BF16)
        JK = _sb(es, nc, "p2_JK", [128, 1], F32)
        S = _ps(es, nc, "p2_S", [128, 4, 512], F32)
        ND = _ps(es, nc, "p2_ND", [128, 4, 512], F32)
        s_c = P.sem("p2c")
        s_ld = [P.sem(f"p2ld{i}") for i in range(2)]
        s_bt = P.sem("p2bt")
        s_qk = [P.sem(f"p2qk{i}") for i in range(2)]
        s_ex = [P.sem(f"p2ex{i}") for i in range(2)]
        s_pv = [P.sem(f"p2pv{i}") for i in range(2)]
        s_nf = [P.sem(f"p2nf{i}") for i in range(2)]
        s_dv = P.sem("p2dv")
        s_fin = P.sem("p2fin")
        s_pm = [P.sem(f"p2pm{i}") for i in range(2)]
        s_st = P.sem("p2st")
        pw = P.pw()

        vc = P.add("sync", lambda e: e.dma_start(out=MA[:], in_=D["ma"]), waits=[pw], inc=s_c, dma=True)
        vc = P.add("sync", lambda e: e.dma_start(out=PVF[:], in_=D["pvalid"]), waits=[pw], inc=s_c, dma=True)
        P.add("vector", lambda e: e.tensor_copy(out=PV64[:], in_=PVF[:]), waits=[pw, (s_c, vc)], inc=s_c)
        vconst = P.add("vector", lambda e: e.memset(ONE64[:], 1.0), waits=[pw], inc=s_c)

        passes = [(hp, g) for hp in range(4) for g in range(3)]
        pass_last_pe = {}
        pass_last_dve = {}
        ld_val = {}
        bt_val = {}

        def sched_loads(pi):
            hp, g = passes[pi]
            d = DILS[g]
            slot = pi % 2
            nb = 64 // d
            w = [pw] + (pass_last_pe[pi - 2] if pi >= 2 else [])
            P.add("sync", lambda e: e.dma_start(out=QA[slot][:], in_=D["QTA"][g * 4 + hp]), waits=w, inc=s_ld[slot], dma=True)
            P.add("sync", lambda e: e.dma_start(out=KA[slot][:], in_=D["KTA"][g * 4 + hp]), waits=w, inc=s_ld[slot], dma=True)
            vav = D["VA"].rearrange("(n i r) c -> r i n c", i=128, r=d)
            c0 = g * 512 + hp * 128
            for r in range(d):
                for n0 in range(0, nb, 16):
                    n1 = min(nb, n0 + 16)
                    P.add("sync", lambda e, r=r, n0=n0, n1=n1: e.dma_start(
                        out=VS[slot][:, r * nb + n0:r * nb + n1, :], in_=vav[r][:, n0:n1, c0:c0 + 128]),
                        waits=w, inc=s_ld[slot], dma=True)
            gh0 = g * 8 + 2 * hp
            v = P.add("sync", lambda e: e.dma_start(out=GF[slot][:], in_=D["ga"][gh0:gh0 + 2].rearrange("h i c -> i h c")),
                      waits=w, inc=s_ld[slot], dma=True)
            ld_val[pi] = v

        def sched_tables(pi):
            slot = pi % 2
            wb = [pw, (s_ld[slot], ld_val[pi]), (s_c, vconst)]
            v1 = None
            for hh in range(2):
                v1 = P.add("vector", lambda e, hh=hh: e.tensor_tensor(
                    out=BTF[:, hh, :], in0=GF[slot][:, hh, :], in1=MA[:], op=ALU.add),
                    waits=wb + [(s_bt, s_bt.v)], inc=s_bt)
            P.add("scalar", lambda e: e.activation(out=EB[slot][:], in_=BTF[:], func=AF.Exp),
                  waits=[pw, (s_bt, v1)] + (pass_last_dve[pi - 2] if pi >= 2 else []), inc=s_bt)
            bt_val[pi] = s_bt.v

        sched_loads(0)
        sched_tables(0)
        cn = {"qi": 0, "gi": 0, "fin": 0}

        def run_pass(pi, hp, g):
            d = DILS[g]
            slot = pi % 2
            nb = 64 // d
            nbB = 16 // d
            if pi + 1 < len(passes):
                sched_loads(pi + 1)

            def own_off(n):
                return 2048 if n < 2 * nbB else 4096

            groups = []
            if d == 16:
                for B in (1, 3):
                    base = 0 if B == 1 else 2048
                    for r0 in range(0, 16, 4):
                        qbs = [(r0 + i, B) for i in range(4)]
                        groups.append((qbs,
                                       lambda T, base=base, r0=r0: T[:, base:base + 2048].rearrange("p (j r) -> p r j", r=16)[:, r0:r0 + 4, :],
                                       lambda bank: ND[:, bank, 0:512].rearrange("p (r j) -> p r j", r=4)))
            else:
                for r in range(d):
                    for B in (1, 3):
                        for n0 in range(B * nbB, (B + 1) * nbB, 4):
                            qbs = [(r, n0 + i) for i in range(4)]
                            t0 = d * 128 * n0 + r - own_off(n0)
                            sl = slice(t0, t0 + 511 * d + 1, d)
                            groups.append((qbs, lambda T, sl=sl: T[:, sl], lambda bank: ND[:, bank, 0:512]))
            qblocks = [(r, n, gidx, i) for gidx, (qbs, _, _) in enumerate(groups) for i, (r, n) in enumerate(qbs)]
            pend = None
            pass_dv_start = s_dv.v

            def emit_pv(info):
                (qslot, kq, r, n, gslot, kg, first, last, gidx, col) = info

                def pv(e):
                    ins = None
                    for hh in range(2):
                        for ch in range(2):
                            nk = n - 1 + ch
                            blk = r * nb + nk
                            ins = e.matmul(ND[hh * 64:(hh + 1) * 64, 2 * gslot, col:col + 128],
                                           lhsT=VS[slot][:, blk, hh * 64:(hh + 1) * 64],
                                           rhs=PT[qslot][:, hh, ch * 128:(ch + 1) * 128],
                                           start=(ch == 0), stop=(ch == 1))
                            vt = PV64 if nk < nbB else ONE64
                            ins = e.matmul(ND[hh * 64:(hh + 1) * 64, 2 * gslot + 1, col:col + 128],
                                           lhsT=vt[:, :], rhs=PT[qslot][:, hh, ch * 128:(ch + 1) * 128],
                                           start=(ch == 0), stop=(ch == 1))
                    return ins

                w = [(s_pm[qslot], kq + 1)]
                if first:
                    w.append((s_nf[gslot], kg))
                vpv = P.add("tensor", pv, waits=w, inc=s_pv[qslot])
                if last:
                    _, dst, srcf = groups[gidx]
                    wd = [(s_pv[qslot], vpv)]
                    if g > 0:
                        wd += [(s_dv, pass_dv_start), (s_nf[0], nf_start[0]), (s_nf[1], nf_start[1])]
                    else:
                        wd += [(s_fin, cn["fin"])]
                    if g == 0:
                        P.add("vector", lambda e: e.tensor_copy(out=dst(NACC), in_=srcf(2 * gslot)), waits=wd, inc=s_dv)
                        P.add("vector", lambda e: e.tensor_copy(out=dst(DACC), in_=srcf(2 * gslot + 1)), waits=wd, inc=s_nf[gslot])
                    else:
                        P.add("vector", lambda e: e.tensor_tensor(out=dst(NACC), in0=srcf(2 * gslot), in1=dst(NACC), op=ALU.add),
                              waits=wd, inc=s_dv)
                        P.add("vector", lambda e: e.tensor_tensor(out=dst(DACC), in0=srcf(2 * gslot + 1), in1=dst(DACC), op=ALU.add),
                              waits=wd, inc=s_nf[gslot])

            nf_start = [s_nf[0].v, s_nf[1].v]
            for bi, (r, n, gidx, gi_) in enumerate(qblocks):
                if bi == len(qblocks) // 2 and pi + 1 < len(passes):
                    sched_tables(pi + 1)
                qslot = cn["qi"] % 2
                kq = cn["qi"] // 2
                first = (gi_ == 0)
                last = (gi_ == 3)
                if first:
                    gslot = cn["gi"] % 2
                    kg = cn["gi"] // 2
                    cn["gi"] += 1
                t0 = d * 128 * n + r - own_off(n)
                qs = slice(t0, t0 + 127 * d + 1, d)

                def qk(e, r=r, n=n, qslot=qslot, qs=qs):
                    ins = None
                    for hh in range(2):
                        bank = 2 * qslot + hh
                        for ch in range(2):
                            nk = n - 1 + ch
                            ks = slice(d * 128 * nk + r, d * 128 * nk + r + 127 * d + 1, d)
                            ins = e.matmul(S[:, bank, ch * 128:(ch + 1) * 128],
                                           lhsT=KA[slot][hh * 64:(hh + 1) * 64, ks],
                                           rhs=QA[slot][hh * 64:(hh + 1) * 64, qs],
                                           start=True, stop=True)
                    return ins

                P.add("tensor", qk, waits=[pw, (s_ld[slot], ld_val[pi]), (s_c, vconst), (s_ex[qslot], kq)],
                      inc=s_qk[qslot])
                P.add("scalar", lambda e, qslot=qslot: e.activation(
                    out=PTF[qslot][:], in_=S[:, 2 * qslot:2 * qslot + 2, 0:256], func=AF.Exp, scale=0.125),
                    waits=[pw, (s_qk[qslot], kq + 1), (s_pm[qslot], kq)], inc=s_ex[qslot])
                P.add("vector", lambda e, qslot=qslot: e.tensor_tensor(
                    out=PT[qslot][:], in0=PTF[qslot][:], in1=EB[slot][:], op=ALU.mult),
                    waits=[pw, (s_ex[qslot], kq + 1), (s_pv[qslot], kq), (s_bt, bt_val[pi])], inc=s_pm[qslot])
                if pend is not None:
                    emit_pv(pend)
                pend = (qslot, kq, r, n, gslot, kg, first, last, gidx, gi_ * 128)
                cn["qi"] += 1
            emit_pv(pend)
            pass_last_pe[pi] = [(s_pv[0], s_pv[0].v), (s_pv[1], s_pv[1].v)]
            pass_last_dve[pi] = [(s_pm[0], s_pm[0].v), (s_pm[1], s_pm[1].v)]
            if g == 2:
                wd = [(s_dv, s_dv.v), (s_nf[0], s_nf[0].v), (s_nf[1], s_nf[1].v)]
                v1 = P.add("vector", lambda e: e.reciprocal(out=DACC[:], in_=DACC[:]), waits=wd, inc=s_fin)
                v2 = P.add("vector", lambda e: e.tensor_tensor(out=OAS[:], in0=NACC[:], in1=DACC[:], op=ALU.mult),
                           waits=[(s_fin, v1), (s_st, 16 * hp)], inc=s_fin)
                cn["fin"] = v2
                P.add("gpsimd", lambda e, hp=hp: e.dma_start(out=D["OAT"][hp], in_=OAS[:]),
                      waits=[pw, (s_fin, v2)], inc=s_st, dma=True)
        for pi, (hp, g) in enumerate(passes):
            run_pass(pi, hp, g)
        P.end_phase(JK[:])


def phase3(nc, P, D):
    with ExitStack() as es:
        KT = [_sb(es, nc, f"p3_KT{i}", [128, NK], BF16) for i in range(2)]
        QT = [_sb(es, nc, f"p3_QT{i}", [128, NQ], BF16) for i in range(2)]
        VH = [_sb(es, nc, f"p3_VH{i}", [128, 64, 129], BF16) for i in range(2)]
        GB = _sb(es, nc, "p3_GB", [128, 8, 256], F32)
        CB = _sb(es, nc, "p3_CB", [128, 8], F32)
        MB = _sb(es, nc, "p3_MB", [128, 256], F32)
        BTF = _sb(es, nc, "p3_BTF", [128, 256], F32)
        BT = _sb(es, nc, "p3_BT", [128, 8, 2, 256], BF16)
        IDF = _sb(es, nc, "p3_IDF", [128, 128], F32)
        IDB = _sb(es, nc, "p3_IDB", [128, 128], BF16)
        ZB = _sb(es, nc, "p3_ZB", [128, 512], BF16)
        PVF = _sb(es, nc, "p3_PVF", [128, 64], F32)
        LAMI = _sb(es, nc, "p3_LAMI", [128, 4, 64], F32)
        LPR = _sb(es, nc, "p3_LPR", [128, 2, 64], F32)
        LS = _sb(es, nc, "p3_LS", [128, 2], F32)
        LE = _sb(es, nc, "p3_LE", [128, 2], F32)
        NLAM = _sb(es, nc, "p3_NLAM", [128, 1], F32)
        GS = _sb(es, nc, "p3_GS", [128, 128], F32)
        EPSB = _sb(es, nc, "p3_EPSB", [128, 1], F32)
        PT = [_sb(es, nc, f"p3_PT{i}", [128, 2, 512], BF16) for i in range(3)]
        EP = _sb(es, nc, "p3_EP", [128, 3, 512], F32)
        RD = _sb(es, nc, "p3_RD", [128, 8], F32)
        TT = _sb(es, nc, "p3_TT", [128, 128], F32)
        OO = _sb(es, nc, "p3_OO", [128, 4, 128], F32)
        SQ = _sb(es, nc, "p3_SQ", [128, 128], F32)
        SSQ = _sb(es, nc, "p3_SSQ", [128, 4], F32)
        LNV = _sb(es, nc, "p3_LNV", [128, 4], F32)
        RSTD = _sb(es, nc, "p3_RSTD", [128, 4], F32)
        ON = _sb(es, nc, "p3_ON", [128, 4, 128], BF16)
        OBS = [_sb(es, nc, f"p3_OBS{i}", [128, 512], BF16) for i in range(2)]
        JK = _sb(es, nc, "p3_JK", [128, 1], F32)
        CF = [_sb(es, nc, f"p3_CF{i}", [128, 1024], F32) for i in range(2)]
        CBT = [_sb(es, nc, f"p3_CBT{i}", [128, 1024], BF16) for i in range(2)]
        S = _ps(es, nc, "p3_S", [128, 4, 512], F32)
        ACC = _ps(es, nc, "p3_ACC", [128, 3, 512], F32)
        TP = _ps(es, nc, "p3_TP", [128, 4, 128], BF16)
        s_c = P.sem("p3c")
        s_ld = [P.sem(f"p3ld{i}") for i in range(2)]
        s_qk = [P.sem(f"p3qk{i}") for i in range(2)]
        s_ex = [P.sem(f"p3ex{i}") for i in range(2)]
        s_pv = [P.sem(f"p3pv{i}") for i in range(3)]
        s_af = P.sem("p3af")
        s_ep = P.sem("p3ep")
        s_ea = P.sem("p3ea")
        s_tp = P.sem("p3tp")
        s_tc = P.sem("p3tc")
        s_st = [P.sem(f"p3st{i}") for i in range(2)]
        s_cl = [P.sem(f"p3cl{i}") for i in range(2)]
        s_cc = P.sem("p3cc")
        s_cs = [P.sem(f"p3cs{i}") for i in range(2)]
        pw = P.pw()

        chunks = []
        w1v = D["w1"].rearrange("(kc p) f -> p kc f", p=128)
        w2v = D["w2"].rearrange("(fc p) d -> p fc d", p=128)
        k0 = 0
        for (wsrc, nk) in ((D["wpa"], 4), (D["wpb"], 8), (D["wo"], 8)):
            wv = wsrc.rearrange("(kc p) c -> p kc c", p=128)
            for k in range(nk):
                chunks.append((wv[:, k, :], D["WAB"][:, k0 + k, :]))
            k0 += nk
        for kc in range(8):
            for q in range(4):
                chunks.append((w1v[:, kc, q * 1024:(q + 1) * 1024], D["W1B"][:, kc, q * 1024:(q + 1) * 1024]))
        for fc in range(32):
            chunks.append((w2v[:, fc, :], D["W2B"][:, fc, :]))
        cv = {"k": 0, "ld": {}}

        def wconv_step():
            k = cv["k"]
            if k - 1 >= 0 and k - 1 < len(chunks):
                j = k - 1
                sl = j % 2
                src_, dst_ = chunks[j]
                vcast = P.add("vector", lambda e, sl=sl: e.tensor_copy(out=CBT[sl][:], in_=CF[sl][:]),
                              waits=[pw, (s_cl[sl], cv["ld"][j]), (s_cs[sl], 16 * (j // 2))], inc=s_cc)
                P.add("gpsimd", lambda e, sl=sl, dst_=dst_: e.dma_start(out=dst_, in_=CBT[sl][:]), waits=[pw, (s_cc, vcast)],
                      inc=s_cs[sl], dma=True)
            if k < len(chunks):
                sl = k % 2
                src_, dst_ = chunks[k]
                cv["ld"][k] = P.add("sync", lambda e, sl=sl, src_=src_: e.dma_start(out=CF[sl][:], in_=src_), waits=[pw, (s_cc, k - 1)],
                                    inc=s_cl[sl], dma=True)
            cv["k"] += 1

        def accap(m, j):
            idx = m * 4 + j
            return idx // 3, (idx % 3) * 170

        for (dst, src) in ((GB, D["gb"].rearrange("h i c -> i h c")), (CB, D["cb"]), (MB, D["mb"]), (IDF, D["ident"]),
                           (PVF, D["pvalid"]), (LAMI, D["lam"]), (GS, D["gs"])):
            vc = P.add("sync", lambda e, dst=dst, src=src: e.dma_start(out=dst[:], in_=src), waits=[pw], inc=s_c, dma=True)
        w0 = [pw, (s_c, vc)]
        P.add("vector", lambda e: e.tensor_copy(out=IDB[:], in_=IDF[:]), waits=w0, inc=s_c)
        P.add("vector", lambda e: e.memset(ZB[:], 0.0), waits=w0, inc=s_c)
        P.add("vector", lambda e: e.memset(EPSB[:], LN_EPS), waits=w0, inc=s_c)
        v = P.add("vector", lambda e: e.tensor_scalar(out=GS[:], in0=GS[:], scalar1=1.0 - LAM_INIT, scalar2=None, op0=ALU.mult),
                  waits=w0, inc=s_c)
        for i in range(2):
            P.add("vector", lambda e, i=i: e.memset(VH[i][:, 16:64, 128:129], 1.0), waits=w0, inc=s_c)
            P.add("vector", lambda e, i=i: e.tensor_copy(out=VH[i][:, 0:16, 128:129],
                                                         in_=PVF[:, 0:16].rearrange("p (a b) -> p a b", b=1)),
                  waits=w0, inc=s_c)
        v = P.add("vector", lambda e: e.tensor_tensor(out=LPR[:], in0=LAMI[:, 0:2, :], in1=LAMI[:, 2:4, :], op=ALU.mult),
                  waits=w0, inc=s_c)
        v = P.add("vector", lambda e: e.reduce_sum(out=LS[:], in_=LPR[:], axis=AX.X), waits=[(s_c, v)], inc=s_c)
        v = P.add("scalar", lambda e: e.activation(out=LE[:], in_=LS[:], func=AF.Exp), waits=[pw, (s_c, v)], inc=s_c)
        v = P.add("vector", lambda e: e.tensor_tensor(out=NLAM[:], in0=LE[:, 1:2], in1=LE[:, 0:1], op=ALU.subtract),
                  waits=[(s_c, v)], inc=s_c)
        v = P.add("vector", lambda e: e.tensor_scalar(out=NLAM[:], in0=NLAM[:], scalar1=-LAM_INIT, scalar2=None, op0=ALU.add),
                  waits=[(s_c, v)], inc=s_c)
        for h in range(8):
            v = P.add("vector", lambda e, h=h: e.tensor_scalar(out=BTF[:], in0=GB[:, h, :], scalar1=CB[:, h:h + 1], scalar2=8.0,
                                                               op0=ALU.subtract, op1=ALU.mult), waits=[(s_c, v)], inc=s_c)
            v = P.add("vector", lambda e: e.tensor_tensor(out=BTF[:], in0=BTF[:], in1=MB[:], op=ALU.add), waits=[(s_c, v)], inc=s_c)
            v = P.add("vector", lambda e, h=h: e.tensor_copy(out=BT[:, h, 0, :], in_=BTF[:]), waits=[(s_c, v)], inc=s_c)
            v = P.add("vector", lambda e, h=h: e.tensor_tensor(out=BT[:, h, 1, :], in0=BTF[:], in1=BT[:, h, 0, :], op=ALU.subtract),
                      waits=[(s_c, v)], inc=s_c)
        vconst = v

        head_last_pe = {}
        ld_val = {}
        vbv = D["VB"].rearrange("(kb p) c -> p kb c", p=128)

        def sched_loads(h):
            slot = h % 2
            w = [pw, (s_c, vconst)] + (head_last_pe[h - 2] if h >= 2 else [])
            P.add("sync", lambda e: e.dma_start(out=KT[slot][:], in_=D["KTB"][h]), waits=w, inc=s_ld[slot], dma=True)
            P.add("sync", lambda e: e.dma_start(out=QT[slot][:], in_=D["QTB"][h]), waits=w, inc=s_ld[slot], dma=True)
            for q in range(4):
                v = P.add("sync", lambda e, q=q: e.dma_start(out=VH[slot][:, q * 16:(q + 1) * 16, 0:128],
                                                           in_=vbv[:, q * 16:(q + 1) * 16, h * 128:(h + 1) * 128]),
                          waits=w, inc=s_ld[slot], dma=True)
            ld_val[h] = v

        deferred = []
        ui = [0]

        def flush(force=False):
            while deferred and (force or deferred[0][0] <= ui[0]):
                _, fn = deferred.pop(0)
                fn()

        sched_loads(0)
        cn = {"nqt": 0}
        pending = []

        def emit_pv(info):
            (sslot, ku, pslot, kb, jmin, firstu, lastu, hs, kq_tile, on_last) = info

            def pv(e):
                ins = None
                if firstu:
                    for b in range(3):
                        e.matmul(ACC[:, b, :], lhsT=ZB[:, 0:128], rhs=ZB[:, :], start=True, stop=True,
                                 skip_group_check=True)
                for m in range(2):
                    for j in range(jmin, 4):
                        b, off = accap(m, j)
                        ins = e.matmul(ACC[:, b, off:off + 129], lhsT=PT[pslot][:, m, j * 128:(j + 1) * 128],
                                       rhs=VH[hs][:, kb, :], start=False, stop=lastu, skip_group_check=True)
                return ins

            w = [(s_ex[sslot], ku + 1)]
            if firstu:
                w.append((s_af, kq_tile))
            v = P.add("tensor", pv, waits=w, inc=s_pv[pslot])
            if lastu:
                on_last(pslot, v)

        def drain(keep):
            while len(pending) > keep:
                emit_pv(pending.pop(0))

        def run_qt(h, qt):
                hs = h % 2
                base = 4 * (qt + 4 if qt < 4 else qt + 8)
                nkb = base + 4
                kq_tile = cn["nqt"]
                cn["nqt"] += 1
                nqt = cn["nqt"]

                def on_last(last_slot, vlast):
                    epilogue(last_slot, vlast)

                for kb in range(nkb):
                    u = ui[0]
                    slot = u % 2
                    ku = u // 2
                    pslot = u % 3
                    kp = u // 3
                    jd = kb - base
                    c0 = max(jd, 0) * 128
                    jmin = max(jd, 0)

                    def qk(e, kb=kb, jd=jd, c0=c0, slot=slot, qt=qt):
                        ins = None
                        for m in range(2):
                            bank = 2 * slot + m
                            near = jd >= -1
                            if near:
                                if jd == -1:
                                    oc, bc, n = 0, 128, 128
                                elif jd == 3:
                                    oc, bc, n = 384, 0, 128
                                else:
                                    oc, bc, n = jd * 128, 0, 256
                                e.matmul(S[:, bank, oc:oc + n], lhsT=IDB[:], rhs=BT[:, h, 0, bc:bc + n], start=True, stop=False)
                                e.matmul(S[:, bank, oc:oc + n], lhsT=IDB[:], rhs=BT[:, h, 1, bc:bc + n], start=False, stop=False)
                            ins = e.matmul(S[:, bank, c0:512], lhsT=KT[hs][m * 64:(m + 1) * 64, kb * 128:(kb + 1) * 128],
                                           rhs=QT[hs][m * 64:(m + 1) * 64, qt * 512 + c0:(qt + 1) * 512],
                                           start=(not near), stop=True, skip_group_check=True)
                        return ins

                    P.add("tensor", qk, waits=[pw, (s_ld[hs], ld_val[h]), (s_c, vconst), (s_ex[slot], ku)], inc=s_qk[slot])
                    P.add("scalar", lambda e, slot=slot, c0=c0, pslot=pslot: e.activation(
                        out=PT[pslot][:, :, c0:512], in_=S[:, 2 * slot:2 * slot + 2, c0:512], func=AF.Exp, scale=0.125),
                        waits=[pw, (s_qk[slot], ku + 1), (s_pv[pslot], kp)], inc=s_ex[slot])
                    pending.append((slot, ku, pslot, kb, jmin, kb == 0, kb == nkb - 1, hs, kq_tile, on_last))
                    drain(2)
                    ui[0] += 1
                    if ui[0] % 24 == 0:
                        wconv_step()
                    flush()

                def epilogue(last_slot, vlast):

                    def stage_a(last_slot=last_slot, vlast=vlast, h=h, qt=qt, k=nqt):
                        v = P.add("vector", lambda e: e.tensor_copy(out=EP[:], in_=ACC[:]),
                                  waits=[(s_pv[last_slot], vlast), (s_ep, s_ep.v)], inc=s_af)
                        we = [(s_af, v)]
                        for m in range(2):
                            for j in range(4):
                                b, off = accap(m, j)
                                idx = m * 4 + j
                                P.add("vector", lambda e, b=b, off=off, idx=idx: e.reciprocal(
                                    out=RD[:, idx:idx + 1], in_=EP[:, b, off + 128:off + 129]), waits=we, inc=s_ep)
                        v = P.add("vector", lambda e: e.tensor_scalar(out=RD[:, 4:8], in0=RD[:, 4:8], scalar1=NLAM[:, 0:1], scalar2=None,
                                                                      op0=ALU.mult), waits=[(s_ep, s_ep.v)], inc=s_ep)
                        for j in range(4):
                            b1, o1 = accap(0, j)
                            b2, o2 = accap(1, j)
                            v = P.add("vector", lambda e, b2=b2, o2=o2, j=j: e.tensor_scalar(
                                out=TT[:], in0=EP[:, b2, o2:o2 + 128], scalar1=RD[:, 4 + j:5 + j], scalar2=None, op0=ALU.mult),
                                waits=[(s_ep, v)], inc=s_ep)
                            v = P.add("vector", lambda e, b1=b1, o1=o1, j=j: e.scalar_tensor_tensor(
                                out=OO[:, j, :], in0=EP[:, b1, o1:o1 + 128], scalar=RD[:, j:j + 1], in1=TT[:],
                                op0=ALU.mult, op1=ALU.add), waits=[(s_ep, v)], inc=s_ep)
                            v = P.add("vector", lambda e, j=j: e.scalar_tensor_tensor(
                                out=SQ[:], in0=OO[:, j, :], scalar=1.0, in1=OO[:, j, :], op0=ALU.mult, op1=ALU.mult,
                                accum_out=SSQ[:, j:j + 1]), waits=[(s_ep, v)], inc=s_ep)
                        return v

                    va = stage_a()

                    def stage_bc(va=va):
                        v = P.add("scalar", lambda e: e.activation(out=LNV[:], in_=SSQ[:], func=AF.Ln, bias=EPSB[:, 0:1], scale=1.0 / 128.0),
                                  waits=[(s_ep, va), (s_ea, s_ea.v)], inc=s_ea)
                        v = P.add("scalar", lambda e: e.activation(out=RSTD[:], in_=LNV[:], func=AF.Exp, scale=-0.5),
                                  waits=[(s_ea, v)], inc=s_ea)
                        vv = None
                        for j in range(4):
                            vv = P.add("vector", lambda e, j=j: e.scalar_tensor_tensor(
                                out=ON[:, j, :], in0=OO[:, j, :], scalar=RSTD[:, j:j + 1], in1=GS[:], op0=ALU.mult, op1=ALU.mult),
                                waits=[(s_ea, v), (s_tp, s_tp.v)], inc=s_ep)
                        return vv

                    def stage_de(vc, h=h, qt=qt, k=nqt):
                        def tp(e):
                            ins = None
                            for j in range(4):
                                ins = e.transpose(out=TP[:, j, :], in_=ON[:, j, :], identity=IDB[:])
                            return ins
                        v = P.add("tensor", tp, waits=[(s_ep, vc), (s_tc, s_tc.v)], inc=s_tp)
                        os_ = (k - 1) % 2
                        v2 = P.add("vector", lambda e: e.tensor_copy(out=OBS[os_][:].rearrange("p (a b) -> p a b", a=4), in_=TP[:]),
                                   waits=[(s_tp, v), (s_st[os_], 16 * ((k - 1) // 2))], inc=s_tc)
                        P.add("gpsimd", lambda e: e.dma_start(out=D["OBT"][h][:, qt * 512:(qt + 1) * 512], in_=OBS[os_][:]),
                              waits=[pw, (s_tc, v2)], inc=s_st[os_], dma=True)

                    def chain(stage_bc=stage_bc, stage_de=stage_de):
                        vc = stage_bc()
                        deferred.append((ui[0] + 3, lambda: stage_de(vc)))

                    deferred.append((ui[0] + 3, chain))
        for h in range(8):
            if h + 1 < 8:
                sched_loads(h + 1)
            for qt in range(8):
                run_qt(h, qt)
            drain(0)
            head_last_pe[h] = [(s_pv[i], s_pv[i].v) for i in range(3)]
        flush(force=True)
        flush(force=True)
        while cv["k"] <= len(chunks):
            wconv_step()
        P.end_phase(JK[:])


def layer_norm_ops(P, V, ST6, MV, STD, RSTD, TMP, OUT, LG, LB, EPSB, sem, wait_in):
    v = None
    for hf in range(2):
        v = P.add("vector", lambda e, hf=hf: e.bn_stats(out=ST6[:, hf * 6:(hf + 1) * 6], in_=V[:, hf * 512:(hf + 1) * 512]),
                  waits=wait_in + [(sem, sem.v)], inc=sem)
    v = P.add("vector", lambda e: e.bn_aggr(out=MV[:], in_=ST6[:]), waits=[(sem, v)], inc=sem)
    v = P.add("scalar", lambda e: e.activation(out=STD[:], in_=MV[:, 1:2], func=AF.Sqrt, bias=EPSB[:, 0:1], scale=1.0),
              waits=[(sem, v)], inc=sem)
    v = P.add("vector", lambda e: e.reciprocal(out=RSTD[:], in_=STD[:]), waits=[(sem, v)], inc=sem)
    v = P.add("vector", lambda e: e.scalar_tensor_tensor(out=TMP[:], in0=V[:], scalar=MV[:, 0:1], in1=LG[:],
                                                         op0=ALU.subtract, op1=ALU.mult), waits=[(sem, v)], inc=sem)
    v = P.add("vector", lambda e: e.scalar_tensor_tensor(out=OUT, in0=TMP[:], scalar=RSTD[:, 0:1], in1=LB[:],
                                                         op0=ALU.mult, op1=ALU.add), waits=[(sem, v)], inc=sem)
    return v


def load_cast_weight(P, pw, dst_chunks, src_chunks, WF, s_wld, s_wcv, cnt):
    for dst, src in zip(dst_chunks, src_chunks):
        n = cnt[0]
        slot = n % 2
        v = P.add("sync", lambda e, slot=slot, src=src: e.dma_start(out=WF[slot][:], in_=src),
                  waits=[pw, (s_wcv, n - 1)], inc=s_wld[slot], dma=True)
        P.add("vector", lambda e, slot=slot, dst=dst: e.tensor_copy(out=dst, in_=WF[slot][:]),
              waits=[pw, (s_wld[slot], v)], inc=s_wcv)
        cnt[0] += 1
    return s_wcv.v


def phase4a(nc, P, D):
    with ExitStack() as es:
        WPA = _sb(es, nc, "p4_WPA", [128, 4, 1024], BF16)
        WPB = _sb(es, nc, "p4_WPB", [128, 8, 1024], BF16)
        WO = _sb(es, nc, "p4_WO", [128, 8, 1024], BF16)
        OAt = [_sb(es, nc, f"p4_OA{i}", [128, 4, 512], BF16) for i in range(2)]
        OBt = [_sb(es, nc, f"p4_OB{i}", [128, 8, 512], BF16) for i in range(2)]
        SGt = _sb(es, nc, "p4_SG", [128, 16, 512], BF16)
        XS = _sb(es, nc, "p4_XS", [128, 4, 1024], F32)
        MT = _sb(es, nc, "p4_MT", [128, 8, 512], BF16)
        T1 = [_sb(es, nc, f"p4_T1{i}", [128, 512], F32) for i in range(2)]
        T2 = [_sb(es, nc, f"p4_T2{i}", [128, 512], F32) for i in range(2)]
        V = _sb(es, nc, "p4_V", [128, 1024], F32)
        TMP = _sb(es, nc, "p4_TMP", [128, 1024], F32)
        X1O = [_sb(es, nc, f"p4_X1O{i}", [128, 1024], F32) for i in range(2)]
        X1B = [_sb(es, nc, f"p4_X1B{i}", [128, 1024], BF16) for i in range(2)]
        X1TS = _sb(es, nc, "p4_X1TS", [128, 8, 512], BF16)
        LG = _sb(es, nc, "p4_LG", [128, 1024], F32)
        LB = _sb(es, nc, "p4_LB", [128, 1024], F32)
        ST6 = _sb(es, nc, "p4_ST6", [128, 12], F32)
        MV = _sb(es, nc, "p4_MV", [128, 2], F32)
        STD = _sb(es, nc, "p4_STD", [128, 1], F32)
        RSTD = _sb(es, nc, "p4_RSTD", [128, 1], F32)
        EPSB = _sb(es, nc, "p4_EPSB", [128, 1], F32)
        IDF = _sb(es, nc, "p4_IDF", [128, 128], F32)
        IDB = _sb(es, nc, "p4_IDB", [128, 128], BF16)
        JK = _sb(es, nc, "p4_JK", [128, 1], F32)
        PSY = _ps(es, nc, "p4_PSY", [128, 4, 512], F32)
        PSM = _ps(es, nc, "p4_PSM", [128, 2, 512], F32)
        PST = _ps(es, nc, "p4_PST", [128, 8, 128], BF16)
        s_c = P.sem("p4c")
        s_wld = [P.sem(f"p4wld{i}") for i in range(2)]
        s_wcv = P.sem("p4wcv")
        s_ldab = [P.sem(f"p4ldab{i}") for i in range(2)]
        s_ldsg = P.sem("p4ldsg")
        s_ldx = P.sem("p4ldx")
        s_y = [P.sem(f"p4y{i}") for i in range(2)]
        s_t = [P.sem(f"p4t{i}") for i in range(2)]
        s_mt = P.sem("p4mt")
        s_mm = [P.sem(f"p4mm{i}") for i in range(2)]
        s_v = [P.sem(f"p4v{i}") for i in range(2)]
        s_ln = P.sem("p4ln")
        s_xb = P.sem("p4xb")
        s_tp = P.sem("p4tp")
        s_tc = P.sem("p4tc")
        s_so = [P.sem(f"p4so{i}") for i in range(2)]
        s_sx = P.sem("p4sx")
        pw = P.pw()

        for (dst, src) in ((LG, D["ln1g"]), (LB, D["ln1b"]), (IDF, D["ident"])):
            vc = P.add("sync", lambda e, dst=dst, src=src: e.dma_start(out=dst[:], in_=src), waits=[pw], inc=s_c, dma=True)
        P.add("vector", lambda e: e.tensor_copy(out=IDB[:], in_=IDF[:]), waits=[pw, (s_c, vc)], inc=s_c)
        vconst = P.add("vector", lambda e: e.memset(EPSB[:], LN_EPS), waits=[pw], inc=s_c)

        P.add("sync", lambda e: e.dma_start(out=WPA[:], in_=D["WAB"][:, 0:4, :]), waits=[pw], inc=s_wcv, dma=True)
        P.add("sync", lambda e: e.dma_start(out=WPB[:], in_=D["WAB"][:, 4:12, :]), waits=[pw], inc=s_wcv, dma=True)
        vw = P.add("sync", lambda e: e.dma_start(out=WO[:], in_=D["WAB"][:, 12:20, :]), waits=[pw], inc=s_wcv, dma=True)

        oat = D["OAT"].rearrange("k p t -> p k t")
        obt = D["OBT"].rearrange("k p t -> p k t")
        sgt = D["SG"].rearrange("k p t -> p k t")
        xo = D["xo"].rearrange("(s p) d -> p s d", p=128)
        x1 = D["X1"].rearrange("(s p) d -> p s d", p=128)
        x1t = D["X1T"].rearrange("k p t -> p k t")

        last_y_pe = {}
        ld_ab = {}

        def load_ab(t):
            sl = t % 2
            w = [pw] + (last_y_pe[t - 2] if t >= 2 else [])
            P.add("sync", lambda e: e.dma_start(out=OAt[sl][:], in_=oat[:, :, t * 512:(t + 1) * 512]), waits=w, inc=s_ldab[sl], dma=True)
            ld_ab[t] = P.add("sync", lambda e: e.dma_start(out=OBt[sl][:], in_=obt[:, :, t * 512:(t + 1) * 512]), waits=w,
                             inc=s_ldab[sl], dma=True)

        load_ab(0)
        yj = 0
        mj = 0
        sbk = 0
        pending_tp = []

        def flush_tp():
            while pending_tp:
                pending_tp.pop(0)()

        for t in range(8):
            sl = t % 2
            if t + 1 < 8:
                load_ab(t + 1)
            v_sg = P.add("sync", lambda e, t=t: e.dma_start(out=SGt[:], in_=sgt[:, :, t * 512:(t + 1) * 512]),
                         waits=[pw, (s_t[0], s_t[0].v), (s_t[1], s_t[1].v)], inc=s_ldsg, dma=True)
            v_x = P.add("sync", lambda e, t=t: e.dma_start(out=XS[:], in_=xo[:, 4 * t:4 * t + 4, :]),
                        waits=[pw, (s_v[0], s_v[0].v), (s_v[1], s_v[1].v)], inc=s_ldx, dma=True)
            mt_w0 = (s_mm[0], s_mm[0].v), (s_mm[1], s_mm[1].v)
            for dc in range(8):
                ys = yj % 2
                ky = yj // 2

                def ymm(e, ys=ys, dc=dc, sl=sl):
                    ins = None
                    for k in range(4):
                        ins = e.matmul(PSY[:, 2 * ys, :], lhsT=WPA[:, k, dc * 128:(dc + 1) * 128], rhs=OAt[sl][:, k, :],
                                       start=(k == 0), stop=(k == 3))
                    for k in range(8):
                        ins = e.matmul(PSY[:, 2 * ys + 1, :], lhsT=WPB[:, k, dc * 128:(dc + 1) * 128], rhs=OBt[sl][:, k, :],
                                       start=(k == 0), stop=(k == 7))
                    return ins

                P.add("tensor", ymm, waits=[pw, (s_wcv, vw), (s_ldab[sl], ld_ab[t]), (s_t[ys], 2 * ky)], inc=s_y[ys])
                wt = [pw, (s_y[ys], ky + 1), (s_ldsg, v_sg), (s_mt, s_mt.v - 1)]
                P.add("vector", lambda e, ys=ys, dc=dc: e.tensor_tensor(out=T1[ys][:], in0=PSY[:, 2 * ys, :], in1=SGt[:, dc, :], op=ALU.mult),
                      waits=wt, inc=s_t[ys])
                vt = P.add("vector", lambda e, ys=ys, dc=dc: e.tensor_tensor(out=T2[ys][:], in0=PSY[:, 2 * ys + 1, :], in1=SGt[:, 8 + dc, :],
                                                                         op=ALU.mult), waits=wt, inc=s_t[ys])
                P.add("vector", lambda e, ys=ys, dc=dc: e.tensor_tensor(out=MT[:, dc, :], in0=T1[ys][:], in1=T2[ys][:], op=ALU.add),
                      waits=[pw, (s_t[ys], vt)] + list(mt_w0), inc=s_mt)
                yj += 1
                if dc == 1:
                    flush_tp()
            last_y_pe[t] = [(s_y[0], s_y[0].v), (s_y[1], s_y[1].v)]
            v_mt = s_mt.v
            for s in range(4):
                xs_o = sbk % 2
                for hf in range(2):
                    ms = mj % 2
                    km = mj // 2

                    def mmm(e, ms=ms, s=s, hf=hf):
                        ins = None
                        for k in range(8):
                            ins = e.matmul(PSM[:, ms, :], lhsT=MT[:, k, s * 128:(s + 1) * 128], rhs=WO[:, k, hf * 512:(hf + 1) * 512],
                                           start=(k == 0), stop=(k == 7))
                        return ins

                    P.add("tensor", mmm, waits=[pw, (s_mt, v_mt), (s_v[ms], km)], inc=s_mm[ms])
                    vv = P.add("vector", lambda e, ms=ms, s=s, hf=hf: e.scalar_tensor_tensor(
                        out=V[:, hf * 512:(hf + 1) * 512], in0=XS[:, s, hf * 512:(hf + 1) * 512], scalar=ALPHA, in1=PSM[:, ms, :],
                        op0=ALU.mult, op1=ALU.add), waits=[pw, (s_mm[ms], km + 1), (s_ldx, v_x), (s_ln, s_ln.v)], inc=s_v[ms])
                    mj += 1
                flush_tp()
                win = [(s_v[0], s_v[0].v), (s_v[1], s_v[1].v), (s_c, vconst), (s_so[xs_o], 16 * (sbk // 2)), (s_xb, sbk - 1)]
                vln = layer_norm_ops(P, V, ST6, MV, STD, RSTD, TMP, X1O[xs_o][:], LG, LB, EPSB, s_ln, win)
                row = 4 * t + s
                P.add("sync", lambda e, xs_o=xs_o, row=row: e.dma_start(out=x1[:, row, :], in_=X1O[xs_o][:]),
                      waits=[pw, (s_ln, vln)], inc=s_so[xs_o], dma=True)
                vb = P.add("scalar", lambda e, xs_o=xs_o: e.copy(out=X1B[xs_o][:], in_=X1O[xs_o][:]),
                           waits=[pw, (s_ln, vln), (s_tp, sbk - 1)], inc=s_xb)

                def tp_group(t=t, s=s, xs_o=xs_o, vb=vb):
                    def tp(e):
                        ins = None
                        for k in range(8):
                            ins = e.transpose(out=PST[:, k, :], in_=X1B[xs_o][:, k * 128:(k + 1) * 128], identity=IDB[:])
                        return ins

                    vtp = P.add("tensor", tp, waits=[pw, (s_xb, vb), (s_tc, s_tc.v)], inc=s_tp)
                    P.add("scalar", lambda e: e.copy(out=X1TS[:, :, s * 128:(s + 1) * 128], in_=PST[:]),
                          waits=[pw, (s_tp, vtp), (s_sx, 16 * t)], inc=s_tc)
                    if s == 3:
                        P.add("gpsimd", lambda e: e.dma_start(out=x1t[:, :, t * 512:(t + 1) * 512], in_=X1TS[:]),
                              waits=[pw, (s_tc, s_tc.v)], inc=s_sx, dma=True)

                pending_tp.append(tp_group)
                sbk += 1
        flush_tp()
        P.end_phase(JK[:])


def phase4b(nc, P, D):
    TT = 256
    NT = NQ // TT
    with ExitStack() as es:
        W1 = _sb(es, nc, "p5_W1", [128, 8, 4096], BF16)
        W2 = _sb(es, nc, "p5_W2", [128, 32, 1024], BF16)
        XT = [_sb(es, nc, f"p5_XT{i}", [128, 8, TT], BF16) for i in range(2)]
        X1 = _sb(es, nc, "p5_X1", [128, 2, 1024], F32)
        HT = _sb(es, nc, "p5_HT", [128, 32, TT], BF16)
        RT = [_sb(es, nc, f"p5_RT{i}", [128, TT], F32) for i in range(2)]
        V = _sb(es, nc, "p5_V", [128, 1024], F32)
        TMP = _sb(es, nc, "p5_TMP", [128, 1024], F32)
        YO = [_sb(es, nc, f"p5_YO{i}", [128, 1024], F32) for i in range(2)]
        LG = _sb(es, nc, "p5_LG", [128, 1024], F32)
        LB = _sb(es, nc, "p5_LB", [128, 1024], F32)
        ST6 = _sb(es, nc, "p5_ST6", [128, 12], F32)
        MV = _sb(es, nc, "p5_MV", [128, 2], F32)
        STD = _sb(es, nc, "p5_STD", [128, 1], F32)
        RSTD = _sb(es, nc, "p5_RSTD", [128, 1], F32)
        EPSB = _sb(es, nc, "p5_EPSB", [128, 1], F32)
        JK = _sb(es, nc, "p5_JK", [128, 1], F32)
        PSH = _ps(es, nc, "p5_PSH", [128, 4, 512], F32)
        PSF = _ps(es, nc, "p5_PSF", [128, 2, 512], F32)
        s_c = P.sem("p5c")
        s_wld = [P.sem(f"p5wld{i}") for i in range(2)]
        s_wcv = P.sem("p5wcv")
        s_ldt = [P.sem(f"p5ldt{i}") for i in range(2)]
        s_ldx = P.sem("p5ldx")
        s_h = [P.sem(f"p5h{i}") for i in range(4)]
        s_r = [P.sem(f"p5r{i}") for i in range(4)]
        s_sq = [P.sem(f"p5sq{i}") for i in range(2)]
        s_f = [P.sem(f"p5f{i}") for i in range(2)]
        s_v = [P.sem(f"p5v{i}") for i in range(2)]
        s_ln = P.sem("p5ln")
        s_so = [P.sem(f"p5so{i}") for i in range(2)]
        pw = P.pw()

        for (dst, src) in ((LG, D["ln2g"]), (LB, D["ln2b"])):
            vc = P.add("sync", lambda e, dst=dst, src=src: e.dma_start(out=dst[:], in_=src), waits=[pw], inc=s_c, dma=True)
        vconst = P.add("vector", lambda e: e.memset(EPSB[:], LN_EPS), waits=[pw, (s_c, vc)], inc=s_c)

        vw1 = None
        for q in range(4):
            vw1 = P.add("sync", lambda e, q=q: e.dma_start(out=W1[:, 2 * q:2 * q + 2, :], in_=D["W1B"][:, 2 * q:2 * q + 2, :]),
                        waits=[pw], inc=s_wld[0], dma=True)
        vw2 = None
        for q in range(4):
            vw2 = P.add("sync", lambda e, q=q: e.dma_start(out=W2[:, 8 * q:8 * q + 8, :], in_=D["W2B"][:, 8 * q:8 * q + 8, :]),
                        waits=[pw], inc=s_wld[1], dma=True)

        x1t = D["X1T"].rearrange("k p t -> p k t")
        x1 = D["X1"].rearrange("(s p) d -> p s d", p=128)
        yv = D["y"].rearrange("(s p) d -> p s d", p=128)
        last_h_pe = {}
        ld_t = {}

        def load_t(t):
            sl = t % 2
            w = [pw] + (last_h_pe[t - 2] if t >= 2 else [])
            ld_t[t] = P.add("sync", lambda e: e.dma_start(out=XT[sl][:], in_=x1t[:, :, t * TT:(t + 1) * TT]), waits=w,
                            inc=s_ldt[sl], dma=True)

        load_t(0)
        hj = 0
        fj = 0
        sbk = 0
        for t in range(NT):
            sl = t % 2
            if t + 1 < NT:
                load_t(t + 1)
            v_x = P.add("sync", lambda e, t=t: e.dma_start(out=X1[:], in_=x1[:, 2 * t:2 * t + 2, :]),
                        waits=[pw, (s_v[0], s_v[0].v), (s_v[1], s_v[1].v)], inc=s_ldx, dma=True)
            ht_w0 = [(s_f[0], s_f[0].v), (s_f[1], s_f[1].v)]
            for fc in range(32):
                hs = hj % 4
                kh = hj // 4
                rs = hj % 2
                kr = hj // 2

                def hmm(e, hs=hs, fc=fc, sl=sl):
                    ins = None
                    for k in range(8):
                        ins = e.matmul(PSH[:, hs, 0:TT], lhsT=W1[:, k, fc * 128:(fc + 1) * 128], rhs=XT[sl][:, k, :],
                                       start=(k == 0), stop=(k == 7))
                    return ins

                P.add("tensor", hmm, waits=[pw, (s_wld[0], vw1), (s_ldt[sl], ld_t[t]), (s_r[hs], kh)], inc=s_h[hs])
                P.add("scalar", lambda e, hs=hs, rs=rs: e.activation(out=RT[rs][:], in_=PSH[:, hs, 0:TT], func=AF.Relu),
                      waits=[pw, (s_h[hs], kh + 1), (s_sq[rs], kr)], inc=s_r[hs])
                P.add("vector",
                      lambda e, rs=rs, fc=fc: e.tensor_tensor(out=HT[:, fc, :], in0=RT[rs][:], in1=RT[rs][:], op=ALU.mult),
                      waits=[pw, (s_r[hs], kh + 1)] + ht_w0, inc=s_sq[rs])
                hj += 1
            last_h_pe[t] = [(s_h[i], s_h[i].v) for i in range(4)]
            v_sq = [(s_sq[0], s_sq[0].v), (s_sq[1], s_sq[1].v)]
            for s in range(2):
                yo = sbk % 2
                for hf in range(2):
                    fs = fj % 2
                    kf = fj // 2

                    def fmm(e, fs=fs, s=s, hf=hf):
                        ins = None
                        for fc in range(32):
                            ins = e.matmul(PSF[:, fs, :], lhsT=HT[:, fc, s * 128:(s + 1) * 128], rhs=W2[:, fc, hf * 512:(hf + 1) * 512],
                                           start=(fc == 0), stop=(fc == 31))
                        return ins

                    P.add("tensor", fmm, waits=[pw, (s_wld[1], vw2), (s_v[fs], kf)] + v_sq, inc=s_f[fs])
                    P.add("vector", lambda e, fs=fs, s=s, hf=hf: e.scalar_tensor_tensor(
                        out=V[:, hf * 512:(hf + 1) * 512], in0=X1[:, s, hf * 512:(hf + 1) * 512], scalar=ALPHA, in1=PSF[:, fs, :],
                        op0=ALU.mult, op1=ALU.add), waits=[pw, (s_f[fs], kf + 1), (s_ldx, v_x), (s_ln, s_ln.v)], inc=s_v[fs])
                    fj += 1
                win = [(s_v[0], s_v[0].v), (s_v[1], s_v[1].v), (s_c, vconst), (s_so[yo], 16 * (sbk // 2))]
                vln = layer_norm_ops(P, V, ST6, MV, STD, RSTD, TMP, YO[yo][:], LG, LB, EPSB, s_ln, win)
                row = 2 * t + s
                P.add("sync", lambda e, yo=yo, row=row: e.dma_start(out=yv[:, row, :], in_=YO[yo][:]),
                      waits=[pw, (s_ln, vln)], inc=s_so[yo], dma=True)
                sbk += 1
        P.end_phase(JK[:])


def build_nc(debug=False, phases=(1, 2, 3, 4, 5)):
    nc = bass.Bass("TRN2", target_bir_lowering=False)
    D = {}

    def din(name, shape):
        D[name] = nc.dram_tensor(name, shape, F32, kind="ExternalInput").ap()

    din("xT", [1024, NK]); din("xo", [NQ, 1024]); din("wp", [1024, COLS_IN]); din("bg", [128, 16])
    din("ga", [24, 128, 256]); din("ma", [128, 256]); din("gb", [8, 128, 256]); din("cb", [128, 8]); din("mb", [128, 256])
    din("ident", [128, 128]); din("pvalid", [128, 64]); din("lam", [128, 4, 64]); din("gs", [128, 128])
    din("ln1g", [128, 1024]); din("ln1b", [128, 1024]); din("ln2g", [128, 1024]); din("ln2b", [128, 1024])
    din("wpa", [512, 1024]); din("wpb", [1024, 1024]); din("wo", [1024, 1024]); din("w1", [1024, 4096]); din("w2", [4096, 1024])
    D["y"] = nc.dram_tensor("y", [NQ, 1024], F32, kind="ExternalOutput").ap()
    kind = "ExternalOutput" if debug else "Internal"

    def scr(name, shape, dt):
        D[name] = nc.dram_tensor(name, shape, dt, kind=kind).ap()

    scr("QTB", [8, 128, NQ], BF16); scr("KTB", [8, 128, NK], BF16); scr("VB", [NK, 1024], BF16)
    scr("QTA", [12, 128, NQ], BF16); scr("KTA", [12, 128, NKA], BF16); scr("VA", [NKA, 1536], BF16)
    scr("SG", [16, 128, NQ], BF16); scr("OAT", [4, 128, NQ], BF16); scr("OBT", [8, 128, NQ], BF16)
    scr("X1", [NQ, 1024], F32); scr("X1T", [8, 128, NQ], BF16)
    scr("WAB", [128, 20, 1024], BF16); scr("W1B", [128, 8, 4096], BF16); scr("W2B", [128, 32, 1024], BF16)
    with ExitStack() as es:
        P = Prog(nc, es)
        if 1 in phases:
            phase1(nc, P, D)
        if 2 in phases:
            phase2(nc, P, D)
        if 3 in phases:
            phase3(nc, P, D)
        if 4 in phases:
            phase4a(nc, P, D)
        if 5 in phases:
            phase4b(nc, P, D)
    return nc


def t5_bucket_np(dist):
    n = np.maximum(dist, 0)
    nf = np.maximum(n, 1).astype(np.float32)
    large = 16 + (np.log(nf / np.float32(16.0)) / np.float32(math.log(8.0)) * np.float32(16.0)).astype(np.int32)
    large = np.minimum(large, 31)
    return np.where(n < 16, n, large)


def prep_shared(inp):
    f = lambda a: np.ascontiguousarray(np.asarray(a, dtype=np.float32))
    w = np.asarray(inp["w_in"], dtype=np.float32)[0]
    B0 = 4608
    cols = []
    for h in range(8):
        cols += list(range(B0 + h * 64, B0 + h * 64 + 64)) + list(range(B0 + 512 + h * 64, B0 + 512 + h * 64 + 64))
    for h in range(8):
        cols += list(range(B0 + 1024 + h * 64, B0 + 1024 + h * 64 + 64)) + list(range(B0 + 1536 + h * 64, B0 + 1536 + h * 64 + 64))
    cols += list(range(0, 1536)) + list(range(1536, 3072)) + list(range(7680, 9728))
    cols += list(range(B0 + 2048, B0 + 3072)) + list(range(3072, 4608))
    assert len(cols) == COLS_IN
    sh = {"wp": f(w[:, cols])}
    sh["bg"] = f(np.asarray(inp["b_gate"], np.float32)[0].reshape(16, 128).T)
    rb = np.asarray(inp["rel_bias"], np.float32)
    i = np.arange(128)[:, None]
    c = np.arange(256)[None, :]
    ch = c // 128
    j = c % 128
    steps = (1 - ch) * 128 + j - i
    ga = np.zeros((24, 128, 256), np.float32)
    for g, d in enumerate(DILS):
        bk = t5_bucket_np(np.maximum(steps, 0) * d)
        for h in range(8):
            ga[g * 8 + h] = rb[bk, g * 8 + h]
    sh["ga"] = ga
    sh["ma"] = f(np.where((steps >= 0) & (steps <= 128), 0.0, MASKV / 8.0))
    dist = c - i
    bk = t5_bucket_np(np.maximum(dist, 0))
    gb = np.zeros((8, 128, 256), np.float32)
    for h in range(8):
        gb[h] = rb[bk, 24 + h]
    sh["gb"] = gb
    sh["cb"] = f(np.broadcast_to(rb[31, 24:32][None, :], (128, 8)))
    sh["mb"] = f(np.where(dist >= 0, 0.0, MASKV))
    sh["ident"] = np.eye(128, dtype=np.float32)
    lam = np.stack([np.asarray(inp[k], np.float32)[0] for k in ("lambda_q1", "lambda_q2", "lambda_k1", "lambda_k2")])
    sh["lam"] = f(np.broadcast_to(lam[None], (128, 4, 64)))
    sh["gs"] = f(np.broadcast_to(np.asarray(inp["subln_g"], np.float32)[0][None, :], (128, 128)))
    for k, n in (("ln1g", "ln1_g"), ("ln1b", "ln1_b"), ("ln2g", "ln2_g"), ("ln2b", "ln2_b")):
        sh[k] = f(np.broadcast_to(np.asarray(inp[n], np.float32)[0][None, :], (128, 1024)))
    sh["wpa"] = f(np.asarray(inp["w_proj_a"])[0]); sh["wpb"] = f(np.asarray(inp["w_proj_b"])[0])
    sh["wo"] = f(np.asarray(inp["w_out"])[0]); sh["w1"] = f(np.asarray(inp["w_mlp1"])[0]); sh["w2"] = f(np.asarray(inp["w_mlp2"])[0])
    return sh


def prep_core(x, c):
    b, par = c // 2, c % 2
    xb = np.asarray(x[b], dtype=np.float32)
    xl = np.zeros((NK, 1024), np.float32)
    if par == 1:
        xl[:] = xb
    else:
        xl[2048:] = xb[0:NK - 2048]
    own = np.concatenate([xl[2048:4096], xl[6144:8192]], axis=0)
    pv = np.full((128, 64), float(par), np.float32)
    return {"xT": np.ascontiguousarray(xl.T), "xo": np.ascontiguousarray(own), "pvalid": pv}


_NC_CACHE = {}


def kernel(**inputs):
    x = np.asarray(inputs["x"], dtype=np.float32)
    sh = prep_shared(inputs)
    in_maps = []
    for c in range(8):
        m = dict(sh)
        m.update(prep_core(x, c))
        in_maps.append(m)
    if "nc" not in _NC_CACHE:
        _NC_CACHE["nc"] = build_nc()
    res = run_bass_kernel_spmd(_NC_CACHE["nc"], in_maps, core_ids=list(range(8)))
    out = np.zeros((BATCH, SEQ, D_MODEL), np.float32)
    for c in range(8):
        b, par = c // 2, c % 2
        y = np.asarray(res.results[c]["y"], dtype=np.float32)
        r0 = 2048 if par == 1 else 0
        out[b, r0:r0 + 2048] = y[0:2048]
        out[b, r0 + 4096:r0 + 6144] = y[2048:4096]
    return out
```

```python
import math
from contextlib import ExitStack

import numpy as np

import concourse.bass as bass
import concourse.mybir as mybir
from concourse.bass_utils import run_bass_kernel_spmd

F32 = mybir.dt.float32
BF16 = mybir.dt.bfloat16
AF = mybir.ActivationFunctionType
ALU = mybir.AluOpType
AX = mybir.AxisListType

D_MODEL = 1024
SEQ = 8192
BATCH = 4
NQ = 4096
NK = 8192
NKA = 8192
COLS_IN = 9728
DILS = (1, 4, 16)
ALPHA = 2.0 ** 0.25
LAM_INIT = 0.8 - 0.6 * math.exp(0.0)
LN_EPS = 1e-5
MASKV = -240000.0
ENGS = ("sync", "scalar", "vector", "gpsimd", "tensor")

C_FQ, C_FK, C_AQ, C_AK, C_GT, C_BV, C_AV = 0, 1024, 2048, 3584, 5120, 7168, 8192


class Sem:
    __slots__ = ("h", "abs", "base", "name")

    def __init__(self, h, name):
        self.h, self.abs, self.base, self.name = h, 0, 0, name

    @property
    def v(self):
        return self.abs - self.base


class Prog:
    def __init__(self, nc, es):
        self.nc, self.es = nc, es
        self.q = {k: [] for k in ENGS}
        self.waited = {k: {} for k in ENGS}
        self.sems = []
        self.pool = []
        self.used = 0
        self.nphase = 0
        self.bar = self.sem("bar")

    def sem(self, name):
        if name != "bar" and self.used < len(self.pool):
            s = self.pool[self.used]
            self.used += 1
            s.base = s.abs
            return s
        s = Sem(self.es.enter_context(self.nc.semaphore(f"s{len(self.sems)}")), f"s{len(self.sems)}")
        self.sems.append(s)
        if name != "bar":
            self.pool.append(s)
            self.used += 1
        return s

    def add(self, eng, fn, waits=(), inc=None, dma=False):
        ws = []
        for w in waits:
            if w is None:
                continue
            s, v = w
            if v <= 0:
                continue
            assert v <= s.v, f"deadlock: {eng} waits {s.name}>={v} but only {s.v} scheduled"
            va = s.base + v
            if self.waited[eng].get(s.name, 0) >= va:
                continue
            self.waited[eng][s.name] = va
            ws.append((s, va))
        amt = 16 if dma else 1
        newv = None
        if inc is not None:
            inc.abs += amt
            newv = inc.v
        self.q[eng].append((ws, fn, inc, amt))
        return newv

    def end_phase(self, junk):
        waits = [(s, s.v) for s in self.sems if s is not self.bar]
        self.nphase += 1
        self.add("gpsimd", lambda e: e.memset(junk, 0.0), waits=waits, inc=self.bar)
        with self.nc.Block() as block:
            for name in ENGS:
                items = self.q[name]

                def body(e, items=items):
                    for ws, fn, inc, amt in items:
                        for s, v in ws:
                            e.wait_ge(s.h, v)
                        ins = fn(e)
                        if inc is not None:
                            ins.then_inc(inc.h, amt)

                getattr(block, name)(body)
        self.q = {k: [] for k in ENGS}
        self.used = 0
        for name in ENGS:
            if name == "gpsimd":
                continue
            self.waited[name].pop(self.bar.name, None)
        self.phase_wait = (self.bar, self.nphase)

    def pw(self):
        return getattr(self, "phase_wait", None)


def _sb(es, nc, name, shape, dt):
    return es.enter_context(nc.sbuf_tensor(name, shape, dt))


def _ps(es, nc, name, shape, dt):
    return es.enter_context(nc.psum_tensor(name, shape, dt))


def phase1(nc, P, D):
    with ExitStack() as es:
        W = _sb(es, nc, "p1_W", [128, 8, COLS_IN], BF16)
        WF = [_sb(es, nc, f"p1_WF{i}", [128, 8, 128], F32) for i in range(2)]
        XF = _sb(es, nc, "p1_XF", [128, 8, 512], F32)
        XB = [_sb(es, nc, f"p1_XB{i}", [128, 8, 512], BF16) for i in range(2)]
        OST = [_sb(es, nc, f"p1_OST{i}", [128, 512], BF16) for i in range(4)]
        BG = _sb(es, nc, "p1_BG", [128, 16], F32)
        JK = _sb(es, nc, "p1_JK", [128, 1], F32)
        PS = _ps(es, nc, "p1_PS", [128, 4, 512], F32)
        s_xld, s_xcv, s_wcv, s_misc = P.sem("p1xld"), P.sem("p1xcv"), P.sem("p1wcv"), P.sem("p1misc")
        s_wld = [P.sem(f"p1wld{i}") for i in range(2)]
        s_mm = [P.sem(f"p1mm{i}") for i in range(4)]
        s_ev = [P.sem(f"p1ev{i}") for i in range(4)]
        s_st = [P.sem(f"p1st{i}") for i in range(4)]
        xT = D["xT"].rearrange("(kc p) t -> p kc t", p=128)
        wp = D["wp"].rearrange("(kc p) c -> p kc c", p=128)
        pw = P.pw()

        v_bg = P.add("sync", lambda e: e.dma_start(out=BG[:], in_=D["bg"]), waits=[pw], inc=s_misc, dma=True)

        wcv_of = {}
        wl = [0]

        def load_w(cid):
            n = wl[0]
            slot = n % 2
            v = P.add("sync", lambda e: e.dma_start(out=WF[slot][:], in_=wp[:, :, cid * 128:(cid + 1) * 128]),
                      waits=[pw, (s_wcv, n - 1)], inc=s_wld[slot], dma=True)
            P.add("vector", lambda e: e.tensor_copy(out=W[:, :, cid * 128:(cid + 1) * 128], in_=WF[slot][:]),
                  waits=[pw, (s_wld[slot], v)], inc=s_wcv)
            wcv_of[cid] = s_wcv.v
            wl[0] += 1

        worder = (list(range(8, 16)) + list(range(56, 64)) + list(range(28, 40)) + list(range(64, 76))
                  + list(range(0, 8)) + list(range(16, 28)) + list(range(40, 56)))
        wplan = {-1: worder[0:40]}
        for t in range(4):
            wplan[t] = worder[40 + 9 * t:40 + 9 * (t + 1)]

        tile_last = {}

        def load_x(t):
            v = P.add("sync", lambda e: e.dma_start(out=XF[:], in_=xT[:, :, t * 512:(t + 1) * 512]),
                      waits=[pw, (s_xcv, t)], inc=s_xld, dma=True)
            P.add("vector", lambda e: e.tensor_copy(out=XB[t % 2][:], in_=XF[:]),
                  waits=[pw, (s_xld, v)] + (tile_last[t - 2] if t >= 2 else []), inc=s_xcv)

        def jobs_for(t):
            jobs = []
            own = (t // 4) % 2 == 1
            to = t - 4 if t < 8 else t - 8
            for h in range(8):
                jobs.append(("F", C_FK + h * 128, D["KTB"][h][:, t * 512:(t + 1) * 512], None))
            for cb in range(2):
                for s in range(4):
                    r0 = t * 512 + s * 128
                    jobs.append(("T", C_BV + cb * 512, D["VB"][r0:r0 + 128, cb * 512:(cb + 1) * 512], s))
            gs = (0, 1, 2) if (own or t % 4 == 3) else (2,)
            for g in gs:
                for hp in range(4):
                    m = g * 4 + hp
                    jobs.append(("F", C_AK + m * 128, D["KTA"][m][:, t * 512:(t + 1) * 512], None))
            for g in gs:
                for s in range(4):
                    r0 = t * 512 + s * 128
                    jobs.append(("T", C_AV + g * 512, D["VA"][r0:r0 + 128, g * 512:(g + 1) * 512], s))
            if own:
                for h in range(8):
                    jobs.append(("F", C_FQ + h * 128, D["QTB"][h][:, to * 512:(to + 1) * 512], None))
                for m in range(12):
                    jobs.append(("F", C_AQ + m * 128, D["QTA"][m][:, to * 512:(to + 1) * 512], None))
                for i in range(16):
                    jobs.append(("G", C_GT + i * 128, D["SG"][i][:, to * 512:(to + 1) * 512], i))
            return jobs

        for cid in wplan[-1]:
            load_w(cid)
        load_x(0)
        jc = 0
        for t in range(16):
            jobs = jobs_for(t)
            half = len(jobs) // 2
            for ji, (kind, col, dst, aux) in enumerate(jobs):
                if ji == half:
                    if t + 1 < 16:
                        load_x(t + 1)
                    for cid in wplan.get(t, []):
                        load_w(cid)
                slot = jc % 4
                k = jc // 4
                xb = XB[t % 2]
                if kind == "T":
                    wneed = max(wcv_of[col // 128 + i] for i in range(4))
                else:
                    wneed = wcv_of[col // 128]

                def mm(e, kind=kind, col=col, aux=aux, xb=xb, slot=slot):
                    ins = None
                    for kc in range(8):
                        if kind == "T":
                            ins = e.matmul(PS[:, slot, :], lhsT=xb[:, kc, aux * 128:(aux + 1) * 128],
                                           rhs=W[:, kc, col:col + 512], start=(kc == 0), stop=(kc == 7))
                        else:
                            ins = e.matmul(PS[:, slot, :], lhsT=W[:, kc, col:col + 128],
                                           rhs=xb[:, kc, :], start=(kc == 0), stop=(kc == 7))
                    return ins

                P.add("tensor", mm, waits=[pw, (s_xcv, t + 1), (s_wcv, wneed), (s_ev[slot], k)], inc=s_mm[slot])
                evw = [pw, (s_mm[slot], k + 1), (s_st[slot], 16 * k)]
                if kind == "G":
                    P.add("scalar", lambda e, slot=slot, aux=aux: e.activation(
                        out=OST[slot][:], in_=PS[:, slot, :], func=AF.Sigmoid, bias=BG[:, aux:aux + 1], scale=1.0),
                        waits=evw + [(s_misc, v_bg)], inc=s_ev[slot])
                elif jc % 2 == 0:
                    P.add("vector", lambda e, slot=slot: e.tensor_copy(out=OST[slot][:], in_=PS[:, slot, :]),
                          waits=evw, inc=s_ev[slot])
                else:
                    P.add("scalar", lambda e, slot=slot: e.copy(out=OST[slot][:], in_=PS[:, slot, :]),
                          waits=evw, inc=s_ev[slot])
                P.add("gpsimd", lambda e, slot=slot, dst=dst: e.dma_start(out=dst, in_=OST[slot][:]),
                      waits=[pw, (s_ev[slot], k + 1)], inc=s_st[slot], dma=True)
                jc += 1
            tile_last[t] = [(s_mm[s], s_mm[s].v) for s in range(4)]
        P.end_phase(JK[:])


def phase2(nc, P, D):
    with ExitStack() as es:
        QA = [_sb(es, nc, f"p2_QA{i}", [128, NQ], BF16) for i in range(2)]
        KA = [_sb(es, nc, f"p2_KA{i}", [128, NKA], BF16) for i in range(2)]
        VS = [_sb(es, nc, f"p2_VS{i}", [128, 64, 128], BF16) for i in range(2)]
        GF = [_sb(es, nc, f"p2_GF{i}", [128, 2, 256], F32) for i in range(2)]
        BTF = _sb(es, nc, "p2_BTF", [128, 2, 256], F32)
        EB = [_sb(es, nc, f"p2_EB{i}", [128, 2, 256], F32) for i in range(2)]
        PTF = [_sb(es, nc, f"p2_PTF{i}", [128, 2, 256], F32) for i in range(2)]
        MA = _sb(es, nc, "p2_MA", [128, 256], F32)
        IDF = _sb(es, nc, "p2_IDF", [128, 128], F32)
        IDB = _sb(es, nc, "p2_IDB", [128, 128], BF16)
        PVF = _sb(es, nc, "p2_PVF", [128, 64], F32)
        PV64 = _sb(es, nc, "p2_PV64", [128, 64], BF16)
        ONE64 = _sb(es, nc, "p2_ONE64", [128, 64], BF16)
        PT = [_sb(es, nc, f"p2_PT{i}", [128, 2, 256], BF16) for i in range(3)]
        NACC = _sb(es, nc, "p2_NACC", [128, NQ], F32)
        DACC = _sb(es, nc, "p2_DACC", [128, NQ], F32)
        OAS = _sb(es, nc, "p2_OAS", [128, NQ], BF16)
        JK = _sb(es, nc, "p2_JK", [128, 1], F32)
        S = _ps(es, nc, "p2_S", [128, 4, 512], F32)
        ND = _ps(es, nc, "p2_ND", [128, 4, 512], F32)
        s_c = P.sem("p2c")
        s_ld = [P.sem(f"p2ld{i}") for i in range(2)]
        s_bt = P.sem("p2bt")
        s_qk = [P.sem(f"p2qk{i}") for i in range(2)]
        s_ex = [P.sem(f"p2ex{i}") for i in range(2)]
        s_pv = [P.sem(f"p2pv{i}") for i in range(3)]
        s_nf = [P.sem(f"p2nf{i}") for i in range(2)]
        s_dv = P.sem("p2dv")
        s_fin = P.sem("p2fin")
        s_pm = [P.sem(f"p2pm{i}") for i in range(2)]
        s_st = P.sem("p2st")
        pw = P.pw()

        vc = P.add("sync", lambda e: e.dma_start(out=MA[:], in_=D["ma"]), waits=[pw], inc=s_c, dma=True)
        vc = P.add("sync", lambda e: e.dma_start(out=PVF[:], in_=D["pvalid"]), waits=[pw], inc=s_c, dma=True)
        P.add("vector", lambda e: e.tensor_copy(out=PV64[:], in_=PVF[:]), waits=[pw, (s_c, vc)], inc=s_c)
        vconst = P.add("vector", lambda e: e.memset(ONE64[:], 1.0), waits=[pw], inc=s_c)

        passes = [(hp, g) for hp in range(4) for g in range(3)]
        pass_last_pe = {}
        pass_last_dve = {}
        ld_val = {}
        bt_val = {}

        def sched_loads(pi):
            hp, g = passes[pi]
            d = DILS[g]
            slot = pi % 2
            nb = 64 // d
            w = [pw] + (pass_last_pe[pi - 2] if pi >= 2 else [])
            P.add("sync", lambda e: e.dma_start(out=QA[slot][:], in_=D["QTA"][g * 4 + hp]), waits=w, inc=s_ld[slot], dma=True)
            P.add("sync", lambda e: e.dma_start(out=KA[slot][:], in_=D["KTA"][g * 4 + hp]), waits=w, inc=s_ld[slot], dma=True)
            vav = D["VA"].rearrange("(n i r) c -> r i n c", i=128, r=d)
            c0 = g * 512 + hp * 128
            for r in range(d):
                for n0 in range(0, nb, 16):
                    n1 = min(nb, n0 + 16)
                    P.add("sync", lambda e, r=r, n0=n0, n1=n1: e.dma_start(
                        out=VS[slot][:, r * nb + n0:r * nb + n1, :], in_=vav[r][:, n0:n1, c0:c0 + 128]),
                        waits=w, inc=s_ld[slot], dma=True)
            gh0 = g * 8 + 2 * hp
            v = P.add("sync", lambda e: e.dma_start(out=GF[slot][:], in_=D["ga"][gh0:gh0 + 2].rearrange("h i c -> i h c")),
                      waits=w, inc=s_ld[slot], dma=True)
            ld_val[pi] = v

        def sched_tables(pi):
            slot = pi % 2
            wb = [pw, (s_ld[slot], ld_val[pi]), (s_c, vconst)]
            v1 = None
            for hh in range(2):
                v1 = P.add("vector", lambda e, hh=hh: e.tensor_tensor(
                    out=BTF[:, hh, :], in0=GF[slot][:, hh, :], in1=MA[:], op=ALU.add),
                    waits=wb + [(s_bt, s_bt.v)], inc=s_bt)
            P.add("scalar", lambda e: e.activation(out=EB[slot][:], in_=BTF[:], func=AF.Exp),
                  waits=[pw, (s_bt, v1)] + (pass_last_dve[pi - 2] if pi >= 2 else []), inc=s_bt)
            bt_val[pi] = s_bt.v

        sched_loads(0)
        sched_tables(0)
        cn = {"qi": 0, "gi": 0, "fin": 0}

        def run_pass(pi, hp, g):
            d = DILS[g]
            slot = pi % 2
            nb = 64 // d
            nbB = 16 // d
            if pi + 1 < len(passes):
                sched_loads(pi + 1)

            def own_off(n):
                return 2048 if n < 2 * nbB else 4096

            groups = []
            if d == 16:
                for B in (1, 3):
                    base = 0 if B == 1 else 2048
                    for r0 in range(0, 16, 4):
                        qbs = [(r0 + i, B) for i in range(4)]
                        groups.append((qbs,
                                       lambda T, base=base, r0=r0: T[:, base:base + 2048].rearrange("p (j r) -> p r j", r=16)[:, r0:r0 + 4, :],
                                       lambda bank: ND[:, bank, 0:512].rearrange("p (r j) -> p r j", r=4)))
            else:
                for r in range(d):
                    for B in (1, 3):
                        for n0 in range(B * nbB, (B + 1) * nbB, 4):
                            qbs = [(r, n0 + i) for i in range(4)]
                            t0 = d * 128 * n0 + r - own_off(n0)
                            sl = slice(t0, t0 + 511 * d + 1, d)
                            groups.append((qbs, lambda T, sl=sl: T[:, sl], lambda bank: ND[:, bank, 0:512]))
            qblocks = [(r, n, gidx, i) for gidx, (qbs, _, _) in enumerate(groups) for i, (r, n) in enumerate(qbs)]
            pendl = []
            pass_dv_start = s_dv.v

            def emit_pv(info):
                (qslot, kq, pslot, r, n, gslot, kg, first, last, gidx, col) = info

                def pv(e):
                    ins = None
                    for hh in range(2):
                        for ch in range(2):
                            nk = n - 1 + ch
                            blk = r * nb + nk
                            ins = e.matmul(ND[hh * 64:(hh + 1) * 64, 2 * gslot, col:col + 128],
                                           lhsT=VS[slot][:, blk, hh * 64:(hh + 1) * 64],
                                           rhs=PT[pslot][:, hh, ch * 128:(ch + 1) * 128],
                                           start=(ch == 0), stop=(ch == 1))
                            vt = PV64 if nk < nbB else ONE64
                            ins = e.matmul(ND[hh * 64:(hh + 1) * 64, 2 * gslot + 1, col:col + 128],
                                           lhsT=vt[:, :], rhs=PT[pslot][:, hh, ch * 128:(ch + 1) * 128],
                                           start=(ch == 0), stop=(ch == 1))
                    return ins

                w = [(s_pm[qslot], kq + 1)]
                if first:
                    w.append((s_nf[gslot], kg))
                vpv = P.add("tensor", pv, waits=w, inc=s_pv[pslot])
                if last:
                    _, dst, srcf = groups[gidx]
                    wd = [(s_pv[pslot], vpv)]
                    if g > 0:
                        wd += [(s_dv, pass_dv_start), (s_nf[0], nf_start[0]), (s_nf[1], nf_start[1])]
                    else:
                        wd += [(s_fin, cn["fin"])]
                    if g == 0:
                        P.add("scalar", lambda e: e.copy(out=dst(NACC), in_=srcf(2 * gslot)), waits=[pw] + wd, inc=s_dv)
                        P.add("scalar", lambda e: e.copy(out=dst(DACC), in_=srcf(2 * gslot + 1)), waits=[pw] + wd, inc=s_nf[gslot])
                    else:
                        P.add("vector", lambda e: e.tensor_tensor(out=dst(NACC), in0=srcf(2 * gslot), in1=dst(NACC), op=ALU.add),
                              waits=wd, inc=s_dv)
                        P.add("vector", lambda e: e.tensor_tensor(out=dst(DACC), in0=srcf(2 * gslot + 1), in1=dst(DACC), op=ALU.add),
                              waits=wd, inc=s_nf[gslot])

            nf_start = [s_nf[0].v, s_nf[1].v]
            for bi, (r, n, gidx, gi_) in enumerate(qblocks):
                if bi == len(qblocks) // 2 and pi + 1 < len(passes):
                    sched_tables(pi + 1)
                qslot = cn["qi"] % 2
                kq = cn["qi"] // 2
                pslot = cn["qi"] % 3
                kp = cn["qi"] // 3
                first = (gi_ == 0)
                last = (gi_ == 3)
                if first:
                    gslot = cn["gi"] % 2
                    kg = cn["gi"] // 2
                    cn["gi"] += 1
                t0 = d * 128 * n + r - own_off(n)
                qs = slice(t0, t0 + 127 * d + 1, d)

                def qk(e, r=r, n=n, qslot=qslot, qs=qs):
                    ins = None
                    for hh in range(2):
                        bank = 2 * qslot + hh
                        for ch in range(2):
                            nk = n - 1 + ch
                            ks = slice(d * 128 * nk + r, d * 128 * nk + r + 127 * d + 1, d)
                            ins = e.matmul(S[:, bank, ch * 128:(ch + 1) * 128],
                                           lhsT=KA[slot][hh * 64:(hh + 1) * 64, ks],
                                           rhs=QA[slot][hh * 64:(hh + 1) * 64, qs],
                                           start=True, stop=True)
                    return ins

                P.add("tensor", qk, waits=[pw, (s_ld[slot], ld_val[pi]), (s_c, vconst), (s_ex[qslot], kq)],
                      inc=s_qk[qslot])
                P.add("scalar", lambda e, qslot=qslot: e.activation(
                    out=PTF[qslot][:], in_=S[:, 2 * qslot:2 * qslot + 2, 0:256], func=AF.Exp, scale=0.125),
                    waits=[pw, (s_qk[qslot], kq + 1), (s_pm[qslot], kq)], inc=s_ex[qslot])
                P.add("vector", lambda e, qslot=qslot, pslot=pslot: e.tensor_tensor(
                    out=PT[pslot][:], in0=PTF[qslot][:], in1=EB[slot][:], op=ALU.mult),
                    waits=[pw, (s_ex[qslot], kq + 1), (s_pv[pslot], kp), (s_bt, bt_val[pi])], inc=s_pm[qslot])
                pendl.append((qslot, kq, pslot, r, n, gslot, kg, first, last, gidx, gi_ * 128))
                while len(pendl) > 2:
                    emit_pv(pendl.pop(0))
                cn["qi"] += 1
            while pendl:
                emit_pv(pendl.pop(0))
            pass_last_pe[pi] = [(s_pv[i], s_pv[i].v) for i in range(3)]
            pass_last_dve[pi] = [(s_pm[0], s_pm[0].v), (s_pm[1], s_pm[1].v)]
            if g == 2:
                wd = [(s_dv, s_dv.v), (s_nf[0], s_nf[0].v), (s_nf[1], s_nf[1].v)]
                v1 = P.add("vector", lambda e: e.reciprocal(out=DACC[:], in_=DACC[:]), waits=wd, inc=s_fin)
                v2 = P.add("vector", lambda e: e.tensor_tensor(out=OAS[:], in0=NACC[:], in1=DACC[:], op=ALU.mult),
                           waits=[(s_fin, v1), (s_st, 16 * hp)], inc=s_fin)
                cn["fin"] = v2
                P.add("gpsimd", lambda e, hp=hp: e.dma_start(out=D["OAT"][hp], in_=OAS[:]),
                      waits=[pw, (s_fin, v2)], inc=s_st, dma=True)
        for pi, (hp, g) in enumerate(passes):
            run_pass(pi, hp, g)
        P.end_phase(JK[:])


def phase3(nc, P, D):
    with ExitStack() as es:
        KT = [_sb(es, nc, f"p3_KT{i}", [128, NK], BF16) for i in range(2)]
        QT = [_sb(es, nc, f"p3_QT{i}", [128, NQ], BF16) for i in range(2)]
        VH = [_sb(es, nc, f"p3_VH{i}", [128, 64, 129], BF16) for i in range(2)]
        GB = _sb(es, nc, "p3_GB", [128, 8, 256], F32)
        CB = _sb(es, nc, "p3_CB", [128, 8], F32)
        MB = _sb(es, nc, "p3_MB", [128, 256], F32)
        BTF = _sb(es, nc, "p3_BTF", [128, 256], F32)
        BT = _sb(es, nc, "p3_BT", [128, 8, 2, 256], BF16)
        IDF = _sb(es, nc, "p3_IDF", [128, 128], F32)
        IDB = _sb(es, nc, "p3_IDB", [128, 128], BF16)
        ZB = _sb(es, nc, "p3_ZB", [128, 512], BF16)
        PVF = _sb(es, nc, "p3_PVF", [128, 64], F32)
        LAMI = _sb(es, nc, "p3_LAMI", [128, 4, 64], F32)
        LPR = _sb(es, nc, "p3_LPR", [128, 2, 64], F32)
        LS = _sb(es, nc, "p3_LS", [128, 2], F32)
        LE = _sb(es, nc, "p3_LE", [128, 2], F32)
        NLAM = _sb(es, nc, "p3_NLAM", [128, 1], F32)
        GS = _sb(es, nc, "p3_GS", [128, 128], F32)
        EPSB = _sb(es, nc, "p3_EPSB", [128, 1], F32)
        PT = [_sb(es, nc, f"p3_PT{i}", [128, 2, 512], BF16) for i in range(3)]
        EP = _sb(es, nc, "p3_EP", [128, 3, 512], F32)
        RD = _sb(es, nc, "p3_RD", [128, 8], F32)
        TT = _sb(es, nc, "p3_TT", [128, 128], F32)
        OO = _sb(es, nc, "p3_OO", [128, 4, 128], F32)
        SQ = _sb(es, nc, "p3_SQ", [128, 128], F32)
        SSQ = _sb(es, nc, "p3_SSQ", [128, 4], F32)
        LNV = _sb(es, nc, "p3_LNV", [128, 4], F32)
        RSTD = _sb(es, nc, "p3_RSTD", [128, 4], F32)
        ON = _sb(es, nc, "p3_ON", [128, 4, 128], BF16)
        OBS = [_sb(es, nc, f"p3_OBS{i}", [128, 512], BF16) for i in range(2)]
        JK = _sb(es, nc, "p3_JK", [128, 1], F32)
        CF = [_sb(es, nc, f"p3_CF{i}", [128, 1024], F32) for i in range(2)]
        CBT = [_sb(es, nc, f"p3_CBT{i}", [128, 1024], BF16) for i in range(2)]
        S = _ps(es, nc, "p3_S", [128, 4, 512], F32)
        ACC = _ps(es, nc, "p3_ACC", [128, 3, 512], F32)
        TP = _ps(es, nc, "p3_TP", [128, 4, 128], BF16)
        s_c = P.sem("p3c")
        s_ld = [P.sem(f"p3ld{i}") for i in range(2)]
        s_qk = [P.sem(f"p3qk{i}") for i in range(2)]
        s_ex = [P.sem(f"p3ex{i}") for i in range(2)]
        s_pv = [P.sem(f"p3pv{i}") for i in range(3)]
        s_af = P.sem("p3af")
        s_ep = P.sem("p3ep")
        s_ea = P.sem("p3ea")
        s_tp = P.sem("p3tp")
        s_tc = P.sem("p3tc")
        s_st = [P.sem(f"p3st{i}") for i in range(2)]
        s_cl = [P.sem(f"p3cl{i}") for i in range(2)]
        s_cc = P.sem("p3cc")
        s_cs = [P.sem(f"p3cs{i}") for i in range(2)]
        pw = P.pw()

        chunks = []
        w1v = D["w1"].rearrange("(kc p) f -> p kc f", p=128)
        w2v = D["w2"].rearrange("(fc p) d -> p fc d", p=128)
        k0 = 0
        for (wsrc, nk) in ((D["wpa"], 4), (D["wpb"], 8), (D["wo"], 8)):
            wv = wsrc.rearrange("(kc p) c -> p kc c", p=128)
            for k in range(nk):
                chunks.append((wv[:, k, :], D["WAB"][:, k0 + k, :]))
            k0 += nk
        for kc in range(8):
            for q in range(4):
                chunks.append((w1v[:, kc, q * 1024:(q + 1) * 1024], D["W1B"][:, kc, q * 1024:(q + 1) * 1024]))
        for fc in range(32):
            chunks.append((w2v[:, fc, :], D["W2B"][:, fc, :]))
        cv = {"k": 0, "ld": {}}

        def wconv_step():
            k = cv["k"]
            if k - 1 >= 0 and k - 1 < len(chunks):
                j = k - 1
                sl = j % 2
                src_, dst_ = chunks[j]
                vcast = P.add("vector", lambda e, sl=sl: e.tensor_copy(out=CBT[sl][:], in_=CF[sl][:]),
                              waits=[pw, (s_cl[sl], cv["ld"][j]), (s_cs[sl], 16 * (j // 2))], inc=s_cc)
                P.add("gpsimd", lambda e, sl=sl, dst_=dst_: e.dma_start(out=dst_, in_=CBT[sl][:]), waits=[pw, (s_cc, vcast)],
                      inc=s_cs[sl], dma=True)
            if k < len(chunks):
                sl = k % 2
                src_, dst_ = chunks[k]
                cv["ld"][k] = P.add("sync", lambda e, sl=sl, src_=src_: e.dma_start(out=CF[sl][:], in_=src_), waits=[pw, (s_cc, k - 1)],
                                    inc=s_cl[sl], dma=True)
            cv["k"] += 1

        def accap(m, j):
            idx = m * 4 + j
            return idx // 3, (idx % 3) * 170

        for (dst, src) in ((GB, D["gb"].rearrange("h i c -> i h c")), (CB, D["cb"]), (MB, D["mb"]), (IDF, D["ident"]),
                           (PVF, D["pvalid"]), (LAMI, D["lam"]), (GS, D["gs"])):
            vc = P.add("sync", lambda e, dst=dst, src=src: e.dma_start(out=dst[:], in_=src), waits=[pw], inc=s_c, dma=True)
        w0 = [pw, (s_c, vc)]
        P.add("vector", lambda e: e.tensor_copy(out=IDB[:], in_=IDF[:]), waits=w0, inc=s_c)
        P.add("vector", lambda e: e.memset(ZB[:], 0.0), waits=w0, inc=s_c)
        P.add("vector", lambda e: e.memset(EPSB[:], LN_EPS), waits=w0, inc=s_c)
        v = P.add("vector", lambda e: e.tensor_scalar(out=GS[:], in0=GS[:], scalar1=1.0 - LAM_INIT, scalar2=None, op0=ALU.mult),
                  waits=w0, inc=s_c)
        for i in range(2):
            P.add("vector", lambda e, i=i: e.memset(VH[i][:, 16:64, 128:129], 1.0), waits=w0, inc=s_c)
            P.add("vector", lambda e, i=i: e.tensor_copy(out=VH[i][:, 0:16, 128:129],
                                                         in_=PVF[:, 0:16].rearrange("p (a b) -> p a b", b=1)),
                  waits=w0, inc=s_c)
        v = P.add("vector", lambda e: e.tensor_tensor(out=LPR[:], in0=LAMI[:, 0:2, :], in1=LAMI[:, 2:4, :], op=ALU.mult),
                  waits=w0, inc=s_c)
        v = P.add("vector", lambda e: e.reduce_sum(out=LS[:], in_=LPR[:], axis=AX.X), waits=[(s_c, v)], inc=s_c)
        v = P.add("scalar", lambda e: e.activation(out=LE[:], in_=LS[:], func=AF.Exp), waits=[pw, (s_c, v)], inc=s_c)
        v = P.add("vector", lambda e: e.tensor_tensor(out=NLAM[:], in0=LE[:, 1:2], in1=LE[:, 0:1], op=ALU.subtract),
                  waits=[(s_c, v)], inc=s_c)
        v = P.add("vector", lambda e: e.tensor_scalar(out=NLAM[:], in0=NLAM[:], scalar1=-LAM_INIT, scalar2=None, op0=ALU.add),
                  waits=[(s_c, v)], inc=s_c)
        for h in range(8):
            v = P.add("vector", lambda e, h=h: e.tensor_scalar(out=BTF[:], in0=GB[:, h, :], scalar1=CB[:, h:h + 1], scalar2=8.0,
                                                               op0=ALU.subtract, op1=ALU.mult), waits=[(s_c, v)], inc=s_c)
            v = P.add("vector", lambda e: e.tensor_tensor(out=BTF[:], in0=BTF[:], in1=MB[:], op=ALU.add), waits=[(s_c, v)], inc=s_c)
            v = P.add("vector", lambda e, h=h: e.tensor_copy(out=BT[:, h, 0, :], in_=BTF[:]), waits=[(s_c, v)], inc=s_c)
            v = P.add("vector", lambda e, h=h: e.tensor_tensor(out=BT[:, h, 1, :], in0=BTF[:], in1=BT[:, h, 0, :], op=ALU.subtract),
                      waits=[(s_c, v)], inc=s_c)
        vconst = v

        head_last_pe = {}
        ld_val = {}
        vbv = D["VB"].rearrange("(kb p) c -> p kb c", p=128)

        def sched_loads(h):
            slot = h % 2
            w = [pw, (s_c, vconst)] + (head_last_pe[h - 2] if h >= 2 else [])
            P.add("sync", lambda e: e.dma_start(out=KT[slot][:], in_=D["KTB"][h]), waits=w, inc=s_ld[slot], dma=True)
            P.add("sync", lambda e: e.dma_start(out=QT[slot][:], in_=D["QTB"][h]), waits=w, inc=s_ld[slot], dma=True)
            for q in range(4):
                v = P.add("sync", lambda e, q=q: e.dma_start(out=VH[slot][:, q * 16:(q + 1) * 16, 0:128],
                                                           in_=vbv[:, q * 16:(q + 1) * 16, h * 128:(h + 1) * 128]),
                          waits=w, inc=s_ld[slot], dma=True)
            ld_val[h] = v

        deferred = []
        ui = [0]

        def flush(force=False):
            while deferred and (force or deferred[0][0] <= ui[0]):
                _, fn = deferred.pop(0)
                fn()

        sched_loads(0)
        cn = {"nqt": 0}
        pending = []

        def emit_pv(info):
            (sslot, ku, pslot, kb, jmin, firstu, lastu, hs, kq_tile, on_last) = info

            def pv(e):
                ins = None
                if firstu:
                    for b in range(3):
                        e.matmul(ACC[:, b, :], lhsT=ZB[:, 0:128], rhs=ZB[:, :], start=True, stop=True,
                                 skip_group_check=True)
                for m in range(2):
                    for j in range(jmin, 4):
                        b, off = accap(m, j)
                        ins = e.matmul(ACC[:, b, off:off + 129], lhsT=PT[pslot][:, m, j * 128:(j + 1) * 128],
                                       rhs=VH[hs][:, kb, :], start=False, stop=lastu, skip_group_check=True)
                return ins

            w = [(s_ex[sslot], ku + 1)]
            if firstu:
                w.append((s_af, kq_tile))
            v = P.add("tensor", pv, waits=w, inc=s_pv[pslot])
            if lastu:
                on_last(pslot, v)

        def drain(keep):
            while len(pending) > keep:
                emit_pv(pending.pop(0))

        def run_qt(h, qt):
                hs = h % 2
                base = 4 * (qt + 4 if qt < 4 else qt + 8)
                nkb = base + 4
                kq_tile = cn["nqt"]
                cn["nqt"] += 1
                nqt = cn["nqt"]

                def on_last(last_slot, vlast):
                    epilogue(last_slot, vlast)

                for kb in range(nkb):
                    u = ui[0]
                    slot = u % 2
                    ku = u // 2
                    pslot = u % 3
                    kp = u // 3
                    jd = kb - base
                    c0 = max(jd, 0) * 128
                    jmin = max(jd, 0)

                    def qk(e, kb=kb, jd=jd, c0=c0, slot=slot, qt=qt):
                        ins = None
                        for m in range(2):
                            bank = 2 * slot + m
                            near = jd >= -1
                            if near:
                                if jd == -1:
                                    oc, bc, n = 0, 128, 128
                                elif jd == 3:
                                    oc, bc, n = 384, 0, 128
                                else:
                                    oc, bc, n = jd * 128, 0, 256
                                e.matmul(S[:, bank, oc:oc + n], lhsT=IDB[:], rhs=BT[:, h, 0, bc:bc + n], start=True, stop=False)
                                e.matmul(S[:, bank, oc:oc + n], lhsT=IDB[:], rhs=BT[:, h, 1, bc:bc + n], start=False, stop=False)
                            ins = e.matmul(S[:, bank, c0:512], lhsT=KT[hs][m * 64:(m + 1) * 64, kb * 128:(kb + 1) * 128],
                                           rhs=QT[hs][m * 64:(m + 1) * 64, qt * 512 + c0:(qt + 1) * 512],
                                           start=(not near), stop=True, skip_group_check=True)
                        return ins

                    P.add("tensor", qk, waits=[pw, (s_ld[hs], ld_val[h]), (s_c, vconst), (s_ex[slot], ku)], inc=s_qk[slot])
                    P.add("scalar", lambda e, slot=slot, c0=c0, pslot=pslot: e.activation(
                        out=PT[pslot][:, :, c0:512], in_=S[:, 2 * slot:2 * slot + 2, c0:512], func=AF.Exp, scale=0.125),
                        waits=[pw, (s_qk[slot], ku + 1), (s_pv[pslot], kp)], inc=s_ex[slot])
                    pending.append((slot, ku, pslot, kb, jmin, kb == 0, kb == nkb - 1, hs, kq_tile, on_last))
                    drain(2)
                    ui[0] += 1
                    if ui[0] % 24 == 0:
                        wconv_step()
                    flush()

                def epilogue(last_slot, vlast):

                    def stage_a(last_slot=last_slot, vlast=vlast, h=h, qt=qt, k=nqt):
                        v = P.add("vector", lambda e: e.tensor_copy(out=EP[:], in_=ACC[:]),
                                  waits=[(s_pv[last_slot], vlast), (s_ep, s_ep.v)], inc=s_af)
                        we = [(s_af, v)]
                        for m in range(2):
                            for j in range(4):
                                b, off = accap(m, j)
                                idx = m * 4 + j
                                P.add("vector", lambda e, b=b, off=off, idx=idx: e.reciprocal(
                                    out=RD[:, idx:idx + 1], in_=EP[:, b, off + 128:off + 129]), waits=we, inc=s_ep)
                        v = P.add("vector", lambda e: e.tensor_scalar(out=RD[:, 4:8], in0=RD[:, 4:8], scalar1=NLAM[:, 0:1], scalar2=None,
                                                                      op0=ALU.mult), waits=[(s_ep, s_ep.v)], inc=s_ep)
                        for j in range(4):
                            b1, o1 = accap(0, j)
                            b2, o2 = accap(1, j)
                            v = P.add("vector", lambda e, b2=b2, o2=o2, j=j: e.tensor_scalar(
                                out=TT[:], in0=EP[:, b2, o2:o2 + 128], scalar1=RD[:, 4 + j:5 + j], scalar2=None, op0=ALU.mult),
                                waits=[(s_ep, v)], inc=s_ep)
                            v = P.add("vector", lambda e, b1=b1, o1=o1, j=j: e.scalar_tensor_tensor(
                                out=OO[:, j, :], in0=EP[:, b1, o1:o1 + 128], scalar=RD[:, j:j + 1], in1=TT[:],
                                op0=ALU.mult, op1=ALU.add), waits=[(s_ep, v)], inc=s_ep)
                            v = P.add("vector", lambda e, j=j: e.scalar_tensor_tensor(
                                out=SQ[:], in0=OO[:, j, :], scalar=1.0, in1=OO[:, j, :], op0=ALU.mult, op1=ALU.mult,
                                accum_out=SSQ[:, j:j + 1]), waits=[(s_ep, v)], inc=s_ep)
                        return v

                    va = stage_a()

                    def stage_bc(va=va):
                        v = P.add("scalar", lambda e: e.activation(out=LNV[:], in_=SSQ[:], func=AF.Ln, bias=EPSB[:, 0:1], scale=1.0 / 128.0),
                                  waits=[(s_ep, va), (s_ea, s_ea.v)], inc=s_ea)
                        v = P.add("scalar", lambda e: e.activation(out=RSTD[:], in_=LNV[:], func=AF.Exp, scale=-0.5),
                                  waits=[(s_ea, v)], inc=s_ea)
                        vv = None
                        for j in range(4):
                            vv = P.add("vector", lambda e, j=j: e.scalar_tensor_tensor(
                                out=ON[:, j, :], in0=OO[:, j, :], scalar=RSTD[:, j:j + 1], in1=GS[:], op0=ALU.mult, op1=ALU.mult),
                                waits=[(s_ea, v), (s_tp, s_tp.v)], inc=s_ep)
                        return vv

                    def stage_de(vc, h=h, qt=qt, k=nqt):
                        def tp(e):
                            ins = None
                            for j in range(4):
                                ins = e.transpose(out=TP[:, j, :], in_=ON[:, j, :], identity=IDB[:])
                            return ins
                        v = P.add("tensor", tp, waits=[(s_ep, vc), (s_tc, s_tc.v)], inc=s_tp)
                        os_ = (k - 1) % 2
                        v2 = P.add("vector", lambda e: e.tensor_copy(out=OBS[os_][:].rearrange("p (a b) -> p a b", a=4), in_=TP[:]),
                                   waits=[(s_tp, v), (s_st[os_], 16 * ((k - 1) // 2))], inc=s_tc)
                        P.add("gpsimd", lambda e: e.dma_start(out=D["OBT"][h][:, qt * 512:(qt + 1) * 512], in_=OBS[os_][:]),
                              waits=[pw, (s_tc, v2)], inc=s_st[os_], dma=True)

                    def chain(stage_bc=stage_bc, stage_de=stage_de):
                        vc = stage_bc()
                        deferred.append((ui[0] + 3, lambda: stage_de(vc)))

                    deferred.append((ui[0] + 3, chain))
        for h in range(8):
            if h + 1 < 8:
                sched_loads(h + 1)
            for qt in range(8):
                run_qt(h, qt)
            drain(0)
            head_last_pe[h] = [(s_pv[i], s_pv[i].v) for i in range(3)]
        flush(force=True)
        flush(force=True)
        while cv["k"] <= len(chunks):
            wconv_step()
        P.end_phase(JK[:])


def layer_norm_ops(P, V, ST6, MV, STD, RSTD, TMP, OUT, LG, LB, EPSB, sem, wait_in):
    v = None
    for hf in range(2):
        v = P.add("vector", lambda e, hf=hf: e.bn_stats(out=ST6[:, hf * 6:(hf + 1) * 6], in_=V[:, hf * 512:(hf + 1) * 512]),
                  waits=wait_in + [(sem, sem.v)], inc=sem)
    v = P.add("vector", lambda e: e.bn_aggr(out=MV[:], in_=ST6[:]), waits=[(sem, v)], inc=sem)
    v = P.add("scalar", lambda e: e.activation(out=STD[:], in_=MV[:, 1:2], func=AF.Sqrt, bias=EPSB[:, 0:1], scale=1.0),
              waits=[(sem, v)], inc=sem)
    v = P.add("vector", lambda e: e.reciprocal(out=RSTD[:], in_=STD[:]), waits=[(sem, v)], inc=sem)
    v = P.add("vector", lambda e: e.scalar_tensor_tensor(out=TMP[:], in0=V[:], scalar=MV[:, 0:1], in1=LG[:],
                                                         op0=ALU.subtract, op1=ALU.mult), waits=[(sem, v)], inc=sem)
    v = P.add("vector", lambda e: e.scalar_tensor_tensor(out=OUT, in0=TMP[:], scalar=RSTD[:, 0:1], in1=LB[:],
                                                         op0=ALU.mult, op1=ALU.add), waits=[(sem, v)], inc=sem)
    return v


def load_cast_weight(P, pw, dst_chunks, src_chunks, WF, s_wld, s_wcv, cnt):
    for dst, src in zip(dst_chunks, src_chunks):
        n = cnt[0]
        slot = n % 2
        v = P.add("sync", lambda e, slot=slot, src=src: e.dma_start(out=WF[slot][:], in_=src),
                  waits=[pw, (s_wcv, n - 1)], inc=s_wld[slot], dma=True)
        P.add("vector", lambda e, slot=slot, dst=dst: e.tensor_copy(out=dst, in_=WF[slot][:]),
              waits=[pw, (s_wld[slot], v)], inc=s_wcv)
        cnt[0] += 1
    return s_wcv.v


def phase4a(nc, P, D):
    with ExitStack() as es:
        WPA = _sb(es, nc, "p4_WPA", [128, 4, 1024], BF16)
        WPB = _sb(es, nc, "p4_WPB", [128, 8, 1024], BF16)
        WO = _sb(es, nc, "p4_WO", [128, 8, 1024], BF16)
        OAt = [_sb(es, nc, f"p4_OA{i}", [128, 4, 512], BF16) for i in range(2)]
        OBt = [_sb(es, nc, f"p4_OB{i}", [128, 8, 512], BF16) for i in range(2)]
        SGt = _sb(es, nc, "p4_SG", [128, 16, 512], BF16)
        XS = _sb(es, nc, "p4_XS", [128, 4, 1024], F32)
        MT = _sb(es, nc, "p4_MT", [128, 8, 512], BF16)
        T1 = [_sb(es, nc, f"p4_T1{i}", [128, 512], F32) for i in range(2)]
        T2 = [_sb(es, nc, f"p4_T2{i}", [128, 512], F32) for i in range(2)]
        V = _sb(es, nc, "p4_V", [128, 1024], F32)
        TMP = _sb(es, nc, "p4_TMP", [128, 1024], F32)
        X1O = [_sb(es, nc, f"p4_X1O{i}", [128, 1024], F32) for i in range(2)]
        X1B = [_sb(es, nc, f"p4_X1B{i}", [128, 1024], BF16) for i in range(2)]
        X1TS = _sb(es, nc, "p4_X1TS", [128, 8, 512], BF16)
        LG = _sb(es, nc, "p4_LG", [128, 1024], F32)
        LB = _sb(es, nc, "p4_LB", [128, 1024], F32)
        ST6 = _sb(es, nc, "p4_ST6", [128, 12], F32)
        MV = _sb(es, nc, "p4_MV", [128, 2], F32)
        STD = _sb(es, nc, "p4_STD", [128, 1], F32)
        RSTD = _sb(es, nc, "p4_RSTD", [128, 1], F32)
        EPSB = _sb(es, nc, "p4_EPSB", [128, 1], F32)
        IDF = _sb(es, nc, "p4_IDF", [128, 128], F32)
        IDB = _sb(es, nc, "p4_IDB", [128, 128], BF16)
        JK = _sb(es, nc, "p4_JK", [128, 1], F32)
        PSY = _ps(es, nc, "p4_PSY", [128, 4, 512], F32)
        PSM = _ps(es, nc, "p4_PSM", [128, 2, 512], F32)
        PST = _ps(es, nc, "p4_PST", [128, 8, 128], BF16)
        s_c = P.sem("p4c")
        s_wld = [P.sem(f"p4wld{i}") for i in range(2)]
        s_wcv = P.sem("p4wcv")
        s_ldab = [P.sem(f"p4ldab{i}") for i in range(2)]
        s_ldsg = P.sem("p4ldsg")
        s_ldx = P.sem("p4ldx")
        s_y = [P.sem(f"p4y{i}") for i in range(2)]
        s_t = [P.sem(f"p4t{i}") for i in range(2)]
        s_mt = P.sem("p4mt")
        s_mm = [P.sem(f"p4mm{i}") for i in range(2)]
        s_v = [P.sem(f"p4v{i}") for i in range(2)]
        s_ln = P.sem("p4ln")
        s_xb = P.sem("p4xb")
        s_tp = P.sem("p4tp")
        s_tc = P.sem("p4tc")
        s_so = [P.sem(f"p4so{i}") for i in range(2)]
        s_sx = P.sem("p4sx")
        pw = P.pw()

        for (dst, src) in ((LG, D["ln1g"]), (LB, D["ln1b"]), (IDF, D["ident"])):
            vc = P.add("sync", lambda e, dst=dst, src=src: e.dma_start(out=dst[:], in_=src), waits=[pw], inc=s_c, dma=True)
        P.add("vector", lambda e: e.tensor_copy(out=IDB[:], in_=IDF[:]), waits=[pw, (s_c, vc)], inc=s_c)
        vconst = P.add("vector", lambda e: e.memset(EPSB[:], LN_EPS), waits=[pw], inc=s_c)

        P.add("sync", lambda e: e.dma_start(out=WPA[:], in_=D["WAB"][:, 0:4, :]), waits=[pw], inc=s_wcv, dma=True)
        P.add("sync", lambda e: e.dma_start(out=WPB[:], in_=D["WAB"][:, 4:12, :]), waits=[pw], inc=s_wcv, dma=True)
        vw = P.add("sync", lambda e: e.dma_start(out=WO[:], in_=D["WAB"][:, 12:20, :]), waits=[pw], inc=s_wcv, dma=True)

        oat = D["OAT"].rearrange("k p t -> p k t")
        obt = D["OBT"].rearrange("k p t -> p k t")
        sgt = D["SG"].rearrange("k p t -> p k t")
        xo = D["xo"].rearrange("(s p) d -> p s d", p=128)
        x1 = D["X1"].rearrange("(s p) d -> p s d", p=128)
        x1t = D["X1T"].rearrange("k p t -> p k t")

        last_y_pe = {}
        ld_ab = {}

        def load_ab(t):
            sl = t % 2
            w = [pw] + (last_y_pe[t - 2] if t >= 2 else [])
            P.add("sync", lambda e: e.dma_start(out=OAt[sl][:], in_=oat[:, :, t * 512:(t + 1) * 512]), waits=w, inc=s_ldab[sl], dma=True)
            ld_ab[t] = P.add("sync", lambda e: e.dma_start(out=OBt[sl][:], in_=obt[:, :, t * 512:(t + 1) * 512]), waits=w,
                             inc=s_ldab[sl], dma=True)

        load_ab(0)
        yj = 0
        mj = 0
        sbk = 0
        pending_tp = []

        def flush_tp():
            while pending_tp:
                pending_tp.pop(0)()

        for t in range(8):
            sl = t % 2
            if t + 1 < 8:
                load_ab(t + 1)
            v_sg = P.add("sync", lambda e, t=t: e.dma_start(out=SGt[:], in_=sgt[:, :, t * 512:(t + 1) * 512]),
                         waits=[pw, (s_t[0], s_t[0].v), (s_t[1], s_t[1].v)], inc=s_ldsg, dma=True)
            v_x = P.add("sync", lambda e, t=t: e.dma_start(out=XS[:], in_=xo[:, 4 * t:4 * t + 4, :]),
                        waits=[pw, (s_v[0], s_v[0].v), (s_v[1], s_v[1].v)], inc=s_ldx, dma=True)
            mt_w0 = (s_mm[0], s_mm[0].v), (s_mm[1], s_mm[1].v)
            for dc in range(8):
                ys = yj % 2
                ky = yj // 2

                def ymm(e, ys=ys, dc=dc, sl=sl):
                    ins = None
                    for k in range(4):
                        ins = e.matmul(PSY[:, 2 * ys, :], lhsT=WPA[:, k, dc * 128:(dc + 1) * 128], rhs=OAt[sl][:, k, :],
                                       start=(k == 0), stop=(k == 3))
                    for k in range(8):
                        ins = e.matmul(PSY[:, 2 * ys + 1, :], lhsT=WPB[:, k, dc * 128:(dc + 1) * 128], rhs=OBt[sl][:, k, :],
                                       start=(k == 0), stop=(k == 7))
                    return ins

                P.add("tensor", ymm, waits=[pw, (s_wcv, vw), (s_ldab[sl], ld_ab[t]), (s_t[ys], 2 * ky)], inc=s_y[ys])
                wt = [pw, (s_y[ys], ky + 1), (s_ldsg, v_sg), (s_mt, s_mt.v - 1)]
                P.add("vector", lambda e, ys=ys, dc=dc: e.tensor_tensor(out=T1[ys][:], in0=PSY[:, 2 * ys, :], in1=SGt[:, dc, :], op=ALU.mult),
                      waits=wt, inc=s_t[ys])
                vt = P.add("vector", lambda e, ys=ys, dc=dc: e.tensor_tensor(out=T2[ys][:], in0=PSY[:, 2 * ys + 1, :], in1=SGt[:, 8 + dc, :],
                                                                         op=ALU.mult), waits=wt, inc=s_t[ys])
                P.add("vector", lambda e, ys=ys, dc=dc: e.tensor_tensor(out=MT[:, dc, :], in0=T1[ys][:], in1=T2[ys][:], op=ALU.add),
                      waits=[pw, (s_t[ys], vt)] + list(mt_w0), inc=s_mt)
                yj += 1
                if dc == 1:
                    flush_tp()
            last_y_pe[t] = [(s_y[0], s_y[0].v), (s_y[1], s_y[1].v)]
            v_mt = s_mt.v
            for s in range(4):
                xs_o = sbk % 2
                for hf in range(2):
                    ms = mj % 2
                    km = mj // 2

                    def mmm(e, ms=ms, s=s, hf=hf):
                        ins = None
                        for k in range(8):
                            ins = e.matmul(PSM[:, ms, :], lhsT=MT[:, k, s * 128:(s + 1) * 128], rhs=WO[:, k, hf * 512:(hf + 1) * 512],
                                           start=(k == 0), stop=(k == 7))
                        return ins

                    P.add("tensor", mmm, waits=[pw, (s_mt, v_mt), (s_v[ms], km)], inc=s_mm[ms])
                    vv = P.add("vector", lambda e, ms=ms, s=s, hf=hf: e.scalar_tensor_tensor(
                        out=V[:, hf * 512:(hf + 1) * 512], in0=XS[:, s, hf * 512:(hf + 1) * 512], scalar=ALPHA, in1=PSM[:, ms, :],
                        op0=ALU.mult, op1=ALU.add), waits=[pw, (s_mm[ms], km + 1), (s_ldx, v_x), (s_ln, s_ln.v)], inc=s_v[ms])
                    mj += 1
                flush_tp()
                win = [(s_v[0], s_v[0].v), (s_v[1], s_v[1].v), (s_c, vconst), (s_so[xs_o], 16 * (sbk // 2)), (s_xb, sbk - 1)]
                vln = layer_norm_ops(P, V, ST6, MV, STD, RSTD, TMP, X1O[xs_o][:], LG, LB, EPSB, s_ln, win)
                row = 4 * t + s
                P.add("sync", lambda e, xs_o=xs_o, row=row: e.dma_start(out=x1[:, row, :], in_=X1O[xs_o][:]),
                      waits=[pw, (s_ln, vln)], inc=s_so[xs_o], dma=True)
                vb = P.add("scalar", lambda e, xs_o=xs_o: e.copy(out=X1B[xs_o][:], in_=X1O[xs_o][:]),
                           waits=[pw, (s_ln, vln), (s_tp, sbk - 1)], inc=s_xb)

                def tp_group(t=t, s=s, xs_o=xs_o, vb=vb):
                    def tp(e):
                        ins = None
                        for k in range(8):
                            ins = e.transpose(out=PST[:, k, :], in_=X1B[xs_o][:, k * 128:(k + 1) * 128], identity=IDB[:])
                        return ins

                    vtp = P.add("tensor", tp, waits=[pw, (s_xb, vb), (s_tc, s_tc.v)], inc=s_tp)
                    P.add("scalar", lambda e: e.copy(out=X1TS[:, :, s * 128:(s + 1) * 128], in_=PST[:]),
                          waits=[pw, (s_tp, vtp), (s_sx, 16 * t)], inc=s_tc)
                    if s == 3:
                        P.add("gpsimd", lambda e: e.dma_start(out=x1t[:, :, t * 512:(t + 1) * 512], in_=X1TS[:]),
                              waits=[pw, (s_tc, s_tc.v)], inc=s_sx, dma=True)

                pending_tp.append(tp_group)
                sbk += 1
        flush_tp()
        P.end_phase(JK[:])


def phase4b(nc, P, D):
    TT = 256
    NT = NQ // TT
    with ExitStack() as es:
        W1 = _sb(es, nc, "p5_W1", [128, 8, 4096], BF16)
        W2 = _sb(es, nc, "p5_W2", [128, 32, 1024], BF16)
        XT = [_sb(es, nc, f"p5_XT{i}", [128, 8, TT], BF16) for i in range(2)]
        X1 = _sb(es, nc, "p5_X1", [128, 2, 1024], F32)
        HT = _sb(es, nc, "p5_HT", [128, 32, TT], BF16)
        RT = [_sb(es, nc, f"p5_RT{i}", [128, TT], F32) for i in range(2)]
        V = _sb(es, nc, "p5_V", [128, 1024], F32)
        TMP = _sb(es, nc, "p5_TMP", [128, 1024], F32)
        YO = [_sb(es, nc, f"p5_YO{i}", [128, 1024], F32) for i in range(2)]
        LG = _sb(es, nc, "p5_LG", [128, 1024], F32)
        LB = _sb(es, nc, "p5_LB", [128, 1024], F32)
        ST6 = _sb(es, nc, "p5_ST6", [128, 12], F32)
        MV = _sb(es, nc, "p5_MV", [128, 2], F32)
        STD = _sb(es, nc, "p5_STD", [128, 1], F32)
        RSTD = _sb(es, nc, "p5_RSTD", [128, 1], F32)
        EPSB = _sb(es, nc, "p5_EPSB", [128, 1], F32)
        JK = _sb(es, nc, "p5_JK", [128, 1], F32)
        PSH = _ps(es, nc, "p5_PSH", [128, 4, 512], F32)
        PSF = _ps(es, nc, "p5_PSF", [128, 2, 512], F32)
        s_c = P.sem("p5c")
        s_wld = [P.sem(f"p5wld{i}") for i in range(2)]
        s_wcv = P.sem("p5wcv")
        s_ldt = [P.sem(f"p5ldt{i}") for i in range(2)]
        s_ldx = P.sem("p5ldx")
        s_h = [P.sem(f"p5h{i}") for i in range(4)]
        s_r = [P.sem(f"p5r{i}") for i in range(4)]
        s_sq = [P.sem(f"p5sq{i}") for i in range(2)]
        s_f = [P.sem(f"p5f{i}") for i in range(2)]
        s_v = [P.sem(f"p5v{i}") for i in range(2)]
        s_ln = P.sem("p5ln")
        s_so = [P.sem(f"p5so{i}") for i in range(2)]
        pw = P.pw()

        for (dst, src) in ((LG, D["ln2g"]), (LB, D["ln2b"])):
            vc = P.add("sync", lambda e, dst=dst, src=src: e.dma_start(out=dst[:], in_=src), waits=[pw], inc=s_c, dma=True)
        vconst = P.add("vector", lambda e: e.memset(EPSB[:], LN_EPS), waits=[pw, (s_c, vc)], inc=s_c)

        vw1 = None
        for q in range(4):
            vw1 = P.add("sync", lambda e, q=q: e.dma_start(out=W1[:, 2 * q:2 * q + 2, :], in_=D["W1B"][:, 2 * q:2 * q + 2, :]),
                        waits=[pw], inc=s_wld[0], dma=True)
        vw2 = None
        for q in range(4):
            vw2 = P.add("sync", lambda e, q=q: e.dma_start(out=W2[:, 8 * q:8 * q + 8, :], in_=D["W2B"][:, 8 * q:8 * q + 8, :]),
                        waits=[pw], inc=s_wld[1], dma=True)

        x1t = D["X1T"].rearrange("k p t -> p k t")
        x1 = D["X1"].rearrange("(s p) d -> p s d", p=128)
        yv = D["y"].rearrange("(s p) d -> p s d", p=128)
        last_h_pe = {}
        ld_t = {}

        def load_t(t):
            sl = t % 2
            w = [pw] + (last_h_pe[t - 2] if t >= 2 else [])
            ld_t[t] = P.add("sync", lambda e: e.dma_start(out=XT[sl][:], in_=x1t[:, :, t * TT:(t + 1) * TT]), waits=w,
                            inc=s_ldt[sl], dma=True)

        load_t(0)
        hj = 0
        fj = 0
        sbk = 0
        for t in range(NT):
            sl = t % 2
            if t + 1 < NT:
                load_t(t + 1)
            v_x = P.add("sync", lambda e, t=t: e.dma_start(out=X1[:], in_=x1[:, 2 * t:2 * t + 2, :]),
                        waits=[pw, (s_v[0], s_v[0].v), (s_v[1], s_v[1].v)], inc=s_ldx, dma=True)
            ht_w0 = [(s_f[0], s_f[0].v), (s_f[1], s_f[1].v)]
            for fc in range(32):
                hs = hj % 4
                kh = hj // 4
                rs = hj % 2
                kr = hj // 2

                def hmm(e, hs=hs, fc=fc, sl=sl):
                    ins = None
                    for k in range(8):
                        ins = e.matmul(PSH[:, hs, 0:TT], lhsT=W1[:, k, fc * 128:(fc + 1) * 128], rhs=XT[sl][:, k, :],
                                       start=(k == 0), stop=(k == 7))
                    return ins

                P.add("tensor", hmm, waits=[pw, (s_wld[0], vw1), (s_ldt[sl], ld_t[t]), (s_r[hs], kh)], inc=s_h[hs])
                P.add("scalar", lambda e, hs=hs, rs=rs: e.activation(out=RT[rs][:], in_=PSH[:, hs, 0:TT], func=AF.Relu),
                      waits=[pw, (s_h[hs], kh + 1), (s_sq[rs], kr)], inc=s_r[hs])
                P.add("vector",
                      lambda e, rs=rs, fc=fc: e.tensor_tensor(out=HT[:, fc, :], in0=RT[rs][:], in1=RT[rs][:], op=ALU.mult),
                      waits=[pw, (s_r[hs], kh + 1)] + ht_w0, inc=s_sq[rs])
                hj += 1
            last_h_pe[t] = [(s_h[i], s_h[i].v) for i in range(4)]
            v_sq = [(s_sq[0], s_sq[0].v), (s_sq[1], s_sq[1].v)]
            for s in range(2):
                yo = sbk % 2
                for hf in range(2):
                    fs = fj % 2
                    kf = fj // 2

                    def fmm(e, fs=fs, s=s, hf=hf):
                        ins = None
                        for fc in range(32):
                            ins = e.matmul(PSF[:, fs, :], lhsT=HT[:, fc, s * 128:(s + 1) * 128], rhs=W2[:, fc, hf * 512:(hf + 1) * 512],
                                           start=(fc == 0), stop=(fc == 31))
                        return ins

                    P.add("tensor", fmm, waits=[pw, (s_wld[1], vw2), (s_v[fs], kf)] + v_sq, inc=s_f[fs])
                    P.add("vector", lambda e, fs=fs, s=s, hf=hf: e.scalar_tensor_tensor(
                        out=V[:, hf * 512:(hf + 1) * 512], in0=X1[:, s, hf * 512:(hf + 1) * 512], scalar=ALPHA, in1=PSF[:, fs, :],
                        op0=ALU.mult, op1=ALU.add), waits=[pw, (s_f[fs], kf + 1), (s_ldx, v_x), (s_ln, s_ln.v)], inc=s_v[fs])
                    fj += 1
                win = [(s_v[0], s_v[0].v), (s_v[1], s_v[1].v), (s_c, vconst), (s_so[yo], 16 * (sbk // 2))]
                vln = layer_norm_ops(P, V, ST6, MV, STD, RSTD, TMP, YO[yo][:], LG, LB, EPSB, s_ln, win)
                row = 2 * t + s
                P.add("sync", lambda e, yo=yo, row=row: e.dma_start(out=yv[:, row, :], in_=YO[yo][:]),
                      waits=[pw, (s_ln, vln)], inc=s_so[yo], dma=True)
                sbk += 1
        P.end_phase(JK[:])


def build_nc(debug=False, phases=(1, 2, 3, 4, 5)):
    nc = bass.Bass("TRN2", target_bir_lowering=False)
    D = {}

    def din(name, shape):
        D[name] = nc.dram_tensor(name, shape, F32, kind="ExternalInput").ap()

    din("xT", [1024, NK]); din("xo", [NQ, 1024]); din("wp", [1024, COLS_IN]); din("bg", [128, 16])
    din("ga", [24, 128, 256]); din("ma", [128, 256]); din("gb", [8, 128, 256]); din("cb", [128, 8]); din("mb", [128, 256])
    din("ident", [128, 128]); din("pvalid", [128, 64]); din("lam", [128, 4, 64]); din("gs", [128, 128])
    din("ln1g", [128, 1024]); din("ln1b", [128, 1024]); din("ln2g", [128, 1024]); din("ln2b", [128, 1024])
    din("wpa", [512, 1024]); din("wpb", [1024, 1024]); din("wo", [1024, 1024]); din("w1", [1024, 4096]); din("w2", [4096, 1024])
    D["y"] = nc.dram_tensor("y", [NQ, 1024], F32, kind="ExternalOutput").ap()
    kind = "ExternalOutput" if debug else "Internal"

    def scr(name, shape, dt):
        D[name] = nc.dram_tensor(name, shape, dt, kind=kind).ap()

    scr("QTB", [8, 128, NQ], BF16); scr("KTB", [8, 128, NK], BF16); scr("VB", [NK, 1024], BF16)
    scr("QTA", [12, 128, NQ], BF16); scr("KTA", [12, 128, NKA], BF16); scr("VA", [NKA, 1536], BF16)
    scr("SG", [16, 128, NQ], BF16); scr("OAT", [4, 128, NQ], BF16); scr("OBT", [8, 128, NQ], BF16)
    scr("X1", [NQ, 1024], F32); scr("X1T", [8, 128, NQ], BF16)
    scr("WAB", [128, 20, 1024], BF16); scr("W1B", [128, 8, 4096], BF16); scr("W2B", [128, 32, 1024], BF16)
    with ExitStack() as es:
        P = Prog(nc, es)
        if 1 in phases:
            phase1(nc, P, D)
        if 2 in phases:
            phase2(nc, P, D)
        if 3 in phases:
            phase3(nc, P, D)
        if 4 in phases:
            phase4a(nc, P, D)
        if 5 in phases:
            phase4b(nc, P, D)
    return nc


def t5_bucket_np(dist):
    n = np.maximum(dist, 0)
    nf = np.maximum(n, 1).astype(np.float32)
    large = 16 + (np.log(nf / np.float32(16.0)) / np.float32(math.log(8.0)) * np.float32(16.0)).astype(np.int32)
    large = np.minimum(large, 31)
    return np.where(n < 16, n, large)


def prep_shared(inp):
    f = lambda a: np.ascontiguousarray(np.asarray(a, dtype=np.float32))
    w = np.asarray(inp["w_in"], dtype=np.float32)[0]
    B0 = 4608
    cols = []
    for h in range(8):
        cols += list(range(B0 + h * 64, B0 + h * 64 + 64)) + list(range(B0 + 512 + h * 64, B0 + 512 + h * 64 + 64))
    for h in range(8):
        cols += list(range(B0 + 1024 + h * 64, B0 + 1024 + h * 64 + 64)) + list(range(B0 + 1536 + h * 64, B0 + 1536 + h * 64 + 64))
    cols += list(range(0, 1536)) + list(range(1536, 3072)) + list(range(7680, 9728))
    cols += list(range(B0 + 2048, B0 + 3072)) + list(range(3072, 4608))
    assert len(cols) == COLS_IN
    sh = {"wp": f(w[:, cols])}
    sh["bg"] = f(np.asarray(inp["b_gate"], np.float32)[0].reshape(16, 128).T)
    rb = np.asarray(inp["rel_bias"], np.float32)
    i = np.arange(128)[:, None]
    c = np.arange(256)[None, :]
    ch = c // 128
    j = c % 128
    steps = (1 - ch) * 128 + j - i
    ga = np.zeros((24, 128, 256), np.float32)
    for g, d in enumerate(DILS):
        bk = t5_bucket_np(np.maximum(steps, 0) * d)
        for h in range(8):
            ga[g * 8 + h] = rb[bk, g * 8 + h]
    sh["ga"] = ga
    sh["ma"] = f(np.where((steps >= 0) & (steps <= 128), 0.0, MASKV / 8.0))
    dist = c - i
    bk = t5_bucket_np(np.maximum(dist, 0))
    gb = np.zeros((8, 128, 256), np.float32)
    for h in range(8):
        gb[h] = rb[bk, 24 + h]
    sh["gb"] = gb
    sh["cb"] = f(np.broadcast_to(rb[31, 24:32][None, :], (128, 8)))
    sh["mb"] = f(np.where(dist >= 0, 0.0, MASKV))
    sh["ident"] = np.eye(128, dtype=np.float32)
    lam = np.stack([np.asarray(inp[k], np.float32)[0] for k in ("lambda_q1", "lambda_q2", "lambda_k1", "lambda_k2")])
    sh["lam"] = f(np.broadcast_to(lam[None], (128, 4, 64)))
    sh["gs"] = f(np.broadcast_to(np.asarray(inp["subln_g"], np.float32)[0][None, :], (128, 128)))
    for k, n in (("ln1g", "ln1_g"), ("ln1b", "ln1_b"), ("ln2g", "ln2_g"), ("ln2b", "ln2_b")):
        sh[k] = f(np.broadcast_to(np.asarray(inp[n], np.float32)[0][None, :], (128, 1024)))
    sh["wpa"] = f(np.asarray(inp["w_proj_a"])[0]); sh["wpb"] = f(np.asarray(inp["w_proj_b"])[0])
    sh["wo"] = f(np.asarray(inp["w_out"])[0]); sh["w1"] = f(np.asarray(inp["w_mlp1"])[0]); sh["w2"] = f(np.asarray(inp["w_mlp2"])[0])
    return sh


def prep_core(x, c):
    b, par = c // 2, c % 2
    xb = np.asarray(x[b], dtype=np.float32)
    xl = np.zeros((NK, 1024), np.float32)
    if par == 1:
        xl[:] = xb
    else:
        xl[2048:] = xb[0:NK - 2048]
    own = np.concatenate([xl[2048:4096], xl[6144:8192]], axis=0)
    pv = np.full((128, 64), float(par), np.float32)
    return {"xT": np.ascontiguousarray(xl.T), "xo": np.ascontiguousarray(own), "pvalid": pv}


_NC_CACHE = {}


def kernel(**inputs):
    x = np.asarray(inputs["x"], dtype=np.float32)
    sh = prep_shared(inputs)
    in_maps = []
    for c in range(8):
        m = dict(sh)
        m.update(prep_core(x, c))
        in_maps.append(m)
    if "nc" not in _NC_CACHE:
        _NC_CACHE["nc"] = build_nc()
    res = run_bass_kernel_spmd(_NC_CACHE["nc"], in_maps, core_ids=list(range(8)))
    out = np.zeros((BATCH, SEQ, D_MODEL), np.float32)
    for c in range(8):
        b, par = c // 2, c % 2
        y = np.asarray(res.results[c]["y"], dtype=np.float32)
        r0 = 2048 if par == 1 else 0
        out[b, r0:r0 + 2048] = y[0:2048]
        out[b, r0 + 4096:r0 + 6144] = y[2048:4096]
    return out
```

```python
import math
from contextlib import ExitStack

import numpy as np

import concourse.bass as bass
import concourse.mybir as mybir
from concourse.bass_utils import run_bass_kernel_spmd

F32 = mybir.dt.float32
BF16 = mybir.dt.bfloat16
AF = mybir.ActivationFunctionType
ALU = mybir.AluOpType
AX = mybir.AxisListType

D_MODEL = 1024
SEQ = 8192
BATCH = 4
NQ = 4096
NK = 8192
NKA = 8192
COLS_IN = 9728
DILS = (1, 4, 16)
ALPHA = 2.0 ** 0.25
LAM_INIT = 0.8 - 0.6 * math.exp(0.0)
LN_EPS = 1e-5
MASKV = -240000.0
ENGS = ("sync", "scalar", "vector", "gpsimd", "tensor")

C_FQ, C_FK, C_AQ, C_AK, C_GT, C_BV, C_AV = 0, 1024, 2048, 3584, 5120, 7168, 8192


class Sem:
    __slots__ = ("h", "abs", "base", "name")

    def __init__(self, h, name):
        self.h, self.abs, self.base, self.name = h, 0, 0, name

    @property
    def v(self):
        return self.abs - self.base


class Prog:
    def __init__(self, nc, es):
        self.nc, self.es = nc, es
        self.q = {k: [] for k in ENGS}
        self.waited = {k: {} for k in ENGS}
        self.sems = []
        self.pool = []
        self.used = 0
        self.nphase = 0
        self.bar = self.sem("bar")

    def sem(self, name):
        if name != "bar" and self.used < len(self.pool):
            s = self.pool[self.used]
            self.used += 1
            s.base = s.abs
            return s
        s = Sem(self.es.enter_context(self.nc.semaphore(f"s{len(self.sems)}")), f"s{len(self.sems)}")
        self.sems.append(s)
        if name != "bar":
            self.pool.append(s)
            self.used += 1
        return s

    def add(self, eng, fn, waits=(), inc=None, dma=False):
        ws = []
        for w in waits:
            if w is None:
                continue
            s, v = w
            if v <= 0:
                continue
            assert v <= s.v, f"deadlock: {eng} waits {s.name}>={v} but only {s.v} scheduled"
            va = s.base + v
            if self.waited[eng].get(s.name, 0) >= va:
                continue
            self.waited[eng][s.name] = va
            ws.append((s, va))
        amt = 16 if dma else 1
        newv = None
        if inc is not None:
            inc.abs += amt
            newv = inc.v
        self.q[eng].append((ws, fn, inc, amt))
        return newv

    def end_phase(self, junk):
        waits = [(s, s.v) for s in self.sems if s is not self.bar]
        self.nphase += 1
        self.add("gpsimd", lambda e: e.memset(junk, 0.0), waits=waits, inc=self.bar)
        with self.nc.Block() as block:
            for name in ENGS:
                items = self.q[name]

                def body(e, items=items):
                    for ws, fn, inc, amt in items:
                        for s, v in ws:
                            e.wait_ge(s.h, v)
                        ins = fn(e)
                        if inc is not None:
                            ins.then_inc(inc.h, amt)

                getattr(block, name)(body)
        self.q = {k: [] for k in ENGS}
        self.used = 0
        for name in ENGS:
            if name == "gpsimd":
                continue
            self.waited[name].pop(self.bar.name, None)
        self.phase_wait = (self.bar, self.nphase)

    def pw(self):
        return getattr(self, "phase_wait", None)


def _sb(es, nc, name, shape, dt):
    return es.enter_context(nc.sbuf_tensor(name, shape, dt))


def _ps(es, nc, name, shape, dt):
    return es.enter_context(nc.psum_tensor(name, shape, dt))


def phase1(nc, P, D):
    with ExitStack() as es:
        W = _sb(es, nc, "p1_W", [128, 8, COLS_IN], BF16)
        WF = [_sb(es, nc, f"p1_WF{i}", [128, 8, 128], F32) for i in range(2)]
        XF = _sb(es, nc, "p1_XF", [128, 8, 512], F32)
        XB = [_sb(es, nc, f"p1_XB{i}", [128, 8, 512], BF16) for i in range(2)]
        OST = [_sb(es, nc, f"p1_OST{i}", [128, 512], BF16) for i in range(4)]
        BG = _sb(es, nc, "p1_BG", [128, 16], F32)
        JK = _sb(es, nc, "p1_JK", [128, 1], F32)
        PS = _ps(es, nc, "p1_PS", [128, 4, 512], F32)
        s_xld, s_xcv, s_wcv, s_misc = P.sem("p1xld"), P.sem("p1xcv"), P.sem("p1wcv"), P.sem("p1misc")
        s_wld = [P.sem(f"p1wld{i}") for i in range(2)]
        s_mm = [P.sem(f"p1mm{i}") for i in range(4)]
        s_ev = [P.sem(f"p1ev{i}") for i in range(4)]
        s_st = [P.sem(f"p1st{i}") for i in range(4)]
        xT = D["xT"].rearrange("(kc p) t -> p kc t", p=128)
        wp = D["wp"].rearrange("(kc p) c -> p kc c", p=128)
        pw = P.pw()

        v_bg = P.add("sync", lambda e: e.dma_start(out=BG[:], in_=D["bg"]), waits=[pw], inc=s_misc, dma=True)

        wcv_of = {}
        wl = [0]

        def load_w(cid):
            n = wl[0]
            slot = n % 2
            v = P.add("sync", lambda e: e.dma_start(out=WF[slot][:], in_=wp[:, :, cid * 128:(cid + 1) * 128]),
                      waits=[pw, (s_wcv, n - 1)], inc=s_wld[slot], dma=True)
            P.add("vector", lambda e: e.tensor_copy(out=W[:, :, cid * 128:(cid + 1) * 128], in_=WF[slot][:]),
                  waits=[pw, (s_wld[slot], v)], inc=s_wcv)
            wcv_of[cid] = s_wcv.v
            wl[0] += 1

        worder = (list(range(8, 16)) + list(range(56, 64)) + list(range(36, 40)) + list(range(72, 76))
                  + list(range(28, 36)) + list(range(64, 72))
                  + list(range(0, 8)) + list(range(16, 28)) + list(range(40, 56)))
        wplan = {-1: worder[0:24], 0: worder[24:40]}
        for t in range(1, 4):
            wplan[t] = worder[40 + 12 * (t - 1):40 + 12 * t]

        tile_last = {}

        def load_x(t):
            v = P.add("sync", lambda e: e.dma_start(out=XF[:], in_=xT[:, :, t * 512:(t + 1) * 512]),
                      waits=[pw, (s_xcv, t)], inc=s_xld, dma=True)
            P.add("vector", lambda e: e.tensor_copy(out=XB[t % 2][:], in_=XF[:]),
                  waits=[pw, (s_xld, v)] + (tile_last[t - 2] if t >= 2 else []), inc=s_xcv)

        def jobs_for(t):
            jobs = []
            own = (t // 4) % 2 == 1
            to = t - 4 if t < 8 else t - 8
            for h in range(8):
                jobs.append(("F", C_FK + h * 128, D["KTB"][h][:, t * 512:(t + 1) * 512], None))
            for cb in range(2):
                for s in range(4):
                    r0 = t * 512 + s * 128
                    jobs.append(("T", C_BV + cb * 512, D["VB"][r0:r0 + 128, cb * 512:(cb + 1) * 512], s))
            gs = (0, 1, 2) if (own or t % 4 == 3) else (2,)
            for g in gs:
                for hp in range(4):
                    m = g * 4 + hp
                    jobs.append(("F", C_AK + m * 128, D["KTA"][m][:, t * 512:(t + 1) * 512], None))
            for g in gs:
                for s in range(4):
                    r0 = t * 512 + s * 128
                    jobs.append(("T", C_AV + g * 512, D["VA"][r0:r0 + 128, g * 512:(g + 1) * 512], s))
            if own:
                for h in range(8):
                    jobs.append(("F", C_FQ + h * 128, D["QTB"][h][:, to * 512:(to + 1) * 512], None))
                for m in range(12):
                    jobs.append(("F", C_AQ + m * 128, D["QTA"][m][:, to * 512:(to + 1) * 512], None))
                for i in range(16):
                    jobs.append(("G", C_GT + i * 128, D["SG"][i][:, to * 512:(to + 1) * 512], i))
            return jobs

        load_x(0)
        for cid in wplan[-1]:
            load_w(cid)
        jc = 0
        for t in range(16):
            jobs = jobs_for(t)
            half = len(jobs) // 2
            for ji, (kind, col, dst, aux) in enumerate(jobs):
                if ji == half:
                    if t + 1 < 16:
                        load_x(t + 1)
                    for cid in wplan.get(t, []):
                        load_w(cid)
                slot = jc % 4
                k = jc // 4
                xb = XB[t % 2]
                if kind == "T":
                    wneed = max(wcv_of[col // 128 + i] for i in range(4))
                else:
                    wneed = wcv_of[col // 128]

                def mm(e, kind=kind, col=col, aux=aux, xb=xb, slot=slot):
                    ins = None
                    for kc in range(8):
                        if kind == "T":
                            ins = e.matmul(PS[:, slot, :], lhsT=xb[:, kc, aux * 128:(aux + 1) * 128],
                                           rhs=W[:, kc, col:col + 512], start=(kc == 0), stop=(kc == 7))
                        else:
                            ins = e.matmul(PS[:, slot, :], lhsT=W[:, kc, col:col + 128],
                                           rhs=xb[:, kc, :], start=(kc == 0), stop=(kc == 7))
                    return ins

                P.add("tensor", mm, waits=[pw, (s_xcv, t + 1), (s_wcv, wneed), (s_ev[slot], k)], inc=s_mm[slot])
                evw = [pw, (s_mm[slot], k + 1), (s_st[slot], 16 * k)]
                if kind == "G":
                    P.add("scalar", lambda e, slot=slot, aux=aux: e.activation(
                        out=OST[slot][:], in_=PS[:, slot, :], func=AF.Sigmoid, bias=BG[:, aux:aux + 1], scale=1.0),
                        waits=evw + [(s_misc, v_bg)], inc=s_ev[slot])
                elif jc % 2 == 0:
                    P.add("vector", lambda e, slot=slot: e.tensor_copy(out=OST[slot][:], in_=PS[:, slot, :]),
                          waits=evw, inc=s_ev[slot])
                else:
                    P.add("scalar", lambda e, slot=slot: e.copy(out=OST[slot][:], in_=PS[:, slot, :]),
                          waits=evw, inc=s_ev[slot])
                P.add("gpsimd", lambda e, slot=slot, dst=dst: e.dma_start(out=dst, in_=OST[slot][:]),
                      waits=[pw, (s_ev[slot], k + 1)], inc=s_st[slot], dma=True)
                jc += 1
            tile_last[t] = [(s_mm[s], s_mm[s].v) for s in range(4)]
        P.end_phase(JK[:])


def phase2(nc, P, D):
    with ExitStack() as es:
        QA = [_sb(es, nc, f"p2_QA{i}", [128, NQ], BF16) for i in range(2)]
        KA = [_sb(es, nc, f"p2_KA{i}", [128, NKA], BF16) for i in range(2)]
        VS = [_sb(es, nc, f"p2_VS{i}", [128, 64, 128], BF16) for i in range(2)]
        GF = [_sb(es, nc, f"p2_GF{i}", [128, 2, 256], F32) for i in range(2)]
        BTF = _sb(es, nc, "p2_BTF", [128, 2, 256], F32)
        EB = [_sb(es, nc, f"p2_EB{i}", [128, 2, 256], F32) for i in range(2)]
        PTF = [_sb(es, nc, f"p2_PTF{i}", [128, 2, 256], F32) for i in range(2)]
        MA = _sb(es, nc, "p2_MA", [128, 256], F32)
        IDF = _sb(es, nc, "p2_IDF", [128, 128], F32)
        IDB = _sb(es, nc, "p2_IDB", [128, 128], BF16)
        PVF = _sb(es, nc, "p2_PVF", [128, 64], F32)
        PV64 = _sb(es, nc, "p2_PV64", [128, 64], BF16)
        ONE64 = _sb(es, nc, "p2_ONE64", [128, 64], BF16)
        PT = [_sb(es, nc, f"p2_PT{i}", [128, 2, 256], BF16) for i in range(3)]
        NACC = _sb(es, nc, "p2_NACC", [128, NQ], F32)
        DACC = _sb(es, nc, "p2_DACC", [128, NQ], F32)
        OAS = _sb(es, nc, "p2_OAS", [128, NQ], BF16)
        JK = _sb(es, nc, "p2_JK", [128, 1], F32)
        S = _ps(es, nc, "p2_S", [128, 4, 512], F32)
        ND = _ps(es, nc, "p2_ND", [128, 4, 512], F32)
        s_c = P.sem("p2c")
        s_ld = [P.sem(f"p2ld{i}") for i in range(2)]
        s_bt = P.sem("p2bt")
        s_qk = [P.sem(f"p2qk{i}") for i in range(2)]
        s_ex = [P.sem(f"p2ex{i}") for i in range(2)]
        s_pv = [P.sem(f"p2pv{i}") for i in range(3)]
        s_nf = [P.sem(f"p2nf{i}") for i in range(2)]
        s_dv = P.sem("p2dv")
        s_fin = P.sem("p2fin")
        s_pm = [P.sem(f"p2pm{i}") for i in range(2)]
        s_st = P.sem("p2st")
        pw = P.pw()

        vc = P.add("sync", lambda e: e.dma_start(out=MA[:], in_=D["ma"]), waits=[pw], inc=s_c, dma=True)
        vc = P.add("sync", lambda e: e.dma_start(out=PVF[:], in_=D["pvalid"]), waits=[pw], inc=s_c, dma=True)
        P.add("vector", lambda e: e.tensor_copy(out=PV64[:], in_=PVF[:]), waits=[pw, (s_c, vc)], inc=s_c)
        vconst = P.add("vector", lambda e: e.memset(ONE64[:], 1.0), waits=[pw], inc=s_c)

        passes = [(hp, g) for hp in range(4) for g in range(3)]
        pass_last_pe = {}
        pass_last_dve = {}
        ld_val = {}
        bt_val = {}

        def sched_loads(pi):
            hp, g = passes[pi]
            d = DILS[g]
            slot = pi % 2
            nb = 64 // d
            w = [pw] + (pass_last_pe[pi - 2] if pi >= 2 else [])
            P.add("sync", lambda e: e.dma_start(out=QA[slot][:], in_=D["QTA"][g * 4 + hp]), waits=w, inc=s_ld[slot], dma=True)
            P.add("sync", lambda e: e.dma_start(out=KA[slot][:], in_=D["KTA"][g * 4 + hp]), waits=w, inc=s_ld[slot], dma=True)
            vav = D["VA"].rearrange("(n i r) c -> r i n c", i=128, r=d)
            c0 = g * 512 + hp * 128
            for r in range(d):
                for n0 in range(0, nb, 16):
                    n1 = min(nb, n0 + 16)
                    P.add("sync", lambda e, r=r, n0=n0, n1=n1: e.dma_start(
                        out=VS[slot][:, r * nb + n0:r * nb + n1, :], in_=vav[r][:, n0:n1, c0:c0 + 128]),
                        waits=w, inc=s_ld[slot], dma=True)
            gh0 = g * 8 + 2 * hp
            v = P.add("sync", lambda e: e.dma_start(out=GF[slot][:], in_=D["ga"][gh0:gh0 + 2].rearrange("h i c -> i h c")),
                      waits=w, inc=s_ld[slot], dma=True)
            ld_val[pi] = v

        def sched_tables(pi):
            slot = pi % 2
            wb = [pw, (s_ld[slot], ld_val[pi]), (s_c, vconst)]
            v1 = None
            for hh in range(2):
                v1 = P.add("vector", lambda e, hh=hh: e.tensor_tensor(
                    out=BTF[:, hh, :], in0=GF[slot][:, hh, :], in1=MA[:], op=ALU.add),
                    waits=wb + [(s_bt, s_bt.v)], inc=s_bt)
            P.add("scalar", lambda e: e.activation(out=EB[slot][:], in_=BTF[:], func=AF.Exp),
                  waits=[pw, (s_bt, v1)] + (pass_last_dve[pi - 2] if pi >= 2 else []), inc=s_bt)
            bt_val[pi] = s_bt.v

        sched_loads(0)
        sched_tables(0)
        cn = {"qi": 0, "gi": 0, "fin": 0}

        def run_pass(pi, hp, g):
            d = DILS[g]
            slot = pi % 2
            nb = 64 // d
            nbB = 16 // d
            if pi + 1 < len(passes):
                sched_loads(pi + 1)

            def own_off(n):
                return 2048 if n < 2 * nbB else 4096

            groups = []
            if d == 16:
                for B in (1, 3):
                    base = 0 if B == 1 else 2048
                    for r0 in range(0, 16, 4):
                        qbs = [(r0 + i, B) for i in range(4)]
                        groups.append((qbs,
                                       lambda T, base=base, r0=r0: T[:, base:base + 2048].rearrange("p (j r) -> p r j", r=16)[:, r0:r0 + 4, :],
                                       lambda bank: ND[:, bank, 0:512].rearrange("p (r j) -> p r j", r=4)))
            else:
                for r in range(d):
                    for B in (1, 3):
                        for n0 in range(B * nbB, (B + 1) * nbB, 4):
                            qbs = [(r, n0 + i) for i in range(4)]
                            t0 = d * 128 * n0 + r - own_off(n0)
                            sl = slice(t0, t0 + 511 * d + 1, d)
                            groups.append((qbs, lambda T, sl=sl: T[:, sl], lambda bank: ND[:, bank, 0:512]))
            qblocks = [(r, n, gidx, i) for gidx, (qbs, _, _) in enumerate(groups) for i, (r, n) in enumerate(qbs)]
            pendl = []
            pass_dv_start = s_dv.v

            def emit_pv(info):
                (qslot, kq, pslot, r, n, gslot, kg, first, last, gidx, col) = info

                def pv(e):
                    ins = None
                    for hh in range(2):
                        for ch in range(2):
                            nk = n - 1 + ch
                            blk = r * nb + nk
                            ins = e.matmul(ND[hh * 64:(hh + 1) * 64, 2 * gslot, col:col + 128],
                                           lhsT=VS[slot][:, blk, hh * 64:(hh + 1) * 64],
                                           rhs=PT[pslot][:, hh, ch * 128:(ch + 1) * 128],
                                           start=(ch == 0), stop=(ch == 1))
                            vt = PV64 if nk < nbB else ONE64
                            ins = e.matmul(ND[hh * 64:(hh + 1) * 64, 2 * gslot + 1, col:col + 128],
                                           lhsT=vt[:, :], rhs=PT[pslot][:, hh, ch * 128:(ch + 1) * 128],
                                           start=(ch == 0), stop=(ch == 1))
                    return ins

                w = [(s_pm[qslot], kq + 1)]
                if first:
                    w.append((s_nf[gslot], kg))
                vpv = P.add("tensor", pv, waits=w, inc=s_pv[pslot])
                if last:
                    _, dst, srcf = groups[gidx]
                    wd = [(s_pv[pslot], vpv)]
                    if g > 0:
                        wd += [(s_dv, pass_dv_start), (s_nf[0], nf_start[0]), (s_nf[1], nf_start[1])]
                    else:
                        wd += [(s_fin, cn["fin"])]
                    if g == 0:
                        P.add("scalar", lambda e: e.copy(out=dst(NACC), in_=srcf(2 * gslot)), waits=[pw] + wd, inc=s_dv)
                        P.add("scalar", lambda e: e.copy(out=dst(DACC), in_=srcf(2 * gslot + 1)), waits=[pw] + wd, inc=s_nf[gslot])
                    else:
                        P.add("vector", lambda e: e.tensor_tensor(out=dst(NACC), in0=srcf(2 * gslot), in1=dst(NACC), op=ALU.add),
                              waits=wd, inc=s_dv)
                        P.add("vector", lambda e: e.tensor_tensor(out=dst(DACC), in0=srcf(2 * gslot + 1), in1=dst(DACC), op=ALU.add),
                              waits=wd, inc=s_nf[gslot])

            nf_start = [s_nf[0].v, s_nf[1].v]
            for bi, (r, n, gidx, gi_) in enumerate(qblocks):
                if bi == len(qblocks) // 2 and pi + 1 < len(passes):
                    sched_tables(pi + 1)
                qslot = cn["qi"] % 2
                kq = cn["qi"] // 2
                pslot = cn["qi"] % 3
                kp = cn["qi"] // 3
                first = (gi_ == 0)
                last = (gi_ == 3)
                if first:
                    gslot = cn["gi"] % 2
                    kg = cn["gi"] // 2
                    cn["gi"] += 1
                t0 = d * 128 * n + r - own_off(n)
                qs = slice(t0, t0 + 127 * d + 1, d)

                def qk(e, r=r, n=n, qslot=qslot, qs=qs):
                    ins = None
                    for hh in range(2):
                        bank = 2 * qslot + hh
                        for ch in range(2):
                            nk = n - 1 + ch
                            ks = slice(d * 128 * nk + r, d * 128 * nk + r + 127 * d + 1, d)
                            ins = e.matmul(S[:, bank, ch * 128:(ch + 1) * 128],
                                           lhsT=KA[slot][hh * 64:(hh + 1) * 64, ks],
                                           rhs=QA[slot][hh * 64:(hh + 1) * 64, qs],
                                           start=True, stop=True)
                    return ins

                P.add("tensor", qk, waits=[pw, (s_ld[slot], ld_val[pi]), (s_c, vconst), (s_ex[qslot], kq)],
                      inc=s_qk[qslot])
                P.add("scalar", lambda e, qslot=qslot: e.activation(
                    out=PTF[qslot][:], in_=S[:, 2 * qslot:2 * qslot + 2, 0:256], func=AF.Exp, scale=0.125),
                    waits=[pw, (s_qk[qslot], kq + 1), (s_pm[qslot], kq)], inc=s_ex[qslot])
                P.add("vector", lambda e, qslot=qslot, pslot=pslot: e.tensor_tensor(
                    out=PT[pslot][:], in0=PTF[qslot][:], in1=EB[slot][:], op=ALU.mult),
                    waits=[pw, (s_ex[qslot], kq + 1), (s_pv[pslot], kp), (s_bt, bt_val[pi])], inc=s_pm[qslot])
                pendl.append((qslot, kq, pslot, r, n, gslot, kg, first, last, gidx, gi_ * 128))
                while len(pendl) > 2:
                    emit_pv(pendl.pop(0))
                cn["qi"] += 1
            while pendl:
                emit_pv(pendl.pop(0))
            pass_last_pe[pi] = [(s_pv[i], s_pv[i].v) for i in range(3)]
            pass_last_dve[pi] = [(s_pm[0], s_pm[0].v), (s_pm[1], s_pm[1].v)]
            if g == 2:
                wd = [(s_dv, s_dv.v), (s_nf[0], s_nf[0].v), (s_nf[1], s_nf[1].v)]
                v1 = P.add("vector", lambda e: e.reciprocal(out=DACC[:], in_=DACC[:]), waits=wd, inc=s_fin)
                v2 = P.add("vector", lambda e: e.tensor_tensor(out=OAS[:], in0=NACC[:], in1=DACC[:], op=ALU.mult),
                           waits=[(s_fin, v1), (s_st, 16 * hp)], inc=s_fin)
                cn["fin"] = v2
                P.add("gpsimd", lambda e, hp=hp: e.dma_start(out=D["OAT"][hp], in_=OAS[:]),
                      waits=[pw, (s_fin, v2)], inc=s_st, dma=True)
        for pi, (hp, g) in enumerate(passes):
            run_pass(pi, hp, g)
        P.end_phase(JK[:])


def phase3(nc, P, D):
    with ExitStack() as es:
        KT = [_sb(es, nc, f"p3_KT{i}", [128, NK], BF16) for i in range(2)]
        QT = [_sb(es, nc, f"p3_QT{i}", [128, NQ], BF16) for i in range(2)]
        VH = [_sb(es, nc, f"p3_VH{i}", [128, 64, 129], BF16) for i in range(2)]
        GB = _sb(es, nc, "p3_GB", [128, 8, 256], F32)
        CB = _sb(es, nc, "p3_CB", [128, 8], F32)
        MB = _sb(es, nc, "p3_MB", [128, 256], F32)
        BTF = _sb(es, nc, "p3_BTF", [128, 256], F32)
        BTA = _sb(es, nc, "p3_BTA", [128, 8, 2, 256], F32)
        IDF = _sb(es, nc, "p3_IDF", [128, 128], F32)
        IDB = _sb(es, nc, "p3_IDB", [128, 128], BF16)
        ZB = _sb(es, nc, "p3_ZB", [128, 512], BF16)
        PVF = _sb(es, nc, "p3_PVF", [128, 64], F32)
        LAMI = _sb(es, nc, "p3_LAMI", [128, 4, 64], F32)
        LPR = _sb(es, nc, "p3_LPR", [128, 2, 64], F32)
        LS = _sb(es, nc, "p3_LS", [128, 2], F32)
        LE = _sb(es, nc, "p3_LE", [128, 2], F32)
        NLAM = _sb(es, nc, "p3_NLAM", [128, 1], F32)
        GS = _sb(es, nc, "p3_GS", [128, 128], F32)
        EPSB = _sb(es, nc, "p3_EPSB", [128, 1], F32)
        PT = [_sb(es, nc, f"p3_PT{i}", [128, 2, 512], BF16) for i in range(3)]
        EP = _sb(es, nc, "p3_EP", [128, 3, 512], F32)
        RD = _sb(es, nc, "p3_RD", [128, 8], F32)
        TT = _sb(es, nc, "p3_TT", [128, 128], F32)
        OO = _sb(es, nc, "p3_OO", [128, 4, 128], F32)
        SQ = _sb(es, nc, "p3_SQ", [128, 128], F32)
        SSQ = _sb(es, nc, "p3_SSQ", [128, 4], F32)
        LNV = _sb(es, nc, "p3_LNV", [128, 4], F32)
        RSTD = _sb(es, nc, "p3_RSTD", [128, 4], F32)
        ON = _sb(es, nc, "p3_ON", [128, 4, 128], BF16)
        OBS = [_sb(es, nc, f"p3_OBS{i}", [128, 512], BF16) for i in range(2)]
        JK = _sb(es, nc, "p3_JK", [128, 1], F32)
        CF = [_sb(es, nc, f"p3_CF{i}", [128, 1024], F32) for i in range(2)]
        CBT = [_sb(es, nc, f"p3_CBT{i}", [128, 1024], BF16) for i in range(2)]
        S = _ps(es, nc, "p3_S", [128, 4, 512], F32)
        ACC = _ps(es, nc, "p3_ACC", [128, 3, 512], F32)
        TP = _ps(es, nc, "p3_TP", [128, 4, 128], BF16)
        s_c = P.sem("p3c")
        s_ld = [P.sem(f"p3ld{i}") for i in range(2)]
        s_qk = [P.sem(f"p3qk{i}") for i in range(2)]
        s_ex = [P.sem(f"p3ex{i}") for i in range(2)]
        s_pv = [P.sem(f"p3pv{i}") for i in range(3)]
        s_af = P.sem("p3af")
        s_ep = P.sem("p3ep")
        s_ea = P.sem("p3ea")
        s_tp = P.sem("p3tp")
        s_tc = P.sem("p3tc")
        s_st = [P.sem(f"p3st{i}") for i in range(2)]
        s_cl = [P.sem(f"p3cl{i}") for i in range(2)]
        s_cc = P.sem("p3cc")
        s_cs = [P.sem(f"p3cs{i}") for i in range(2)]
        s_nb = [P.sem(f"p3nb{i}") for i in range(2)]
        pw = P.pw()

        chunks = []
        w1v = D["w1"].rearrange("(kc p) f -> p kc f", p=128)
        w2v = D["w2"].rearrange("(fc p) d -> p fc d", p=128)
        k0 = 0
        for (wsrc, nk) in ((D["wpa"], 4), (D["wpb"], 8), (D["wo"], 8)):
            wv = wsrc.rearrange("(kc p) c -> p kc c", p=128)
            for k in range(nk):
                chunks.append((wv[:, k, :], D["WAB"][:, k0 + k, :]))
            k0 += nk
        for kc in range(8):
            for q in range(4):
                chunks.append((w1v[:, kc, q * 1024:(q + 1) * 1024], D["W1B"][:, kc, q * 1024:(q + 1) * 1024]))
        for fc in range(32):
            chunks.append((w2v[:, fc, :], D["W2B"][:, fc, :]))
        cv = {"k": 0, "ld": {}}

        def wconv_step():
            k = cv["k"]
            if k - 1 >= 0 and k - 1 < len(chunks):
                j = k - 1
                sl = j % 2
                src_, dst_ = chunks[j]
                vcast = P.add("vector", lambda e, sl=sl: e.tensor_copy(out=CBT[sl][:], in_=CF[sl][:]),
                              waits=[pw, (s_cl[sl], cv["ld"][j]), (s_cs[sl], 16 * (j // 2))], inc=s_cc)
                P.add("gpsimd", lambda e, sl=sl, dst_=dst_: e.dma_start(out=dst_, in_=CBT[sl][:]), waits=[pw, (s_cc, vcast)],
                      inc=s_cs[sl], dma=True)
            if k < len(chunks):
                sl = k % 2
                src_, dst_ = chunks[k]
                cv["ld"][k] = P.add("sync", lambda e, sl=sl, src_=src_: e.dma_start(out=CF[sl][:], in_=src_), waits=[pw, (s_cc, k - 1)],
                                    inc=s_cl[sl], dma=True)
            cv["k"] += 1

        def accap(m, j):
            idx = m * 4 + j
            return idx // 3, (idx % 3) * 170

        for (dst, src) in ((GB, D["gb"].rearrange("h i c -> i h c")), (CB, D["cb"]), (MB, D["mb"]), (IDF, D["ident"]),
                           (PVF, D["pvalid"]), (LAMI, D["lam"]), (GS, D["gs"])):
            vc = P.add("sync", lambda e, dst=dst, src=src: e.dma_start(out=dst[:], in_=src), waits=[pw], inc=s_c, dma=True)
        w0 = [pw, (s_c, vc)]
        P.add("vector", lambda e: e.tensor_copy(out=IDB[:], in_=IDF[:]), waits=w0, inc=s_c)
        P.add("vector", lambda e: e.memset(ZB[:], 0.0), waits=w0, inc=s_c)
        P.add("vector", lambda e: e.memset(EPSB[:], LN_EPS), waits=w0, inc=s_c)
        v = P.add("vector", lambda e: e.tensor_scalar(out=GS[:], in0=GS[:], scalar1=1.0 - LAM_INIT, scalar2=None, op0=ALU.mult),
                  waits=w0, inc=s_c)
        for i in range(2):
            P.add("vector", lambda e, i=i: e.memset(VH[i][:, 16:64, 128:129], 1.0), waits=w0, inc=s_c)
            P.add("vector", lambda e, i=i: e.tensor_copy(out=VH[i][:, 0:16, 128:129],
                                                         in_=PVF[:, 0:16].rearrange("p (a b) -> p a b", b=1)),
                  waits=w0, inc=s_c)
        v = P.add("vector", lambda e: e.tensor_tensor(out=LPR[:], in0=LAMI[:, 0:2, :], in1=LAMI[:, 2:4, :], op=ALU.mult),
                  waits=w0, inc=s_c)
        v = P.add("vector", lambda e: e.reduce_sum(out=LS[:], in_=LPR[:], axis=AX.X), waits=[(s_c, v)], inc=s_c)
        v = P.add("scalar", lambda e: e.activation(out=LE[:], in_=LS[:], func=AF.Exp), waits=[pw, (s_c, v)], inc=s_c)
        v = P.add("vector", lambda e: e.tensor_tensor(out=NLAM[:], in0=LE[:, 1:2], in1=LE[:, 0:1], op=ALU.subtract),
                  waits=[(s_c, v)], inc=s_c)
        v = P.add("vector", lambda e: e.tensor_scalar(out=NLAM[:], in0=NLAM[:], scalar1=-LAM_INIT, scalar2=None, op0=ALU.add),
                  waits=[(s_c, v)], inc=s_c)
        for h in range(8):
            v = P.add("vector", lambda e, h=h: e.tensor_scalar(out=BTF[:], in0=GB[:, h, :], scalar1=CB[:, h:h + 1], scalar2=8.0,
                                                               op0=ALU.subtract, op1=ALU.mult), waits=[(s_c, v)], inc=s_c)
            v = P.add("vector", lambda e, h=h: e.tensor_tensor(out=BTA[:, h, 0, :], in0=BTF[:], in1=MB[:], op=ALU.add), waits=[(s_c, v)], inc=s_c)
            v = P.add("vector", lambda e, h=h: e.tensor_tensor(out=BTA[:, h, 1, :], in0=BTF[:], in1=MB[:], op=ALU.add), waits=[(s_c, v)], inc=s_c)
        vconst = v

        head_last_pe = {}
        ld_val = {}
        vbv = D["VB"].rearrange("(kb p) c -> p kb c", p=128)

        def sched_loads(h):
            slot = h % 2
            w = [pw, (s_c, vconst)] + (head_last_pe[h - 2] if h >= 2 else [])
            P.add("sync", lambda e: e.dma_start(out=KT[slot][:], in_=D["KTB"][h]), waits=w, inc=s_ld[slot], dma=True)
            P.add("sync", lambda e: e.dma_start(out=QT[slot][:], in_=D["QTB"][h]), waits=w, inc=s_ld[slot], dma=True)
            for q in range(4):
                v = P.add("sync", lambda e, q=q: e.dma_start(out=VH[slot][:, q * 16:(q + 1) * 16, 0:128],
                                                           in_=vbv[:, q * 16:(q + 1) * 16, h * 128:(h + 1) * 128]),
                          waits=w, inc=s_ld[slot], dma=True)
            ld_val[h] = v

        deferred = []
        ui = [0]

        def flush(force=False):
            while deferred and (force or deferred[0][0] <= ui[0]):
                _, fn = deferred.pop(0)
                fn()

        sched_loads(0)
        cn = {"nqt": 0}
        pending = []

        def emit_pv(info):
            (sslot, ku, pslot, kb, jmin, firstu, lastu, hs, kq_tile, on_last) = info

            def pv(e):
                ins = None
                if firstu:
                    for b in range(3):
                        e.matmul(ACC[:, b, :], lhsT=ZB[:, 0:128], rhs=ZB[:, :], start=True, stop=True,
                                 skip_group_check=True)
                for m in range(2):
                    for j in range(jmin, 4):
                        b, off = accap(m, j)
                        ins = e.matmul(ACC[:, b, off:off + 129], lhsT=PT[pslot][:, m, j * 128:(j + 1) * 128],
                                       rhs=VH[hs][:, kb, :], start=False, stop=lastu, skip_group_check=True)
                return ins

            w = [(s_ex[sslot], ku + 1)]
            if firstu:
                w.append((s_af, kq_tile))
            v = P.add("tensor", pv, waits=w, inc=s_pv[pslot])
            if lastu:
                on_last(pslot, v)

        def drain(keep):
            while len(pending) > keep:
                emit_pv(pending.pop(0))

        def run_qt(h, qt):
                hs = h % 2
                base = 4 * (qt + 4 if qt < 4 else qt + 8)
                nkb = base + 4
                kq_tile = cn["nqt"]
                cn["nqt"] += 1
                nqt = cn["nqt"]

                def on_last(last_slot, vlast):
                    epilogue(last_slot, vlast)

                for kb in range(nkb):
                    u = ui[0]
                    slot = u % 2
                    ku = u // 2
                    pslot = u % 3
                    kp = u // 3
                    jd = kb - base
                    c0 = max(jd, 0) * 128
                    jmin = max(jd, 0)

                    def qk(e, kb=kb, c0=c0, slot=slot, qt=qt):
                        ins = None
                        for m in range(2):
                            ins = e.matmul(S[:, 2 * slot + m, c0:512], lhsT=KT[hs][m * 64:(m + 1) * 64, kb * 128:(kb + 1) * 128],
                                           rhs=QT[hs][m * 64:(m + 1) * 64, qt * 512 + c0:(qt + 1) * 512], start=True, stop=True)
                        return ins

                    vqk = P.add("tensor", qk, waits=[pw, (s_ld[hs], ld_val[h]), (s_c, vconst), (s_ex[slot], ku)], inc=s_qk[slot])
                    wex = [pw, (s_qk[slot], ku + 1), (s_pv[pslot], kp)]
                    if jd >= -1:
                        if jd == -1:
                            oc, bc, n = 0, 128, 128
                        elif jd == 3:
                            oc, bc, n = 384, 0, 128
                        else:
                            oc, bc, n = jd * 128, 0, 256
                        vnb = P.add("vector", lambda e, slot=slot, oc=oc, bc=bc, n=n: e.tensor_tensor(
                            out=S[:, 2 * slot:2 * slot + 2, oc:oc + n], in0=S[:, 2 * slot:2 * slot + 2, oc:oc + n],
                            in1=BTA[:, h, :, bc:bc + n], op=ALU.add), waits=[pw, (s_qk[slot], vqk)], inc=s_nb[slot])
                        wex.append((s_nb[slot], vnb))
                    P.add("scalar", lambda e, slot=slot, c0=c0, pslot=pslot: e.activation(
                        out=PT[pslot][:, :, c0:512], in_=S[:, 2 * slot:2 * slot + 2, c0:512], func=AF.Exp, scale=0.125),
                        waits=wex, inc=s_ex[slot])
                    pending.append((slot, ku, pslot, kb, jmin, kb == 0, kb == nkb - 1, hs, kq_tile, on_last))
                    drain(2)
                    ui[0] += 1
                    if ui[0] % 24 == 0:
                        wconv_step()
                    flush()

                def epilogue(last_slot, vlast):

                    def stage_a(last_slot=last_slot, vlast=vlast, h=h, qt=qt, k=nqt):
                        v = P.add("vector", lambda e: e.tensor_copy(out=EP[:], in_=ACC[:]),
                                  waits=[(s_pv[last_slot], vlast), (s_ep, s_ep.v)], inc=s_af)
                        we = [(s_af, v)]
                        for m in range(2):
                            for j in range(4):
                                b, off = accap(m, j)
                                idx = m * 4 + j
                                P.add("vector", lambda e, b=b, off=off, idx=idx: e.reciprocal(
                                    out=RD[:, idx:idx + 1], in_=EP[:, b, off + 128:off + 129]), waits=we, inc=s_ep)
                        v = P.add("vector", lambda e: e.tensor_scalar(out=RD[:, 4:8], in0=RD[:, 4:8], scalar1=NLAM[:, 0:1], scalar2=None,
                                                                      op0=ALU.mult), waits=[(s_ep, s_ep.v)], inc=s_ep)
                        for j in range(4):
                            b1, o1 = accap(0, j)
                            b2, o2 = accap(1, j)
                            v = P.add("vector", lambda e, b2=b2, o2=o2, j=j: e.tensor_scalar(
                                out=TT[:], in0=EP[:, b2, o2:o2 + 128], scalar1=RD[:, 4 + j:5 + j], scalar2=None, op0=ALU.mult),
                                waits=[(s_ep, v)], inc=s_ep)
                            v = P.add("vector", lambda e, b1=b1, o1=o1, j=j: e.scalar_tensor_tensor(
                                out=OO[:, j, :], in0=EP[:, b1, o1:o1 + 128], scalar=RD[:, j:j + 1], in1=TT[:],
                                op0=ALU.mult, op1=ALU.add), waits=[(s_ep, v)], inc=s_ep)
                            v = P.add("vector", lambda e, j=j: e.scalar_tensor_tensor(
                                out=SQ[:], in0=OO[:, j, :], scalar=1.0, in1=OO[:, j, :], op0=ALU.mult, op1=ALU.mult,
                                accum_out=SSQ[:, j:j + 1]), waits=[(s_ep, v)], inc=s_ep)
                        return v

                    va = stage_a()

                    def stage_bc(va=va):
                        v = P.add("scalar", lambda e: e.activation(out=LNV[:], in_=SSQ[:], func=AF.Ln, bias=EPSB[:, 0:1], scale=1.0 / 128.0),
                                  waits=[(s_ep, va), (s_ea, s_ea.v)], inc=s_ea)
                        v = P.add("scalar", lambda e: e.activation(out=RSTD[:], in_=LNV[:], func=AF.Exp, scale=-0.5),
                                  waits=[(s_ea, v)], inc=s_ea)
                        vv = None
                        for j in range(4):
                            vv = P.add("vector", lambda e, j=j: e.scalar_tensor_tensor(
                                out=ON[:, j, :], in0=OO[:, j, :], scalar=RSTD[:, j:j + 1], in1=GS[:], op0=ALU.mult, op1=ALU.mult),
                                waits=[(s_ea, v), (s_tp, s_tp.v)], inc=s_ep)
                        return vv

                    def stage_de(vc, h=h, qt=qt, k=nqt):
                        def tp(e):
                            ins = None
                            for j in range(4):
                                ins = e.transpose(out=TP[:, j, :], in_=ON[:, j, :], identity=IDB[:])
                            return ins
                        v = P.add("tensor", tp, waits=[(s_ep, vc), (s_tc, s_tc.v)], inc=s_tp)
                        os_ = (k - 1) % 2
                        v2 = P.add("vector", lambda e: e.tensor_copy(out=OBS[os_][:].rearrange("p (a b) -> p a b", a=4), in_=TP[:]),
                                   waits=[(s_tp, v), (s_st[os_], 16 * ((k - 1) // 2))], inc=s_tc)
                        P.add("gpsimd", lambda e: e.dma_start(out=D["OBT"][h][:, qt * 512:(qt + 1) * 512], in_=OBS[os_][:]),
                              waits=[pw, (s_tc, v2)], inc=s_st[os_], dma=True)

                    def chain(stage_bc=stage_bc, stage_de=stage_de):
                        vc = stage_bc()
                        deferred.append((ui[0] + 3, lambda: stage_de(vc)))

                    deferred.append((ui[0] + 3, chain))
        for h in range(8):
            if h + 1 < 8:
                sched_loads(h + 1)
            for qt in range(8):
                run_qt(h, qt)
            drain(0)
            head_last_pe[h] = [(s_pv[i], s_pv[i].v) for i in range(3)]
        flush(force=True)
        flush(force=True)
        while cv["k"] <= len(chunks):
            wconv_step()
        P.end_phase(JK[:])


def layer_norm_ops(P, V, ST6, MV, STD, RSTD, TMP, OUT, LG, LB, EPSB, sem, wait_in):
    v = None
    for hf in range(2):
        v = P.add("vector", lambda e, hf=hf: e.bn_stats(out=ST6[:, hf * 6:(hf + 1) * 6], in_=V[:, hf * 512:(hf + 1) * 512]),
                  waits=wait_in + [(sem, sem.v)], inc=sem)
    v = P.add("vector", lambda e: e.bn_aggr(out=MV[:], in_=ST6[:]), waits=[(sem, v)], inc=sem)
    v = P.add("scalar", lambda e: e.activation(out=STD[:], in_=MV[:, 1:2], func=AF.Sqrt, bias=EPSB[:, 0:1], scale=1.0),
              waits=[(sem, v)], inc=sem)
    v = P.add("vector", lambda e: e.reciprocal(out=RSTD[:], in_=STD[:]), waits=[(sem, v)], inc=sem)
    v = P.add("vector", lambda e: e.scalar_tensor_tensor(out=TMP[:], in0=V[:], scalar=MV[:, 0:1], in1=LG[:],
                                                         op0=ALU.subtract, op1=ALU.mult), waits=[(sem, v)], inc=sem)
    v = P.add("vector", lambda e: e.scalar_tensor_tensor(out=OUT, in0=TMP[:], scalar=RSTD[:, 0:1], in1=LB[:],
                                                         op0=ALU.mult, op1=ALU.add), waits=[(sem, v)], inc=sem)
    return v


def load_cast_weight(P, pw, dst_chunks, src_chunks, WF, s_wld, s_wcv, cnt):
    for dst, src in zip(dst_chunks, src_chunks):
        n = cnt[0]
        slot = n % 2
        v = P.add("sync", lambda e, slot=slot, src=src: e.dma_start(out=WF[slot][:], in_=src),
                  waits=[pw, (s_wcv, n - 1)], inc=s_wld[slot], dma=True)
        P.add("vector", lambda e, slot=slot, dst=dst: e.tensor_copy(out=dst, in_=WF[slot][:]),
              waits=[pw, (s_wld[slot], v)], inc=s_wcv)
        cnt[0] += 1
    return s_wcv.v


def phase4a(nc, P, D):
    with ExitStack() as es:
        WPA = _sb(es, nc, "p4_WPA", [128, 4, 1024], BF16)
        WPB = _sb(es, nc, "p4_WPB", [128, 8, 1024], BF16)
        WO = _sb(es, nc, "p4_WO", [128, 8, 1024], BF16)
        OAt = [_sb(es, nc, f"p4_OA{i}", [128, 4, 512], BF16) for i in range(2)]
        OBt = [_sb(es, nc, f"p4_OB{i}", [128, 8, 512], BF16) for i in range(2)]
        SGt = _sb(es, nc, "p4_SG", [128, 16, 512], BF16)
        XS = [_sb(es, nc, f"p4_XS{i}", [128, 4, 1024], F32) for i in range(2)]
        MT = _sb(es, nc, "p4_MT", [128, 8, 512], BF16)
        T1 = [_sb(es, nc, f"p4_T1{i}", [128, 512], F32) for i in range(2)]
        T2 = [_sb(es, nc, f"p4_T2{i}", [128, 512], F32) for i in range(2)]
        V = _sb(es, nc, "p4_V", [128, 1024], F32)
        TMP = _sb(es, nc, "p4_TMP", [128, 1024], F32)
        X1O = [_sb(es, nc, f"p4_X1O{i}", [128, 1024], F32) for i in range(2)]
        X1B = [_sb(es, nc, f"p4_X1B{i}", [128, 1024], BF16) for i in range(2)]
        X1TS = _sb(es, nc, "p4_X1TS", [128, 8, 512], BF16)
        LG = _sb(es, nc, "p4_LG", [128, 1024], F32)
        LB = _sb(es, nc, "p4_LB", [128, 1024], F32)
        ST6 = _sb(es, nc, "p4_ST6", [128, 12], F32)
        MV = _sb(es, nc, "p4_MV", [128, 2], F32)
        STD = _sb(es, nc, "p4_STD", [128, 1], F32)
        RSTD = _sb(es, nc, "p4_RSTD", [128, 1], F32)
        EPSB = _sb(es, nc, "p4_EPSB", [128, 1], F32)
        IDF = _sb(es, nc, "p4_IDF", [128, 128], F32)
        IDB = _sb(es, nc, "p4_IDB", [128, 128], BF16)
        JK = _sb(es, nc, "p4_JK", [128, 1], F32)
        PSY = _ps(es, nc, "p4_PSY", [128, 4, 512], F32)
        PSM = _ps(es, nc, "p4_PSM", [128, 2, 512], F32)
        PST = _ps(es, nc, "p4_PST", [128, 8, 128], BF16)
        s_c = P.sem("p4c")
        s_wld = [P.sem(f"p4wld{i}") for i in range(2)]
        s_wcv = P.sem("p4wcv")
        s_ldab = [P.sem(f"p4ldab{i}") for i in range(2)]
        s_ldsg = P.sem("p4ldsg")
        s_ldx = [P.sem(f"p4ldx{i}") for i in range(2)]
        s_y = [P.sem(f"p4y{i}") for i in range(2)]
        s_t = [P.sem(f"p4t{i}") for i in range(2)]
        s_mt = P.sem("p4mt")
        s_mm = [P.sem(f"p4mm{i}") for i in range(2)]
        s_v = [P.sem(f"p4v{i}") for i in range(2)]
        s_ln = P.sem("p4ln")
        s_xb = P.sem("p4xb")
        s_tp = P.sem("p4tp")
        s_tc = P.sem("p4tc")
        s_so = [P.sem(f"p4so{i}") for i in range(2)]
        s_sx = P.sem("p4sx")
        pw = P.pw()

        for (dst, src) in ((LG, D["ln1g"]), (LB, D["ln1b"]), (IDF, D["ident"])):
            vc = P.add("sync", lambda e, dst=dst, src=src: e.dma_start(out=dst[:], in_=src), waits=[pw], inc=s_c, dma=True)
        P.add("vector", lambda e: e.tensor_copy(out=IDB[:], in_=IDF[:]), waits=[pw, (s_c, vc)], inc=s_c)
        vconst = P.add("vector", lambda e: e.memset(EPSB[:], LN_EPS), waits=[pw], inc=s_c)

        P.add("sync", lambda e: e.dma_start(out=WPA[:], in_=D["WAB"][:, 0:4, :]), waits=[pw], inc=s_wcv, dma=True)
        P.add("sync", lambda e: e.dma_start(out=WPB[:], in_=D["WAB"][:, 4:12, :]), waits=[pw], inc=s_wcv, dma=True)
        vw = P.add("sync", lambda e: e.dma_start(out=WO[:], in_=D["WAB"][:, 12:20, :]), waits=[pw], inc=s_wcv, dma=True)

        oat = D["OAT"].rearrange("k p t -> p k t")
        obt = D["OBT"].rearrange("k p t -> p k t")
        sgt = D["SG"].rearrange("k p t -> p k t")
        xo = D["xo"].rearrange("(s p) d -> p s d", p=128)
        x1 = D["X1"].rearrange("(s p) d -> p s d", p=128)
        x1t = D["X1T"].rearrange("k p t -> p k t")

        last_y_pe = {}
        ld_ab = {}

        ld_x = {}
        last_v = {}

        def load_ab(t):
            sl = t % 2
            w = [pw] + (last_y_pe[t - 2] if t >= 2 else [])
            P.add("sync", lambda e: e.dma_start(out=OAt[sl][:], in_=oat[:, :, t * 512:(t + 1) * 512]), waits=w, inc=s_ldab[sl], dma=True)
            ld_ab[t] = P.add("sync", lambda e: e.dma_start(out=OBt[sl][:], in_=obt[:, :, t * 512:(t + 1) * 512]), waits=w,
                             inc=s_ldab[sl], dma=True)
            ld_x[t] = P.add("sync", lambda e: e.dma_start(out=XS[sl][:], in_=xo[:, 4 * t:4 * t + 4, :]),
                            waits=[pw] + (last_v[t - 2] if t >= 2 else []), inc=s_ldx[sl], dma=True)

        ld_sg = {}

        def load_sg(t):
            ld_sg[t] = P.add("sync", lambda e: e.dma_start(out=SGt[:], in_=sgt[:, :, t * 512:(t + 1) * 512]),
                             waits=[pw, (s_t[0], s_t[0].v), (s_t[1], s_t[1].v)], inc=s_ldsg, dma=True)

        load_ab(0)
        load_sg(0)
        yj = 0
        mj = 0
        sbk = 0
        pending_tp = []

        def flush_tp():
            while pending_tp:
                pending_tp.pop(0)()

        for t in range(8):
            sl = t % 2
            if t + 1 < 8:
                load_ab(t + 1)
            v_sg = ld_sg[t]
            v_x = ld_x[t]
            mt_w0 = (s_mm[0], s_mm[0].v), (s_mm[1], s_mm[1].v)
            for dc in range(8):
                ys = yj % 2
                ky = yj // 2

                def ymm(e, ys=ys, dc=dc, sl=sl):
                    ins = None
                    for k in range(4):
                        ins = e.matmul(PSY[:, 2 * ys, :], lhsT=WPA[:, k, dc * 128:(dc + 1) * 128], rhs=OAt[sl][:, k, :],
                                       start=(k == 0), stop=(k == 3))
                    for k in range(8):
                        ins = e.matmul(PSY[:, 2 * ys + 1, :], lhsT=WPB[:, k, dc * 128:(dc + 1) * 128], rhs=OBt[sl][:, k, :],
                                       start=(k == 0), stop=(k == 7))
                    return ins

                P.add("tensor", ymm, waits=[pw, (s_wcv, vw), (s_ldab[sl], ld_ab[t]), (s_t[ys], 2 * ky)], inc=s_y[ys])
                wt = [pw, (s_y[ys], ky + 1), (s_ldsg, v_sg), (s_mt, s_mt.v - 1)]
                P.add("vector", lambda e, ys=ys, dc=dc: e.tensor_tensor(out=T1[ys][:], in0=PSY[:, 2 * ys, :], in1=SGt[:, dc, :], op=ALU.mult),
                      waits=wt, inc=s_t[ys])
                vt = P.add("vector", lambda e, ys=ys, dc=dc: e.tensor_tensor(out=T2[ys][:], in0=PSY[:, 2 * ys + 1, :], in1=SGt[:, 8 + dc, :],
                                                                         op=ALU.mult), waits=wt, inc=s_t[ys])
                P.add("vector", lambda e, ys=ys, dc=dc: e.tensor_tensor(out=MT[:, dc, :], in0=T1[ys][:], in1=T2[ys][:], op=ALU.add),
                      waits=[pw, (s_t[ys], vt)] + list(mt_w0), inc=s_mt)
                yj += 1
                if dc == 1:
                    flush_tp()
            last_y_pe[t] = [(s_y[0], s_y[0].v), (s_y[1], s_y[1].v)]
            if t + 1 < 8:
                load_sg(t + 1)
            v_mt = s_mt.v
            for s in range(4):
                xs_o = sbk % 2
                for hf in range(2):
                    ms = mj % 2
                    km = mj // 2

                    def mmm(e, ms=ms, s=s, hf=hf):
                        ins = None
                        for k in range(8):
                            ins = e.matmul(PSM[:, ms, :], lhsT=MT[:, k, s * 128:(s + 1) * 128], rhs=WO[:, k, hf * 512:(hf + 1) * 512],
                                           start=(k == 0), stop=(k == 7))
                        return ins

                    P.add("tensor", mmm, waits=[pw, (s_mt, v_mt), (s_v[ms], km)], inc=s_mm[ms])
                    vv = P.add("vector", lambda e, ms=ms, s=s, hf=hf, sl=sl: e.scalar_tensor_tensor(
                        out=V[:, hf * 512:(hf + 1) * 512], in0=XS[sl][:, s, hf * 512:(hf + 1) * 512], scalar=ALPHA, in1=PSM[:, ms, :],
                        op0=ALU.mult, op1=ALU.add), waits=[pw, (s_mm[ms], km + 1), (s_ldx[sl], v_x), (s_ln, s_ln.v)], inc=s_v[ms])
                    mj += 1
                flush_tp()
                win = [(s_v[0], s_v[0].v), (s_v[1], s_v[1].v), (s_c, vconst), (s_so[xs_o], 16 * (sbk // 2)), (s_xb, sbk - 1)]
                vln = layer_norm_ops(P, V, ST6, MV, STD, RSTD, TMP, X1O[xs_o][:], LG, LB, EPSB, s_ln, win)
                row = 4 * t + s
                P.add("gpsimd", lambda e, xs_o=xs_o, row=row: e.dma_start(out=x1[:, row, :], in_=X1O[xs_o][:]),
                      waits=[pw, (s_ln, vln)], inc=s_so[xs_o], dma=True)
                vb = P.add("scalar", lambda e, xs_o=xs_o: e.copy(out=X1B[xs_o][:], in_=X1O[xs_o][:]),
                           waits=[pw, (s_ln, vln), (s_tp, sbk - 1)], inc=s_xb)

                def tp_group(t=t, s=s, xs_o=xs_o, vb=vb):
                    def tp(e):
                        ins = None
                        for k in range(8):
                            ins = e.transpose(out=PST[:, k, :], in_=X1B[xs_o][:, k * 128:(k + 1) * 128], identity=IDB[:])
                        return ins

                    vtp = P.add("tensor", tp, waits=[pw, (s_xb, vb), (s_tc, s_tc.v)], inc=s_tp)
                    P.add("scalar", lambda e: e.copy(out=X1TS[:, :, s * 128:(s + 1) * 128], in_=PST[:]),
                          waits=[pw, (s_tp, vtp), (s_sx, 16 * t)], inc=s_tc)
                    if s == 3:
                        P.add("gpsimd", lambda e: e.dma_start(out=x1t[:, :, t * 512:(t + 1) * 512], in_=X1TS[:]),
                              waits=[pw, (s_tc, s_tc.v)], inc=s_sx, dma=True)

                pending_tp.append(tp_group)
                sbk += 1
            last_v[t] = [(s_v[0], s_v[0].v), (s_v[1], s_v[1].v)]
        flush_tp()
        P.end_phase(JK[:])


def phase4b(nc, P, D):
    TT = 256
    NT = NQ // TT
    with ExitStack() as es:
        W1 = _sb(es, nc, "p5_W1", [128, 8, 4096], BF16)
        W2 = _sb(es, nc, "p5_W2", [128, 32, 1024], BF16)
        XT = [_sb(es, nc, f"p5_XT{i}", [128, 8, TT], BF16) for i in range(2)]
        X1 = _sb(es, nc, "p5_X1", [128, 2, 1024], F32)
        HT = _sb(es, nc, "p5_HT", [128, 32, TT], BF16)
        RT = [_sb(es, nc, f"p5_RT{i}", [128, TT], F32) for i in range(2)]
        V = _sb(es, nc, "p5_V", [128, 1024], F32)
        TMP = _sb(es, nc, "p5_TMP", [128, 1024], F32)
        YO = [_sb(es, nc, f"p5_YO{i}", [128, 1024], F32) for i in range(2)]
        LG = _sb(es, nc, "p5_LG", [128, 1024], F32)
        LB = _sb(es, nc, "p5_LB", [128, 1024], F32)
        ST6 = _sb(es, nc, "p5_ST6", [128, 12], F32)
        MV = _sb(es, nc, "p5_MV", [128, 2], F32)
        STD = _sb(es, nc, "p5_STD", [128, 1], F32)
        RSTD = _sb(es, nc, "p5_RSTD", [128, 1], F32)
        EPSB = _sb(es, nc, "p5_EPSB", [128, 1], F32)
        JK = _sb(es, nc, "p5_JK", [128, 1], F32)
        PSH = _ps(es, nc, "p5_PSH", [128, 4, 512], F32)
        PSF = _ps(es, nc, "p5_PSF", [128, 2, 512], F32)
        s_c = P.sem("p5c")
        s_wld = [P.sem(f"p5wld{i}") for i in range(2)]
        s_wcv = P.sem("p5wcv")
        s_ldt = [P.sem(f"p5ldt{i}") for i in range(2)]
        s_ldx = P.sem("p5ldx")
        s_h = [P.sem(f"p5h{i}") for i in range(4)]
        s_r = [P.sem(f"p5r{i}") for i in range(4)]
        s_sq = [P.sem(f"p5sq{i}") for i in range(2)]
        s_f = [P.sem(f"p5f{i}") for i in range(2)]
        s_v = [P.sem(f"p5v{i}") for i in range(2)]
        s_ln = P.sem("p5ln")
        s_so = [P.sem(f"p5so{i}") for i in range(2)]
        pw = P.pw()

        for (dst, src) in ((LG, D["ln2g"]), (LB, D["ln2b"])):
            vc = P.add("sync", lambda e, dst=dst, src=src: e.dma_start(out=dst[:], in_=src), waits=[pw], inc=s_c, dma=True)
        vconst = P.add("vector", lambda e: e.memset(EPSB[:], LN_EPS), waits=[pw, (s_c, vc)], inc=s_c)

        vw1 = None
        for q in range(4):
            vw1 = P.add("sync", lambda e, q=q: e.dma_start(out=W1[:, 2 * q:2 * q + 2, :], in_=D["W1B"][:, 2 * q:2 * q + 2, :]),
                        waits=[pw], inc=s_wld[0], dma=True)
        vw2 = None
        for q in range(4):
            vw2 = P.add("sync", lambda e, q=q: e.dma_start(out=W2[:, 8 * q:8 * q + 8, :], in_=D["W2B"][:, 8 * q:8 * q + 8, :]),
                        waits=[pw], inc=s_wld[1], dma=True)

        x1t = D["X1T"].rearrange("k p t -> p k t")
        x1 = D["X1"].rearrange("(s p) d -> p s d", p=128)
        yv = D["y"].rearrange("(s p) d -> p s d", p=128)
        last_h_pe = {}
        ld_t = {}

        def load_t(t):
            sl = t % 2
            w = [pw] + (last_h_pe[t - 2] if t >= 2 else [])
            ld_t[t] = P.add("sync", lambda e: e.dma_start(out=XT[sl][:], in_=x1t[:, :, t * TT:(t + 1) * TT]), waits=w,
                            inc=s_ldt[sl], dma=True)

        load_t(0)
        hj = 0
        fj = 0
        sbk = 0
        for t in range(NT):
            sl = t % 2
            if t + 1 < NT:
                load_t(t + 1)
            v_x = P.add("sync", lambda e, t=t: e.dma_start(out=X1[:], in_=x1[:, 2 * t:2 * t + 2, :]),
                        waits=[pw, (s_v[0], s_v[0].v), (s_v[1], s_v[1].v)], inc=s_ldx, dma=True)
            ht_w0 = [(s_f[0], s_f[0].v), (s_f[1], s_f[1].v)]
            for fc in range(32):
                hs = hj % 4
                kh = hj // 4
                rs = hj % 2
                kr = hj // 2

                def hmm(e, hs=hs, fc=fc, sl=sl):
                    ins = None
                    for k in range(8):
                        ins = e.matmul(PSH[:, hs, 0:TT], lhsT=W1[:, k, fc * 128:(fc + 1) * 128], rhs=XT[sl][:, k, :],
                                       start=(k == 0), stop=(k == 7))
                    return ins

                P.add("tensor", hmm, waits=[pw, (s_wld[0], vw1), (s_ldt[sl], ld_t[t]), (s_r[hs], kh)], inc=s_h[hs])
                P.add("scalar", lambda e, hs=hs, rs=rs: e.activation(out=RT[rs][:], in_=PSH[:, hs, 0:TT], func=AF.Relu),
                      waits=[pw, (s_h[hs], kh + 1), (s_sq[rs], kr)], inc=s_r[hs])
                P.add("vector",
                      lambda e, rs=rs, fc=fc: e.tensor_tensor(out=HT[:, fc, :], in0=RT[rs][:], in1=RT[rs][:], op=ALU.mult),
                      waits=[pw, (s_r[hs], kh + 1)] + ht_w0, inc=s_sq[rs])
                hj += 1
            last_h_pe[t] = [(s_h[i], s_h[i].v) for i in range(4)]
            v_sq = [(s_sq[0], s_sq[0].v), (s_sq[1], s_sq[1].v)]
            for s in range(2):
                yo = sbk % 2
                for hf in range(2):
                    fs = fj % 2
                    kf = fj // 2

                    def fmm(e, fs=fs, s=s, hf=hf):
                        ins = None
                        for fc in range(32):
                            ins = e.matmul(PSF[:, fs, :], lhsT=HT[:, fc, s * 128:(s + 1) * 128], rhs=W2[:, fc, hf * 512:(hf + 1) * 512],
                                           start=(fc == 0), stop=(fc == 31))
                        return ins

                    P.add("tensor", fmm, waits=[pw, (s_wld[1], vw2), (s_v[fs], kf)] + v_sq, inc=s_f[fs])
                    P.add("vector", lambda e, fs=fs, s=s, hf=hf: e.scalar_tensor_tensor(
                        out=V[:, hf * 512:(hf + 1) * 512], in0=X1[:, s, hf * 512:(hf + 1) * 512], scalar=ALPHA, in1=PSF[:, fs, :],
                        op0=ALU.mult, op1=ALU.add), waits=[pw, (s_f[fs], kf + 1), (s_ldx, v_x), (s_ln, s_ln.v)], inc=s_v[fs])
                    fj += 1
                win = [(s_v[0], s_v[0].v), (s_v[1], s_v[1].v), (s_c, vconst), (s_so[yo], 16 * (sbk // 2))]
                vln = layer_norm_ops(P, V, ST6, MV, STD, RSTD, TMP, YO[yo][:], LG, LB, EPSB, s_ln, win)
                row = 2 * t + s
                P.add("sync", lambda e, yo=yo, row=row: e.dma_start(out=yv[:, row, :], in_=YO[yo][:]),
                      waits=[pw, (s_ln, vln)], inc=s_so[yo], dma=True)
                sbk += 1
        P.end_phase(JK[:])


def build_nc(debug=False, phases=(1, 2, 3, 4, 5)):
    nc = bass.Bass("TRN2", target_bir_lowering=False)
    D = {}

    def din(name, shape):
        D[name] = nc.dram_tensor(name, shape, F32, kind="ExternalInput").ap()

    din("xT", [1024, NK]); din("xo", [NQ, 1024]); din("wp", [1024, COLS_IN]); din("bg", [128, 16])
    din("ga", [24, 128, 256]); din("ma", [128, 256]); din("gb", [8, 128, 256]); din("cb", [128, 8]); din("mb", [128, 256])
    din("ident", [128, 128]); din("pvalid", [128, 64]); din("lam", [128, 4, 64]); din("gs", [128, 128])
    din("ln1g", [128, 1024]); din("ln1b", [128, 1024]); din("ln2g", [128, 1024]); din("ln2b", [128, 1024])
    din("wpa", [512, 1024]); din("wpb", [1024, 1024]); din("wo", [1024, 1024]); din("w1", [1024, 4096]); din("w2", [4096, 1024])
    D["y"] = nc.dram_tensor("y", [NQ, 1024], F32, kind="ExternalOutput").ap()
    kind = "ExternalOutput" if debug else "Internal"

    def scr(name, shape, dt):
        D[name] = nc.dram_tensor(name, shape, dt, kind=kind).ap()

    scr("QTB", [8, 128, NQ], BF16); scr("KTB", [8, 128, NK], BF16); scr("VB", [NK, 1024], BF16)
    scr("QTA", [12, 128, NQ], BF16); scr("KTA", [12, 128, NKA], BF16); scr("VA", [NKA, 1536], BF16)
    scr("SG", [16, 128, NQ], BF16); scr("OAT", [4, 128, NQ], BF16); scr("OBT", [8, 128, NQ], BF16)
    scr("X1", [NQ, 1024], F32); scr("X1T", [8, 128, NQ], BF16)
    scr("WAB", [128, 20, 1024], BF16); scr("W1B", [128, 8, 4096], BF16); scr("W2B", [128, 32, 1024], BF16)
    with ExitStack() as es:
        P = Prog(nc, es)
        if 1 in phases:
            phase1(nc, P, D)
        if 2 in phases:
            phase2(nc, P, D)
        if 3 in phases:
            phase3(nc, P, D)
        if 4 in phases:
            phase4a(nc, P, D)
        if 5 in phases:
            phase4b(nc, P, D)
    return nc


def t5_bucket_np(dist):
    n = np.maximum(dist, 0)
    nf = np.maximum(n, 1).astype(np.float32)
    large = 16 + (np.log(nf / np.float32(16.0)) / np.float32(math.log(8.0)) * np.float32(16.0)).astype(np.int32)
    large = np.minimum(large, 31)
    return np.where(n < 16, n, large)


def prep_shared(inp):
    f = lambda a: np.ascontiguousarray(np.asarray(a, dtype=np.float32))
    w = np.asarray(inp["w_in"], dtype=np.float32)[0]
    B0 = 4608
    cols = []
    for h in range(8):
        cols += list(range(B0 + h * 64, B0 + h * 64 + 64)) + list(range(B0 + 512 + h * 64, B0 + 512 + h * 64 + 64))
    for h in range(8):
        cols += list(range(B0 + 1024 + h * 64, B0 + 1024 + h * 64 + 64)) + list(range(B0 + 1536 + h * 64, B0 + 1536 + h * 64 + 64))
    cols += list(range(0, 1536)) + list(range(1536, 3072)) + list(range(7680, 9728))
    cols += list(range(B0 + 2048, B0 + 3072)) + list(range(3072, 4608))
    assert len(cols) == COLS_IN
    sh = {"wp": f(w[:, cols])}
    sh["bg"] = f(np.asarray(inp["b_gate"], np.float32)[0].reshape(16, 128).T)
    rb = np.asarray(inp["rel_bias"], np.float32)
    i = np.arange(128)[:, None]
    c = np.arange(256)[None, :]
    ch = c // 128
    j = c % 128
    steps = (1 - ch) * 128 + j - i
    ga = np.zeros((24, 128, 256), np.float32)
    for g, d in enumerate(DILS):
        bk = t5_bucket_np(np.maximum(steps, 0) * d)
        for h in range(8):
            ga[g * 8 + h] = rb[bk, g * 8 + h]
    sh["ga"] = ga
    sh["ma"] = f(np.where((steps >= 0) & (steps <= 128), 0.0, MASKV / 8.0))
    dist = c - i
    bk = t5_bucket_np(np.maximum(dist, 0))
    gb = np.zeros((8, 128, 256), np.float32)
    for h in range(8):
        gb[h] = rb[bk, 24 + h]
    sh["gb"] = gb
    sh["cb"] = f(np.broadcast_to(rb[31, 24:32][None, :], (128, 8)))
    sh["mb"] = f(np.where(dist >= 0, 0.0, MASKV))
    sh["ident"] = np.eye(128, dtype=np.float32)
    lam = np.stack([np.asarray(inp[k], np.float32)[0] for k in ("lambda_q1", "lambda_q2", "lambda_k1", "lambda_k2")])
    sh["lam"] = f(np.broadcast_to(lam[None], (128, 4, 64)))
    sh["gs"] = f(np.broadcast_to(np.asarray(inp["subln_g"], np.float32)[0][None, :], (128, 128)))
    for k, n in (("ln1g", "ln1_g"), ("ln1b", "ln1_b"), ("ln2g", "ln2_g"), ("ln2b", "ln2_b")):
        sh[k] = f(np.broadcast_to(np.asarray(inp[n], np.float32)[0][None, :], (128, 1024)))
    sh["wpa"] = f(np.asarray(inp["w_proj_a"])[0]); sh["wpb"] = f(np.asarray(inp["w_proj_b"])[0])
    sh["wo"] = f(np.asarray(inp["w_out"])[0]); sh["w1"] = f(np.asarray(inp["w_mlp1"])[0]); sh["w2"] = f(np.asarray(inp["w_mlp2"])[0])
    return sh


def prep_core(x, c):
    b, par = c // 2, c % 2
    xb = np.asarray(x[b], dtype=np.float32)
    xl = np.zeros((NK, 1024), np.float32)
    if par == 1:
        xl[:] = xb
    else:
        xl[2048:] = xb[0:NK - 2048]
    own = np.concatenate([xl[2048:4096], xl[6144:8192]], axis=0)
    pv = np.full((128, 64), float(par), np.float32)
    return {"xT": np.ascontiguousarray(xl.T), "xo": np.ascontiguousarray(own), "pvalid": pv}


_NC_CACHE = {}


def kernel(**inputs):
    x = np.asarray(inputs["x"], dtype=np.float32)
    sh = prep_shared(inputs)
    in_maps = []
    for c in range(8):
        m = dict(sh)
        m.update(prep_core(x, c))
        in_maps.append(m)
    if "nc" not in _NC_CACHE:
        _NC_CACHE["nc"] = build_nc()
    res = run_bass_kernel_spmd(_NC_CACHE["nc"], in_maps, core_ids=list(range(8)))
    out = np.zeros((BATCH, SEQ, D_MODEL), np.float32)
    for c in range(8):
        b, par = c // 2, c % 2
        y = np.asarray(res.results[c]["y"], dtype=np.float32)
        r0 = 2048 if par == 1 else 0
        out[b, r0:r0 + 2048] = y[0:2048]
        out[b, r0 + 4096:r0 + 6144] = y[2048:4096]
    return out
```

```python
import math
from contextlib import ExitStack

import numpy as np

import concourse.bass as bass
import concourse.mybir as mybir
from concourse.bass_utils import run_bass_kernel_spmd

F32 = mybir.dt.float32
BF16 = mybir.dt.bfloat16
AF = mybir.ActivationFunctionType
ALU = mybir.AluOpType
AX = mybir.AxisListType

D_MODEL = 1024
SEQ = 8192
BATCH = 4
NQ = 4096
NK = 8192
NKA = 8192
COLS_IN = 9728
DILS = (1, 4, 16)
ALPHA = 2.0 ** 0.25
LAM_INIT = 0.8 - 0.6 * math.exp(0.0)
LN_EPS = 1e-5
MASKV = -240000.0
ENGS = ("sync", "scalar", "vector", "gpsimd", "tensor")

C_FQ, C_FK, C_AQ, C_AK, C_GT, C_BV, C_AV = 0, 1024, 2048, 3584, 5120, 7168, 8192


class Sem:
    __slots__ = ("h", "abs", "base", "name")

    def __init__(self, h, name):
        self.h, self.abs, self.base, self.name = h, 0, 0, name

    @property
    def v(self):
        return self.abs - self.base


class Prog:
    def __init__(self, nc, es):
        self.nc, self.es = nc, es
        self.q = {k: [] for k in ENGS}
        self.waited = {k: {} for k in ENGS}
        self.sems = []
        self.pool = []
        self.used = 0
        self.nphase = 0
        self.bar = self.sem("bar")

    def sem(self, name):
        if name != "bar" and self.used < len(self.pool):
            s = self.pool[self.used]
            self.used += 1
            s.base = s.abs
            return s
        s = Sem(self.es.enter_context(self.nc.semaphore(f"s{len(self.sems)}")), f"s{len(self.sems)}")
        self.sems.append(s)
        if name != "bar":
            self.pool.append(s)
            self.used += 1
        return s

    def add(self, eng, fn, waits=(), inc=None, dma=False):
        ws = []
        for w in waits:
            if w is None:
                continue
            s, v = w
            if v <= 0:
                continue
            assert v <= s.v, f"deadlock: {eng} waits {s.name}>={v} but only {s.v} scheduled"
            va = s.base + v
            if self.waited[eng].get(s.name, 0) >= va:
                continue
            self.waited[eng][s.name] = va
            ws.append((s, va))
        amt = 16 if dma else 1
        newv = None
        if inc is not None:
            inc.abs += amt
            newv = inc.v
        self.q[eng].append((ws, fn, inc, amt))
        return newv

    def end_phase(self, junk):
        waits = [(s, s.v) for s in self.sems if s is not self.bar]
        self.nphase += 1
        self.add("gpsimd", lambda e: e.memset(junk, 0.0), waits=waits, inc=self.bar)
        with self.nc.Block() as block:
            for name in ENGS:
                items = self.q[name]

                def body(e, items=items):
                    for ws, fn, inc, amt in items:
                        for s, v in ws:
                            e.wait_ge(s.h, v)
                        ins = fn(e)
                        if inc is not None:
                            ins.then_inc(inc.h, amt)

                getattr(block, name)(body)
        self.q = {k: [] for k in ENGS}
        self.used = 0
        for name in ENGS:
            if name == "gpsimd":
                continue
            self.waited[name].pop(self.bar.name, None)
        self.phase_wait = (self.bar, self.nphase)

    def pw(self):
        return getattr(self, "phase_wait", None)


def _sb(es, nc, name, shape, dt):
    return es.enter_context(nc.sbuf_tensor(name, shape, dt))


def _ps(es, nc, name, shape, dt):
    return es.enter_context(nc.psum_tensor(name, shape, dt))


def phase1(nc, P, D):
    with ExitStack() as es:
        W = _sb(es, nc, "p1_W", [128, 8, COLS_IN], BF16)
        WF = [_sb(es, nc, f"p1_WF{i}", [128, 8, 128], F32) for i in range(2)]
        XF = _sb(es, nc, "p1_XF", [128, 8, 512], F32)
        XB = [_sb(es, nc, f"p1_XB{i}", [128, 8, 512], BF16) for i in range(2)]
        OST = [_sb(es, nc, f"p1_OST{i}", [128, 512], BF16) for i in range(4)]
        BG = _sb(es, nc, "p1_BG", [128, 16], F32)
        JK = _sb(es, nc, "p1_JK", [128, 1], F32)
        PS = _ps(es, nc, "p1_PS", [128, 4, 512], F32)
        s_xld, s_xcv, s_wcv, s_misc = P.sem("p1xld"), P.sem("p1xcv"), P.sem("p1wcv"), P.sem("p1misc")
        s_wld = [P.sem(f"p1wld{i}") for i in range(2)]
        s_mm = [P.sem(f"p1mm{i}") for i in range(4)]
        s_ev = [P.sem(f"p1ev{i}") for i in range(4)]
        s_st = [P.sem(f"p1st{i}") for i in range(4)]
        xT = D["xT"].rearrange("(kc p) t -> p kc t", p=128)
        wp = D["wp"].rearrange("(kc p) c -> p kc c", p=128)
        pw = P.pw()

        v_bg = P.add("sync", lambda e: e.dma_start(out=BG[:], in_=D["bg"]), waits=[pw], inc=s_misc, dma=True)

        wcv_of = {}
        wl = [0]

        def load_w(cid):
            n = wl[0]
            slot = n % 2
            v = P.add("sync", lambda e: e.dma_start(out=WF[slot][:], in_=wp[:, :, cid * 128:(cid + 1) * 128]),
                      waits=[pw, (s_wcv, n - 1)], inc=s_wld[slot], dma=True)
            P.add("vector", lambda e: e.tensor_copy(out=W[:, :, cid * 128:(cid + 1) * 128], in_=WF[slot][:]),
                  waits=[pw, (s_wld[slot], v)], inc=s_wcv)
            wcv_of[cid] = s_wcv.v
            wl[0] += 1

        worder = (list(range(8, 16)) + list(range(56, 64)) + list(range(36, 40)) + list(range(72, 76))
                  + list(range(28, 36)) + list(range(64, 72))
                  + list(range(0, 8)) + list(range(16, 28)) + list(range(40, 56)))
        wplan = {-1: worder[0:24], 0: worder[24:40]}
        for t in range(1, 4):
            wplan[t] = worder[40 + 12 * (t - 1):40 + 12 * t]

        tile_last = {}

        def load_x(t):
            v = P.add("sync", lambda e: e.dma_start(out=XF[:], in_=xT[:, :, t * 512:(t + 1) * 512]),
                      waits=[pw, (s_xcv, t)], inc=s_xld, dma=True)
            P.add("vector", lambda e: e.tensor_copy(out=XB[t % 2][:], in_=XF[:]),
                  waits=[pw, (s_xld, v)] + (tile_last[t - 2] if t >= 2 else []), inc=s_xcv)

        def jobs_for(t):
            jobs = []
            own = (t // 4) % 2 == 1
            to = t - 4 if t < 8 else t - 8
            for h in range(8):
                jobs.append(("F", C_FK + h * 128, D["KTB"][h][:, t * 512:(t + 1) * 512], None))
            for cb in range(2):
                for s in range(4):
                    r0 = t * 512 + s * 128
                    jobs.append(("T", C_BV + cb * 512, D["VB"][r0:r0 + 128, cb * 512:(cb + 1) * 512], s))
            gs = (0, 1, 2) if (own or t % 4 == 3) else (2,)
            for g in gs:
                for hp in range(4):
                    m = g * 4 + hp
                    jobs.append(("F", C_AK + m * 128, D["KTA"][m][:, t * 512:(t + 1) * 512], None))
            for g in gs:
                for s in range(4):
                    r0 = t * 512 + s * 128
                    jobs.append(("T", C_AV + g * 512, D["VA"][r0:r0 + 128, g * 512:(g + 1) * 512], s))
            if own:
                for h in range(8):
                    jobs.append(("F", C_FQ + h * 128, D["QTB"][h][:, to * 512:(to + 1) * 512], None))
                for m in range(12):
                    jobs.append(("F", C_AQ + m * 128, D["QTA"][m][:, to * 512:(to + 1) * 512], None))
                for i in range(16):
                    jobs.append(("G", C_GT + i * 128, D["SG"][i][:, to * 512:(to + 1) * 512], i))
            return jobs

        load_x(0)
        for cid in wplan[-1]:
            load_w(cid)
        jc = 0
        for t in range(16):
            jobs = jobs_for(t)
            half = len(jobs) // 2
            for ji, (kind, col, dst, aux) in enumerate(jobs):
                if ji == half:
                    if t + 1 < 16:
                        load_x(t + 1)
                    for cid in wplan.get(t, []):
                        load_w(cid)
                slot = jc % 4
                k = jc // 4
                xb = XB[t % 2]
                if kind == "T":
                    wneed = max(wcv_of[col // 128 + i] for i in range(4))
                else:
                    wneed = wcv_of[col // 128]

                def mm(e, kind=kind, col=col, aux=aux, xb=xb, slot=slot):
                    ins = None
                    for kc in range(8):
                        if kind == "T":
                            ins = e.matmul(PS[:, slot, :], lhsT=xb[:, kc, aux * 128:(aux + 1) * 128],
                                           rhs=W[:, kc, col:col + 512], start=(kc == 0), stop=(kc == 7))
                        else:
                            ins = e.matmul(PS[:, slot, :], lhsT=W[:, kc, col:col + 128],
                                           rhs=xb[:, kc, :], start=(kc == 0), stop=(kc == 7))
                    return ins

                P.add("tensor", mm, waits=[pw, (s_xcv, t + 1), (s_wcv, wneed), (s_ev[slot], k)], inc=s_mm[slot])
                evw = [pw, (s_mm[slot], k + 1), (s_st[slot], 16 * k)]
                if kind == "G":
                    P.add("scalar", lambda e, slot=slot, aux=aux: e.activation(
                        out=OST[slot][:], in_=PS[:, slot, :], func=AF.Sigmoid, bias=BG[:, aux:aux + 1], scale=1.0),
                        waits=evw + [(s_misc, v_bg)], inc=s_ev[slot])
                elif jc % 2 == 0:
                    P.add("vector", lambda e, slot=slot: e.tensor_copy(out=OST[slot][:], in_=PS[:, slot, :]),
                          waits=evw, inc=s_ev[slot])
                else:
                    P.add("scalar", lambda e, slot=slot: e.copy(out=OST[slot][:], in_=PS[:, slot, :]),
                          waits=evw, inc=s_ev[slot])
                P.add("gpsimd", lambda e, slot=slot, dst=dst: e.dma_start(out=dst, in_=OST[slot][:]),
                      waits=[pw, (s_ev[slot], k + 1)], inc=s_st[slot], dma=True)
                jc += 1
            tile_last[t] = [(s_mm[s], s_mm[s].v) for s in range(4)]
        P.end_phase(JK[:])


def phase2(nc, P, D):
    with ExitStack() as es:
        QA = [_sb(es, nc, f"p2_QA{i}", [128, NQ], BF16) for i in range(2)]
        KA = [_sb(es, nc, f"p2_KA{i}", [128, NKA], BF16) for i in range(2)]
        VS = [_sb(es, nc, f"p2_VS{i}", [128, 64, 128], BF16) for i in range(2)]
        GF = [_sb(es, nc, f"p2_GF{i}", [128, 2, 256], F32) for i in range(2)]
        BTF = _sb(es, nc, "p2_BTF", [128, 2, 256], F32)
        EB = [_sb(es, nc, f"p2_EB{i}", [128, 2, 256], F32) for i in range(2)]
        PTF = [_sb(es, nc, f"p2_PTF{i}", [128, 2, 256], F32) for i in range(2)]
        MA = _sb(es, nc, "p2_MA", [128, 256], F32)
        IDF = _sb(es, nc, "p2_IDF", [128, 128], F32)
        IDB = _sb(es, nc, "p2_IDB", [128, 128], BF16)
        PVF = _sb(es, nc, "p2_PVF", [128, 64], F32)
        PV64 = _sb(es, nc, "p2_PV64", [128, 64], BF16)
        ONE64 = _sb(es, nc, "p2_ONE64", [128, 64], BF16)
        PT = [_sb(es, nc, f"p2_PT{i}", [128, 2, 256], BF16) for i in range(3)]
        NACC = _sb(es, nc, "p2_NACC", [128, NQ], F32)
        DACC = _sb(es, nc, "p2_DACC", [128, NQ], F32)
        OAS = _sb(es, nc, "p2_OAS", [128, NQ], BF16)
        JK = _sb(es, nc, "p2_JK", [128, 1], F32)
        S = _ps(es, nc, "p2_S", [128, 4, 512], F32)
        ND = _ps(es, nc, "p2_ND", [128, 4, 512], F32)
        s_c = P.sem("p2c")
        s_ld = [P.sem(f"p2ld{i}") for i in range(2)]
        s_bt = P.sem("p2bt")
        s_qk = [P.sem(f"p2qk{i}") for i in range(2)]
        s_ex = [P.sem(f"p2ex{i}") for i in range(2)]
        s_pv = [P.sem(f"p2pv{i}") for i in range(3)]
        s_nf = [P.sem(f"p2nf{i}") for i in range(2)]
        s_dv = P.sem("p2dv")
        s_fin = P.sem("p2fin")
        s_pm = [P.sem(f"p2pm{i}") for i in range(2)]
        s_st = P.sem("p2st")
        pw = P.pw()

        vc = P.add("sync", lambda e: e.dma_start(out=MA[:], in_=D["ma"]), waits=[pw], inc=s_c, dma=True)
        vc = P.add("sync", lambda e: e.dma_start(out=PVF[:], in_=D["pvalid"]), waits=[pw], inc=s_c, dma=True)
        P.add("vector", lambda e: e.tensor_copy(out=PV64[:], in_=PVF[:]), waits=[pw, (s_c, vc)], inc=s_c)
        vconst = P.add("vector", lambda e: e.memset(ONE64[:], 1.0), waits=[pw], inc=s_c)

        passes = [(hp, g) for hp in range(4) for g in range(3)]
        pass_last_pe = {}
        pass_last_dve = {}
        ld_val = {}
        bt_val = {}

        def sched_loads(pi):
            hp, g = passes[pi]
            d = DILS[g]
            slot = pi % 2
            nb = 64 // d
            w = [pw] + (pass_last_pe[pi - 2] if pi >= 2 else [])
            P.add("sync", lambda e: e.dma_start(out=QA[slot][:], in_=D["QTA"][g * 4 + hp]), waits=w, inc=s_ld[slot], dma=True)
            tranges = ((0, 16),) if d == 16 else ((3, 8), (11, 16))
            for (ta, tb) in tranges:
                P.add("sync", lambda e, ta=ta, tb=tb: e.dma_start(out=KA[slot][:, ta * 512:tb * 512],
                                                               in_=D["KTA"][g * 4 + hp][:, ta * 512:tb * 512]),
                      waits=w, inc=s_ld[slot], dma=True)
            vav = D["VA"].rearrange("(n i r) c -> r i n c", i=128, r=d)
            c0 = g * 512 + hp * 128
            bpt = 4 // d if d <= 4 else 0
            for r in range(d):
                if d == 16:
                    branges = ((0, nb),)
                else:
                    branges = tuple((ta * bpt, tb * bpt) for (ta, tb) in tranges)
                for (ba, bb) in branges:
                    for n0 in range(ba, bb, 16):
                        n1 = min(bb, n0 + 16)
                        P.add("sync", lambda e, r=r, n0=n0, n1=n1: e.dma_start(
                            out=VS[slot][:, r * nb + n0:r * nb + n1, :], in_=vav[r][:, n0:n1, c0:c0 + 128]),
                            waits=w, inc=s_ld[slot], dma=True)
            gh0 = g * 8 + 2 * hp
            v = P.add("sync", lambda e: e.dma_start(out=GF[slot][:], in_=D["ga"][gh0:gh0 + 2].rearrange("h i c -> i h c")),
                      waits=w, inc=s_ld[slot], dma=True)
            ld_val[pi] = v

        def sched_tables(pi):
            slot = pi % 2
            wb = [pw, (s_ld[slot], ld_val[pi]), (s_c, vconst)]
            v1 = None
            for hh in range(2):
                v1 = P.add("vector", lambda e, hh=hh: e.tensor_tensor(
                    out=BTF[:, hh, :], in0=GF[slot][:, hh, :], in1=MA[:], op=ALU.add),
                    waits=wb + [(s_bt, s_bt.v)], inc=s_bt)
            P.add("scalar", lambda e: e.activation(out=EB[slot][:], in_=BTF[:], func=AF.Exp),
                  waits=[pw, (s_bt, v1)] + (pass_last_dve[pi - 2] if pi >= 2 else []), inc=s_bt)
            bt_val[pi] = s_bt.v

        sched_loads(0)
        sched_tables(0)
        cn = {"qi": 0, "gi": 0, "fin": 0}

        def run_pass(pi, hp, g):
            d = DILS[g]
            slot = pi % 2
            nb = 64 // d
            nbB = 16 // d
            if pi + 1 < len(passes):
                sched_loads(pi + 1)

            def own_off(n):
                return 2048 if n < 2 * nbB else 4096

            groups = []
            if d == 16:
                for B in (1, 3):
                    base = 0 if B == 1 else 2048
                    for r0 in range(0, 16, 4):
                        qbs = [(r0 + i, B) for i in range(4)]
                        groups.append((qbs,
                                       lambda T, base=base, r0=r0: T[:, base:base + 2048].rearrange("p (j r) -> p r j", r=16)[:, r0:r0 + 4, :],
                                       lambda bank: ND[:, bank, 0:512].rearrange("p (r j) -> p r j", r=4)))
            else:
                for r in range(d):
                    for B in (1, 3):
                        for n0 in range(B * nbB, (B + 1) * nbB, 4):
                            qbs = [(r, n0 + i) for i in range(4)]
                            t0 = d * 128 * n0 + r - own_off(n0)
                            sl = slice(t0, t0 + 511 * d + 1, d)
                            groups.append((qbs, lambda T, sl=sl: T[:, sl], lambda bank: ND[:, bank, 0:512]))
            qblocks = [(r, n, gidx, i) for gidx, (qbs, _, _) in enumerate(groups) for i, (r, n) in enumerate(qbs)]
            pendl = []
            pass_dv_start = s_dv.v

            def emit_pv(info):
                (qslot, kq, pslot, r, n, gslot, kg, first, last, gidx, col) = info

                def pv(e):
                    ins = None
                    for hh in range(2):
                        for ch in range(2):
                            nk = n - 1 + ch
                            blk = r * nb + nk
                            ins = e.matmul(ND[hh * 64:(hh + 1) * 64, 2 * gslot, col:col + 128],
                                           lhsT=VS[slot][:, blk, hh * 64:(hh + 1) * 64],
                                           rhs=PT[pslot][:, hh, ch * 128:(ch + 1) * 128],
                                           start=(ch == 0), stop=(ch == 1))
                            vt = PV64 if nk < nbB else ONE64
                            ins = e.matmul(ND[hh * 64:(hh + 1) * 64, 2 * gslot + 1, col:col + 128],
                                           lhsT=vt[:, :], rhs=PT[pslot][:, hh, ch * 128:(ch + 1) * 128],
                                           start=(ch == 0), stop=(ch == 1))
                    return ins

                w = [(s_pm[qslot], kq + 1)]
                if first:
                    w.append((s_nf[gslot], kg))
                vpv = P.add("tensor", pv, waits=w, inc=s_pv[pslot])
                if last:
                    _, dst, srcf = groups[gidx]
                    wd = [(s_pv[pslot], vpv)]
                    if g > 0:
                        wd += [(s_dv, pass_dv_start), (s_nf[0], nf_start[0]), (s_nf[1], nf_start[1])]
                    else:
                        wd += [(s_fin, cn["fin"])]
                    if g == 0:
                        P.add("scalar", lambda e: e.copy(out=dst(NACC), in_=srcf(2 * gslot)), waits=[pw] + wd, inc=s_dv)
                        P.add("scalar", lambda e: e.copy(out=dst(DACC), in_=srcf(2 * gslot + 1)), waits=[pw] + wd, inc=s_nf[gslot])
                    else:
                        P.add("vector", lambda e: e.tensor_tensor(out=dst(NACC), in0=srcf(2 * gslot), in1=dst(NACC), op=ALU.add),
                              waits=wd, inc=s_dv)
                        P.add("vector", lambda e: e.tensor_tensor(out=dst(DACC), in0=srcf(2 * gslot + 1), in1=dst(DACC), op=ALU.add),
                              waits=wd, inc=s_nf[gslot])

            nf_start = [s_nf[0].v, s_nf[1].v]
            for bi, (r, n, gidx, gi_) in enumerate(qblocks):
                if bi == len(qblocks) // 2 and pi + 1 < len(passes):
                    sched_tables(pi + 1)
                qslot = cn["qi"] % 2
                kq = cn["qi"] // 2
                pslot = cn["qi"] % 3
                kp = cn["qi"] // 3
                first = (gi_ == 0)
                last = (gi_ == 3)
                if first:
                    gslot = cn["gi"] % 2
                    kg = cn["gi"] // 2
                    cn["gi"] += 1
                t0 = d * 128 * n + r - own_off(n)
                qs = slice(t0, t0 + 127 * d + 1, d)

                def qk(e, r=r, n=n, qslot=qslot, qs=qs):
                    ins = None
                    for hh in range(2):
                        bank = 2 * qslot + hh
                        for ch in range(2):
                            nk = n - 1 + ch
                            ks = slice(d * 128 * nk + r, d * 128 * nk + r + 127 * d + 1, d)
                            ins = e.matmul(S[:, bank, ch * 128:(ch + 1) * 128],
                                           lhsT=KA[slot][hh * 64:(hh + 1) * 64, ks],
                                           rhs=QA[slot][hh * 64:(hh + 1) * 64, qs],
                                           start=True, stop=True)
                    return ins

                P.add("tensor", qk, waits=[pw, (s_ld[slot], ld_val[pi]), (s_c, vconst), (s_ex[qslot], kq)],
                      inc=s_qk[qslot])
                P.add("scalar", lambda e, qslot=qslot: e.activation(
                    out=PTF[qslot][:], in_=S[:, 2 * qslot:2 * qslot + 2, 0:256], func=AF.Exp, scale=0.125),
                    waits=[pw, (s_qk[qslot], kq + 1), (s_pm[qslot], kq)], inc=s_ex[qslot])
                P.add("vector", lambda e, qslot=qslot, pslot=pslot: e.tensor_tensor(
                    out=PT[pslot][:], in0=PTF[qslot][:], in1=EB[slot][:], op=ALU.mult),
                    waits=[pw, (s_ex[qslot], kq + 1), (s_pv[pslot], kp), (s_bt, bt_val[pi])], inc=s_pm[qslot])
                pendl.append((qslot, kq, pslot, r, n, gslot, kg, first, last, gidx, gi_ * 128))
                while len(pendl) > 2:
                    emit_pv(pendl.pop(0))
                cn["qi"] += 1
            while pendl:
                emit_pv(pendl.pop(0))
            pass_last_pe[pi] = [(s_pv[i], s_pv[i].v) for i in range(3)]
            pass_last_dve[pi] = [(s_pm[0], s_pm[0].v), (s_pm[1], s_pm[1].v)]
            if g == 2:
                wd = [(s_dv, s_dv.v), (s_nf[0], s_nf[0].v), (s_nf[1], s_nf[1].v)]
                v1 = P.add("vector", lambda e: e.reciprocal(out=DACC[:], in_=DACC[:]), waits=wd, inc=s_fin)
                v2 = P.add("vector", lambda e: e.tensor_tensor(out=OAS[:], in0=NACC[:], in1=DACC[:], op=ALU.mult),
                           waits=[(s_fin, v1), (s_st, 16 * hp)], inc=s_fin)
                cn["fin"] = v2
                P.add("gpsimd", lambda e, hp=hp: e.dma_start(out=D["OAT"][hp], in_=OAS[:]),
                      waits=[pw, (s_fin, v2)], inc=s_st, dma=True)
        for pi, (hp, g) in enumerate(passes):
            run_pass(pi, hp, g)
        P.end_phase(JK[:])


def phase3(nc, P, D):
    with ExitStack() as es:
        KT = [_sb(es, nc, f"p3_KT{i}", [128, NK], BF16) for i in range(2)]
        QT = [_sb(es, nc, f"p3_QT{i}", [128, NQ], BF16) for i in range(2)]
        VH = [_sb(es, nc, f"p3_VH{i}", [128, 64, 129], BF16) for i in range(2)]
        GB = _sb(es, nc, "p3_GB", [128, 8, 256], F32)
        CB = _sb(es, nc, "p3_CB", [128, 8], F32)
        MB = _sb(es, nc, "p3_MB", [128, 256], F32)
        BTF = _sb(es, nc, "p3_BTF", [128, 256], F32)
        BTA = _sb(es, nc, "p3_BTA", [128, 8, 2, 256], F32)
        IDF = _sb(es, nc, "p3_IDF", [128, 128], F32)
        IDB = _sb(es, nc, "p3_IDB", [128, 128], BF16)
        ZB = _sb(es, nc, "p3_ZB", [128, 512], BF16)
        PVF = _sb(es, nc, "p3_PVF", [128, 64], F32)
        LAMI = _sb(es, nc, "p3_LAMI", [128, 4, 64], F32)
        LPR = _sb(es, nc, "p3_LPR", [128, 2, 64], F32)
        LS = _sb(es, nc, "p3_LS", [128, 2], F32)
        LE = _sb(es, nc, "p3_LE", [128, 2], F32)
        NLAM = _sb(es, nc, "p3_NLAM", [128, 1], F32)
        GS = _sb(es, nc, "p3_GS", [128, 128], F32)
        EPSB = _sb(es, nc, "p3_EPSB", [128, 1], F32)
        PT = [_sb(es, nc, f"p3_PT{i}", [128, 2, 512], BF16) for i in range(3)]
        EP = _sb(es, nc, "p3_EP", [128, 3, 512], F32)
        RD = _sb(es, nc, "p3_RD", [128, 8], F32)
        TT = _sb(es, nc, "p3_TT", [128, 128], F32)
        OO = _sb(es, nc, "p3_OO", [128, 4, 128], F32)
        SQ = _sb(es, nc, "p3_SQ", [128, 128], F32)
        SSQ = _sb(es, nc, "p3_SSQ", [128, 4], F32)
        LNV = _sb(es, nc, "p3_LNV", [128, 4], F32)
        RSTD = _sb(es, nc, "p3_RSTD", [128, 4], F32)
        ON = _sb(es, nc, "p3_ON", [128, 4, 128], BF16)
        OBS = [_sb(es, nc, f"p3_OBS{i}", [128, 512], BF16) for i in range(2)]
        JK = _sb(es, nc, "p3_JK", [128, 1], F32)
        CF = [_sb(es, nc, f"p3_CF{i}", [128, 1024], F32) for i in range(2)]
        CBT = [_sb(es, nc, f"p3_CBT{i}", [128, 1024], BF16) for i in range(2)]
        S = _ps(es, nc, "p3_S", [128, 4, 512], F32)
        ACC = _ps(es, nc, "p3_ACC", [128, 3, 512], F32)
        TP = _ps(es, nc, "p3_TP", [128, 4, 128], BF16)
        s_c = P.sem("p3c")
        s_ld = [P.sem(f"p3ld{i}") for i in range(2)]
        s_qk = [P.sem(f"p3qk{i}") for i in range(2)]
        s_ex = [P.sem(f"p3ex{i}") for i in range(2)]
        s_pv = [P.sem(f"p3pv{i}") for i in range(3)]
        s_af = P.sem("p3af")
        s_ep = P.sem("p3ep")
        s_ea = P.sem("p3ea")
        s_tp = P.sem("p3tp")
        s_tc = P.sem("p3tc")
        s_st = [P.sem(f"p3st{i}") for i in range(2)]
        s_cl = [P.sem(f"p3cl{i}") for i in range(2)]
        s_cc = P.sem("p3cc")
        s_cs = [P.sem(f"p3cs{i}") for i in range(2)]
        s_nb = [P.sem(f"p3nb{i}") for i in range(2)]
        pw = P.pw()

        chunks = []
        w1v = D["w1"].rearrange("(kc p) f -> p kc f", p=128)
        w2v = D["w2"].rearrange("(fc p) d -> p fc d", p=128)
        k0 = 0
        for (wsrc, nk) in ((D["wpa"], 4), (D["wpb"], 8), (D["wo"], 8)):
            wv = wsrc.rearrange("(kc p) c -> p kc c", p=128)
            for k in range(nk):
                chunks.append((wv[:, k, :], D["WAB"][:, k0 + k, :]))
            k0 += nk
        for kc in range(8):
            for q in range(4):
                chunks.append((w1v[:, kc, q * 1024:(q + 1) * 1024], D["W1B"][:, kc, q * 1024:(q + 1) * 1024]))
        for fc in range(32):
            chunks.append((w2v[:, fc, :], D["W2B"][:, fc, :]))
        cv = {"k": 0, "ld": {}}

        def wconv_step():
            k = cv["k"]
            if k - 1 >= 0 and k - 1 < len(chunks):
                j = k - 1
                sl = j % 2
                src_, dst_ = chunks[j]
                vcast = P.add("vector", lambda e, sl=sl: e.tensor_copy(out=CBT[sl][:], in_=CF[sl][:]),
                              waits=[pw, (s_cl[sl], cv["ld"][j]), (s_cs[sl], 16 * (j // 2))], inc=s_cc)
                P.add("gpsimd", lambda e, sl=sl, dst_=dst_: e.dma_start(out=dst_, in_=CBT[sl][:]), waits=[pw, (s_cc, vcast)],
                      inc=s_cs[sl], dma=True)
            if k < len(chunks):
                sl = k % 2
                src_, dst_ = chunks[k]
                cv["ld"][k] = P.add("sync", lambda e, sl=sl, src_=src_: e.dma_start(out=CF[sl][:], in_=src_), waits=[pw, (s_cc, k - 1)],
                                    inc=s_cl[sl], dma=True)
            cv["k"] += 1

        def accap(m, j):
            idx = m * 4 + j
            return idx // 3, (idx % 3) * 170

        for (dst, src) in ((GB, D["gb"].rearrange("h i c -> i h c")), (CB, D["cb"]), (MB, D["mb"]), (IDF, D["ident"]),
                           (PVF, D["pvalid"]), (LAMI, D["lam"]), (GS, D["gs"])):
            vc = P.add("sync", lambda e, dst=dst, src=src: e.dma_start(out=dst[:], in_=src), waits=[pw], inc=s_c, dma=True)
        w0 = [pw, (s_c, vc)]
        P.add("vector", lambda e: e.tensor_copy(out=IDB[:], in_=IDF[:]), waits=w0, inc=s_c)
        P.add("vector", lambda e: e.memset(ZB[:], 0.0), waits=w0, inc=s_c)
        P.add("vector", lambda e: e.memset(EPSB[:], LN_EPS), waits=w0, inc=s_c)
        v = P.add("vector", lambda e: e.tensor_scalar(out=GS[:], in0=GS[:], scalar1=1.0 - LAM_INIT, scalar2=None, op0=ALU.mult),
                  waits=w0, inc=s_c)
        for i in range(2):
            P.add("vector", lambda e, i=i: e.memset(VH[i][:, 16:64, 128:129], 1.0), waits=w0, inc=s_c)
            P.add("vector", lambda e, i=i: e.tensor_copy(out=VH[i][:, 0:16, 128:129],
                                                         in_=PVF[:, 0:16].rearrange("p (a b) -> p a b", b=1)),
                  waits=w0, inc=s_c)
        v = P.add("vector", lambda e: e.tensor_tensor(out=LPR[:], in0=LAMI[:, 0:2, :], in1=LAMI[:, 2:4, :], op=ALU.mult),
                  waits=w0, inc=s_c)
        v = P.add("vector", lambda e: e.reduce_sum(out=LS[:], in_=LPR[:], axis=AX.X), waits=[(s_c, v)], inc=s_c)
        v = P.add("scalar", lambda e: e.activation(out=LE[:], in_=LS[:], func=AF.Exp), waits=[pw, (s_c, v)], inc=s_c)
        v = P.add("vector", lambda e: e.tensor_tensor(out=NLAM[:], in0=LE[:, 1:2], in1=LE[:, 0:1], op=ALU.subtract),
                  waits=[(s_c, v)], inc=s_c)
        v = P.add("vector", lambda e: e.tensor_scalar(out=NLAM[:], in0=NLAM[:], scalar1=-LAM_INIT, scalar2=None, op0=ALU.add),
                  waits=[(s_c, v)], inc=s_c)
        for h in range(8):
            v = P.add("vector", lambda e, h=h: e.tensor_scalar(out=BTF[:], in0=GB[:, h, :], scalar1=CB[:, h:h + 1], scalar2=8.0,
                                                               op0=ALU.subtract, op1=ALU.mult), waits=[(s_c, v)], inc=s_c)
            v = P.add("vector", lambda e, h=h: e.tensor_tensor(out=BTA[:, h, 0, :], in0=BTF[:], in1=MB[:], op=ALU.add), waits=[(s_c, v)], inc=s_c)
            v = P.add("vector", lambda e, h=h: e.tensor_tensor(out=BTA[:, h, 1, :], in0=BTF[:], in1=MB[:], op=ALU.add), waits=[(s_c, v)], inc=s_c)
        vconst = v

        head_last_pe = {}
        ld_val = {}
        vbv = D["VB"].rearrange("(kb p) c -> p kb c", p=128)

        def sched_loads(h):
            slot = h % 2
            w = [pw, (s_c, vconst)] + (head_last_pe[h - 2] if h >= 2 else [])
            P.add("sync", lambda e: e.dma_start(out=KT[slot][:], in_=D["KTB"][h]), waits=w, inc=s_ld[slot], dma=True)
            P.add("sync", lambda e: e.dma_start(out=QT[slot][:], in_=D["QTB"][h]), waits=w, inc=s_ld[slot], dma=True)
            for q in range(4):
                v = P.add("sync", lambda e, q=q: e.dma_start(out=VH[slot][:, q * 16:(q + 1) * 16, 0:128],
                                                           in_=vbv[:, q * 16:(q + 1) * 16, h * 128:(h + 1) * 128]),
                          waits=w, inc=s_ld[slot], dma=True)
            ld_val[h] = v

        deferred = []
        ui = [0]

        def flush(force=False):
            while deferred and (force or deferred[0][0] <= ui[0]):
                _, fn = deferred.pop(0)
                fn()

        sched_loads(0)
        cn = {"nqt": 0}
        pending = []

        def emit_pv(info):
            (sslot, ku, pslot, kb, jmin, firstu, lastu, hs, kq_tile, on_last) = info

            def pv_bank(bsel):
                def pv(e):
                    ins = None
                    if firstu:
                        e.matmul(ACC[:, bsel, :], lhsT=ZB[:, 0:128], rhs=ZB[:, :], start=True, stop=True, skip_group_check=True)
                    for m in range(2):
                        for j in range(jmin, 4):
                            b, off = accap(m, j)
                            if bsel is not None and b != bsel:
                                continue
                            ins = e.matmul(ACC[:, b, off:off + 129], lhsT=PT[pslot][:, m, j * 128:(j + 1) * 128],
                                           rhs=VH[hs][:, kb, :], start=False, stop=lastu, skip_group_check=True)
                    return ins
                return pv

            w = [(s_ex[sslot], ku + 1)]
            if firstu:
                P.add("tensor", pv_bank(0), waits=w + [(s_af, 3 * kq_tile - 2)], inc=None)
                P.add("tensor", pv_bank(1), waits=[(s_af, 3 * kq_tile - 1)], inc=None)
                v = P.add("tensor", pv_bank(2), waits=[(s_af, 3 * kq_tile)], inc=s_pv[pslot])
            else:
                v = P.add("tensor", pv_bank(None), waits=w, inc=s_pv[pslot])
            if lastu:
                on_last(pslot, v)

        def drain(keep):
            while len(pending) > keep:
                emit_pv(pending.pop(0))

        def run_qt(h, qt):
                hs = h % 2
                base = 4 * (qt + 4 if qt < 4 else qt + 8)
                nkb = base + 4
                kq_tile = cn["nqt"]
                cn["nqt"] += 1
                nqt = cn["nqt"]

                def on_last(last_slot, vlast):
                    epilogue(last_slot, vlast)

                for kb in range(nkb):
                    u = ui[0]
                    slot = u % 2
                    ku = u // 2
                    pslot = u % 3
                    kp = u // 3
                    jd = kb - base
                    c0 = max(jd, 0) * 128
                    jmin = max(jd, 0)

                    def qk(e, kb=kb, c0=c0, slot=slot, qt=qt):
                        ins = None
                        for m in range(2):
                            ins = e.matmul(S[:, 2 * slot + m, c0:512], lhsT=KT[hs][m * 64:(m + 1) * 64, kb * 128:(kb + 1) * 128],
                                           rhs=QT[hs][m * 64:(m + 1) * 64, qt * 512 + c0:(qt + 1) * 512], start=True, stop=True)
                        return ins

                    vqk = P.add("tensor", qk, waits=[pw, (s_ld[hs], ld_val[h]), (s_c, vconst), (s_ex[slot], ku)], inc=s_qk[slot])
                    wex = [pw, (s_qk[slot], ku + 1), (s_pv[pslot], kp)]
                    if jd >= -1:
                        if jd == -1:
                            oc, bc, n = 0, 128, 128
                        elif jd == 3:
                            oc, bc, n = 384, 0, 128
                        else:
                            oc, bc, n = jd * 128, 0, 256
                        vnb = P.add("vector", lambda e, slot=slot, oc=oc, bc=bc, n=n: e.tensor_tensor(
                            out=S[:, 2 * slot:2 * slot + 2, oc:oc + n], in0=S[:, 2 * slot:2 * slot + 2, oc:oc + n],
                            in1=BTA[:, h, :, bc:bc + n], op=ALU.add), waits=[pw, (s_qk[slot], vqk)], inc=s_nb[slot])
                        wex.append((s_nb[slot], vnb))
                    P.add("scalar", lambda e, slot=slot, c0=c0, pslot=pslot: e.activation(
                        out=PT[pslot][:, :, c0:512], in_=S[:, 2 * slot:2 * slot + 2, c0:512], func=AF.Exp, scale=0.125),
                        waits=wex, inc=s_ex[slot])
                    pending.append((slot, ku, pslot, kb, jmin, kb == 0, kb == nkb - 1, hs, kq_tile, on_last))
                    drain(2)
                    ui[0] += 1
                    if ui[0] % 24 == 0:
                        wconv_step()
                    flush()

                def epilogue(last_slot, vlast):

                    def stage_a(last_slot=last_slot, vlast=vlast, h=h, qt=qt, k=nqt):
                        v = None
                        for b in range(3):
                            v = P.add("vector", lambda e, b=b: e.tensor_copy(out=EP[:, b, :], in_=ACC[:, b, :]),
                                      waits=[(s_pv[last_slot], vlast), (s_ep, s_ep.v)], inc=s_af)
                        we = [(s_af, v)]
                        for m in range(2):
                            for j in range(4):
                                b, off = accap(m, j)
                                idx = m * 4 + j
                                P.add("vector", lambda e, b=b, off=off, idx=idx: e.reciprocal(
                                    out=RD[:, idx:idx + 1], in_=EP[:, b, off + 128:off + 129]), waits=we, inc=s_ep)
                        v = P.add("vector", lambda e: e.tensor_scalar(out=RD[:, 4:8], in0=RD[:, 4:8], scalar1=NLAM[:, 0:1], scalar2=None,
                                                                      op0=ALU.mult), waits=[(s_ep, s_ep.v)], inc=s_ep)
                        for j in range(4):
                            b1, o1 = accap(0, j)
                            b2, o2 = accap(1, j)
                            v = P.add("vector", lambda e, b2=b2, o2=o2, j=j: e.tensor_scalar(
                                out=TT[:], in0=EP[:, b2, o2:o2 + 128], scalar1=RD[:, 4 + j:5 + j], scalar2=None, op0=ALU.mult),
                                waits=[(s_ep, v)], inc=s_ep)
                            v = P.add("vector", lambda e, b1=b1, o1=o1, j=j: e.scalar_tensor_tensor(
                                out=OO[:, j, :], in0=EP[:, b1, o1:o1 + 128], scalar=RD[:, j:j + 1], in1=TT[:],
                                op0=ALU.mult, op1=ALU.add), waits=[(s_ep, v)], inc=s_ep)
                            v = P.add("vector", lambda e, j=j: e.scalar_tensor_tensor(
                                out=SQ[:], in0=OO[:, j, :], scalar=1.0, in1=OO[:, j, :], op0=ALU.mult, op1=ALU.mult,
                                accum_out=SSQ[:, j:j + 1]), waits=[(s_ep, v)], inc=s_ep)
                        return v

                    va = stage_a()

                    def stage_bc(va=va):
                        v = P.add("scalar", lambda e: e.activation(out=LNV[:], in_=SSQ[:], func=AF.Ln, bias=EPSB[:, 0:1], scale=1.0 / 128.0),
                                  waits=[(s_ep, va), (s_ea, s_ea.v)], inc=s_ea)
                        v = P.add("scalar", lambda e: e.activation(out=RSTD[:], in_=LNV[:], func=AF.Exp, scale=-0.5),
                                  waits=[(s_ea, v)], inc=s_ea)
                        vv = None
                        for j in range(4):
                            vv = P.add("vector", lambda e, j=j: e.scalar_tensor_tensor(
                                out=ON[:, j, :], in0=OO[:, j, :], scalar=RSTD[:, j:j + 1], in1=GS[:], op0=ALU.mult, op1=ALU.mult),
                                waits=[(s_ea, v), (s_tp, s_tp.v)], inc=s_ep)
                        return vv

                    def stage_de(vc, h=h, qt=qt, k=nqt):
                        def tp(e):
                            ins = None
                            for j in range(4):
                                ins = e.transpose(out=TP[:, j, :], in_=ON[:, j, :], identity=IDB[:])
                            return ins
                        v = P.add("tensor", tp, waits=[(s_ep, vc), (s_tc, s_tc.v)], inc=s_tp)
                        os_ = (k - 1) % 2
                        v2 = P.add("vector", lambda e: e.tensor_copy(out=OBS[os_][:].rearrange("p (a b) -> p a b", a=4), in_=TP[:]),
                                   waits=[(s_tp, v), (s_st[os_], 16 * ((k - 1) // 2))], inc=s_tc)
                        P.add("gpsimd", lambda e: e.dma_start(out=D["OBT"][h][:, qt * 512:(qt + 1) * 512], in_=OBS[os_][:]),
                              waits=[pw, (s_tc, v2)], inc=s_st[os_], dma=True)

                    def chain(stage_bc=stage_bc, stage_de=stage_de):
                        vc = stage_bc()
                        deferred.append((ui[0] + 4, lambda: stage_de(vc)))

                    deferred.append((ui[0] + 12, chain))
        for h in range(8):
            if h + 1 < 8:
                sched_loads(h + 1)
            for qt in range(8):
                run_qt(h, qt)
            drain(0)
            head_last_pe[h] = [(s_pv[i], s_pv[i].v) for i in range(3)]
        flush(force=True)
        flush(force=True)
        while cv["k"] <= len(chunks):
            wconv_step()
        P.end_phase(JK[:])


def layer_norm_ops(P, V, ST6, MV, STD, RSTD, TMP, OUT, LG, LB, EPSB, sem, wait_in):
    v = None
    for hf in range(2):
        v = P.add("vector", lambda e, hf=hf: e.bn_stats(out=ST6[:, hf * 6:(hf + 1) * 6], in_=V[:, hf * 512:(hf + 1) * 512]),
                  waits=wait_in + [(sem, sem.v)], inc=sem)
    v = P.add("vector", lambda e: e.bn_aggr(out=MV[:], in_=ST6[:]), waits=[(sem, v)], inc=sem)
    v = P.add("scalar", lambda e: e.activation(out=STD[:], in_=MV[:, 1:2], func=AF.Sqrt, bias=EPSB[:, 0:1], scale=1.0),
              waits=[(sem, v)], inc=sem)
    v = P.add("vector", lambda e: e.reciprocal(out=RSTD[:], in_=STD[:]), waits=[(sem, v)], inc=sem)
    v = P.add("vector", lambda e: e.scalar_tensor_tensor(out=TMP[:], in0=V[:], scalar=MV[:, 0:1], in1=LG[:],
                                                         op0=ALU.subtract, op1=ALU.mult), waits=[(sem, v)], inc=sem)
    v = P.add("vector", lambda e: e.scalar_tensor_tensor(out=OUT, in0=TMP[:], scalar=RSTD[:, 0:1], in1=LB[:],
                                                         op0=ALU.mult, op1=ALU.add), waits=[(sem, v)], inc=sem)
    return v


def load_cast_weight(P, pw, dst_chunks, src_chunks, WF, s_wld, s_wcv, cnt):
    for dst, src in zip(dst_chunks, src_chunks):
        n = cnt[0]
        slot = n % 2
        v = P.add("sync", lambda e, slot=slot, src=src: e.dma_start(out=WF[slot][:], in_=src),
                  waits=[pw, (s_wcv, n - 1)], inc=s_wld[slot], dma=True)
        P.add("vector", lambda e, slot=slot, dst=dst: e.tensor_copy(out=dst, in_=WF[slot][:]),
              waits=[pw, (s_wld[slot], v)], inc=s_wcv)
        cnt[0] += 1
    return s_wcv.v


def phase4a(nc, P, D):
    with ExitStack() as es:
        WPA = _sb(es, nc, "p4_WPA", [128, 4, 1024], BF16)
        WPB = _sb(es, nc, "p4_WPB", [128, 8, 1024], BF16)
        WO = _sb(es, nc, "p4_WO", [128, 8, 1024], BF16)
        OAt = [_sb(es, nc, f"p4_OA{i}", [128, 4, 512], BF16) for i in range(2)]
        OBt = [_sb(es, nc, f"p4_OB{i}", [128, 8, 512], BF16) for i in range(2)]
        SGt = _sb(es, nc, "p4_SG", [128, 16, 512], BF16)
        XS = [_sb(es, nc, f"p4_XS{i}", [128, 4, 1024], F32) for i in range(2)]
        MT = _sb(es, nc, "p4_MT", [128, 8, 512], BF16)
        T1 = [_sb(es, nc, f"p4_T1{i}", [128, 512], F32) for i in range(2)]
        T2 = [_sb(es, nc, f"p4_T2{i}", [128, 512], F32) for i in range(2)]
        V = _sb(es, nc, "p4_V", [128, 1024], F32)
        TMP = _sb(es, nc, "p4_TMP", [128, 1024], F32)
        X1O = [_sb(es, nc, f"p4_X1O{i}", [128, 1024], F32) for i in range(2)]
        X1B = [_sb(es, nc, f"p4_X1B{i}", [128, 1024], BF16) for i in range(2)]
        X1TS = _sb(es, nc, "p4_X1TS", [128, 8, 512], BF16)
        LG = _sb(es, nc, "p4_LG", [128, 1024], F32)
        LB = _sb(es, nc, "p4_LB", [128, 1024], F32)
        ST6 = _sb(es, nc, "p4_ST6", [128, 12], F32)
        MV = _sb(es, nc, "p4_MV", [128, 2], F32)
        STD = _sb(es, nc, "p4_STD", [128, 1], F32)
        RSTD = _sb(es, nc, "p4_RSTD", [128, 1], F32)
        EPSB = _sb(es, nc, "p4_EPSB", [128, 1], F32)
        IDF = _sb(es, nc, "p4_IDF", [128, 128], F32)
        IDB = _sb(es, nc, "p4_IDB", [128, 128], BF16)
        JK = _sb(es, nc, "p4_JK", [128, 1], F32)
        PSY = _ps(es, nc, "p4_PSY", [128, 4, 512], F32)
        PSM = _ps(es, nc, "p4_PSM", [128, 2, 512], F32)
        PST = _ps(es, nc, "p4_PST", [128, 8, 128], BF16)
        s_c = P.sem("p4c")
        s_wld = [P.sem(f"p4wld{i}") for i in range(2)]
        s_wcv = P.sem("p4wcv")
        s_ldab = [P.sem(f"p4ldab{i}") for i in range(2)]
        s_ldsg = P.sem("p4ldsg")
        s_ldx = [P.sem(f"p4ldx{i}") for i in range(2)]
        s_y = [P.sem(f"p4y{i}") for i in range(2)]
        s_t = [P.sem(f"p4t{i}") for i in range(2)]
        s_mt = P.sem("p4mt")
        s_mm = [P.sem(f"p4mm{i}") for i in range(2)]
        s_v = [P.sem(f"p4v{i}") for i in range(2)]
        s_ln = P.sem("p4ln")
        s_xb = P.sem("p4xb")
        s_tp = P.sem("p4tp")
        s_tc = P.sem("p4tc")
        s_so = [P.sem(f"p4so{i}") for i in range(2)]
        s_sx = P.sem("p4sx")
        pw = P.pw()

        for (dst, src) in ((LG, D["ln1g"]), (LB, D["ln1b"]), (IDF, D["ident"])):
            vc = P.add("sync", lambda e, dst=dst, src=src: e.dma_start(out=dst[:], in_=src), waits=[pw], inc=s_c, dma=True)
        P.add("vector", lambda e: e.tensor_copy(out=IDB[:], in_=IDF[:]), waits=[pw, (s_c, vc)], inc=s_c)
        vconst = P.add("vector", lambda e: e.memset(EPSB[:], LN_EPS), waits=[pw], inc=s_c)

        P.add("sync", lambda e: e.dma_start(out=WPA[:], in_=D["WAB"][:, 0:4, :]), waits=[pw], inc=s_wcv, dma=True)
        P.add("sync", lambda e: e.dma_start(out=WPB[:], in_=D["WAB"][:, 4:12, :]), waits=[pw], inc=s_wcv, dma=True)
        vw = P.add("sync", lambda e: e.dma_start(out=WO[:], in_=D["WAB"][:, 12:20, :]), waits=[pw], inc=s_wcv, dma=True)

        oat = D["OAT"].rearrange("k p t -> p k t")
        obt = D["OBT"].rearrange("k p t -> p k t")
        sgt = D["SG"].rearrange("k p t -> p k t")
        xo = D["xo"].rearrange("(s p) d -> p s d", p=128)
        x1 = D["X1"].rearrange("(s p) d -> p s d", p=128)
        x1t = D["X1T"].rearrange("k p t -> p k t")

        last_y_pe = {}
        ld_ab = {}

        ld_x = {}
        last_v = {}

        def load_ab(t):
            sl = t % 2
            w = [pw] + (last_y_pe[t - 2] if t >= 2 else [])
            P.add("sync", lambda e: e.dma_start(out=OAt[sl][:], in_=oat[:, :, t * 512:(t + 1) * 512]), waits=w, inc=s_ldab[sl], dma=True)
            ld_ab[t] = P.add("sync", lambda e: e.dma_start(out=OBt[sl][:], in_=obt[:, :, t * 512:(t + 1) * 512]), waits=w,
                             inc=s_ldab[sl], dma=True)
            ld_x[t] = P.add("sync", lambda e: e.dma_start(out=XS[sl][:], in_=xo[:, 4 * t:4 * t + 4, :]),
                            waits=[pw] + (last_v[t - 2] if t >= 2 else []), inc=s_ldx[sl], dma=True)

        ld_sg = {}

        def load_sg(t):
            ld_sg[t] = P.add("sync", lambda e: e.dma_start(out=SGt[:], in_=sgt[:, :, t * 512:(t + 1) * 512]),
                             waits=[pw, (s_t[0], s_t[0].v), (s_t[1], s_t[1].v)], inc=s_ldsg, dma=True)

        load_ab(0)
        load_sg(0)
        yj = 0
        mj = 0
        sbk = 0
        pending_tp = []

        def flush_tp():
            while pending_tp:
                pending_tp.pop(0)()

        for t in range(8):
            sl = t % 2
            if t + 1 < 8:
                load_ab(t + 1)
            v_sg = ld_sg[t]
            v_x = ld_x[t]
            mt_w0 = (s_mm[0], s_mm[0].v), (s_mm[1], s_mm[1].v)
            for dc in range(8):
                ys = yj % 2
                ky = yj // 2

                def ymm(e, ys=ys, dc=dc, sl=sl):
                    ins = None
                    for k in range(4):
                        ins = e.matmul(PSY[:, 2 * ys, :], lhsT=WPA[:, k, dc * 128:(dc + 1) * 128], rhs=OAt[sl][:, k, :],
                                       start=(k == 0), stop=(k == 3))
                    for k in range(8):
                        ins = e.matmul(PSY[:, 2 * ys + 1, :], lhsT=WPB[:, k, dc * 128:(dc + 1) * 128], rhs=OBt[sl][:, k, :],
                                       start=(k == 0), stop=(k == 7))
                    return ins

                P.add("tensor", ymm, waits=[pw, (s_wcv, vw), (s_ldab[sl], ld_ab[t]), (s_t[ys], 2 * ky)], inc=s_y[ys])
                wt = [pw, (s_y[ys], ky + 1), (s_ldsg, v_sg), (s_mt, s_mt.v - 1)]
                P.add("vector", lambda e, ys=ys, dc=dc: e.tensor_tensor(out=T1[ys][:], in0=PSY[:, 2 * ys, :], in1=SGt[:, dc, :], op=ALU.mult),
                      waits=wt, inc=s_t[ys])
                vt = P.add("vector", lambda e, ys=ys, dc=dc: e.tensor_tensor(out=T2[ys][:], in0=PSY[:, 2 * ys + 1, :], in1=SGt[:, 8 + dc, :],
                                                                         op=ALU.mult), waits=wt, inc=s_t[ys])
                P.add("vector", lambda e, ys=ys, dc=dc: e.tensor_tensor(out=MT[:, dc, :], in0=T1[ys][:], in1=T2[ys][:], op=ALU.add),
                      waits=[pw, (s_t[ys], vt)] + list(mt_w0), inc=s_mt)
                yj += 1
                if dc == 1:
                    flush_tp()
            last_y_pe[t] = [(s_y[0], s_y[0].v), (s_y[1], s_y[1].v)]
            if t + 1 < 8:
                load_sg(t + 1)
            v_mt = s_mt.v
            for s in range(4):
                xs_o = sbk % 2
                for hf in range(2):
                    ms = mj % 2
                    km = mj // 2

                    def mmm(e, ms=ms, s=s, hf=hf):
                        ins = None
                        for k in range(8):
                            ins = e.matmul(PSM[:, ms, :], lhsT=MT[:, k, s * 128:(s + 1) * 128], rhs=WO[:, k, hf * 512:(hf + 1) * 512],
                                           start=(k == 0), stop=(k == 7))
                        return ins

                    P.add("tensor", mmm, waits=[pw, (s_mt, v_mt), (s_v[ms], km)], inc=s_mm[ms])
                    vv = P.add("vector", lambda e, ms=ms, s=s, hf=hf, sl=sl: e.scalar_tensor_tensor(
                        out=V[:, hf * 512:(hf + 1) * 512], in0=XS[sl][:, s, hf * 512:(hf + 1) * 512], scalar=ALPHA, in1=PSM[:, ms, :],
                        op0=ALU.mult, op1=ALU.add), waits=[pw, (s_mm[ms], km + 1), (s_ldx[sl], v_x), (s_ln, s_ln.v)], inc=s_v[ms])
                    mj += 1
                flush_tp()
                win = [(s_v[0], s_v[0].v), (s_v[1], s_v[1].v), (s_c, vconst), (s_so[xs_o], 16 * (sbk // 2)), (s_xb, sbk - 1)]
                vln = layer_norm_ops(P, V, ST6, MV, STD, RSTD, TMP, X1O[xs_o][:], LG, LB, EPSB, s_ln, win)
                row = 4 * t + s
                P.add("gpsimd", lambda e, xs_o=xs_o, row=row: e.dma_start(out=x1[:, row, :], in_=X1O[xs_o][:]),
                      waits=[pw, (s_ln, vln)], inc=s_so[xs_o], dma=True)
                vb = P.add("scalar", lambda e, xs_o=xs_o: e.copy(out=X1B[xs_o][:], in_=X1O[xs_o][:]),
                           waits=[pw, (s_ln, vln), (s_tp, sbk - 1)], inc=s_xb)

                def tp_group(t=t, s=s, xs_o=xs_o, vb=vb):
                    def tp(e):
                        ins = None
                        for k in range(8):
                            ins = e.transpose(out=PST[:, k, :], in_=X1B[xs_o][:, k * 128:(k + 1) * 128], identity=IDB[:])
                        return ins

                    vtp = P.add("tensor", tp, waits=[pw, (s_xb, vb), (s_tc, s_tc.v)], inc=s_tp)
                    P.add("scalar", lambda e: e.copy(out=X1TS[:, :, s * 128:(s + 1) * 128], in_=PST[:]),
                          waits=[pw, (s_tp, vtp), (s_sx, 16 * t)], inc=s_tc)
                    if s == 3:
                        P.add("gpsimd", lambda e: e.dma_start(out=x1t[:, :, t * 512:(t + 1) * 512], in_=X1TS[:]),
                              waits=[pw, (s_tc, s_tc.v)], inc=s_sx, dma=True)

                pending_tp.append(tp_group)
                sbk += 1
            last_v[t] = [(s_v[0], s_v[0].v), (s_v[1], s_v[1].v)]
        flush_tp()
        P.end_phase(JK[:])


def phase4b(nc, P, D):
    TT = 256
    NT = NQ // TT
    with ExitStack() as es:
        W1 = _sb(es, nc, "p5_W1", [128, 8, 4096], BF16)
        W2 = _sb(es, nc, "p5_W2", [128, 32, 1024], BF16)
        XT = [_sb(es, nc, f"p5_XT{i}", [128, 8, TT], BF16) for i in range(2)]
        X1 = _sb(es, nc, "p5_X1", [128, 2, 1024], F32)
        HT = _sb(es, nc, "p5_HT", [128, 32, TT], BF16)
        RT = [_sb(es, nc, f"p5_RT{i}", [128, TT], F32) for i in range(2)]
        V = _sb(es, nc, "p5_V", [128, 1024], F32)
        TMP = _sb(es, nc, "p5_TMP", [128, 1024], F32)
        YO = [_sb(es, nc, f"p5_YO{i}", [128, 1024], F32) for i in range(2)]
        LG = _sb(es, nc, "p5_LG", [128, 1024], F32)
        LB = _sb(es, nc, "p5_LB", [128, 1024], F32)
        ST6 = _sb(es, nc, "p5_ST6", [128, 12], F32)
        MV = _sb(es, nc, "p5_MV", [128, 2], F32)
        STD = _sb(es, nc, "p5_STD", [128, 1], F32)
        RSTD = _sb(es, nc, "p5_RSTD", [128, 1], F32)
        EPSB = _sb(es, nc, "p5_EPSB", [128, 1], F32)
        JK = _sb(es, nc, "p5_JK", [128, 1], F32)
        PSH = _ps(es, nc, "p5_PSH", [128, 4, 512], F32)
        PSF = _ps(es, nc, "p5_PSF", [128, 2, 512], F32)
        s_c = P.sem("p5c")
        s_wld = [P.sem(f"p5wld{i}") for i in range(2)]
        s_wcv = P.sem("p5wcv")
        s_ldt = [P.sem(f"p5ldt{i}") for i in range(2)]
        s_ldx = P.sem("p5ldx")
        s_h = [P.sem(f"p5h{i}") for i in range(4)]
        s_r = [P.sem(f"p5r{i}") for i in range(4)]
        s_sq = [P.sem(f"p5sq{i}") for i in range(2)]
        s_f = [P.sem(f"p5f{i}") for i in range(2)]
        s_v = [P.sem(f"p5v{i}") for i in range(2)]
        s_ln = P.sem("p5ln")
        s_so = [P.sem(f"p5so{i}") for i in range(2)]
        pw = P.pw()

        for (dst, src) in ((LG, D["ln2g"]), (LB, D["ln2b"])):
            vc = P.add("sync", lambda e, dst=dst, src=src: e.dma_start(out=dst[:], in_=src), waits=[pw], inc=s_c, dma=True)
        vconst = P.add("vector", lambda e: e.memset(EPSB[:], LN_EPS), waits=[pw, (s_c, vc)], inc=s_c)

        vw1 = None
        for q in range(4):
            vw1 = P.add("sync", lambda e, q=q: e.dma_start(out=W1[:, 2 * q:2 * q + 2, :], in_=D["W1B"][:, 2 * q:2 * q + 2, :]),
                        waits=[pw], inc=s_wld[0], dma=True)
        vw2 = None
        for q in range(4):
            vw2 = P.add("sync", lambda e, q=q: e.dma_start(out=W2[:, 8 * q:8 * q + 8, :], in_=D["W2B"][:, 8 * q:8 * q + 8, :]),
                        waits=[pw], inc=s_wld[1], dma=True)

        x1t = D["X1T"].rearrange("k p t -> p k t")
        x1 = D["X1"].rearrange("(s p) d -> p s d", p=128)
        yv = D["y"].rearrange("(s p) d -> p s d", p=128)
        last_h_pe = {}
        ld_t = {}

        def load_t(t):
            sl = t % 2
            w = [pw] + (last_h_pe[t - 2] if t >= 2 else [])
            ld_t[t] = P.add("sync", lambda e: e.dma_start(out=XT[sl][:], in_=x1t[:, :, t * TT:(t + 1) * TT]), waits=w,
                            inc=s_ldt[sl], dma=True)

        load_t(0)
        hj = 0
        fj = 0
        sbk = 0
        for t in range(NT):
            sl = t % 2
            if t + 1 < NT:
                load_t(t + 1)
            v_x = P.add("sync", lambda e, t=t: e.dma_start(out=X1[:], in_=x1[:, 2 * t:2 * t + 2, :]),
                        waits=[pw, (s_v[0], s_v[0].v), (s_v[1], s_v[1].v)], inc=s_ldx, dma=True)
            ht_w0 = [(s_f[0], s_f[0].v), (s_f[1], s_f[1].v)]
            for fc in range(32):
                hs = hj % 4
                kh = hj // 4
                rs = hj % 2
                kr = hj // 2

                def hmm(e, hs=hs, fc=fc, sl=sl):
                    ins = None
                    for k in range(8):
                        ins = e.matmul(PSH[:, hs, 0:TT], lhsT=W1[:, k, fc * 128:(fc + 1) * 128], rhs=XT[sl][:, k, :],
                                       start=(k == 0), stop=(k == 7))
                    return ins

                P.add("tensor", hmm, waits=[pw, (s_wld[0], vw1), (s_ldt[sl], ld_t[t]), (s_r[hs], kh)], inc=s_h[hs])
                P.add("scalar", lambda e, hs=hs, rs=rs: e.activation(out=RT[rs][:], in_=PSH[:, hs, 0:TT], func=AF.Relu),
                      waits=[pw, (s_h[hs], kh + 1), (s_sq[rs], kr)], inc=s_r[hs])
                P.add("vector",
                      lambda e, rs=rs, fc=fc: e.tensor_tensor(out=HT[:, fc, :], in0=RT[rs][:], in1=RT[rs][:], op=ALU.mult),
                      waits=[pw, (s_r[hs], kh + 1)] + ht_w0, inc=s_sq[rs])
                hj += 1
            last_h_pe[t] = [(s_h[i], s_h[i].v) for i in range(4)]
            v_sq = [(s_sq[0], s_sq[0].v), (s_sq[1], s_sq[1].v)]
            for s in range(2):
                yo = sbk % 2
                for hf in range(2):
                    fs = fj % 2
                    kf = fj // 2

                    def fmm(e, fs=fs, s=s, hf=hf):
                        ins = None
                        for fc in range(32):
                            ins = e.matmul(PSF[:, fs, :], lhsT=HT[:, fc, s * 128:(s + 1) * 128], rhs=W2[:, fc, hf * 512:(hf + 1) * 512],
                                           start=(fc == 0), stop=(fc == 31))
                        return ins

                    P.add("tensor", fmm, waits=[pw, (s_wld[1], vw2), (s_v[fs], kf)] + v_sq, inc=s_f[fs])
                    P.add("vector", lambda e, fs=fs, s=s, hf=hf: e.scalar_tensor_tensor(
                        out=V[:, hf * 512:(hf + 1) * 512], in0=X1[:, s, hf * 512:(hf + 1) * 512], scalar=ALPHA, in1=PSF[:, fs, :],
                        op0=ALU.mult, op1=ALU.add), waits=[pw, (s_f[fs], kf + 1), (s_ldx, v_x), (s_ln, s_ln.v)], inc=s_v[fs])
                    fj += 1
                win = [(s_v[0], s_v[0].v), (s_v[1], s_v[1].v), (s_c, vconst), (s_so[yo], 16 * (sbk // 2))]
                vln = layer_norm_ops(P, V, ST6, MV, STD, RSTD, TMP, YO[yo][:], LG, LB, EPSB, s_ln, win)
                row = 2 * t + s
                P.add("sync", lambda e, yo=yo, row=row: e.dma_start(out=yv[:, row, :], in_=YO[yo][:]),
                      waits=[pw, (s_ln, vln)], inc=s_so[yo], dma=True)
                sbk += 1
        P.end_phase(JK[:])


def build_nc(debug=False, phases=(1, 2, 3, 4, 5)):
    nc = bass.Bass("TRN2", target_bir_lowering=False)
    D = {}

    def din(name, shape):
        D[name] = nc.dram_tensor(name, shape, F32, kind="ExternalInput").ap()

    din("xT", [1024, NK]); din("xo", [NQ, 1024]); din("wp", [1024, COLS_IN]); din("bg", [128, 16])
    din("ga", [24, 128, 256]); din("ma", [128, 256]); din("gb", [8, 128, 256]); din("cb", [128, 8]); din("mb", [128, 256])
    din("ident", [128, 128]); din("pvalid", [128, 64]); din("lam", [128, 4, 64]); din("gs", [128, 128])
    din("ln1g", [128, 1024]); din("ln1b", [128, 1024]); din("ln2g", [128, 1024]); din("ln2b", [128, 1024])
    din("wpa", [512, 1024]); din("wpb", [1024, 1024]); din("wo", [1024, 1024]); din("w1", [1024, 4096]); din("w2", [4096, 1024])
    D["y"] = nc.dram_tensor("y", [NQ, 1024], F32, kind="ExternalOutput").ap()
    kind = "ExternalOutput" if debug else "Internal"

    def scr(name, shape, dt):
        D[name] = nc.dram_tensor(name, shape, dt, kind=kind).ap()

    scr("QTB", [8, 128, NQ], BF16); scr("KTB", [8, 128, NK], BF16); scr("VB", [NK, 1024], BF16)
    scr("QTA", [12, 128, NQ], BF16); scr("KTA", [12, 128, NKA], BF16); scr("VA", [NKA, 1536], BF16)
    scr("SG", [16, 128, NQ], BF16); scr("OAT", [4, 128, NQ], BF16); scr("OBT", [8, 128, NQ], BF16)
    scr("X1", [NQ, 1024], F32); scr("X1T", [8, 128, NQ], BF16)
    scr("WAB", [128, 20, 1024], BF16); scr("W1B", [128, 8, 4096], BF16); scr("W2B", [128, 32, 1024], BF16)
    with ExitStack() as es:
        P = Prog(nc, es)
        if 1 in phases:
            phase1(nc, P, D)
        if 2 in phases:
            phase2(nc, P, D)
        if 3 in phases:
            phase3(nc, P, D)
        if 4 in phases:
            phase4a(nc, P, D)
        if 5 in phases:
            phase4b(nc, P, D)
    return nc


def t5_bucket_np(dist):
    n = np.maximum(dist, 0)
    nf = np.maximum(n, 1).astype(np.float32)
    large = 16 + (np.log(nf / np.float32(16.0)) / np.float32(math.log(8.0)) * np.float32(16.0)).astype(np.int32)
    large = np.minimum(large, 31)
    return np.where(n < 16, n, large)


def prep_shared(inp):
    f = lambda a: np.ascontiguousarray(np.asarray(a, dtype=np.float32))
    w = np.asarray(inp["w_in"], dtype=np.float32)[0]
    B0 = 4608
    cols = []
    for h in range(8):
        cols += list(range(B0 + h * 64, B0 + h * 64 + 64)) + list(range(B0 + 512 + h * 64, B0 + 512 + h * 64 + 64))
    for h in range(8):
        cols += list(range(B0 + 1024 + h * 64, B0 + 1024 + h * 64 + 64)) + list(range(B0 + 1536 + h * 64, B0 + 1536 + h * 64 + 64))
    cols += list(range(0, 1536)) + list(range(1536, 3072)) + list(range(7680, 9728))
    cols += list(range(B0 + 2048, B0 + 3072)) + list(range(3072, 4608))
    assert len(cols) == COLS_IN
    sh = {"wp": f(w[:, cols])}
    sh["bg"] = f(np.asarray(inp["b_gate"], np.float32)[0].reshape(16, 128).T)
    rb = np.asarray(inp["rel_bias"], np.float32)
    i = np.arange(128)[:, None]
    c = np.arange(256)[None, :]
    ch = c // 128
    j = c % 128
    steps = (1 - ch) * 128 + j - i
    ga = np.zeros((24, 128, 256), np.float32)
    for g, d in enumerate(DILS):
        bk = t5_bucket_np(np.maximum(steps, 0) * d)
        for h in range(8):
            ga[g * 8 + h] = rb[bk, g * 8 + h]
    sh["ga"] = ga
    sh["ma"] = f(np.where((steps >= 0) & (steps <= 128), 0.0, MASKV / 8.0))
    dist = c - i
    bk = t5_bucket_np(np.maximum(dist, 0))
    gb = np.zeros((8, 128, 256), np.float32)
    for h in range(8):
        gb[h] = rb[bk, 24 + h]
    sh["gb"] = gb
    sh["cb"] = f(np.broadcast_to(rb[31, 24:32][None, :], (128, 8)))
    sh["mb"] = f(np.where(dist >= 0, 0.0, MASKV))
    sh["ident"] = np.eye(128, dtype=np.float32)
    lam = np.stack([np.asarray(inp[k], np.float32)[0] for k in ("lambda_q1", "lambda_q2", "lambda_k1", "lambda_k2")])
    sh["lam"] = f(np.broadcast_to(lam[None], (128, 4, 64)))
    sh["gs"] = f(np.broadcast_to(np.asarray(inp["subln_g"], np.float32)[0][None, :], (128, 128)))
    for k, n in (("ln1g", "ln1_g"), ("ln1b", "ln1_b"), ("ln2g", "ln2_g"), ("ln2b", "ln2_b")):
        sh[k] = f(np.broadcast_to(np.asarray(inp[n], np.float32)[0][None, :], (128, 1024)))
    sh["wpa"] = f(np.asarray(inp["w_proj_a"])[0]); sh["wpb"] = f(np.asarray(inp["w_proj_b"])[0])
    sh["wo"] = f(np.asarray(inp["w_out"])[0]); sh["w1"] = f(np.asarray(inp["w_mlp1"])[0]); sh["w2"] = f(np.asarray(inp["w_mlp2"])[0])
    return sh


def prep_core(x, c):
    b, par = c // 2, c % 2
    xb = np.asarray(x[b], dtype=np.float32)
    xl = np.zeros((NK, 1024), np.float32)
    if par == 1:
        xl[:] = xb
    else:
        xl[2048:] = xb[0:NK - 2048]
    own = np.concatenate([xl[2048:4096], xl[6144:8192]], axis=0)
    pv = np.full((128, 64), float(par), np.float32)
    return {"xT": np.ascontiguousarray(xl.T), "xo": np.ascontiguousarray(own), "pvalid": pv}


_NC_CACHE = {}


def kernel(**inputs):
    x = np.asarray(inputs["x"], dtype=np.float32)
    sh = prep_shared(inputs)
    in_maps = []
    for c in range(8):
        m = dict(sh)
        m.update(prep_core(x, c))
        in_maps.append(m)
    if "nc" not in _NC_CACHE:
        _NC_CACHE["nc"] = build_nc()
    res = run_bass_kernel_spmd(_NC_CACHE["nc"], in_maps, core_ids=list(range(8)))
    out = np.zeros((BATCH, SEQ, D_MODEL), np.float32)
    for c in range(8):
        b, par = c // 2, c % 2
        y = np.asarray(res.results[c]["y"], dtype=np.float32)
        r0 = 2048 if par == 1 else 0
        out[b, r0:r0 + 2048] = y[0:2048]
        out[b, r0 + 4096:r0 + 6144] = y[2048:4096]
    return out
```

```python
import math
from contextlib import ExitStack

import numpy as np

import concourse.bass as bass
import concourse.mybir as mybir
from concourse.bass_utils import run_bass_kernel_spmd

F32 = mybir.dt.float32
BF16 = mybir.dt.bfloat16
AF = mybir.ActivationFunctionType
ALU = mybir.AluOpType
AX = mybir.AxisListType

D_MODEL = 1024
SEQ = 8192
BATCH = 4
NQ = 4096
NK = 8192
NKA = 8192
COLS_IN = 9728
DILS = (1, 4, 16)
ALPHA = 2.0 ** 0.25
LAM_INIT = 0.8 - 0.6 * math.exp(0.0)
LN_EPS = 1e-5
MASKV = -240000.0
ENGS = ("sync", "scalar", "vector", "gpsimd", "tensor")

C_FQ, C_FK, C_AQ, C_AK, C_GT, C_BV, C_AV = 0, 1024, 2048, 3584, 5120, 7168, 8192


class Sem:
    __slots__ = ("h", "abs", "base", "name")

    def __init__(self, h, name):
        self.h, self.abs, self.base, self.name = h, 0, 0, name

    @property
    def v(self):
        return self.abs - self.base


class Prog:
    def __init__(self, nc, es):
        self.nc, self.es = nc, es
        self.q = {k: [] for k in ENGS}
        self.waited = {k: {} for k in ENGS}
        self.sems = []
        self.pool = []
        self.used = 0
        self.nphase = 0
        self.bar = self.sem("bar")

    def sem(self, name):
        if name != "bar" and self.used < len(self.pool):
            s = self.pool[self.used]
            self.used += 1
            s.base = s.abs
            return s
        s = Sem(self.es.enter_context(self.nc.semaphore(f"s{len(self.sems)}")), f"s{len(self.sems)}")
        self.sems.append(s)
        if name != "bar":
            self.pool.append(s)
            self.used += 1
        return s

    def add(self, eng, fn, waits=(), inc=None, dma=False):
        ws = []
        for w in waits:
            if w is None:
                continue
            s, v = w
            if v <= 0:
                continue
            assert v <= s.v, f"deadlock: {eng} waits {s.name}>={v} but only {s.v} scheduled"
            va = s.base + v
            if self.waited[eng].get(s.name, 0) >= va:
                continue
            self.waited[eng][s.name] = va
            ws.append((s, va))
        amt = 16 if dma else 1
        newv = None
        if inc is not None:
            inc.abs += amt
            newv = inc.v
        self.q[eng].append((ws, fn, inc, amt))
        return newv

    def end_phase(self, junk):
        waits = [(s, s.v) for s in self.sems if s is not self.bar]
        self.nphase += 1
        self.add("gpsimd", lambda e: e.memset(junk, 0.0), waits=waits, inc=self.bar)
        with self.nc.Block() as block:
            for name in ENGS:
                items = self.q[name]

                def body(e, items=items):
                    for ws, fn, inc, amt in items:
                        for s, v in ws:
                            e.wait_ge(s.h, v)
                        ins = fn(e)
                        if inc is not None:
                            ins.then_inc(inc.h, amt)

                getattr(block, name)(body)
        self.q = {k: [] for k in ENGS}
        self.used = 0
        for name in ENGS:
            if name == "gpsimd":
                continue
            self.waited[name].pop(self.bar.name, None)
        self.phase_wait = (self.bar, self.nphase)

    def pw(self):
        return getattr(self, "phase_wait", None)


def _sb(es, nc, name, shape, dt):
    return es.enter_context(nc.sbuf_tensor(name, shape, dt))


def _ps(es, nc, name, shape, dt):
    return es.enter_context(nc.psum_tensor(name, shape, dt))


def phase1(nc, P, D):
    with ExitStack() as es:
        W = _sb(es, nc, "p1_W", [128, 8, COLS_IN], BF16)
        WF = [_sb(es, nc, f"p1_WF{i}", [128, 8, 256], F32) for i in range(2)]
        XF = _sb(es, nc, "p1_XF", [128, 8, 512], F32)
        XB = [_sb(es, nc, f"p1_XB{i}", [128, 8, 512], BF16) for i in range(2)]
        OST = [_sb(es, nc, f"p1_OST{i}", [128, 512], BF16) for i in range(4)]
        BG = _sb(es, nc, "p1_BG", [128, 16], F32)
        JK = _sb(es, nc, "p1_JK", [128, 1], F32)
        PS = _ps(es, nc, "p1_PS", [128, 4, 512], F32)
        s_xld, s_xcv, s_wcv, s_misc = P.sem("p1xld"), P.sem("p1xcv"), P.sem("p1wcv"), P.sem("p1misc")
        s_wld = [P.sem(f"p1wld{i}") for i in range(2)]
        s_mm = [P.sem(f"p1mm{i}") for i in range(4)]
        s_ev = [P.sem(f"p1ev{i}") for i in range(4)]
        s_st = [P.sem(f"p1st{i}") for i in range(4)]
        xT = D["xT"].rearrange("(kc p) t -> p kc t", p=128)
        wp = D["wp"].rearrange("(kc p) c -> p kc c", p=128)
        pw = P.pw()

        v_bg = P.add("sync", lambda e: e.dma_start(out=BG[:], in_=D["bg"]), waits=[pw], inc=s_misc, dma=True)

        wcv_of = {}
        wl = [0]

        def load_w(cid):
            n = wl[0]
            slot = n % 2
            v = P.add("sync", lambda e: e.dma_start(out=WF[slot][:], in_=wp[:, :, cid * 128:(cid + 2) * 128]),
                      waits=[pw, (s_wcv, n - 1)], inc=s_wld[slot], dma=True)
            P.add("vector", lambda e: e.tensor_copy(out=W[:, :, cid * 128:(cid + 2) * 128], in_=WF[slot][:]),
                  waits=[pw, (s_wld[slot], v)], inc=s_wcv)
            wcv_of[cid] = s_wcv.v
            wcv_of[cid + 1] = s_wcv.v
            wl[0] += 1

        worder = (list(range(8, 16)) + list(range(56, 64)) + list(range(36, 40)) + list(range(72, 76))
                  + list(range(28, 36)) + list(range(64, 72))
                  + list(range(0, 8)) + list(range(16, 28)) + list(range(40, 56)))
        worder = worder[0::2]
        wplan = {-1: worder[0:12], 0: worder[12:20]}
        for t in range(1, 4):
            wplan[t] = worder[20 + 6 * (t - 1):20 + 6 * t]

        tile_last = {}

        def load_x(t):
            v = P.add("sync", lambda e: e.dma_start(out=XF[:], in_=xT[:, :, t * 512:(t + 1) * 512]),
                      waits=[pw, (s_xcv, t)], inc=s_xld, dma=True)
            P.add("vector", lambda e: e.tensor_copy(out=XB[t % 2][:], in_=XF[:]),
                  waits=[pw, (s_xld, v)] + (tile_last[t - 2] if t >= 2 else []), inc=s_xcv)

        def jobs_for(t):
            jobs = []
            own = (t // 4) % 2 == 1
            to = t - 4 if t < 8 else t - 8
            for h in range(8):
                jobs.append(("F", C_FK + h * 128, D["KTB"][h][:, t * 512:(t + 1) * 512], None))
            for cb in range(2):
                for s in range(4):
                    r0 = t * 512 + s * 128
                    jobs.append(("T", C_BV + cb * 512, D["VB"][r0:r0 + 128, cb * 512:(cb + 1) * 512], s))
            gs = (0, 1, 2) if (own or t % 4 == 3) else (2,)
            for g in gs:
                for hp in range(4):
                    m = g * 4 + hp
                    jobs.append(("F", C_AK + m * 128, D["KTA"][m][:, t * 512:(t + 1) * 512], None))
            for g in gs:
                for s in range(4):
                    r0 = t * 512 + s * 128
                    jobs.append(("T", C_AV + g * 512, D["VA"][r0:r0 + 128, g * 512:(g + 1) * 512], s))
            if own:
                for h in range(8):
                    jobs.append(("F", C_FQ + h * 128, D["QTB"][h][:, to * 512:(to + 1) * 512], None))
                for m in range(12):
                    jobs.append(("F", C_AQ + m * 128, D["QTA"][m][:, to * 512:(to + 1) * 512], None))
                for i in range(16):
                    jobs.append(("G", C_GT + i * 128, D["SG"][i][:, to * 512:(to + 1) * 512], i))
            return jobs

        load_x(0)
        for cid in wplan[-1]:
            load_w(cid)
        jc = 0
        for t in range(16):
            jobs = jobs_for(t)
            half = len(jobs) // 2
            for ji, (kind, col, dst, aux) in enumerate(jobs):
                if ji == half:
                    if t + 1 < 16:
                        load_x(t + 1)
                    for cid in wplan.get(t, []):
                        load_w(cid)
                slot = jc % 4
                k = jc // 4
                xb = XB[t % 2]
                if kind == "T":
                    wneed = max(wcv_of[col // 128 + i] for i in range(4))
                else:
                    wneed = wcv_of[col // 128]

                def mm(e, kind=kind, col=col, aux=aux, xb=xb, slot=slot):
                    ins = None
                    for kc in range(8):
                        if kind == "T":
                            ins = e.matmul(PS[:, slot, :], lhsT=xb[:, kc, aux * 128:(aux + 1) * 128],
                                           rhs=W[:, kc, col:col + 512], start=(kc == 0), stop=(kc == 7))
                        else:
                            ins = e.matmul(PS[:, slot, :], lhsT=W[:, kc, col:col + 128],
                                           rhs=xb[:, kc, :], start=(kc == 0), stop=(kc == 7))
                    return ins

                P.add("tensor", mm, waits=[pw, (s_xcv, t + 1), (s_wcv, wneed), (s_ev[slot], k)], inc=s_mm[slot])
                evw = [pw, (s_mm[slot], k + 1), (s_st[slot], 16 * k)]
                if kind == "G":
                    P.add("scalar", lambda e, slot=slot, aux=aux: e.activation(
                        out=OST[slot][:], in_=PS[:, slot, :], func=AF.Sigmoid, bias=BG[:, aux:aux + 1], scale=1.0),
                        waits=evw + [(s_misc, v_bg)], inc=s_ev[slot])
                elif jc % 2 == 0:
                    P.add("vector", lambda e, slot=slot: e.tensor_copy(out=OST[slot][:], in_=PS[:, slot, :]),
                          waits=evw, inc=s_ev[slot])
                else:
                    P.add("scalar", lambda e, slot=slot: e.copy(out=OST[slot][:], in_=PS[:, slot, :]),
                          waits=evw, inc=s_ev[slot])
                P.add("gpsimd", lambda e, slot=slot, dst=dst: e.dma_start(out=dst, in_=OST[slot][:]),
                      waits=[pw, (s_ev[slot], k + 1)], inc=s_st[slot], dma=True)
                jc += 1
            tile_last[t] = [(s_mm[s], s_mm[s].v) for s in range(4)]
        P.end_phase(JK[:])


def phase2(nc, P, D):
    with ExitStack() as es:
        QA = [_sb(es, nc, f"p2_QA{i}", [128, NQ], BF16) for i in range(2)]
        KA = [_sb(es, nc, f"p2_KA{i}", [128, NKA], BF16) for i in range(2)]
        VS = [_sb(es, nc, f"p2_VS{i}", [128, 64, 128], BF16) for i in range(2)]
        GF = [_sb(es, nc, f"p2_GF{i}", [128, 2, 256], F32) for i in range(2)]
        BTF = _sb(es, nc, "p2_BTF", [128, 2, 256], F32)
        EB = [_sb(es, nc, f"p2_EB{i}", [128, 2, 256], F32) for i in range(2)]
        PTF = [_sb(es, nc, f"p2_PTF{i}", [128, 2, 256], F32) for i in range(2)]
        MA = _sb(es, nc, "p2_MA", [128, 256], F32)
        IDF = _sb(es, nc, "p2_IDF", [128, 128], F32)
        IDB = _sb(es, nc, "p2_IDB", [128, 128], BF16)
        PVF = _sb(es, nc, "p2_PVF", [128, 64], F32)
        PV64 = _sb(es, nc, "p2_PV64", [128, 64], BF16)
        ONE64 = _sb(es, nc, "p2_ONE64", [128, 64], BF16)
        PT = [_sb(es, nc, f"p2_PT{i}", [128, 2, 256], BF16) for i in range(3)]
        NACC = _sb(es, nc, "p2_NACC", [128, NQ], F32)
        DACC = _sb(es, nc, "p2_DACC", [128, NQ], F32)
        OAS = _sb(es, nc, "p2_OAS", [128, NQ], BF16)
        JK = _sb(es, nc, "p2_JK", [128, 1], F32)
        S = _ps(es, nc, "p2_S", [128, 4, 512], F32)
        ND = _ps(es, nc, "p2_ND", [128, 4, 512], F32)
        s_c = P.sem("p2c")
        s_ld = [P.sem(f"p2ld{i}") for i in range(2)]
        s_bt = P.sem("p2bt")
        s_qk = [P.sem(f"p2qk{i}") for i in range(2)]
        s_ex = [P.sem(f"p2ex{i}") for i in range(2)]
        s_pv = [P.sem(f"p2pv{i}") for i in range(3)]
        s_nf = [P.sem(f"p2nf{i}") for i in range(2)]
        s_dv = P.sem("p2dv")
        s_fin = P.sem("p2fin")
        s_pm = [P.sem(f"p2pm{i}") for i in range(2)]
        s_st = P.sem("p2st")
        pw = P.pw()

        vc = P.add("sync", lambda e: e.dma_start(out=MA[:], in_=D["ma"]), waits=[pw], inc=s_c, dma=True)
        vc = P.add("sync", lambda e: e.dma_start(out=PVF[:], in_=D["pvalid"]), waits=[pw], inc=s_c, dma=True)
        P.add("vector", lambda e: e.tensor_copy(out=PV64[:], in_=PVF[:]), waits=[pw, (s_c, vc)], inc=s_c)
        vconst = P.add("vector", lambda e: e.memset(ONE64[:], 1.0), waits=[pw], inc=s_c)

        passes = [(hp, g) for hp in range(4) for g in range(3)]
        pass_last_pe = {}
        pass_last_dve = {}
        ld_val = {}
        bt_val = {}

        def sched_loads(pi):
            hp, g = passes[pi]
            d = DILS[g]
            slot = pi % 2
            nb = 64 // d
            w = [pw] + (pass_last_pe[pi - 2] if pi >= 2 else [])
            P.add("sync", lambda e: e.dma_start(out=QA[slot][:], in_=D["QTA"][g * 4 + hp]), waits=w, inc=s_ld[slot], dma=True)
            tranges = ((0, 16),) if d == 16 else ((3, 8), (11, 16))
            for (ta, tb) in tranges:
                P.add("sync", lambda e, ta=ta, tb=tb: e.dma_start(out=KA[slot][:, ta * 512:tb * 512],
                                                               in_=D["KTA"][g * 4 + hp][:, ta * 512:tb * 512]),
                      waits=w, inc=s_ld[slot], dma=True)
            vav = D["VA"].rearrange("(n i r) c -> r i n c", i=128, r=d)
            c0 = g * 512 + hp * 128
            bpt = 4 // d if d <= 4 else 0
            for r in range(d):
                if d == 16:
                    branges = ((0, nb),)
                else:
                    branges = tuple((ta * bpt, tb * bpt) for (ta, tb) in tranges)
                for (ba, bb) in branges:
                    for n0 in range(ba, bb, 16):
                        n1 = min(bb, n0 + 16)
                        P.add("sync", lambda e, r=r, n0=n0, n1=n1: e.dma_start(
                            out=VS[slot][:, r * nb + n0:r * nb + n1, :], in_=vav[r][:, n0:n1, c0:c0 + 128]),
                            waits=w, inc=s_ld[slot], dma=True)
            gh0 = g * 8 + 2 * hp
            v = P.add("sync", lambda e: e.dma_start(out=GF[slot][:], in_=D["ga"][gh0:gh0 + 2].rearrange("h i c -> i h c")),
                      waits=w, inc=s_ld[slot], dma=True)
            ld_val[pi] = v

        def sched_tables(pi):
            slot = pi % 2
            wb = [pw, (s_ld[slot], ld_val[pi]), (s_c, vconst)]
            v1 = None
            for hh in range(2):
                v1 = P.add("vector", lambda e, hh=hh: e.tensor_tensor(
                    out=BTF[:, hh, :], in0=GF[slot][:, hh, :], in1=MA[:], op=ALU.add),
                    waits=wb + [(s_bt, s_bt.v)], inc=s_bt)
            P.add("scalar", lambda e: e.activation(out=EB[slot][:], in_=BTF[:], func=AF.Exp),
                  waits=[pw, (s_bt, v1)] + (pass_last_dve[pi - 2] if pi >= 2 else []), inc=s_bt)
            bt_val[pi] = s_bt.v

        sched_loads(0)
        sched_tables(0)
        cn = {"qi": 0, "gi": 0, "fin": 0}

        def run_pass(pi, hp, g):
            d = DILS[g]
            slot = pi % 2
            nb = 64 // d
            nbB = 16 // d
            if pi + 1 < len(passes):
                sched_loads(pi + 1)

            def own_off(n):
                return 2048 if n < 2 * nbB else 4096

            groups = []
            if d == 16:
                for B in (1, 3):
                    base = 0 if B == 1 else 2048
                    for r0 in range(0, 16, 4):
                        qbs = [(r0 + i, B) for i in range(4)]
                        groups.append((qbs,
                                       lambda T, base=base, r0=r0: T[:, base:base + 2048].rearrange("p (j r) -> p r j", r=16)[:, r0:r0 + 4, :],
                                       lambda bank: ND[:, bank, 0:512].rearrange("p (r j) -> p r j", r=4)))
            else:
                for r in range(d):
                    for B in (1, 3):
                        for n0 in range(B * nbB, (B + 1) * nbB, 4):
                            qbs = [(r, n0 + i) for i in range(4)]
                            t0 = d * 128 * n0 + r - own_off(n0)
                            sl = slice(t0, t0 + 511 * d + 1, d)
                            groups.append((qbs, lambda T, sl=sl: T[:, sl], lambda bank: ND[:, bank, 0:512]))
            qblocks = [(r, n, gidx, i) for gidx, (qbs, _, _) in enumerate(groups) for i, (r, n) in enumerate(qbs)]
            pendl = []
            pass_dv_start = s_dv.v

            def emit_pv(info):
                (qslot, kq, pslot, r, n, gslot, kg, first, last, gidx, col) = info

                def pv(e):
                    ins = None
                    for hh in range(2):
                        for ch in range(2):
                            nk = n - 1 + ch
                            blk = r * nb + nk
                            ins = e.matmul(ND[hh * 64:(hh + 1) * 64, 2 * gslot, col:col + 128],
                                           lhsT=VS[slot][:, blk, hh * 64:(hh + 1) * 64],
                                           rhs=PT[pslot][:, hh, ch * 128:(ch + 1) * 128],
                                           start=(ch == 0), stop=(ch == 1))
                            vt = PV64 if nk < nbB else ONE64
                            ins = e.matmul(ND[hh * 64:(hh + 1) * 64, 2 * gslot + 1, col:col + 128],
                                           lhsT=vt[:, :], rhs=PT[pslot][:, hh, ch * 128:(ch + 1) * 128],
                                           start=(ch == 0), stop=(ch == 1))
                    return ins

                w = [(s_pm[qslot], kq + 1)]
                if first:
                    w.append((s_nf[gslot], kg))
                vpv = P.add("tensor", pv, waits=w, inc=s_pv[pslot])
                if last:
                    _, dst, srcf = groups[gidx]
                    wd = [(s_pv[pslot], vpv)]
                    if g > 0:
                        wd += [(s_dv, pass_dv_start), (s_nf[0], nf_start[0]), (s_nf[1], nf_start[1])]
                    else:
                        wd += [(s_fin, cn["fin"])]
                    if g == 0:
                        P.add("scalar", lambda e: e.copy(out=dst(NACC), in_=srcf(2 * gslot)), waits=[pw] + wd, inc=s_dv)
                        P.add("scalar", lambda e: e.copy(out=dst(DACC), in_=srcf(2 * gslot + 1)), waits=[pw] + wd, inc=s_nf[gslot])
                    else:
                        P.add("vector", lambda e: e.tensor_tensor(out=dst(NACC), in0=srcf(2 * gslot), in1=dst(NACC), op=ALU.add),
                              waits=wd, inc=s_dv)
                        P.add("vector", lambda e: e.tensor_tensor(out=dst(DACC), in0=srcf(2 * gslot + 1), in1=dst(DACC), op=ALU.add),
                              waits=wd, inc=s_nf[gslot])

            nf_start = [s_nf[0].v, s_nf[1].v]
            for bi, (r, n, gidx, gi_) in enumerate(qblocks):
                if bi == len(qblocks) // 2 and pi + 1 < len(passes):
                    sched_tables(pi + 1)
                qslot = cn["qi"] % 2
                kq = cn["qi"] // 2
                pslot = cn["qi"] % 3
                kp = cn["qi"] // 3
                first = (gi_ == 0)
                last = (gi_ == 3)
                if first:
                    gslot = cn["gi"] % 2
                    kg = cn["gi"] // 2
                    cn["gi"] += 1
                t0 = d * 128 * n + r - own_off(n)
                qs = slice(t0, t0 + 127 * d + 1, d)

                def qk(e, r=r, n=n, qslot=qslot, qs=qs):
                    ins = None
                    for hh in range(2):
                        bank = 2 * qslot + hh
                        for ch in range(2):
                            nk = n - 1 + ch
                            ks = slice(d * 128 * nk + r, d * 128 * nk + r + 127 * d + 1, d)
                            ins = e.matmul(S[:, bank, ch * 128:(ch + 1) * 128],
                                           lhsT=KA[slot][hh * 64:(hh + 1) * 64, ks],
                                           rhs=QA[slot][hh * 64:(hh + 1) * 64, qs],
                                           start=True, stop=True)
                    return ins

                P.add("tensor", qk, waits=[pw, (s_ld[slot], ld_val[pi]), (s_c, vconst), (s_ex[qslot], kq)],
                      inc=s_qk[qslot])
                P.add("scalar", lambda e, qslot=qslot: e.activation(
                    out=PTF[qslot][:], in_=S[:, 2 * qslot:2 * qslot + 2, 0:256], func=AF.Exp, scale=0.125),
                    waits=[pw, (s_qk[qslot], kq + 1), (s_pm[qslot], kq)], inc=s_ex[qslot])
                P.add("vector", lambda e, qslot=qslot, pslot=pslot: e.tensor_tensor(
                    out=PT[pslot][:], in0=PTF[qslot][:], in1=EB[slot][:], op=ALU.mult),
                    waits=[pw, (s_ex[qslot], kq + 1), (s_pv[pslot], kp), (s_bt, bt_val[pi])], inc=s_pm[qslot])
                pendl.append((qslot, kq, pslot, r, n, gslot, kg, first, last, gidx, gi_ * 128))
                while len(pendl) > 2:
                    emit_pv(pendl.pop(0))
                cn["qi"] += 1
            while pendl:
                emit_pv(pendl.pop(0))
            pass_last_pe[pi] = [(s_pv[i], s_pv[i].v) for i in range(3)]
            pass_last_dve[pi] = [(s_pm[0], s_pm[0].v), (s_pm[1], s_pm[1].v)]
            if g == 2:
                wd = [(s_dv, s_dv.v), (s_nf[0], s_nf[0].v), (s_nf[1], s_nf[1].v)]
                v1 = P.add("vector", lambda e: e.reciprocal(out=DACC[:], in_=DACC[:]), waits=wd, inc=s_fin)
                v2 = P.add("vector", lambda e: e.tensor_tensor(out=OAS[:], in0=NACC[:], in1=DACC[:], op=ALU.mult),
                           waits=[(s_fin, v1), (s_st, 16 * hp)], inc=s_fin)
                cn["fin"] = v2
                P.add("gpsimd", lambda e, hp=hp: e.dma_start(out=D["OAT"][hp], in_=OAS[:]),
                      waits=[pw, (s_fin, v2)], inc=s_st, dma=True)
        for pi, (hp, g) in enumerate(passes):
            run_pass(pi, hp, g)
        P.end_phase(JK[:])


def phase3(nc, P, D):
    with ExitStack() as es:
        KT = [_sb(es, nc, f"p3_KT{i}", [128, NK], BF16) for i in range(2)]
        QT = [_sb(es, nc, f"p3_QT{i}", [128, NQ], BF16) for i in range(2)]
        VH = [_sb(es, nc, f"p3_VH{i}", [128, 64, 129], BF16) for i in range(2)]
        GB = _sb(es, nc, "p3_GB", [128, 8, 256], F32)
        CB = _sb(es, nc, "p3_CB", [128, 8], F32)
        MB = _sb(es, nc, "p3_MB", [128, 256], F32)
        BTF = _sb(es, nc, "p3_BTF", [128, 256], F32)
        BTA = _sb(es, nc, "p3_BTA", [128, 8, 2, 256], F32)
        IDF = _sb(es, nc, "p3_IDF", [128, 128], F32)
        IDB = _sb(es, nc, "p3_IDB", [128, 128], BF16)
        ZB = _sb(es, nc, "p3_ZB", [128, 512], BF16)
        PVF = _sb(es, nc, "p3_PVF", [128, 64], F32)
        LAMI = _sb(es, nc, "p3_LAMI", [128, 4, 64], F32)
        LPR = _sb(es, nc, "p3_LPR", [128, 2, 64], F32)
        LS = _sb(es, nc, "p3_LS", [128, 2], F32)
        LE = _sb(es, nc, "p3_LE", [128, 2], F32)
        NLAM = _sb(es, nc, "p3_NLAM", [128, 1], F32)
        GS = _sb(es, nc, "p3_GS", [128, 128], F32)
        EPSB = _sb(es, nc, "p3_EPSB", [128, 1], F32)
        PT = [_sb(es, nc, f"p3_PT{i}", [128, 2, 512], BF16) for i in range(3)]
        EP = _sb(es, nc, "p3_EP", [128, 3, 512], F32)
        RD = _sb(es, nc, "p3_RD", [128, 8], F32)
        TT = _sb(es, nc, "p3_TT", [128, 128], F32)
        OO = _sb(es, nc, "p3_OO", [128, 4, 128], F32)
        SQ = _sb(es, nc, "p3_SQ", [128, 128], F32)
        SSQ = _sb(es, nc, "p3_SSQ", [128, 4], F32)
        LNV = _sb(es, nc, "p3_LNV", [128, 4], F32)
        RSTD = _sb(es, nc, "p3_RSTD", [128, 4], F32)
        ON = _sb(es, nc, "p3_ON", [128, 4, 128], BF16)
        OBS = [_sb(es, nc, f"p3_OBS{i}", [128, 512], BF16) for i in range(2)]
        JK = _sb(es, nc, "p3_JK", [128, 1], F32)
        CF = [_sb(es, nc, f"p3_CF{i}", [128, 1024], F32) for i in range(2)]
        CBT = [_sb(es, nc, f"p3_CBT{i}", [128, 1024], BF16) for i in range(2)]
        S = _ps(es, nc, "p3_S", [128, 4, 512], F32)
        ACC = _ps(es, nc, "p3_ACC", [128, 3, 512], F32)
        TP = _ps(es, nc, "p3_TP", [128, 4, 128], BF16)
        s_c = P.sem("p3c")
        s_ld = [P.sem(f"p3ld{i}") for i in range(2)]
        s_qk = [P.sem(f"p3qk{i}") for i in range(2)]
        s_ex = [P.sem(f"p3ex{i}") for i in range(2)]
        s_pv = [P.sem(f"p3pv{i}") for i in range(3)]
        s_af = P.sem("p3af")
        s_ep = P.sem("p3ep")
        s_ea = P.sem("p3ea")
        s_tp = P.sem("p3tp")
        s_tc = P.sem("p3tc")
        s_st = [P.sem(f"p3st{i}") for i in range(2)]
        s_cl = [P.sem(f"p3cl{i}") for i in range(2)]
        s_cc = P.sem("p3cc")
        s_cs = [P.sem(f"p3cs{i}") for i in range(2)]
        s_nb = [P.sem(f"p3nb{i}") for i in range(2)]
        pw = P.pw()

        chunks = []
        w1v = D["w1"].rearrange("(kc p) f -> p kc f", p=128)
        w2v = D["w2"].rearrange("(fc p) d -> p fc d", p=128)
        k0 = 0
        for (wsrc, nk) in ((D["wpa"], 4), (D["wpb"], 8), (D["wo"], 8)):
            wv = wsrc.rearrange("(kc p) c -> p kc c", p=128)
            for k in range(nk):
                chunks.append((wv[:, k, :], D["WAB"][:, k0 + k, :]))
            k0 += nk
        for kc in range(8):
            for q in range(4):
                chunks.append((w1v[:, kc, q * 1024:(q + 1) * 1024], D["W1B"][:, kc, q * 1024:(q + 1) * 1024]))
        for fc in range(32):
            chunks.append((w2v[:, fc, :], D["W2B"][:, fc, :]))
        cv = {"k": 0, "ld": {}}

        def wconv_step():
            k = cv["k"]
            if k - 1 >= 0 and k - 1 < len(chunks):
                j = k - 1
                sl = j % 2
                src_, dst_ = chunks[j]
                vcast = P.add("vector", lambda e, sl=sl: e.tensor_copy(out=CBT[sl][:], in_=CF[sl][:]),
                              waits=[pw, (s_cl[sl], cv["ld"][j]), (s_cs[sl], 16 * (j // 2))], inc=s_cc)
                P.add("gpsimd", lambda e, sl=sl, dst_=dst_: e.dma_start(out=dst_, in_=CBT[sl][:]), waits=[pw, (s_cc, vcast)],
                      inc=s_cs[sl], dma=True)
            if k < len(chunks):
                sl = k % 2
                src_, dst_ = chunks[k]
                cv["ld"][k] = P.add("sync", lambda e, sl=sl, src_=src_: e.dma_start(out=CF[sl][:], in_=src_), waits=[pw, (s_cc, k - 1)],
                                    inc=s_cl[sl], dma=True)
            cv["k"] += 1

        def accap(m, j):
            idx = m * 4 + j
            return idx // 3, (idx % 3) * 170

        for (dst, src) in ((GB, D["gb"].rearrange("h i c -> i h c")), (CB, D["cb"]), (MB, D["mb"]), (IDF, D["ident"]),
                           (PVF, D["pvalid"]), (LAMI, D["lam"]), (GS, D["gs"])):
            vc = P.add("sync", lambda e, dst=dst, src=src: e.dma_start(out=dst[:], in_=src), waits=[pw], inc=s_c, dma=True)
        w0 = [pw, (s_c, vc)]
        P.add("vector", lambda e: e.tensor_copy(out=IDB[:], in_=IDF[:]), waits=w0, inc=s_c)
        P.add("vector", lambda e: e.memset(ZB[:], 0.0), waits=w0, inc=s_c)
        P.add("vector", lambda e: e.memset(EPSB[:], LN_EPS), waits=w0, inc=s_c)
        v = P.add("vector", lambda e: e.tensor_scalar(out=GS[:], in0=GS[:], scalar1=1.0 - LAM_INIT, scalar2=None, op0=ALU.mult),
                  waits=w0, inc=s_c)
        for i in range(2):
            P.add("vector", lambda e, i=i: e.memset(VH[i][:, 16:64, 128:129], 1.0), waits=w0, inc=s_c)
            P.add("vector", lambda e, i=i: e.tensor_copy(out=VH[i][:, 0:16, 128:129],
                                                         in_=PVF[:, 0:16].rearrange("p (a b) -> p a b", b=1)),
                  waits=w0, inc=s_c)
        v = P.add("vector", lambda e: e.tensor_tensor(out=LPR[:], in0=LAMI[:, 0:2, :], in1=LAMI[:, 2:4, :], op=ALU.mult),
                  waits=w0, inc=s_c)
        v = P.add("vector", lambda e: e.reduce_sum(out=LS[:], in_=LPR[:], axis=AX.X), waits=[(s_c, v)], inc=s_c)
        v = P.add("scalar", lambda e: e.activation(out=LE[:], in_=LS[:], func=AF.Exp), waits=[pw, (s_c, v)], inc=s_c)
        v = P.add("vector", lambda e: e.tensor_tensor(out=NLAM[:], in0=LE[:, 1:2], in1=LE[:, 0:1], op=ALU.subtract),
                  waits=[(s_c, v)], inc=s_c)
        v = P.add("vector", lambda e: e.tensor_scalar(out=NLAM[:], in0=NLAM[:], scalar1=-LAM_INIT, scalar2=None, op0=ALU.add),
                  waits=[(s_c, v)], inc=s_c)
        for h in range(8):
            v = P.add("vector", lambda e, h=h: e.tensor_scalar(out=BTF[:], in0=GB[:, h, :], scalar1=CB[:, h:h + 1], scalar2=8.0,
                                                               op0=ALU.subtract, op1=ALU.mult), waits=[(s_c, v)], inc=s_c)
            v = P.add("vector", lambda e, h=h: e.tensor_tensor(out=BTA[:, h, 0, :], in0=BTF[:], in1=MB[:], op=ALU.add), waits=[(s_c, v)], inc=s_c)
            v = P.add("vector", lambda e, h=h: e.tensor_tensor(out=BTA[:, h, 1, :], in0=BTF[:], in1=MB[:], op=ALU.add), waits=[(s_c, v)], inc=s_c)
        vconst = v

        head_last_pe = {}
        ld_val = {}
        vbv = D["VB"].rearrange("(kb p) c -> p kb c", p=128)

        def sched_loads(h):
            slot = h % 2
            w = [pw, (s_c, vconst)] + (head_last_pe[h - 2] if h >= 2 else [])
            P.add("sync", lambda e: e.dma_start(out=KT[slot][:], in_=D["KTB"][h]), waits=w, inc=s_ld[slot], dma=True)
            P.add("sync", lambda e: e.dma_start(out=QT[slot][:], in_=D["QTB"][h]), waits=w, inc=s_ld[slot], dma=True)
            for q in range(4):
                v = P.add("sync", lambda e, q=q: e.dma_start(out=VH[slot][:, q * 16:(q + 1) * 16, 0:128],
                                                           in_=vbv[:, q * 16:(q + 1) * 16, h * 128:(h + 1) * 128]),
                          waits=w, inc=s_ld[slot], dma=True)
            ld_val[h] = v

        deferred = []
        ui = [0]

        def flush(force=False):
            while deferred and (force or deferred[0][0] <= ui[0]):
                _, fn = deferred.pop(0)
                fn()

        sched_loads(0)
        cn = {"nqt": 0}
        pending = []

        def emit_pv(info):
            (sslot, ku, pslot, kb, jmin, firstu, lastu, hs, kq_tile, on_last) = info

            def pv_bank(bsel):
                def pv(e):
                    ins = None
                    if firstu:
                        e.matmul(ACC[:, bsel, :], lhsT=ZB[:, 0:128], rhs=ZB[:, :], start=True, stop=True, skip_group_check=True)
                    for m in range(2):
                        for j in range(jmin, 4):
                            b, off = accap(m, j)
                            if bsel is not None and b != bsel:
                                continue
                            ins = e.matmul(ACC[:, b, off:off + 129], lhsT=PT[pslot][:, m, j * 128:(j + 1) * 128],
                                           rhs=VH[hs][:, kb, :], start=False, stop=lastu, skip_group_check=True)
                    return ins
                return pv

            w = [(s_ex[sslot], ku + 1)]
            if firstu:
                P.add("tensor", pv_bank(0), waits=w + [(s_af, 3 * kq_tile - 2)], inc=None)
                P.add("tensor", pv_bank(1), waits=[(s_af, 3 * kq_tile - 1)], inc=None)
                v = P.add("tensor", pv_bank(2), waits=[(s_af, 3 * kq_tile)], inc=s_pv[pslot])
            else:
                v = P.add("tensor", pv_bank(None), waits=w, inc=s_pv[pslot])
            if lastu:
                on_last(pslot, v)

        def drain(keep):
            while len(pending) > keep:
                emit_pv(pending.pop(0))

        def run_qt(h, qt):
                hs = h % 2
                base = 4 * (qt + 4 if qt < 4 else qt + 8)
                nkb = base + 4
                kq_tile = cn["nqt"]
                cn["nqt"] += 1
                nqt = cn["nqt"]

                def on_last(last_slot, vlast):
                    epilogue(last_slot, vlast)

                for kb in range(nkb):
                    u = ui[0]
                    slot = u % 2
                    ku = u // 2
                    pslot = u % 3
                    kp = u // 3
                    jd = kb - base
                    c0 = max(jd, 0) * 128
                    jmin = max(jd, 0)

                    def qk(e, kb=kb, c0=c0, slot=slot, qt=qt):
                        ins = None
                        for m in range(2):
                            ins = e.matmul(S[:, 2 * slot + m, c0:512], lhsT=KT[hs][m * 64:(m + 1) * 64, kb * 128:(kb + 1) * 128],
                                           rhs=QT[hs][m * 64:(m + 1) * 64, qt * 512 + c0:(qt + 1) * 512], start=True, stop=True)
                        return ins

                    vqk = P.add("tensor", qk, waits=[pw, (s_ld[hs], ld_val[h]), (s_c, vconst), (s_ex[slot], ku)], inc=s_qk[slot])
                    wex = [pw, (s_qk[slot], ku + 1), (s_pv[pslot], kp)]
                    if jd >= -1:
                        if jd == -1:
                            oc, bc, n = 0, 128, 128
                        elif jd == 3:
                            oc, bc, n = 384, 0, 128
                        else:
                            oc, bc, n = jd * 128, 0, 256
                        vnb = P.add("vector", lambda e, slot=slot, oc=oc, bc=bc, n=n: e.tensor_tensor(
                            out=S[:, 2 * slot:2 * slot + 2, oc:oc + n], in0=S[:, 2 * slot:2 * slot + 2, oc:oc + n],
                            in1=BTA[:, h, :, bc:bc + n], op=ALU.add), waits=[pw, (s_qk[slot], vqk)], inc=s_nb[slot])
                        wex.append((s_nb[slot], vnb))
                    P.add("scalar", lambda e, slot=slot, c0=c0, pslot=pslot: e.activation(
                        out=PT[pslot][:, :, c0:512], in_=S[:, 2 * slot:2 * slot + 2, c0:512], func=AF.Exp, scale=0.125),
                        waits=wex, inc=s_ex[slot])
                    pending.append((slot, ku, pslot, kb, jmin, kb == 0, kb == nkb - 1, hs, kq_tile, on_last))
                    drain(2)
                    ui[0] += 1
                    if ui[0] % 24 == 0:
                        wconv_step()
                    flush()

                def epilogue(last_slot, vlast):

                    def stage_a(last_slot=last_slot, vlast=vlast, h=h, qt=qt, k=nqt):
                        v = None
                        for b in range(3):
                            v = P.add("vector", lambda e, b=b: e.tensor_copy(out=EP[:, b, :], in_=ACC[:, b, :]),
                                      waits=[(s_pv[last_slot], vlast), (s_ep, s_ep.v)], inc=s_af)
                        we = [(s_af, v)]
                        for m in range(2):
                            for j in range(4):
                                b, off = accap(m, j)
                                idx = m * 4 + j
                                P.add("vector", lambda e, b=b, off=off, idx=idx: e.reciprocal(
                                    out=RD[:, idx:idx + 1], in_=EP[:, b, off + 128:off + 129]), waits=we, inc=s_ep)
                        v = P.add("vector", lambda e: e.tensor_scalar(out=RD[:, 4:8], in0=RD[:, 4:8], scalar1=NLAM[:, 0:1], scalar2=None,
                                                                      op0=ALU.mult), waits=[(s_ep, s_ep.v)], inc=s_ep)
                        for j in range(4):
                            b1, o1 = accap(0, j)
                            b2, o2 = accap(1, j)
                            v = P.add("vector", lambda e, b2=b2, o2=o2, j=j: e.tensor_scalar(
                                out=TT[:], in0=EP[:, b2, o2:o2 + 128], scalar1=RD[:, 4 + j:5 + j], scalar2=None, op0=ALU.mult),
                                waits=[(s_ep, v)], inc=s_ep)
                            v = P.add("vector", lambda e, b1=b1, o1=o1, j=j: e.scalar_tensor_tensor(
                                out=OO[:, j, :], in0=EP[:, b1, o1:o1 + 128], scalar=RD[:, j:j + 1], in1=TT[:],
                                op0=ALU.mult, op1=ALU.add), waits=[(s_ep, v)], inc=s_ep)
                            v = P.add("vector", lambda e, j=j: e.scalar_tensor_tensor(
                                out=SQ[:], in0=OO[:, j, :], scalar=1.0, in1=OO[:, j, :], op0=ALU.mult, op1=ALU.mult,
                                accum_out=SSQ[:, j:j + 1]), waits=[(s_ep, v)], inc=s_ep)
                        return v

                    va = stage_a()

                    def stage_bc(va=va):
                        v = P.add("scalar", lambda e: e.activation(out=LNV[:], in_=SSQ[:], func=AF.Ln, bias=EPSB[:, 0:1], scale=1.0 / 128.0),
                                  waits=[(s_ep, va), (s_ea, s_ea.v)], inc=s_ea)
                        v = P.add("scalar", lambda e: e.activation(out=RSTD[:], in_=LNV[:], func=AF.Exp, scale=-0.5),
                                  waits=[(s_ea, v)], inc=s_ea)
                        vv = None
                        for j in range(4):
                            vv = P.add("vector", lambda e, j=j: e.scalar_tensor_tensor(
                                out=ON[:, j, :], in0=OO[:, j, :], scalar=RSTD[:, j:j + 1], in1=GS[:], op0=ALU.mult, op1=ALU.mult),
                                waits=[(s_ea, v), (s_tp, s_tp.v)], inc=s_ep)
                        return vv

                    def stage_de(vc, h=h, qt=qt, k=nqt):
                        def tp(e):
                            ins = None
                            for j in range(4):
                                ins = e.transpose(out=TP[:, j, :], in_=ON[:, j, :], identity=IDB[:])
                            return ins
                        v = P.add("tensor", tp, waits=[(s_ep, vc), (s_tc, s_tc.v)], inc=s_tp)
                        os_ = (k - 1) % 2
                        v2 = P.add("vector", lambda e: e.tensor_copy(out=OBS[os_][:].rearrange("p (a b) -> p a b", a=4), in_=TP[:]),
                                   waits=[(s_tp, v), (s_st[os_], 16 * ((k - 1) // 2))], inc=s_tc)
                        P.add("gpsimd", lambda e: e.dma_start(out=D["OBT"][h][:, qt * 512:(qt + 1) * 512], in_=OBS[os_][:]),
                              waits=[pw, (s_tc, v2)], inc=s_st[os_], dma=True)

                    def chain(stage_bc=stage_bc, stage_de=stage_de):
                        vc = stage_bc()
                        deferred.append((ui[0] + 4, lambda: stage_de(vc)))

                    deferred.append((ui[0] + 12, chain))
        for h in range(8):
            if h + 1 < 8:
                sched_loads(h + 1)
            for qt in range(8):
                run_qt(h, qt)
            drain(0)
            head_last_pe[h] = [(s_pv[i], s_pv[i].v) for i in range(3)]
        flush(force=True)
        flush(force=True)
        while cv["k"] <= len(chunks):
            wconv_step()
        P.end_phase(JK[:])


def layer_norm_ops(P, V, ST6, MV, STD, RSTD, TMP, OUT, LG, LB, EPSB, sem, wait_in):
    v = None
    for hf in range(2):
        v = P.add("vector", lambda e, hf=hf: e.bn_stats(out=ST6[:, hf * 6:(hf + 1) * 6], in_=V[:, hf * 512:(hf + 1) * 512]),
                  waits=wait_in + [(sem, sem.v)], inc=sem)
    v = P.add("vector", lambda e: e.bn_aggr(out=MV[:], in_=ST6[:]), waits=[(sem, v)], inc=sem)
    v = P.add("scalar", lambda e: e.activation(out=STD[:], in_=MV[:, 1:2], func=AF.Sqrt, bias=EPSB[:, 0:1], scale=1.0),
              waits=[(sem, v)], inc=sem)
    v = P.add("vector", lambda e: e.reciprocal(out=RSTD[:], in_=STD[:]), waits=[(sem, v)], inc=sem)
    v = P.add("vector", lambda e: e.scalar_tensor_tensor(out=TMP[:], in0=V[:], scalar=MV[:, 0:1], in1=LG[:],
                                                         op0=ALU.subtract, op1=ALU.mult), waits=[(sem, v)], inc=sem)
    v = P.add("vector", lambda e: e.scalar_tensor_tensor(out=OUT, in0=TMP[:], scalar=RSTD[:, 0:1], in1=LB[:],
                                                         op0=ALU.mult, op1=ALU.add), waits=[(sem, v)], inc=sem)
    return v


def load_cast_weight(P, pw, dst_chunks, src_chunks, WF, s_wld, s_wcv, cnt):
    for dst, src in zip(dst_chunks, src_chunks):
        n = cnt[0]
        slot = n % 2
        v = P.add("sync", lambda e, slot=slot, src=src: e.dma_start(out=WF[slot][:], in_=src),
                  waits=[pw, (s_wcv, n - 1)], inc=s_wld[slot], dma=True)
        P.add("vector", lambda e, slot=slot, dst=dst: e.tensor_copy(out=dst, in_=WF[slot][:]),
              waits=[pw, (s_wld[slot], v)], inc=s_wcv)
        cnt[0] += 1
    return s_wcv.v


def phase4a(nc, P, D):
    with ExitStack() as es:
        WPA = _sb(es, nc, "p4_WPA", [128, 4, 1024], BF16)
        WPB = _sb(es, nc, "p4_WPB", [128, 8, 1024], BF16)
        WO = _sb(es, nc, "p4_WO", [128, 8, 1024], BF16)
        OAt = [_sb(es, nc, f"p4_OA{i}", [128, 4, 512], BF16) for i in range(2)]
        OBt = [_sb(es, nc, f"p4_OB{i}", [128, 8, 512], BF16) for i in range(2)]
        SGt = _sb(es, nc, "p4_SG", [128, 16, 512], BF16)
        XS = [_sb(es, nc, f"p4_XS{i}", [128, 4, 1024], F32) for i in range(2)]
        MT = _sb(es, nc, "p4_MT", [128, 8, 512], BF16)
        T1 = [_sb(es, nc, f"p4_T1{i}", [128, 512], F32) for i in range(2)]
        T2 = [_sb(es, nc, f"p4_T2{i}", [128, 512], F32) for i in range(2)]
        V = _sb(es, nc, "p4_V", [128, 1024], F32)
        TMP = _sb(es, nc, "p4_TMP", [128, 1024], F32)
        X1O = [_sb(es, nc, f"p4_X1O{i}", [128, 1024], F32) for i in range(2)]
        X1B = [_sb(es, nc, f"p4_X1B{i}", [128, 1024], BF16) for i in range(2)]
        X1TS = _sb(es, nc, "p4_X1TS", [128, 8, 512], BF16)
        LG = _sb(es, nc, "p4_LG", [128, 1024], F32)
        LB = _sb(es, nc, "p4_LB", [128, 1024], F32)
        ST6 = _sb(es, nc, "p4_ST6", [128, 12], F32)
        MV = _sb(es, nc, "p4_MV", [128, 2], F32)
        STD = _sb(es, nc, "p4_STD", [128, 1], F32)
        RSTD = _sb(es, nc, "p4_RSTD", [128, 1], F32)
        EPSB = _sb(es, nc, "p4_EPSB", [128, 1], F32)
        IDF = _sb(es, nc, "p4_IDF", [128, 128], F32)
        IDB = _sb(es, nc, "p4_IDB", [128, 128], BF16)
        JK = _sb(es, nc, "p4_JK", [128, 1], F32)
        PSY = _ps(es, nc, "p4_PSY", [128, 4, 512], F32)
        PSM = _ps(es, nc, "p4_PSM", [128, 2, 512], F32)
        PST = _ps(es, nc, "p4_PST", [128, 8, 128], BF16)
        s_c = P.sem("p4c")
        s_wld = [P.sem(f"p4wld{i}") for i in range(2)]
        s_wcv = P.sem("p4wcv")
        s_ldab = [P.sem(f"p4ldab{i}") for i in range(2)]
        s_ldsg = P.sem("p4ldsg")
        s_ldx = [P.sem(f"p4ldx{i}") for i in range(2)]
        s_y = [P.sem(f"p4y{i}") for i in range(2)]
        s_t = [P.sem(f"p4t{i}") for i in range(2)]
        s_mt = P.sem("p4mt")
        s_mm = [P.sem(f"p4mm{i}") for i in range(2)]
        s_v = [P.sem(f"p4v{i}") for i in range(2)]
        s_ln = P.sem("p4ln")
        s_xb = P.sem("p4xb")
        s_tp = P.sem("p4tp")
        s_tc = P.sem("p4tc")
        s_so = [P.sem(f"p4so{i}") for i in range(2)]
        s_sx = P.sem("p4sx")
        pw = P.pw()

        for (dst, src) in ((LG, D["ln1g"]), (LB, D["ln1b"]), (IDF, D["ident"])):
            vc = P.add("sync", lambda e, dst=dst, src=src: e.dma_start(out=dst[:], in_=src), waits=[pw], inc=s_c, dma=True)
        P.add("vector", lambda e: e.tensor_copy(out=IDB[:], in_=IDF[:]), waits=[pw, (s_c, vc)], inc=s_c)
        vconst = P.add("vector", lambda e: e.memset(EPSB[:], LN_EPS), waits=[pw], inc=s_c)

        P.add("sync", lambda e: e.dma_start(out=WPA[:], in_=D["WAB"][:, 0:4, :]), waits=[pw], inc=s_wcv, dma=True)
        P.add("sync", lambda e: e.dma_start(out=WPB[:], in_=D["WAB"][:, 4:12, :]), waits=[pw], inc=s_wcv, dma=True)
        vw = P.add("sync", lambda e: e.dma_start(out=WO[:], in_=D["WAB"][:, 12:20, :]), waits=[pw], inc=s_wcv, dma=True)

        oat = D["OAT"].rearrange("k p t -> p k t")
        obt = D["OBT"].rearrange("k p t -> p k t")
        sgt = D["SG"].rearrange("k p t -> p k t")
        xo = D["xo"].rearrange("(s p) d -> p s d", p=128)
        x1 = D["X1"].rearrange("(s p) d -> p s d", p=128)
        x1t = D["X1T"].rearrange("k p t -> p k t")

        last_y_pe = {}
        ld_ab = {}

        ld_x = {}
        last_v = {}

        def load_ab(t):
            sl = t % 2
            w = [pw] + (last_y_pe[t - 2] if t >= 2 else [])
            P.add("sync", lambda e: e.dma_start(out=OAt[sl][:], in_=oat[:, :, t * 512:(t + 1) * 512]), waits=w, inc=s_ldab[sl], dma=True)
            ld_ab[t] = P.add("sync", lambda e: e.dma_start(out=OBt[sl][:], in_=obt[:, :, t * 512:(t + 1) * 512]), waits=w,
                             inc=s_ldab[sl], dma=True)
            ld_x[t] = P.add("sync", lambda e: e.dma_start(out=XS[sl][:], in_=xo[:, 4 * t:4 * t + 4, :]),
                            waits=[pw] + (last_v[t - 2] if t >= 2 else []), inc=s_ldx[sl], dma=True)

        ld_sg = {}

        def load_sg(t):
            ld_sg[t] = P.add("sync", lambda e: e.dma_start(out=SGt[:], in_=sgt[:, :, t * 512:(t + 1) * 512]),
                             waits=[pw, (s_t[0], s_t[0].v), (s_t[1], s_t[1].v)], inc=s_ldsg, dma=True)

        load_ab(0)
        load_sg(0)
        yj = 0
        mj = 0
        sbk = 0
        pending_tp = []

        def flush_tp():
            while pending_tp:
                pending_tp.pop(0)()

        for t in range(8):
            sl = t % 2
            if t + 1 < 8:
                load_ab(t + 1)
            v_sg = ld_sg[t]
            v_x = ld_x[t]
            mt_w0 = (s_mm[0], s_mm[0].v), (s_mm[1], s_mm[1].v)
            for dc in range(8):
                ys = yj % 2
                ky = yj // 2

                def ymm(e, ys=ys, dc=dc, sl=sl):
                    ins = None
                    for k in range(4):
                        ins = e.matmul(PSY[:, 2 * ys, :], lhsT=WPA[:, k, dc * 128:(dc + 1) * 128], rhs=OAt[sl][:, k, :],
                                       start=(k == 0), stop=(k == 3))
                    for k in range(8):
                        ins = e.matmul(PSY[:, 2 * ys + 1, :], lhsT=WPB[:, k, dc * 128:(dc + 1) * 128], rhs=OBt[sl][:, k, :],
                                       start=(k == 0), stop=(k == 7))
                    return ins

                P.add("tensor", ymm, waits=[pw, (s_wcv, vw), (s_ldab[sl], ld_ab[t]), (s_t[ys], 2 * ky)], inc=s_y[ys])
                wt = [pw, (s_y[ys], ky + 1), (s_ldsg, v_sg), (s_mt, s_mt.v - 1)]
                P.add("vector", lambda e, ys=ys, dc=dc: e.tensor_tensor(out=T1[ys][:], in0=PSY[:, 2 * ys, :], in1=SGt[:, dc, :], op=ALU.mult),
                      waits=wt, inc=s_t[ys])
                vt = P.add("vector", lambda e, ys=ys, dc=dc: e.tensor_tensor(out=T2[ys][:], in0=PSY[:, 2 * ys + 1, :], in1=SGt[:, 8 + dc, :],
                                                                         op=ALU.mult), waits=wt, inc=s_t[ys])
                P.add("vector", lambda e, ys=ys, dc=dc: e.tensor_tensor(out=MT[:, dc, :], in0=T1[ys][:], in1=T2[ys][:], op=ALU.add),
                      waits=[pw, (s_t[ys], vt)] + list(mt_w0), inc=s_mt)
                yj += 1
                if dc == 1:
                    flush_tp()
            last_y_pe[t] = [(s_y[0], s_y[0].v), (s_y[1], s_y[1].v)]
            if t + 1 < 8:
                load_sg(t + 1)
            v_mt = s_mt.v
            for s in range(4):
                xs_o = sbk % 2
                for hf in range(2):
                    ms = mj % 2
                    km = mj // 2

                    def mmm(e, ms=ms, s=s, hf=hf):
                        ins = None
                        for k in range(8):
                            ins = e.matmul(PSM[:, ms, :], lhsT=MT[:, k, s * 128:(s + 1) * 128], rhs=WO[:, k, hf * 512:(hf + 1) * 512],
                                           start=(k == 0), stop=(k == 7))
                        return ins

                    P.add("tensor", mmm, waits=[pw, (s_mt, v_mt), (s_v[ms], km)], inc=s_mm[ms])
                    vv = P.add("vector", lambda e, ms=ms, s=s, hf=hf, sl=sl: e.scalar_tensor_tensor(
                        out=V[:, hf * 512:(hf + 1) * 512], in0=XS[sl][:, s, hf * 512:(hf + 1) * 512], scalar=ALPHA, in1=PSM[:, ms, :],
                        op0=ALU.mult, op1=ALU.add), waits=[pw, (s_mm[ms], km + 1), (s_ldx[sl], v_x), (s_ln, s_ln.v)], inc=s_v[ms])
                    mj += 1
                flush_tp()
                win = [(s_v[0], s_v[0].v), (s_v[1], s_v[1].v), (s_c, vconst), (s_so[xs_o], 16 * (sbk // 2)), (s_xb, sbk - 1)]
                vln = layer_norm_ops(P, V, ST6, MV, STD, RSTD, TMP, X1O[xs_o][:], LG, LB, EPSB, s_ln, win)
                row = 4 * t + s
                P.add("gpsimd", lambda e, xs_o=xs_o, row=row: e.dma_start(out=x1[:, row, :], in_=X1O[xs_o][:]),
                      waits=[pw, (s_ln, vln)], inc=s_so[xs_o], dma=True)
                vb = P.add("scalar", lambda e, xs_o=xs_o: e.copy(out=X1B[xs_o][:], in_=X1O[xs_o][:]),
                           waits=[pw, (s_ln, vln), (s_tp, sbk - 1)], inc=s_xb)

                def tp_group(t=t, s=s, xs_o=xs_o, vb=vb):
                    def tp(e):
                        ins = None
                        for k in range(8):
                            ins = e.transpose(out=PST[:, k, :], in_=X1B[xs_o][:, k * 128:(k + 1) * 128], identity=IDB[:])
                        return ins

                    vtp = P.add("tensor", tp, waits=[pw, (s_xb, vb), (s_tc, s_tc.v)], inc=s_tp)
                    P.add("scalar", lambda e: e.copy(out=X1TS[:, :, s * 128:(s + 1) * 128], in_=PST[:]),
                          waits=[pw, (s_tp, vtp), (s_sx, 16 * t)], inc=s_tc)
                    if s == 3:
                        P.add("gpsimd", lambda e: e.dma_start(out=x1t[:, :, t * 512:(t + 1) * 512], in_=X1TS[:]),
                              waits=[pw, (s_tc, s_tc.v)], inc=s_sx, dma=True)

                pending_tp.append(tp_group)
                sbk += 1
            last_v[t] = [(s_v[0], s_v[0].v), (s_v[1], s_v[1].v)]
        flush_tp()
        P.end_phase(JK[:])


def phase4b(nc, P, D):
    TT = 256
    NT = NQ // TT
    with ExitStack() as es:
        W1 = _sb(es, nc, "p5_W1", [128, 8, 4096], BF16)
        W2 = _sb(es, nc, "p5_W2", [128, 32, 1024], BF16)
        XT = [_sb(es, nc, f"p5_XT{i}", [128, 8, TT], BF16) for i in range(2)]
        X1 = _sb(es, nc, "p5_X1", [128, 2, 1024], F32)
        HT = _sb(es, nc, "p5_HT", [128, 32, TT], BF16)
        RT = [_sb(es, nc, f"p5_RT{i}", [128, TT], F32) for i in range(2)]
        V = _sb(es, nc, "p5_V", [128, 1024], F32)
        TMP = _sb(es, nc, "p5_TMP", [128, 1024], F32)
        YO = [_sb(es, nc, f"p5_YO{i}", [128, 1024], F32) for i in range(2)]
        LG = _sb(es, nc, "p5_LG", [128, 1024], F32)
        LB = _sb(es, nc, "p5_LB", [128, 1024], F32)
        ST6 = _sb(es, nc, "p5_ST6", [128, 12], F32)
        MV = _sb(es, nc, "p5_MV", [128, 2], F32)
        STD = _sb(es, nc, "p5_STD", [128, 1], F32)
        RSTD = _sb(es, nc, "p5_RSTD", [128, 1], F32)
        EPSB = _sb(es, nc, "p5_EPSB", [128, 1], F32)
        JK = _sb(es, nc, "p5_JK", [128, 1], F32)
        PSH = _ps(es, nc, "p5_PSH", [128, 4, 512], F32)
        PSF = _ps(es, nc, "p5_PSF", [128, 2, 512], F32)
        s_c = P.sem("p5c")
        s_wld = [P.sem(f"p5wld{i}") for i in range(2)]
        s_wcv = P.sem("p5wcv")
        s_ldt = [P.sem(f"p5ldt{i}") for i in range(2)]
        s_ldx = P.sem("p5ldx")
        s_h = [P.sem(f"p5h{i}") for i in range(4)]
        s_r = [P.sem(f"p5r{i}") for i in range(4)]
        s_sq = [P.sem(f"p5sq{i}") for i in range(2)]
        s_f = [P.sem(f"p5f{i}") for i in range(2)]
        s_v = [P.sem(f"p5v{i}") for i in range(2)]
        s_ln = P.sem("p5ln")
        s_so = [P.sem(f"p5so{i}") for i in range(2)]
        pw = P.pw()

        for (dst, src) in ((LG, D["ln2g"]), (LB, D["ln2b"])):
            vc = P.add("sync", lambda e, dst=dst, src=src: e.dma_start(out=dst[:], in_=src), waits=[pw], inc=s_c, dma=True)
        vconst = P.add("vector", lambda e: e.memset(EPSB[:], LN_EPS), waits=[pw, (s_c, vc)], inc=s_c)

        vw1 = None
        for q in range(4):
            vw1 = P.add("sync", lambda e, q=q: e.dma_start(out=W1[:, 2 * q:2 * q + 2, :], in_=D["W1B"][:, 2 * q:2 * q + 2, :]),
                        waits=[pw], inc=s_wld[0], dma=True)
        vw2 = None
        for q in range(4):
            vw2 = P.add("sync", lambda e, q=q: e.dma_start(out=W2[:, 8 * q:8 * q + 8, :], in_=D["W2B"][:, 8 * q:8 * q + 8, :]),
                        waits=[pw], inc=s_wld[1], dma=True)

        x1t = D["X1T"].rearrange("k p t -> p k t")
        x1 = D["X1"].rearrange("(s p) d -> p s d", p=128)
        yv = D["y"].rearrange("(s p) d -> p s d", p=128)
        last_h_pe = {}
        ld_t = {}

        def load_t(t):
            sl = t % 2
            w = [pw] + (last_h_pe[t - 2] if t >= 2 else [])
            ld_t[t] = P.add("sync", lambda e: e.dma_start(out=XT[sl][:], in_=x1t[:, :, t * TT:(t + 1) * TT]), waits=w,
                            inc=s_ldt[sl], dma=True)

        load_t(0)
        hj = 0
        fj = 0
        sbk = 0
        for t in range(NT):
            sl = t % 2
            if t + 1 < NT:
                load_t(t + 1)
            v_x = P.add("sync", lambda e, t=t: e.dma_start(out=X1[:], in_=x1[:, 2 * t:2 * t + 2, :]),
                        waits=[pw, (s_v[0], s_v[0].v), (s_v[1], s_v[1].v)], inc=s_ldx, dma=True)
            ht_w0 = [(s_f[0], s_f[0].v), (s_f[1], s_f[1].v)]
            for fc in range(32):
                hs = hj % 4
                kh = hj // 4
                rs = hj % 2
                kr = hj // 2

                def hmm(e, hs=hs, fc=fc, sl=sl):
                    ins = None
                    for k in range(8):
                        ins = e.matmul(PSH[:, hs, 0:TT], lhsT=W1[:, k, fc * 128:(fc + 1) * 128], rhs=XT[sl][:, k, :],
                                       start=(k == 0), stop=(k == 7))
                    return ins

                P.add("tensor", hmm, waits=[pw, (s_wld[0], vw1), (s_ldt[sl], ld_t[t]), (s_r[hs], kh)], inc=s_h[hs])
                P.add("scalar", lambda e, hs=hs, rs=rs: e.activation(out=RT[rs][:], in_=PSH[:, hs, 0:TT], func=AF.Relu),
                      waits=[pw, (s_h[hs], kh + 1), (s_sq[rs], kr)], inc=s_r[hs])
                P.add("vector",
                      lambda e, rs=rs, fc=fc: e.tensor_tensor(out=HT[:, fc, :], in0=RT[rs][:], in1=RT[rs][:], op=ALU.mult),
                      waits=[pw, (s_r[hs], kh + 1)] + ht_w0, inc=s_sq[rs])
                hj += 1
            last_h_pe[t] = [(s_h[i], s_h[i].v) for i in range(4)]
            v_sq = [(s_sq[0], s_sq[0].v), (s_sq[1], s_sq[1].v)]
            for s in range(2):
                yo = sbk % 2
                for hf in range(2):
                    fs = fj % 2
                    kf = fj // 2

                    def fmm(e, fs=fs, s=s, hf=hf):
                        ins = None
                        for fc in range(32):
                            ins = e.matmul(PSF[:, fs, :], lhsT=HT[:, fc, s * 128:(s + 1) * 128], rhs=W2[:, fc, hf * 512:(hf + 1) * 512],
                                           start=(fc == 0), stop=(fc == 31))
                        return ins

                    P.add("tensor", fmm, waits=[pw, (s_wld[1], vw2), (s_v[fs], kf)] + v_sq, inc=s_f[fs])
                    P.add("vector", lambda e, fs=fs, s=s, hf=hf: e.scalar_tensor_tensor(
                        out=V[:, hf * 512:(hf + 1) * 512], in0=X1[:, s, hf * 512:(hf + 1) * 512], scalar=ALPHA, in1=PSF[:, fs, :],
                        op0=ALU.mult, op1=ALU.add), waits=[pw, (s_f[fs], kf + 1), (s_ldx, v_x), (s_ln, s_ln.v)], inc=s_v[fs])
                    fj += 1
                win = [(s_v[0], s_v[0].v), (s_v[1], s_v[1].v), (s_c, vconst), (s_so[yo], 16 * (sbk // 2))]
                vln = layer_norm_ops(P, V, ST6, MV, STD, RSTD, TMP, YO[yo][:], LG, LB, EPSB, s_ln, win)
                row = 2 * t + s
                P.add("sync", lambda e, yo=yo, row=row: e.dma_start(out=yv[:, row, :], in_=YO[yo][:]),
                      waits=[pw, (s_ln, vln)], inc=s_so[yo], dma=True)
                sbk += 1
        P.end_phase(JK[:])


def build_nc(debug=False, phases=(1, 2, 3, 4, 5)):
    nc = bass.Bass("TRN2", target_bir_lowering=False)
    D = {}

    def din(name, shape):
        D[name] = nc.dram_tensor(name, shape, F32, kind="ExternalInput").ap()

    din("xT", [1024, NK]); din("xo", [NQ, 1024]); din("wp", [1024, COLS_IN]); din("bg", [128, 16])
    din("ga", [24, 128, 256]); din("ma", [128, 256]); din("gb", [8, 128, 256]); din("cb", [128, 8]); din("mb", [128, 256])
    din("ident", [128, 128]); din("pvalid", [128, 64]); din("lam", [128, 4, 64]); din("gs", [128, 128])
    din("ln1g", [128, 1024]); din("ln1b", [128, 1024]); din("ln2g", [128, 1024]); din("ln2b", [128, 1024])
    din("wpa", [512, 1024]); din("wpb", [1024, 1024]); din("wo", [1024, 1024]); din("w1", [1024, 4096]); din("w2", [4096, 1024])
    D["y"] = nc.dram_tensor("y", [NQ, 1024], F32, kind="ExternalOutput").ap()
    kind = "ExternalOutput" if debug else "Internal"

    def scr(name, shape, dt):
        D[name] = nc.dram_tensor(name, shape, dt, kind=kind).ap()

    scr("QTB", [8, 128, NQ], BF16); scr("KTB", [8, 128, NK], BF16); scr("VB", [NK, 1024], BF16)
    scr("QTA", [12, 128, NQ], BF16); scr("KTA", [12, 128, NKA], BF16); scr("VA", [NKA, 1536], BF16)
    scr("SG", [16, 128, NQ], BF16); scr("OAT", [4, 128, NQ], BF16); scr("OBT", [8, 128, NQ], BF16)
    scr("X1", [NQ, 1024], F32); scr("X1T", [8, 128, NQ], BF16)
    scr("WAB", [128, 20, 1024], BF16); scr("W1B", [128, 8, 4096], BF16); scr("W2B", [128, 32, 1024], BF16)
    with ExitStack() as es:
        P = Prog(nc, es)
        if 1 in phases:
            phase1(nc, P, D)
        if 2 in phases:
            phase2(nc, P, D)
        if 3 in phases:
            phase3(nc, P, D)
        if 4 in phases:
            phase4a(nc, P, D)
        if 5 in phases:
            phase4b(nc, P, D)
    return nc


def t5_bucket_np(dist):
    n = np.maximum(dist, 0)
    nf = np.maximum(n, 1).astype(np.float32)
    large = 16 + (np.log(nf / np.float32(16.0)) / np.float32(math.log(8.0)) * np.float32(16.0)).astype(np.int32)
    large = np.minimum(large, 31)
    return np.where(n < 16, n, large)


def prep_shared(inp):
    f = lambda a: np.ascontiguousarray(np.asarray(a, dtype=np.float32))
    w = np.asarray(inp["w_in"], dtype=np.float32)[0]
    B0 = 4608
    cols = []
    for h in range(8):
        cols += list(range(B0 + h * 64, B0 + h * 64 + 64)) + list(range(B0 + 512 + h * 64, B0 + 512 + h * 64 + 64))
    for h in range(8):
        cols += list(range(B0 + 1024 + h * 64, B0 + 1024 + h * 64 + 64)) + list(range(B0 + 1536 + h * 64, B0 + 1536 + h * 64 + 64))
    cols += list(range(0, 1536)) + list(range(1536, 3072)) + list(range(7680, 9728))
    cols += list(range(B0 + 2048, B0 + 3072)) + list(range(3072, 4608))
    assert len(cols) == COLS_IN
    sh = {"wp": f(w[:, cols])}
    sh["bg"] = f(np.asarray(inp["b_gate"], np.float32)[0].reshape(16, 128).T)
    rb = np.asarray(inp["rel_bias"], np.float32)
    i = np.arange(128)[:, None]
    c = np.arange(256)[None, :]
    ch = c // 128
    j = c % 128
    steps = (1 - ch) * 128 + j - i
    ga = np.zeros((24, 128, 256), np.float32)
    for g, d in enumerate(DILS):
        bk = t5_bucket_np(np.maximum(steps, 0) * d)
        for h in range(8):
            ga[g * 8 + h] = rb[bk, g * 8 + h]
    sh["ga"] = ga
    sh["ma"] = f(np.where((steps >= 0) & (steps <= 128), 0.0, MASKV / 8.0))
    dist = c - i
    bk = t5_bucket_np(np.maximum(dist, 0))
    gb = np.zeros((8, 128, 256), np.float32)
    for h in range(8):
        gb[h] = rb[bk, 24 + h]
    sh["gb"] = gb
    sh["cb"] = f(np.broadcast_to(rb[31, 24:32][None, :], (128, 8)))
    sh["mb"] = f(np.where(dist >= 0, 0.0, MASKV))
    sh["ident"] = np.eye(128, dtype=np.float32)
    lam = np.stack([np.asarray(inp[k], np.float32)[0] for k in ("lambda_q1", "lambda_q2", "lambda_k1", "lambda_k2")])
    sh["lam"] = f(np.broadcast_to(lam[None], (128, 4, 64)))
    sh["gs"] = f(np.broadcast_to(np.asarray(inp["subln_g"], np.float32)[0][None, :], (128, 128)))
    for k, n in (("ln1g", "ln1_g"), ("ln1b", "ln1_b"), ("ln2g", "ln2_g"), ("ln2b", "ln2_b")):
        sh[k] = f(np.broadcast_to(np.asarray(inp[n], np.float32)[0][None, :], (128, 1024)))
    sh["wpa"] = f(np.asarray(inp["w_proj_a"])[0]); sh["wpb"] = f(np.asarray(inp["w_proj_b"])[0])
    sh["wo"] = f(np.asarray(inp["w_out"])[0]); sh["w1"] = f(np.asarray(inp["w_mlp1"])[0]); sh["w2"] = f(np.asarray(inp["w_mlp2"])[0])
    return sh


def prep_core(x, c):
    b, par = c // 2, c % 2
    xb = np.asarray(x[b], dtype=np.float32)
    xl = np.zeros((NK, 1024), np.float32)
    if par == 1:
        xl[:] = xb
    else:
        xl[2048:] = xb[0:NK - 2048]
    own = np.concatenate([xl[2048:4096], xl[6144:8192]], axis=0)
    pv = np.full((128, 64), float(par), np.float32)
    return {"xT": np.ascontiguousarray(xl.T), "xo": np.ascontiguousarray(own), "pvalid": pv}


_NC_CACHE = {}


def kernel(**inputs):
    x = np.asarray(inputs["x"], dtype=np.float32)
    sh = prep_shared(inputs)
    in_maps = []
    for c in range(8):
        m = dict(sh)
        m.update(prep_core(x, c))
        in_maps.append(m)
    if "nc" not in _NC_CACHE:
        _NC_CACHE["nc"] = build_nc()
    res = run_bass_kernel_spmd(_NC_CACHE["nc"], in_maps, core_ids=list(range(8)))
    out = np.zeros((BATCH, SEQ, D_MODEL), np.float32)
    for c in range(8):
        b, par = c // 2, c % 2
        y = np.asarray(res.results[c]["y"], dtype=np.float32)
        r0 = 2048 if par == 1 else 0
        out[b, r0:r0 + 2048] = y[0:2048]
        out[b, r0 + 4096:r0 + 6144] = y[2048:4096]
    return out
```

```python
import math
from contextlib import ExitStack

import numpy as np

import concourse.bass as bass
import concourse.mybir as mybir
from concourse.bass_utils import run_bass_kernel_spmd

F32 = mybir.dt.float32
BF16 = mybir.dt.bfloat16
AF = mybir.ActivationFunctionType
ALU = mybir.AluOpType
AX = mybir.AxisListType

D_MODEL = 1024
SEQ = 8192
BATCH = 4
NQ = 4096
NK = 8192
NKA = 8192
COLS_IN = 9728
DILS = (1, 4, 16)
ALPHA = 2.0 ** 0.25
LAM_INIT = 0.8 - 0.6 * math.exp(0.0)
LN_EPS = 1e-5
MASKV = -240000.0
ENGS = ("sync", "scalar", "vector", "gpsimd", "tensor")

C_FQ, C_FK, C_AQ, C_AK, C_GT, C_BV, C_AV = 0, 1024, 2048, 3584, 5120, 7168, 8192


class Sem:
    __slots__ = ("h", "abs", "base", "name")

    def __init__(self, h, name):
        self.h, self.abs, self.base, self.name = h, 0, 0, name

    @property
    def v(self):
        return self.abs - self.base


class Prog:
    def __init__(self, nc, es):
        self.nc, self.es = nc, es
        self.q = {k: [] for k in ENGS}
        self.waited = {k: {} for k in ENGS}
        self.sems = []
        self.pool = []
        self.used = 0
        self.nphase = 0
        self.bar = self.sem("bar")

    def sem(self, name):
        if name != "bar" and self.used < len(self.pool):
            s = self.pool[self.used]
            self.used += 1
            s.base = s.abs
            return s
        s = Sem(self.es.enter_context(self.nc.semaphore(f"s{len(self.sems)}")), f"s{len(self.sems)}")
        self.sems.append(s)
        if name != "bar":
            self.pool.append(s)
            self.used += 1
        return s

    def add(self, eng, fn, waits=(), inc=None, dma=False):
        ws = []
        for w in waits:
            if w is None:
                continue
            s, v = w
            if v <= 0:
                continue
            assert v <= s.v, f"deadlock: {eng} waits {s.name}>={v} but only {s.v} scheduled"
            va = s.base + v
            if self.waited[eng].get(s.name, 0) >= va:
                continue
            self.waited[eng][s.name] = va
            ws.append((s, va))
        amt = 16 if dma else 1
        newv = None
        if inc is not None:
            inc.abs += amt
            newv = inc.v
        self.q[eng].append((ws, fn, inc, amt))
        return newv

    def end_phase(self, junk):
        waits = [(s, s.v) for s in self.sems if s is not self.bar]
        self.nphase += 1
        self.add("gpsimd", lambda e: e.memset(junk, 0.0), waits=waits, inc=self.bar)
        with self.nc.Block() as block:
            for name in ENGS:
                items = self.q[name]

                def body(e, items=items):
                    for ws, fn, inc, amt in items:
                        for s, v in ws:
                            e.wait_ge(s.h, v)
                        ins = fn(e)
                        if inc is not None:
                            ins.then_inc(inc.h, amt)

                getattr(block, name)(body)
        self.q = {k: [] for k in ENGS}
        self.used = 0
        for name in ENGS:
            if name == "gpsimd":
                continue
            self.waited[name].pop(self.bar.name, None)
        self.phase_wait = (self.bar, self.nphase)

    def pw(self):
        return getattr(self, "phase_wait", None)


def _sb(es, nc, name, shape, dt):
    return es.enter_context(nc.sbuf_tensor(name, shape, dt))


def _ps(es, nc, name, shape, dt):
    return es.enter_context(nc.psum_tensor(name, shape, dt))


def phase1(nc, P, D):
    with ExitStack() as es:
        W = _sb(es, nc, "p1_W", [128, 8, COLS_IN], BF16)
        WF = [_sb(es, nc, f"p1_WF{i}", [128, 8, 256], F32) for i in range(2)]
        XF = _sb(es, nc, "p1_XF", [128, 8, 512], F32)
        XB = [_sb(es, nc, f"p1_XB{i}", [128, 8, 512], BF16) for i in range(2)]
        OST = [_sb(es, nc, f"p1_OST{i}", [128, 512], BF16) for i in range(4)]
        BG = _sb(es, nc, "p1_BG", [128, 16], F32)
        JK = _sb(es, nc, "p1_JK", [128, 1], F32)
        PS = _ps(es, nc, "p1_PS", [128, 4, 512], F32)
        s_xld, s_xcv, s_wcv, s_misc = P.sem("p1xld"), P.sem("p1xcv"), P.sem("p1wcv"), P.sem("p1misc")
        s_wld = [P.sem(f"p1wld{i}") for i in range(2)]
        s_mm = [P.sem(f"p1mm{i}") for i in range(4)]
        s_ev = [P.sem(f"p1ev{i}") for i in range(4)]
        s_st = [P.sem(f"p1st{i}") for i in range(4)]
        xT = D["xT"].rearrange("(kc p) t -> p kc t", p=128)
        wp = D["wp"].rearrange("(kc p) c -> p kc c", p=128)
        pw = P.pw()

        v_bg = P.add("sync", lambda e: e.dma_start(out=BG[:], in_=D["bg"]), waits=[pw], inc=s_misc, dma=True)

        wcv_of = {}
        wl = [0]

        def load_w(cid):
            n = wl[0]
            slot = n % 2
            v = P.add("sync", lambda e: e.dma_start(out=WF[slot][:], in_=wp[:, :, cid * 128:(cid + 2) * 128]),
                      waits=[pw, (s_wcv, n - 1)], inc=s_wld[slot], dma=True)
            P.add("vector", lambda e: e.tensor_copy(out=W[:, :, cid * 128:(cid + 2) * 128], in_=WF[slot][:]),
                  waits=[pw, (s_wld[slot], v)], inc=s_wcv)
            wcv_of[cid] = s_wcv.v
            wcv_of[cid + 1] = s_wcv.v
            wl[0] += 1

        worder = (list(range(8, 16)) + list(range(56, 64)) + list(range(36, 40)) + list(range(72, 76))
                  + list(range(28, 36)) + list(range(64, 72))
                  + list(range(0, 8)) + list(range(16, 28)) + list(range(40, 56)))
        worder = worder[0::2]
        wplan = {-1: worder[0:12], 0: worder[12:20]}
        for t in range(1, 4):
            wplan[t] = worder[20 + 6 * (t - 1):20 + 6 * t]

        tile_last = {}

        def load_x(t):
            v = P.add("sync", lambda e: e.dma_start(out=XF[:], in_=xT[:, :, t * 512:(t + 1) * 512]),
                      waits=[pw, (s_xcv, t)], inc=s_xld, dma=True)
            P.add("vector", lambda e: e.tensor_copy(out=XB[t % 2][:], in_=XF[:]),
                  waits=[pw, (s_xld, v)] + (tile_last[t - 2] if t >= 2 else []), inc=s_xcv)

        def jobs_for(t):
            jobs = []
            own = (t // 4) % 2 == 1
            to = t - 4 if t < 8 else t - 8
            for h in range(8):
                jobs.append(("F", C_FK + h * 128, D["KTB"][h][:, t * 512:(t + 1) * 512], None))
            for cb in range(2):
                for s in range(4):
                    r0 = t * 512 + s * 128
                    jobs.append(("T", C_BV + cb * 512, D["VB"][r0:r0 + 128, cb * 512:(cb + 1) * 512], s))
            gs = (0, 1, 2) if (own or t % 4 == 3) else (2,)
            for g in gs:
                for hp in range(4):
                    m = g * 4 + hp
                    jobs.append(("F", C_AK + m * 128, D["KTA"][m][:, t * 512:(t + 1) * 512], None))
            for g in gs:
                for s in range(4):
                    r0 = t * 512 + s * 128
                    jobs.append(("T", C_AV + g * 512, D["VA"][r0:r0 + 128, g * 512:(g + 1) * 512], s))
            if own:
                for h in range(8):
                    jobs.append(("F", C_FQ + h * 128, D["QTB"][h][:, to * 512:(to + 1) * 512], None))
                for m in range(12):
                    jobs.append(("F", C_AQ + m * 128, D["QTA"][m][:, to * 512:(to + 1) * 512], None))
                for i in range(16):
                    jobs.append(("G", C_GT + i * 128, D["SG"][i][:, to * 512:(to + 1) * 512], i))
            return jobs

        load_x(0)
        for cid in wplan[-1]:
            load_w(cid)
        jc = 0
        for t in range(16):
            jobs = jobs_for(t)
            half = len(jobs) // 2
            for ji, (kind, col, dst, aux) in enumerate(jobs):
                if ji == half:
                    if t + 1 < 16:
                        load_x(t + 1)
                    for cid in wplan.get(t, []):
                        load_w(cid)
                slot = jc % 4
                k = jc // 4
                xb = XB[t % 2]
                if kind == "T":
                    wneed = max(wcv_of[col // 128 + i] for i in range(4))
                else:
                    wneed = wcv_of[col // 128]

                def mm(e, kind=kind, col=col, aux=aux, xb=xb, slot=slot):
                    ins = None
                    for kc in range(8):
                        if kind == "T":
                            ins = e.matmul(PS[:, slot, :], lhsT=xb[:, kc, aux * 128:(aux + 1) * 128],
                                           rhs=W[:, kc, col:col + 512], start=(kc == 0), stop=(kc == 7))
                        else:
                            ins = e.matmul(PS[:, slot, :], lhsT=W[:, kc, col:col + 128],
                                           rhs=xb[:, kc, :], start=(kc == 0), stop=(kc == 7))
                    return ins

                P.add("tensor", mm, waits=[pw, (s_xcv, t + 1), (s_wcv, wneed), (s_ev[slot], k)], inc=s_mm[slot])
                evw = [pw, (s_mm[slot], k + 1), (s_st[slot], 16 * k)]
                if kind == "G":
                    P.add("scalar", lambda e, slot=slot, aux=aux: e.activation(
                        out=OST[slot][:], in_=PS[:, slot, :], func=AF.Sigmoid, bias=BG[:, aux:aux + 1], scale=1.0),
                        waits=evw + [(s_misc, v_bg)], inc=s_ev[slot])
                elif jc % 2 == 0:
                    P.add("vector", lambda e, slot=slot: e.tensor_copy(out=OST[slot][:], in_=PS[:, slot, :]),
                          waits=evw, inc=s_ev[slot])
                else:
                    P.add("scalar", lambda e, slot=slot: e.copy(out=OST[slot][:], in_=PS[:, slot, :]),
                          waits=evw, inc=s_ev[slot])
                P.add("gpsimd", lambda e, slot=slot, dst=dst: e.dma_start(out=dst, in_=OST[slot][:]),
                      waits=[pw, (s_ev[slot], k + 1)], inc=s_st[slot], dma=True)
                jc += 1
            tile_last[t] = [(s_mm[s], s_mm[s].v) for s in range(4)]
        P.end_phase(JK[:])


def phase2(nc, P, D):
    with ExitStack() as es:
        QA = [_sb(es, nc, f"p2_QA{i}", [128, NQ], BF16) for i in range(2)]
        KA = [_sb(es, nc, f"p2_KA{i}", [128, NKA], BF16) for i in range(2)]
        VS = [_sb(es, nc, f"p2_VS{i}", [128, 64, 128], BF16) for i in range(2)]
        GF = [_sb(es, nc, f"p2_GF{i}", [128, 2, 256], F32) for i in range(2)]
        BTF = _sb(es, nc, "p2_BTF", [128, 2, 256], F32)
        EB = [_sb(es, nc, f"p2_EB{i}", [128, 2, 256], F32) for i in range(2)]
        PTF = [_sb(es, nc, f"p2_PTF{i}", [128, 2, 256], F32) for i in range(2)]
        MA = _sb(es, nc, "p2_MA", [128, 256], F32)
        IDF = _sb(es, nc, "p2_IDF", [128, 128], F32)
        IDB = _sb(es, nc, "p2_IDB", [128, 128], BF16)
        PVF = _sb(es, nc, "p2_PVF", [128, 64], F32)
        PV64 = _sb(es, nc, "p2_PV64", [128, 64], BF16)
        ONE64 = _sb(es, nc, "p2_ONE64", [128, 64], BF16)
        PT = [_sb(es, nc, f"p2_PT{i}", [128, 2, 256], BF16) for i in range(3)]
        NACC = _sb(es, nc, "p2_NACC", [128, NQ], F32)
        DACC = _sb(es, nc, "p2_DACC", [128, NQ], F32)
        OAS = _sb(es, nc, "p2_OAS", [128, NQ], BF16)
        JK = _sb(es, nc, "p2_JK", [128, 1], F32)
        S = _ps(es, nc, "p2_S", [128, 4, 512], F32)
        ND = _ps(es, nc, "p2_ND", [128, 4, 512], F32)
        s_c = P.sem("p2c")
        s_ld = [P.sem(f"p2ld{i}") for i in range(2)]
        s_bt = P.sem("p2bt")
        s_qk = [P.sem(f"p2qk{i}") for i in range(2)]
        s_ex = [P.sem(f"p2ex{i}") for i in range(2)]
        s_pv = [P.sem(f"p2pv{i}") for i in range(3)]
        s_nf = [P.sem(f"p2nf{i}") for i in range(2)]
        s_dv = P.sem("p2dv")
        s_fin = P.sem("p2fin")
        s_pm = [P.sem(f"p2pm{i}") for i in range(2)]
        s_st = P.sem("p2st")
        pw = P.pw()

        vc = P.add("sync", lambda e: e.dma_start(out=MA[:], in_=D["ma"]), waits=[pw], inc=s_c, dma=True)
        vc = P.add("sync", lambda e: e.dma_start(out=PVF[:], in_=D["pvalid"]), waits=[pw], inc=s_c, dma=True)
        P.add("vector", lambda e: e.tensor_copy(out=PV64[:], in_=PVF[:]), waits=[pw, (s_c, vc)], inc=s_c)
        vconst = P.add("vector", lambda e: e.memset(ONE64[:], 1.0), waits=[pw], inc=s_c)

        passes = [(hp, g) for hp in range(4) for g in range(3)]
        pass_last_pe = {}
        pass_last_dve = {}
        ld_val = {}
        bt_val = {}

        def sched_loads(pi):
            hp, g = passes[pi]
            d = DILS[g]
            slot = pi % 2
            nb = 64 // d
            w = [pw] + (pass_last_pe[pi - 2] if pi >= 2 else [])
            P.add("sync", lambda e: e.dma_start(out=QA[slot][:], in_=D["QTA"][g * 4 + hp]), waits=w, inc=s_ld[slot], dma=True)
            tranges = ((0, 16),) if d == 16 else ((3, 8), (11, 16))
            for (ta, tb) in tranges:
                P.add("sync", lambda e, ta=ta, tb=tb: e.dma_start(out=KA[slot][:, ta * 512:tb * 512],
                                                               in_=D["KTA"][g * 4 + hp][:, ta * 512:tb * 512]),
                      waits=w, inc=s_ld[slot], dma=True)
            vav = D["VA"].rearrange("(n i r) c -> r i n c", i=128, r=d)
            c0 = g * 512 + hp * 128
            bpt = 4 // d if d <= 4 else 0
            for r in range(d):
                if d == 16:
                    branges = ((0, nb),)
                else:
                    branges = tuple((ta * bpt, tb * bpt) for (ta, tb) in tranges)
                for (ba, bb) in branges:
                    for n0 in range(ba, bb, 16):
                        n1 = min(bb, n0 + 16)
                        P.add("sync", lambda e, r=r, n0=n0, n1=n1: e.dma_start(
                            out=VS[slot][:, r * nb + n0:r * nb + n1, :], in_=vav[r][:, n0:n1, c0:c0 + 128]),
                            waits=w, inc=s_ld[slot], dma=True)
            gh0 = g * 8 + 2 * hp
            v = P.add("sync", lambda e: e.dma_start(out=GF[slot][:], in_=D["ga"][gh0:gh0 + 2].rearrange("h i c -> i h c")),
                      waits=w, inc=s_ld[slot], dma=True)
            ld_val[pi] = v

        def sched_tables(pi):
            slot = pi % 2
            wb = [pw, (s_ld[slot], ld_val[pi]), (s_c, vconst)]
            v1 = None
            for hh in range(2):
                v1 = P.add("vector", lambda e, hh=hh: e.tensor_tensor(
                    out=BTF[:, hh, :], in0=GF[slot][:, hh, :], in1=MA[:], op=ALU.add),
                    waits=wb + [(s_bt, s_bt.v)], inc=s_bt)
            P.add("scalar", lambda e: e.activation(out=EB[slot][:], in_=BTF[:], func=AF.Exp),
                  waits=[pw, (s_bt, v1)] + (pass_last_dve[pi - 2] if pi >= 2 else []), inc=s_bt)
            bt_val[pi] = s_bt.v

        sched_loads(0)
        sched_tables(0)
        cn = {"qi": 0, "gi": 0, "fin": 0}

        def run_pass(pi, hp, g):
            d = DILS[g]
            slot = pi % 2
            nb = 64 // d
            nbB = 16 // d
            if pi + 1 < len(passes):
                sched_loads(pi + 1)

            def own_off(n):
                return 2048 if n < 2 * nbB else 4096

            groups = []
            if d == 16:
                for B in (1, 3):
                    base = 0 if B == 1 else 2048
                    for r0 in range(0, 16, 4):
                        qbs = [(r0 + i, B) for i in range(4)]
                        groups.append((qbs,
                                       lambda T, base=base, r0=r0: T[:, base:base + 2048].rearrange("p (j r) -> p r j", r=16)[:, r0:r0 + 4, :],
                                       lambda bank: ND[:, bank, 0:512].rearrange("p (r j) -> p r j", r=4)))
            else:
                for r in range(d):
                    for B in (1, 3):
                        for n0 in range(B * nbB, (B + 1) * nbB, 4):
                            qbs = [(r, n0 + i) for i in range(4)]
                            t0 = d * 128 * n0 + r - own_off(n0)
                            sl = slice(t0, t0 + 511 * d + 1, d)
                            groups.append((qbs, lambda T, sl=sl: T[:, sl], lambda bank: ND[:, bank, 0:512]))
            qblocks = [(r, n, gidx, i) for gidx, (qbs, _, _) in enumerate(groups) for i, (r, n) in enumerate(qbs)]
            pendl = []
            pass_dv_start = s_dv.v

            def emit_pv(info):
                (qslot, kq, pslot, r, n, gslot, kg, first, last, gidx, col) = info

                def pv(e):
                    ins = None
                    for hh in range(2):
                        for ch in range(2):
                            nk = n - 1 + ch
                            blk = r * nb + nk
                            ins = e.matmul(ND[hh * 64:(hh + 1) * 64, 2 * gslot, col:col + 128],
                                           lhsT=VS[slot][:, blk, hh * 64:(hh + 1) * 64],
                                           rhs=PT[pslot][:, hh, ch * 128:(ch + 1) * 128],
                                           start=(ch == 0), stop=(ch == 1))
                            vt = PV64 if nk < nbB else ONE64
                            ins = e.matmul(ND[hh * 64:(hh + 1) * 64, 2 * gslot + 1, col:col + 128],
                                           lhsT=vt[:, :], rhs=PT[pslot][:, hh, ch * 128:(ch + 1) * 128],
                                           start=(ch == 0), stop=(ch == 1))
                    return ins

                w = [(s_pm[qslot], kq + 1)]
                if first:
                    w.append((s_nf[gslot], kg))
                vpv = P.add("tensor", pv, waits=w, inc=s_pv[pslot])
                if last:
                    _, dst, srcf = groups[gidx]
                    wd = [(s_pv[pslot], vpv)]
                    if g > 0:
                        wd += [(s_dv, pass_dv_start), (s_nf[0], nf_start[0]), (s_nf[1], nf_start[1])]
                    else:
                        wd += [(s_fin, cn["fin"])]
                    if g == 0:
                        P.add("scalar", lambda e: e.copy(out=dst(NACC), in_=srcf(2 * gslot)), waits=[pw] + wd, inc=s_dv)
                        P.add("scalar", lambda e: e.copy(out=dst(DACC), in_=srcf(2 * gslot + 1)), waits=[pw] + wd, inc=s_nf[gslot])
                    else:
                        P.add("vector", lambda e: e.tensor_tensor(out=dst(NACC), in0=srcf(2 * gslot), in1=dst(NACC), op=ALU.add),
                              waits=wd, inc=s_dv)
                        P.add("vector", lambda e: e.tensor_tensor(out=dst(DACC), in0=srcf(2 * gslot + 1), in1=dst(DACC), op=ALU.add),
                              waits=wd, inc=s_nf[gslot])

            nf_start = [s_nf[0].v, s_nf[1].v]
            for bi, (r, n, gidx, gi_) in enumerate(qblocks):
                if bi == len(qblocks) // 2 and pi + 1 < len(passes):
                    sched_tables(pi + 1)
                qslot = cn["qi"] % 2
                kq = cn["qi"] // 2
                pslot = cn["qi"] % 3
                kp = cn["qi"] // 3
                first = (gi_ == 0)
                last = (gi_ == 3)
                if first:
                    gslot = cn["gi"] % 2
                    kg = cn["gi"] // 2
                    cn["gi"] += 1
                t0 = d * 128 * n + r - own_off(n)
                qs = slice(t0, t0 + 127 * d + 1, d)

                def qk(e, r=r, n=n, qslot=qslot, qs=qs):
                    ins = None
                    for hh in range(2):
                        bank = 2 * qslot + hh
                        for ch in range(2):
                            nk = n - 1 + ch
                            ks = slice(d * 128 * nk + r, d * 128 * nk + r + 127 * d + 1, d)
                            ins = e.matmul(S[:, bank, ch * 128:(ch + 1) * 128],
                                           lhsT=KA[slot][hh * 64:(hh + 1) * 64, ks],
                                           rhs=QA[slot][hh * 64:(hh + 1) * 64, qs],
                                           start=True, stop=True)
                    return ins

                P.add("tensor", qk, waits=[pw, (s_ld[slot], ld_val[pi]), (s_c, vconst), (s_ex[qslot], kq)],
                      inc=s_qk[qslot])
                P.add("scalar", lambda e, qslot=qslot: e.activation(
                    out=PTF[qslot][:], in_=S[:, 2 * qslot:2 * qslot + 2, 0:256], func=AF.Exp, scale=0.125),
                    waits=[pw, (s_qk[qslot], kq + 1), (s_pm[qslot], kq)], inc=s_ex[qslot])
                P.add("vector", lambda e, qslot=qslot, pslot=pslot: e.tensor_tensor(
                    out=PT[pslot][:], in0=PTF[qslot][:], in1=EB[slot][:], op=ALU.mult),
                    waits=[pw, (s_ex[qslot], kq + 1), (s_pv[pslot], kp), (s_bt, bt_val[pi])], inc=s_pm[qslot])
                pendl.append((qslot, kq, pslot, r, n, gslot, kg, first, last, gidx, gi_ * 128))
                while len(pendl) > 2:
                    emit_pv(pendl.pop(0))
                cn["qi"] += 1
            while pendl:
                emit_pv(pendl.pop(0))
            pass_last_pe[pi] = [(s_pv[i], s_pv[i].v) for i in range(3)]
            pass_last_dve[pi] = [(s_pm[0], s_pm[0].v), (s_pm[1], s_pm[1].v)]
            if g == 2:
                wd = [(s_dv, s_dv.v), (s_nf[0], s_nf[0].v), (s_nf[1], s_nf[1].v)]
                v0 = P.add("scalar", lambda e: e.activation(out=DACC[:], in_=DACC[:], func=AF.Ln), waits=[pw] + wd, inc=s_fin)
                v1 = P.add("scalar", lambda e: e.activation(out=DACC[:], in_=DACC[:], func=AF.Exp, scale=-1.0),
                           waits=[(s_fin, v0)], inc=s_fin)
                v2 = P.add("vector", lambda e: e.tensor_tensor(out=OAS[:], in0=NACC[:], in1=DACC[:], op=ALU.mult),
                           waits=[(s_fin, v1), (s_st, 16 * hp)], inc=s_fin)
                cn["fin"] = v2
                P.add("gpsimd", lambda e, hp=hp: e.dma_start(out=D["OAT"][hp], in_=OAS[:]),
                      waits=[pw, (s_fin, v2)], inc=s_st, dma=True)
        for pi, (hp, g) in enumerate(passes):
            run_pass(pi, hp, g)
        P.end_phase(JK[:])


def phase3(nc, P, D):
    with ExitStack() as es:
        KT = [_sb(es, nc, f"p3_KT{i}", [128, NK], BF16) for i in range(2)]
        QT = [_sb(es, nc, f"p3_QT{i}", [128, NQ], BF16) for i in range(2)]
        VH = [_sb(es, nc, f"p3_VH{i}", [128, 64, 129], BF16) for i in range(2)]
        GB = _sb(es, nc, "p3_GB", [128, 8, 256], F32)
        CB = _sb(es, nc, "p3_CB", [128, 8], F32)
        MB = _sb(es, nc, "p3_MB", [128, 256], F32)
        BTF = _sb(es, nc, "p3_BTF", [128, 256], F32)
        BTA = _sb(es, nc, "p3_BTA", [128, 8, 2, 256], F32)
        IDF = _sb(es, nc, "p3_IDF", [128, 128], F32)
        IDB = _sb(es, nc, "p3_IDB", [128, 128], BF16)
        ZB = _sb(es, nc, "p3_ZB", [128, 512], BF16)
        PVF = _sb(es, nc, "p3_PVF", [128, 64], F32)
        LAMI = _sb(es, nc, "p3_LAMI", [128, 4, 64], F32)
        LPR = _sb(es, nc, "p3_LPR", [128, 2, 64], F32)
        LS = _sb(es, nc, "p3_LS", [128, 2], F32)
        LE = _sb(es, nc, "p3_LE", [128, 2], F32)
        NLAM = _sb(es, nc, "p3_NLAM", [128, 1], F32)
        GS = _sb(es, nc, "p3_GS", [128, 128], F32)
        EPSB = _sb(es, nc, "p3_EPSB", [128, 1], F32)
        PT = [_sb(es, nc, f"p3_PT{i}", [128, 2, 512], BF16) for i in range(3)]
        EP = _sb(es, nc, "p3_EP", [128, 3, 512], F32)
        RD = _sb(es, nc, "p3_RD", [128, 8], F32)
        TT = _sb(es, nc, "p3_TT", [128, 128], F32)
        OO = _sb(es, nc, "p3_OO", [128, 4, 128], F32)
        SQ = _sb(es, nc, "p3_SQ", [128, 128], F32)
        SSQ = _sb(es, nc, "p3_SSQ", [128, 4], F32)
        LNV = _sb(es, nc, "p3_LNV", [128, 4], F32)
        RSTD = _sb(es, nc, "p3_RSTD", [128, 4], F32)
        ON = _sb(es, nc, "p3_ON", [128, 4, 128], BF16)
        OBS = [_sb(es, nc, f"p3_OBS{i}", [128, 512], BF16) for i in range(2)]
        JK = _sb(es, nc, "p3_JK", [128, 1], F32)
        CF = [_sb(es, nc, f"p3_CF{i}", [128, 1024], F32) for i in range(2)]
        CBT = [_sb(es, nc, f"p3_CBT{i}", [128, 1024], BF16) for i in range(2)]
        S = _ps(es, nc, "p3_S", [128, 4, 512], F32)
        ACC = _ps(es, nc, "p3_ACC", [128, 3, 512], F32)
        TP = _ps(es, nc, "p3_TP", [128, 4, 128], BF16)
        s_c = P.sem("p3c")
        s_ld = [P.sem(f"p3ld{i}") for i in range(2)]
        s_qk = [P.sem(f"p3qk{i}") for i in range(2)]
        s_ex = [P.sem(f"p3ex{i}") for i in range(2)]
        s_pv = [P.sem(f"p3pv{i}") for i in range(3)]
        s_af = P.sem("p3af")
        s_ep = P.sem("p3ep")
        s_ea = P.sem("p3ea")
        s_tp = P.sem("p3tp")
        s_tc = P.sem("p3tc")
        s_st = [P.sem(f"p3st{i}") for i in range(2)]
        s_cl = [P.sem(f"p3cl{i}") for i in range(2)]
        s_cc = P.sem("p3cc")
        s_cs = [P.sem(f"p3cs{i}") for i in range(2)]
        s_nb = [P.sem(f"p3nb{i}") for i in range(2)]
        pw = P.pw()

        chunks = []
        w1v = D["w1"].rearrange("(kc p) f -> p kc f", p=128)
        w2v = D["w2"].rearrange("(fc p) d -> p fc d", p=128)
        k0 = 0
        for (wsrc, nk) in ((D["wpa"], 4), (D["wpb"], 8), (D["wo"], 8)):
            wv = wsrc.rearrange("(kc p) c -> p kc c", p=128)
            for k in range(nk):
                chunks.append((wv[:, k, :], D["WAB"][:, k0 + k, :]))
            k0 += nk
        for kc in range(8):
            for q in range(4):
                chunks.append((w1v[:, kc, q * 1024:(q + 1) * 1024], D["W1B"][:, kc, q * 1024:(q + 1) * 1024]))
        for fc in range(32):
            chunks.append((w2v[:, fc, :], D["W2B"][:, fc, :]))
        cv = {"k": 0, "ld": {}}

        def wconv_step():
            k = cv["k"]
            if k - 1 >= 0 and k - 1 < len(chunks):
                j = k - 1
                sl = j % 2
                src_, dst_ = chunks[j]
                vcast = P.add("vector", lambda e, sl=sl: e.tensor_copy(out=CBT[sl][:], in_=CF[sl][:]),
                              waits=[pw, (s_cl[sl], cv["ld"][j]), (s_cs[sl], 16 * (j // 2))], inc=s_cc)
                P.add("gpsimd", lambda e, sl=sl, dst_=dst_: e.dma_start(out=dst_, in_=CBT[sl][:]), waits=[pw, (s_cc, vcast)],
                      inc=s_cs[sl], dma=True)
            if k < len(chunks):
                sl = k % 2
                src_, dst_ = chunks[k]
                cv["ld"][k] = P.add("sync", lambda e, sl=sl, src_=src_: e.dma_start(out=CF[sl][:], in_=src_), waits=[pw, (s_cc, k - 1)],
                                    inc=s_cl[sl], dma=True)
            cv["k"] += 1

        def accap(m, j):
            idx = m * 4 + j
            return idx // 3, (idx % 3) * 170

        for (dst, src) in ((GB, D["gb"].rearrange("h i c -> i h c")), (CB, D["cb"]), (MB, D["mb"]), (IDF, D["ident"]),
                           (PVF, D["pvalid"]), (LAMI, D["lam"]), (GS, D["gs"])):
            vc = P.add("sync", lambda e, dst=dst, src=src: e.dma_start(out=dst[:], in_=src), waits=[pw], inc=s_c, dma=True)
        w0 = [pw, (s_c, vc)]
        P.add("vector", lambda e: e.tensor_copy(out=IDB[:], in_=IDF[:]), waits=w0, inc=s_c)
        P.add("vector", lambda e: e.memset(ZB[:], 0.0), waits=w0, inc=s_c)
        P.add("vector", lambda e: e.memset(EPSB[:], LN_EPS), waits=w0, inc=s_c)
        v = P.add("vector", lambda e: e.tensor_scalar(out=GS[:], in0=GS[:], scalar1=1.0 - LAM_INIT, scalar2=None, op0=ALU.mult),
                  waits=w0, inc=s_c)
        for i in range(2):
            P.add("vector", lambda e, i=i: e.memset(VH[i][:, 16:64, 128:129], 1.0), waits=w0, inc=s_c)
            P.add("vector", lambda e, i=i: e.tensor_copy(out=VH[i][:, 0:16, 128:129],
                                                         in_=PVF[:, 0:16].rearrange("p (a b) -> p a b", b=1)),
                  waits=w0, inc=s_c)
        v = P.add("vector", lambda e: e.tensor_tensor(out=LPR[:], in0=LAMI[:, 0:2, :], in1=LAMI[:, 2:4, :], op=ALU.mult),
                  waits=w0, inc=s_c)
        v = P.add("vector", lambda e: e.reduce_sum(out=LS[:], in_=LPR[:], axis=AX.X), waits=[(s_c, v)], inc=s_c)
        v = P.add("scalar", lambda e: e.activation(out=LE[:], in_=LS[:], func=AF.Exp), waits=[pw, (s_c, v)], inc=s_c)
        v = P.add("vector", lambda e: e.tensor_tensor(out=NLAM[:], in0=LE[:, 1:2], in1=LE[:, 0:1], op=ALU.subtract),
                  waits=[(s_c, v)], inc=s_c)
        v = P.add("vector", lambda e: e.tensor_scalar(out=NLAM[:], in0=NLAM[:], scalar1=-LAM_INIT, scalar2=None, op0=ALU.add),
                  waits=[(s_c, v)], inc=s_c)
        for h in range(8):
            v = P.add("vector", lambda e, h=h: e.tensor_scalar(out=BTF[:], in0=GB[:, h, :], scalar1=CB[:, h:h + 1], scalar2=8.0,
                                                               op0=ALU.subtract, op1=ALU.mult), waits=[(s_c, v)], inc=s_c)
            v = P.add("vector", lambda e, h=h: e.tensor_tensor(out=BTA[:, h, 0, :], in0=BTF[:], in1=MB[:], op=ALU.add), waits=[(s_c, v)], inc=s_c)
            v = P.add("vector", lambda e, h=h: e.tensor_tensor(out=BTA[:, h, 1, :], in0=BTF[:], in1=MB[:], op=ALU.add), waits=[(s_c, v)], inc=s_c)
        vconst = v

        head_last_pe = {}
        ld_val = {}
        vbv = D["VB"].rearrange("(kb p) c -> p kb c", p=128)

        def sched_loads(h):
            slot = h % 2
            w = [pw, (s_c, vconst)] + (head_last_pe[h - 2] if h >= 2 else [])
            P.add("sync", lambda e: e.dma_start(out=KT[slot][:], in_=D["KTB"][h]), waits=w, inc=s_ld[slot], dma=True)
            P.add("sync", lambda e: e.dma_start(out=QT[slot][:], in_=D["QTB"][h]), waits=w, inc=s_ld[slot], dma=True)
            for q in range(4):
                v = P.add("sync", lambda e, q=q: e.dma_start(out=VH[slot][:, q * 16:(q + 1) * 16, 0:128],
                                                           in_=vbv[:, q * 16:(q + 1) * 16, h * 128:(h + 1) * 128]),
                          waits=w, inc=s_ld[slot], dma=True)
            ld_val[h] = v

        deferred = []
        ui = [0]

        def flush(force=False):
            while deferred and (force or deferred[0][0] <= ui[0]):
                _, fn = deferred.pop(0)
                fn()

        sched_loads(0)
        cn = {"nqt": 0}
        pending = []

        def emit_pv(info):
            (sslot, ku, pslot, kb, jmin, firstu, lastu, hs, kq_tile, on_last) = info

            def pv_bank(bsel):
                def pv(e):
                    ins = None
                    if firstu:
                        e.matmul(ACC[:, bsel, :], lhsT=ZB[:, 0:128], rhs=ZB[:, :], start=True, stop=True, skip_group_check=True)
                    for m in range(2):
                        for j in range(jmin, 4):
                            b, off = accap(m, j)
                            if bsel is not None and b != bsel:
                                continue
                            ins = e.matmul(ACC[:, b, off:off + 129], lhsT=PT[pslot][:, m, j * 128:(j + 1) * 128],
                                           rhs=VH[hs][:, kb, :], start=False, stop=lastu, skip_group_check=True)
                    return ins
                return pv

            w = [(s_ex[sslot], ku + 1)]
            if firstu:
                P.add("tensor", pv_bank(0), waits=w + [(s_af, 3 * kq_tile - 2)], inc=None)
                P.add("tensor", pv_bank(1), waits=[(s_af, 3 * kq_tile - 1)], inc=None)
                v = P.add("tensor", pv_bank(2), waits=[(s_af, 3 * kq_tile)], inc=s_pv[pslot])
            else:
                v = P.add("tensor", pv_bank(None), waits=w, inc=s_pv[pslot])
            if lastu:
                on_last(pslot, v)

        def drain(keep):
            while len(pending) > keep:
                emit_pv(pending.pop(0))

        def run_qt(h, qt):
                hs = h % 2
                base = 4 * (qt + 4 if qt < 4 else qt + 8)
                nkb = base + 4
                kq_tile = cn["nqt"]
                cn["nqt"] += 1
                nqt = cn["nqt"]

                def on_last(last_slot, vlast):
                    epilogue(last_slot, vlast)

                for kb in range(nkb):
                    u = ui[0]
                    slot = u % 2
                    ku = u // 2
                    pslot = u % 3
                    kp = u // 3
                    jd = kb - base
                    c0 = max(jd, 0) * 128
                    jmin = max(jd, 0)

                    def qk(e, kb=kb, c0=c0, slot=slot, qt=qt):
                        ins = None
                        for m in range(2):
                            ins = e.matmul(S[:, 2 * slot + m, c0:512], lhsT=KT[hs][m * 64:(m + 1) * 64, kb * 128:(kb + 1) * 128],
                                           rhs=QT[hs][m * 64:(m + 1) * 64, qt * 512 + c0:(qt + 1) * 512], start=True, stop=True)
                        return ins

                    vqk = P.add("tensor", qk, waits=[pw, (s_ld[hs], ld_val[h]), (s_c, vconst), (s_ex[slot], ku)], inc=s_qk[slot])
                    wex = [pw, (s_qk[slot], ku + 1), (s_pv[pslot], kp)]
                    if jd >= -1:
                        if jd == -1:
                            oc, bc, n = 0, 128, 128
                        elif jd == 3:
                            oc, bc, n = 384, 0, 128
                        else:
                            oc, bc, n = jd * 128, 0, 256
                        vnb = P.add("vector", lambda e, slot=slot, oc=oc, bc=bc, n=n: e.tensor_tensor(
                            out=S[:, 2 * slot:2 * slot + 2, oc:oc + n], in0=S[:, 2 * slot:2 * slot + 2, oc:oc + n],
                            in1=BTA[:, h, :, bc:bc + n], op=ALU.add), waits=[pw, (s_qk[slot], vqk)], inc=s_nb[slot])
                        wex.append((s_nb[slot], vnb))
                    P.add("scalar", lambda e, slot=slot, c0=c0, pslot=pslot: e.activation(
                        out=PT[pslot][:, :, c0:512], in_=S[:, 2 * slot:2 * slot + 2, c0:512], func=AF.Exp, scale=0.125),
                        waits=wex, inc=s_ex[slot])
                    pending.append((slot, ku, pslot, kb, jmin, kb == 0, kb == nkb - 1, hs, kq_tile, on_last))
                    drain(2)
                    ui[0] += 1
                    if ui[0] % 24 == 0:
                        wconv_step()
                    flush()

                def epilogue(last_slot, vlast):

                    def stage_a(last_slot=last_slot, vlast=vlast, h=h, qt=qt, k=nqt):
                        v = None
                        for b in range(3):
                            v = P.add("vector", lambda e, b=b: e.tensor_copy(out=EP[:, b, :], in_=ACC[:, b, :]),
                                      waits=[(s_pv[last_slot], vlast), (s_ep, s_ep.v)], inc=s_af)
                        we = [(s_af, v)]
                        for m in range(2):
                            for j in range(4):
                                b, off = accap(m, j)
                                idx = m * 4 + j
                                P.add("vector", lambda e, b=b, off=off, idx=idx: e.reciprocal(
                                    out=RD[:, idx:idx + 1], in_=EP[:, b, off + 128:off + 129]), waits=we, inc=s_ep)
                        v = P.add("vector", lambda e: e.tensor_scalar(out=RD[:, 4:8], in0=RD[:, 4:8], scalar1=NLAM[:, 0:1], scalar2=None,
                                                                      op0=ALU.mult), waits=[(s_ep, s_ep.v)], inc=s_ep)
                        for j in range(4):
                            b1, o1 = accap(0, j)
                            b2, o2 = accap(1, j)
                            v = P.add("vector", lambda e, b2=b2, o2=o2, j=j: e.tensor_scalar(
                                out=TT[:], in0=EP[:, b2, o2:o2 + 128], scalar1=RD[:, 4 + j:5 + j], scalar2=None, op0=ALU.mult),
                                waits=[(s_ep, v)], inc=s_ep)
                            v = P.add("vector", lambda e, b1=b1, o1=o1, j=j: e.scalar_tensor_tensor(
                                out=OO[:, j, :], in0=EP[:, b1, o1:o1 + 128], scalar=RD[:, j:j + 1], in1=TT[:],
                                op0=ALU.mult, op1=ALU.add), waits=[(s_ep, v)], inc=s_ep)
                            v = P.add("vector", lambda e, j=j: e.scalar_tensor_tensor(
                                out=SQ[:], in0=OO[:, j, :], scalar=1.0, in1=OO[:, j, :], op0=ALU.mult, op1=ALU.mult,
                                accum_out=SSQ[:, j:j + 1]), waits=[(s_ep, v)], inc=s_ep)
                        return v

                    va = stage_a()

                    def stage_bc(va=va):
                        v = P.add("scalar", lambda e: e.activation(out=LNV[:], in_=SSQ[:], func=AF.Ln, bias=EPSB[:, 0:1], scale=1.0 / 128.0),
                                  waits=[(s_ep, va), (s_ea, s_ea.v)], inc=s_ea)
                        v = P.add("scalar", lambda e: e.activation(out=RSTD[:], in_=LNV[:], func=AF.Exp, scale=-0.5),
                                  waits=[(s_ea, v)], inc=s_ea)
                        vv = None
                        for j in range(4):
                            vv = P.add("vector", lambda e, j=j: e.scalar_tensor_tensor(
                                out=ON[:, j, :], in0=OO[:, j, :], scalar=RSTD[:, j:j + 1], in1=GS[:], op0=ALU.mult, op1=ALU.mult),
                                waits=[(s_ea, v), (s_tp, s_tp.v)], inc=s_ep)
                        return vv

                    def stage_de(vc, h=h, qt=qt, k=nqt):
                        def tp(e):
                            ins = None
                            for j in range(4):
                                ins = e.transpose(out=TP[:, j, :], in_=ON[:, j, :], identity=IDB[:])
                            return ins
                        v = P.add("tensor", tp, waits=[(s_ep, vc), (s_tc, s_tc.v)], inc=s_tp)
                        os_ = (k - 1) % 2
                        v2 = P.add("vector", lambda e: e.tensor_copy(out=OBS[os_][:].rearrange("p (a b) -> p a b", a=4), in_=TP[:]),
                                   waits=[(s_tp, v), (s_st[os_], 16 * ((k - 1) // 2))], inc=s_tc)
                        P.add("gpsimd", lambda e: e.dma_start(out=D["OBT"][h][:, qt * 512:(qt + 1) * 512], in_=OBS[os_][:]),
                              waits=[pw, (s_tc, v2)], inc=s_st[os_], dma=True)

                    def chain(stage_bc=stage_bc, stage_de=stage_de):
                        vc = stage_bc()
                        deferred.append((ui[0] + 4, lambda: stage_de(vc)))

                    deferred.append((ui[0] + 12, chain))
        for h in range(8):
            if h + 1 < 8:
                sched_loads(h + 1)
            for qt in range(8):
                run_qt(h, qt)
            drain(0)
            head_last_pe[h] = [(s_pv[i], s_pv[i].v) for i in range(3)]
        flush(force=True)
        flush(force=True)
        while cv["k"] <= len(chunks):
            wconv_step()
        P.end_phase(JK[:])


def layer_norm_ops(P, V, ST6, MV, STD, RSTD, TMP, OUT, LG, LB, EPSB, sem, wait_in):
    v = None
    for hf in range(2):
        v = P.add("vector", lambda e, hf=hf: e.bn_stats(out=ST6[:, hf * 6:(hf + 1) * 6], in_=V[:, hf * 512:(hf + 1) * 512]),
                  waits=wait_in + [(sem, sem.v)], inc=sem)
    v = P.add("vector", lambda e: e.bn_aggr(out=MV[:], in_=ST6[:]), waits=[(sem, v)], inc=sem)
    v = P.add("scalar", lambda e: e.activation(out=STD[:], in_=MV[:, 1:2], func=AF.Sqrt, bias=EPSB[:, 0:1], scale=1.0),
              waits=[(sem, v)], inc=sem)
    v = P.add("vector", lambda e: e.reciprocal(out=RSTD[:], in_=STD[:]), waits=[(sem, v)], inc=sem)
    v = P.add("vector", lambda e: e.scalar_tensor_tensor(out=TMP[:], in0=V[:], scalar=MV[:, 0:1], in1=LG[:],
                                                         op0=ALU.subtract, op1=ALU.mult), waits=[(sem, v)], inc=sem)
    v = P.add("vector", lambda e: e.scalar_tensor_tensor(out=OUT, in0=TMP[:], scalar=RSTD[:, 0:1], in1=LB[:],
                                                         op0=ALU.mult, op1=ALU.add), waits=[(sem, v)], inc=sem)
    return v


def load_cast_weight(P, pw, dst_chunks, src_chunks, WF, s_wld, s_wcv, cnt):
    for dst, src in zip(dst_chunks, src_chunks):
        n = cnt[0]
        slot = n % 2
        v = P.add("sync", lambda e, slot=slot, src=src: e.dma_start(out=WF[slot][:], in_=src),
                  waits=[pw, (s_wcv, n - 1)], inc=s_wld[slot], dma=True)
        P.add("vector", lambda e, slot=slot, dst=dst: e.tensor_copy(out=dst, in_=WF[slot][:]),
              waits=[pw, (s_wld[slot], v)], inc=s_wcv)
        cnt[0] += 1
    return s_wcv.v


def phase4a(nc, P, D):
    with ExitStack() as es:
        WPA = _sb(es, nc, "p4_WPA", [128, 4, 1024], BF16)
        WPB = _sb(es, nc, "p4_WPB", [128, 8, 1024], BF16)
        WO = _sb(es, nc, "p4_WO", [128, 8, 1024], BF16)
        OAt = [_sb(es, nc, f"p4_OA{i}", [128, 4, 512], BF16) for i in range(2)]
        OBt = [_sb(es, nc, f"p4_OB{i}", [128, 8, 512], BF16) for i in range(2)]
        SGt = _sb(es, nc, "p4_SG", [128, 16, 512], BF16)
        XS = [_sb(es, nc, f"p4_XS{i}", [128, 4, 1024], F32) for i in range(2)]
        MT = _sb(es, nc, "p4_MT", [128, 8, 512], BF16)
        T1 = [_sb(es, nc, f"p4_T1{i}", [128, 512], F32) for i in range(2)]
        T2 = [_sb(es, nc, f"p4_T2{i}", [128, 512], F32) for i in range(2)]
        V = _sb(es, nc, "p4_V", [128, 1024], F32)
        TMP = _sb(es, nc, "p4_TMP", [128, 1024], F32)
        X1O = [_sb(es, nc, f"p4_X1O{i}", [128, 1024], F32) for i in range(2)]
        X1B = [_sb(es, nc, f"p4_X1B{i}", [128, 1024], BF16) for i in range(2)]
        X1TS = _sb(es, nc, "p4_X1TS", [128, 8, 512], BF16)
        LG = _sb(es, nc, "p4_LG", [128, 1024], F32)
        LB = _sb(es, nc, "p4_LB", [128, 1024], F32)
        ST6 = _sb(es, nc, "p4_ST6", [128, 12], F32)
        MV = _sb(es, nc, "p4_MV", [128, 2], F32)
        STD = _sb(es, nc, "p4_STD", [128, 1], F32)
        RSTD = _sb(es, nc, "p4_RSTD", [128, 1], F32)
        EPSB = _sb(es, nc, "p4_EPSB", [128, 1], F32)
        IDF = _sb(es, nc, "p4_IDF", [128, 128], F32)
        IDB = _sb(es, nc, "p4_IDB", [128, 128], BF16)
        JK = _sb(es, nc, "p4_JK", [128, 1], F32)
        PSY = _ps(es, nc, "p4_PSY", [128, 4, 512], F32)
        PSM = _ps(es, nc, "p4_PSM", [128, 2, 512], F32)
        PST = _ps(es, nc, "p4_PST", [128, 8, 128], BF16)
        s_c = P.sem("p4c")
        s_wld = [P.sem(f"p4wld{i}") for i in range(2)]
        s_wcv = P.sem("p4wcv")
        s_ldab = [P.sem(f"p4ldab{i}") for i in range(2)]
        s_ldsg = P.sem("p4ldsg")
        s_ldx = [P.sem(f"p4ldx{i}") for i in range(2)]
        s_y = [P.sem(f"p4y{i}") for i in range(2)]
        s_t = [P.sem(f"p4t{i}") for i in range(2)]
        s_mt = P.sem("p4mt")
        s_mm = [P.sem(f"p4mm{i}") for i in range(2)]
        s_v = [P.sem(f"p4v{i}") for i in range(2)]
        s_ln = P.sem("p4ln")
        s_xb = P.sem("p4xb")
        s_tp = P.sem("p4tp")
        s_tc = P.sem("p4tc")
        s_so = [P.sem(f"p4so{i}") for i in range(2)]
        s_sx = P.sem("p4sx")
        pw = P.pw()

        for (dst, src) in ((LG, D["ln1g"]), (LB, D["ln1b"]), (IDF, D["ident"])):
            vc = P.add("sync", lambda e, dst=dst, src=src: e.dma_start(out=dst[:], in_=src), waits=[pw], inc=s_c, dma=True)
        P.add("vector", lambda e: e.tensor_copy(out=IDB[:], in_=IDF[:]), waits=[pw, (s_c, vc)], inc=s_c)
        vconst = P.add("vector", lambda e: e.memset(EPSB[:], LN_EPS), waits=[pw], inc=s_c)

        P.add("sync", lambda e: e.dma_start(out=WPA[:], in_=D["WAB"][:, 0:4, :]), waits=[pw], inc=s_wcv, dma=True)
        P.add("sync", lambda e: e.dma_start(out=WPB[:], in_=D["WAB"][:, 4:12, :]), waits=[pw], inc=s_wcv, dma=True)
        vw = P.add("sync", lambda e: e.dma_start(out=WO[:], in_=D["WAB"][:, 12:20, :]), waits=[pw], inc=s_wcv, dma=True)

        oat = D["OAT"].rearrange("k p t -> p k t")
        obt = D["OBT"].rearrange("k p t -> p k t")
        sgt = D["SG"].rearrange("k p t -> p k t")
        xo = D["xo"].rearrange("(s p) d -> p s d", p=128)
        x1 = D["X1"].rearrange("(s p) d -> p s d", p=128)
        x1t = D["X1T"].rearrange("k p t -> p k t")

        last_y_pe = {}
        ld_ab = {}

        ld_x = {}
        last_v = {}

        def load_ab(t):
            sl = t % 2
            w = [pw] + (last_y_pe[t - 2] if t >= 2 else [])
            P.add("sync", lambda e: e.dma_start(out=OAt[sl][:], in_=oat[:, :, t * 512:(t + 1) * 512]), waits=w, inc=s_ldab[sl], dma=True)
            ld_ab[t] = P.add("sync", lambda e: e.dma_start(out=OBt[sl][:], in_=obt[:, :, t * 512:(t + 1) * 512]), waits=w,
                             inc=s_ldab[sl], dma=True)
            ld_x[t] = P.add("sync", lambda e: e.dma_start(out=XS[sl][:], in_=xo[:, 4 * t:4 * t + 4, :]),
                            waits=[pw] + (last_v[t - 2] if t >= 2 else []), inc=s_ldx[sl], dma=True)

        ld_sg = {}

        def load_sg(t):
            ld_sg[t] = P.add("sync", lambda e: e.dma_start(out=SGt[:], in_=sgt[:, :, t * 512:(t + 1) * 512]),
                             waits=[pw, (s_t[0], s_t[0].v), (s_t[1], s_t[1].v)], inc=s_ldsg, dma=True)

        load_ab(0)
        load_sg(0)
        yj = 0
        mj = 0
        sbk = 0
        pending_tp = []

        def flush_tp():
            while pending_tp:
                pending_tp.pop(0)()

        for t in range(8):
            sl = t % 2
            if t + 1 < 8:
                load_ab(t + 1)
            v_sg = ld_sg[t]
            v_x = ld_x[t]
            mt_w0 = (s_mm[0], s_mm[0].v), (s_mm[1], s_mm[1].v)
            for dc in range(8):
                ys = yj % 2
                ky = yj // 2

                def ymm(e, ys=ys, dc=dc, sl=sl):
                    ins = None
                    for k in range(4):
                        ins = e.matmul(PSY[:, 2 * ys, :], lhsT=WPA[:, k, dc * 128:(dc + 1) * 128], rhs=OAt[sl][:, k, :],
                                       start=(k == 0), stop=(k == 3))
                    for k in range(8):
                        ins = e.matmul(PSY[:, 2 * ys + 1, :], lhsT=WPB[:, k, dc * 128:(dc + 1) * 128], rhs=OBt[sl][:, k, :],
                                       start=(k == 0), stop=(k == 7))
                    return ins

                P.add("tensor", ymm, waits=[pw, (s_wcv, vw), (s_ldab[sl], ld_ab[t]), (s_t[ys], 2 * ky)], inc=s_y[ys])
                wt = [pw, (s_y[ys], ky + 1), (s_ldsg, v_sg), (s_mt, s_mt.v - 1)]
                P.add("vector", lambda e, ys=ys, dc=dc: e.tensor_tensor(out=T1[ys][:], in0=PSY[:, 2 * ys, :], in1=SGt[:, dc, :], op=ALU.mult),
                      waits=wt, inc=s_t[ys])
                vt = P.add("vector", lambda e, ys=ys, dc=dc: e.tensor_tensor(out=T2[ys][:], in0=PSY[:, 2 * ys + 1, :], in1=SGt[:, 8 + dc, :],
                                                                         op=ALU.mult), waits=wt, inc=s_t[ys])
                P.add("vector", lambda e, ys=ys, dc=dc: e.tensor_tensor(out=MT[:, dc, :], in0=T1[ys][:], in1=T2[ys][:], op=ALU.add),
                      waits=[pw, (s_t[ys], vt)] + list(mt_w0), inc=s_mt)
                yj += 1
                if dc == 1:
                    flush_tp()
            last_y_pe[t] = [(s_y[0], s_y[0].v), (s_y[1], s_y[1].v)]
            if t + 1 < 8:
                load_sg(t + 1)
            v_mt = s_mt.v
            for s in range(4):
                xs_o = sbk % 2
                for hf in range(2):
                    ms = mj % 2
                    km = mj // 2

                    def mmm(e, ms=ms, s=s, hf=hf):
                        ins = None
                        for k in range(8):
                            ins = e.matmul(PSM[:, ms, :], lhsT=MT[:, k, s * 128:(s + 1) * 128], rhs=WO[:, k, hf * 512:(hf + 1) * 512],
                                           start=(k == 0), stop=(k == 7))
                        return ins

                    P.add("tensor", mmm, waits=[pw, (s_mt, v_mt), (s_v[ms], km)], inc=s_mm[ms])
                    vv = P.add("vector", lambda e, ms=ms, s=s, hf=hf, sl=sl: e.scalar_tensor_tensor(
                        out=V[:, hf * 512:(hf + 1) * 512], in0=XS[sl][:, s, hf * 512:(hf + 1) * 512], scalar=ALPHA, in1=PSM[:, ms, :],
                        op0=ALU.mult, op1=ALU.add), waits=[pw, (s_mm[ms], km + 1), (s_ldx[sl], v_x), (s_ln, s_ln.v)], inc=s_v[ms])
                    mj += 1
                flush_tp()
                win = [(s_v[0], s_v[0].v), (s_v[1], s_v[1].v), (s_c, vconst), (s_so[xs_o], 16 * (sbk // 2)), (s_xb, sbk - 1)]
                vln = layer_norm_ops(P, V, ST6, MV, STD, RSTD, TMP, X1O[xs_o][:], LG, LB, EPSB, s_ln, win)
                row = 4 * t + s
                P.add("gpsimd", lambda e, xs_o=xs_o, row=row: e.dma_start(out=x1[:, row, :], in_=X1O[xs_o][:]),
                      waits=[pw, (s_ln, vln)], inc=s_so[xs_o], dma=True)
                vb = P.add("scalar", lambda e, xs_o=xs_o: e.copy(out=X1B[xs_o][:], in_=X1O[xs_o][:]),
                           waits=[pw, (s_ln, vln), (s_tp, sbk - 1)], inc=s_xb)

                def tp_group(t=t, s=s, xs_o=xs_o, vb=vb):
                    def tp(e):
                        ins = None
                        for k in range(8):
                            ins = e.transpose(out=PST[:, k, :], in_=X1B[xs_o][:, k * 128:(k + 1) * 128], identity=IDB[:])
                        return ins

                    vtp = P.add("tensor", tp, waits=[pw, (s_xb, vb), (s_tc, s_tc.v)], inc=s_tp)
                    P.add("scalar", lambda e: e.copy(out=X1TS[:, :, s * 128:(s + 1) * 128], in_=PST[:]),
                          waits=[pw, (s_tp, vtp), (s_sx, 16 * t)], inc=s_tc)
                    if s == 3:
                        P.add("gpsimd", lambda e: e.dma_start(out=x1t[:, :, t * 512:(t + 1) * 512], in_=X1TS[:]),
                              waits=[pw, (s_tc, s_tc.v)], inc=s_sx, dma=True)

                pending_tp.append(tp_group)
                sbk += 1
            last_v[t] = [(s_v[0], s_v[0].v), (s_v[1], s_v[1].v)]
        flush_tp()
        P.end_phase(JK[:])


def phase4b(nc, P, D):
    TT = 256
    NT = NQ // TT
    with ExitStack() as es:
        W1 = _sb(es, nc, "p5_W1", [128, 8, 4096], BF16)
        W2 = _sb(es, nc, "p5_W2", [128, 32, 1024], BF16)
        XT = [_sb(es, nc, f"p5_XT{i}", [128, 8, TT], BF16) for i in range(2)]
        X1 = _sb(es, nc, "p5_X1", [128, 2, 1024], F32)
        HT = _sb(es, nc, "p5_HT", [128, 32, TT], BF16)
        RT = [_sb(es, nc, f"p5_RT{i}", [128, TT], F32) for i in range(2)]
        V = _sb(es, nc, "p5_V", [128, 1024], F32)
        TMP = _sb(es, nc, "p5_TMP", [128, 1024], F32)
        YO = [_sb(es, nc, f"p5_YO{i}", [128, 1024], F32) for i in range(2)]
        LG = _sb(es, nc, "p5_LG", [128, 1024], F32)
        LB = _sb(es, nc, "p5_LB", [128, 1024], F32)
        ST6 = _sb(es, nc, "p5_ST6", [128, 12], F32)
        MV = _sb(es, nc, "p5_MV", [128, 2], F32)
        STD = _sb(es, nc, "p5_STD", [128, 1], F32)
        RSTD = _sb(es, nc, "p5_RSTD", [128, 1], F32)
        EPSB = _sb(es, nc, "p5_EPSB", [128, 1], F32)
        JK = _sb(es, nc, "p5_JK", [128, 1], F32)
        PSH = _ps(es, nc, "p5_PSH", [128, 4, 512], F32)
        PSF = _ps(es, nc, "p5_PSF", [128, 2, 512], F32)
        s_c = P.sem("p5c")
        s_wld = [P.sem(f"p5wld{i}") for i in range(2)]
        s_wcv = P.sem("p5wcv")
        s_ldt = [P.sem(f"p5ldt{i}") for i in range(2)]
        s_ldx = P.sem("p5ldx")
        s_h = [P.sem(f"p5h{i}") for i in range(4)]
        s_r = [P.sem(f"p5r{i}") for i in range(4)]
        s_sq = [P.sem(f"p5sq{i}") for i in range(2)]
        s_f = [P.sem(f"p5f{i}") for i in range(2)]
        s_v = [P.sem(f"p5v{i}") for i in range(2)]
        s_ln = P.sem("p5ln")
        s_so = [P.sem(f"p5so{i}") for i in range(2)]
        pw = P.pw()

        for (dst, src) in ((LG, D["ln2g"]), (LB, D["ln2b"])):
            vc = P.add("sync", lambda e, dst=dst, src=src: e.dma_start(out=dst[:], in_=src), waits=[pw], inc=s_c, dma=True)
        vconst = P.add("vector", lambda e: e.memset(EPSB[:], LN_EPS), waits=[pw, (s_c, vc)], inc=s_c)

        vw1 = None
        for q in range(4):
            vw1 = P.add("sync", lambda e, q=q: e.dma_start(out=W1[:, 2 * q:2 * q + 2, :], in_=D["W1B"][:, 2 * q:2 * q + 2, :]),
                        waits=[pw], inc=s_wld[0], dma=True)
        vw2 = None
        for q in range(4):
            vw2 = P.add("sync", lambda e, q=q: e.dma_start(out=W2[:, 8 * q:8 * q + 8, :], in_=D["W2B"][:, 8 * q:8 * q + 8, :]),
                        waits=[pw], inc=s_wld[1], dma=True)

        x1t = D["X1T"].rearrange("k p t -> p k t")
        x1 = D["X1"].rearrange("(s p) d -> p s d", p=128)
        yv = D["y"].rearrange("(s p) d -> p s d", p=128)
        last_h_pe = {}
        ld_t = {}

        def load_t(t):
            sl = t % 2
            w = [pw] + (last_h_pe[t - 2] if t >= 2 else [])
            ld_t[t] = P.add("sync", lambda e: e.dma_start(out=XT[sl][:], in_=x1t[:, :, t * TT:(t + 1) * TT]), waits=w,
                            inc=s_ldt[sl], dma=True)

        load_t(0)
        hj = 0
        fj = 0
        sbk = 0
        for t in range(NT):
            sl = t % 2
            if t + 1 < NT:
                load_t(t + 1)
            v_x = P.add("sync", lambda e, t=t: e.dma_start(out=X1[:], in_=x1[:, 2 * t:2 * t + 2, :]),
                        waits=[pw, (s_v[0], s_v[0].v), (s_v[1], s_v[1].v)], inc=s_ldx, dma=True)
            ht_w0 = [(s_f[0], s_f[0].v), (s_f[1], s_f[1].v)]
            for fc in range(32):
                hs = hj % 4
                kh = hj // 4
                rs = hj % 2
                kr = hj // 2

                def hmm(e, hs=hs, fc=fc, sl=sl):
                    ins = None
                    for k in range(8):
                        ins = e.matmul(PSH[:, hs, 0:TT], lhsT=W1[:, k, fc * 128:(fc + 1) * 128], rhs=XT[sl][:, k, :],
                                       start=(k == 0), stop=(k == 7))
                    return ins

                P.add("tensor", hmm, waits=[pw, (s_wld[0], vw1), (s_ldt[sl], ld_t[t]), (s_r[hs], kh)], inc=s_h[hs])
                P.add("scalar", lambda e, hs=hs, rs=rs: e.activation(out=RT[rs][:], in_=PSH[:, hs, 0:TT], func=AF.Relu),
                      waits=[pw, (s_h[hs], kh + 1), (s_sq[rs], kr)], inc=s_r[hs])
                P.add("vector",
                      lambda e, rs=rs, fc=fc: e.tensor_tensor(out=HT[:, fc, :], in0=RT[rs][:], in1=RT[rs][:], op=ALU.mult),
                      waits=[pw, (s_r[hs], kh + 1)] + ht_w0, inc=s_sq[rs])
                hj += 1
            last_h_pe[t] = [(s_h[i], s_h[i].v) for i in range(4)]
            v_sq = [(s_sq[0], s_sq[0].v), (s_sq[1], s_sq[1].v)]
            for s in range(2):
                yo = sbk % 2
                for hf in range(2):
                    fs = fj % 2
                    kf = fj // 2

                    def fmm(e, fs=fs, s=s, hf=hf):
                        ins = None
                        for fc in range(32):
                            ins = e.matmul(PSF[:, fs, :], lhsT=HT[:, fc, s * 128:(s + 1) * 128], rhs=W2[:, fc, hf * 512:(hf + 1) * 512],
                                           start=(fc == 0), stop=(fc == 31))
                        return ins

                    P.add("tensor", fmm, waits=[pw, (s_wld[1], vw2), (s_v[fs], kf)] + v_sq, inc=s_f[fs])
                    P.add("vector", lambda e, fs=fs, s=s, hf=hf: e.scalar_tensor_tensor(
                        out=V[:, hf * 512:(hf + 1) * 512], in0=X1[:, s, hf * 512:(hf + 1) * 512], scalar=ALPHA, in1=PSF[:, fs, :],
                        op0=ALU.mult, op1=ALU.add), waits=[pw, (s_f[fs], kf + 1), (s_ldx, v_x), (s_ln, s_ln.v)], inc=s_v[fs])
                    fj += 1
                win = [(s_v[0], s_v[0].v), (s_v[1], s_v[1].v), (s_c, vconst), (s_so[yo], 16 * (sbk // 2))]
                vln = layer_norm_ops(P, V, ST6, MV, STD, RSTD, TMP, YO[yo][:], LG, LB, EPSB, s_ln, win)
                row = 2 * t + s
                P.add("sync", lambda e, yo=yo, row=row: e.dma_start(out=yv[:, row, :], in_=YO[yo][:]),
                      waits=[pw, (s_ln, vln)], inc=s_so[yo], dma=True)
                sbk += 1
        P.end_phase(JK[:])


def build_nc(debug=False, phases=(1, 2, 3, 4, 5)):
    nc = bass.Bass("TRN2", target_bir_lowering=False)
    D = {}

    def din(name, shape):
        D[name] = nc.dram_tensor(name, shape, F32, kind="ExternalInput").ap()

    din("xT", [1024, NK]); din("xo", [NQ, 1024]); din("wp", [1024, COLS_IN]); din("bg", [128, 16])
    din("ga", [24, 128, 256]); din("ma", [128, 256]); din("gb", [8, 128, 256]); din("cb", [128, 8]); din("mb", [128, 256])
    din("ident", [128, 128]); din("pvalid", [128, 64]); din("lam", [128, 4, 64]); din("gs", [128, 128])
    din("ln1g", [128, 1024]); din("ln1b", [128, 1024]); din("ln2g", [128, 1024]); din("ln2b", [128, 1024])
    din("wpa", [512, 1024]); din("wpb", [1024, 1024]); din("wo", [1024, 1024]); din("w1", [1024, 4096]); din("w2", [4096, 1024])
    D["y"] = nc.dram_tensor("y", [NQ, 1024], F32, kind="ExternalOutput").ap()
    kind = "ExternalOutput" if debug else "Internal"

    def scr(name, shape, dt):
        D[name] = nc.dram_tensor(name, shape, dt, kind=kind).ap()

    scr("QTB", [8, 128, NQ], BF16); scr("KTB", [8, 128, NK], BF16); scr("VB", [NK, 1024], BF16)
    scr("QTA", [12, 128, NQ], BF16); scr("KTA", [12, 128, NKA], BF16); scr("VA", [NKA, 1536], BF16)
    scr("SG", [16, 128, NQ], BF16); scr("OAT", [4, 128, NQ], BF16); scr("OBT", [8, 128, NQ], BF16)
    scr("X1", [NQ, 1024], F32); scr("X1T", [8, 128, NQ], BF16)
    scr("WAB", [128, 20, 1024], BF16); scr("W1B", [128, 8, 4096], BF16); scr("W2B", [128, 32, 1024], BF16)
    with ExitStack() as es:
        P = Prog(nc, es)
        if 1 in phases:
            phase1(nc, P, D)
        if 2 in phases:
            phase2(nc, P, D)
        if 3 in phases:
            phase3(nc, P, D)
        if 4 in phases:
            phase4a(nc, P, D)
        if 5 in phases:
            phase4b(nc, P, D)
    return nc


def t5_bucket_np(dist):
    n = np.maximum(dist, 0)
    nf = np.maximum(n, 1).astype(np.float32)
    large = 16 + (np.log(nf / np.float32(16.0)) / np.float32(math.log(8.0)) * np.float32(16.0)).astype(np.int32)
    large = np.minimum(large, 31)
    return np.where(n < 16, n, large)


def prep_shared(inp):
    f = lambda a: np.ascontiguousarray(np.asarray(a, dtype=np.float32))
    w = np.asarray(inp["w_in"], dtype=np.float32)[0]
    B0 = 4608
    cols = []
    for h in range(8):
        cols += list(range(B0 + h * 64, B0 + h * 64 + 64)) + list(range(B0 + 512 + h * 64, B0 + 512 + h * 64 + 64))
    for h in range(8):
        cols += list(range(B0 + 1024 + h * 64, B0 + 1024 + h * 64 + 64)) + list(range(B0 + 1536 + h * 64, B0 + 1536 + h * 64 + 64))
    cols += list(range(0, 1536)) + list(range(1536, 3072)) + list(range(7680, 9728))
    cols += list(range(B0 + 2048, B0 + 3072)) + list(range(3072, 4608))
    assert len(cols) == COLS_IN
    sh = {"wp": f(w[:, cols])}
    sh["bg"] = f(np.asarray(inp["b_gate"], np.float32)[0].reshape(16, 128).T)
    rb = np.asarray(inp["rel_bias"], np.float32)
    i = np.arange(128)[:, None]
    c = np.arange(256)[None, :]
    ch = c // 128
    j = c % 128
    steps = (1 - ch) * 128 + j - i
    ga = np.zeros((24, 128, 256), np.float32)
    for g, d in enumerate(DILS):
        bk = t5_bucket_np(np.maximum(steps, 0) * d)
        for h in range(8):
            ga[g * 8 + h] = rb[bk, g * 8 + h]
    sh["ga"] = ga
    sh["ma"] = f(np.where((steps >= 0) & (steps <= 128), 0.0, MASKV / 8.0))
    dist = c - i
    bk = t5_bucket_np(np.maximum(dist, 0))
    gb = np.zeros((8, 128, 256), np.float32)
    for h in range(8):
        gb[h] = rb[bk, 24 + h]
    sh["gb"] = gb
    sh["cb"] = f(np.broadcast_to(rb[31, 24:32][None, :], (128, 8)))
    sh["mb"] = f(np.where(dist >= 0, 0.0, MASKV))
    sh["ident"] = np.eye(128, dtype=np.float32)
    lam = np.stack([np.asarray(inp[k], np.float32)[0] for k in ("lambda_q1", "lambda_q2", "lambda_k1", "lambda_k2")])
    sh["lam"] = f(np.broadcast_to(lam[None], (128, 4, 64)))
    sh["gs"] = f(np.broadcast_to(np.asarray(inp["subln_g"], np.float32)[0][None, :], (128, 128)))
    for k, n in (("ln1g", "ln1_g"), ("ln1b", "ln1_b"), ("ln2g", "ln2_g"), ("ln2b", "ln2_b")):
        sh[k] = f(np.broadcast_to(np.asarray(inp[n], np.float32)[0][None, :], (128, 1024)))
    sh["wpa"] = f(np.asarray(inp["w_proj_a"])[0]); sh["wpb"] = f(np.asarray(inp["w_proj_b"])[0])
    sh["wo"] = f(np.asarray(inp["w_out"])[0]); sh["w1"] = f(np.asarray(inp["w_mlp1"])[0]); sh["w2"] = f(np.asarray(inp["w_mlp2"])[0])
    return sh


def prep_core(x, c):
    b, par = c // 2, c % 2
    xb = np.asarray(x[b], dtype=np.float32)
    xl = np.zeros((NK, 1024), np.float32)
    if par == 1:
        xl[:] = xb
    else:
        xl[2048:] = xb[0:NK - 2048]
    own = np.concatenate([xl[2048:4096], xl[6144:8192]], axis=0)
    pv = np.full((128, 64), float(par), np.float32)
    return {"xT": np.ascontiguousarray(xl.T), "xo": np.ascontiguousarray(own), "pvalid": pv}


_NC_CACHE = {}


def kernel(**inputs):
    x = np.asarray(inputs["x"], dtype=np.float32)
    sh = prep_shared(inputs)
    in_maps = []
    for c in range(8):
        m = dict(sh)
        m.update(prep_core(x, c))
        in_maps.append(m)
    if "nc" not in _NC_CACHE:
        _NC_CACHE["nc"] = build_nc()
    res = run_bass_kernel_spmd(_NC_CACHE["nc"], in_maps, core_ids=list(range(8)))
    out = np.zeros((BATCH, SEQ, D_MODEL), np.float32)
    for c in range(8):
        b, par = c // 2, c % 2
        y = np.asarray(res.results[c]["y"], dtype=np.float32)
        r0 = 2048 if par == 1 else 0
        out[b, r0:r0 + 2048] = y[0:2048]
        out[b, r0 + 4096:r0 + 6144] = y[2048:4096]
    return out
```
